# Optimizing a Trainium2 kernel written in Bass

```python
import jax, jax.numpy as jnp
from jax import lax
import numpy as np

D_MODEL = 1024
BATCH = 8
SEQ = 2048
DEPTH = 1
DEC_BATCH = 128
DEC_SEQ = 8
PAST_LEN = 16384
PAGE_SIZE = 128

D_MIX = D_MODEL
POOL_WIDTH = D_MIX // 4
POOL_WINDOWS = (2, 4, 8, 16)
POOL_GROUP = POOL_WIDTH // len(POOL_WINDOWS)
POOL_HIST = max(POOL_WINDOWS) - 1
HGRN_WIDTH = D_MIX // 2
HGRN_EXPAND = 128
HGRN_HEADS = HGRN_WIDTH // HGRN_EXPAND
HGRN_DK = HGRN_EXPAND
HGRN_DV = HGRN_WIDTH // HGRN_HEADS
HGRN_FDIM = HGRN_HEADS * HGRN_DK
HGRN_CHUNK = 64
XATTN_WIDTH = D_MIX - POOL_WIDTH - HGRN_WIDTH
XATTN_HEADS = 4
XATTN_DH = XATTN_WIDTH // XATTN_HEADS
N_MEM = 256
D_FF = 2816
CONV_W = 3
EPS = 1e-6
SPLITS = (POOL_WIDTH,
          POOL_WIDTH + HGRN_FDIM,
          POOL_WIDTH + 2 * HGRN_FDIM,
          POOL_WIDTH + 2 * HGRN_FDIM + HGRN_WIDTH,
          POOL_WIDTH + 2 * HGRN_FDIM + 2 * HGRN_WIDTH)
D_IN = POOL_WIDTH + 2 * HGRN_FDIM + 2 * HGRN_WIDTH + XATTN_WIDTH

kernel_name = "hymba_pool_hgrn2_memxattn_convffn_step"


def rmsnorm(x, g):
    xf = x.astype(jnp.float32)
    y = xf * lax.rsqrt(jnp.mean(xf * xf, axis=-1, keepdims=True) + EPS)
    return (y * g.astype(jnp.float32)).astype(x.dtype)


def pool_mixer(u, hist, pos0, pool_w, pool_scale):
    B, L, _ = u.shape
    xp = jnp.concatenate([hist.astype(jnp.float32), u.astype(jnp.float32)], axis=1)
    cs = jnp.concatenate([jnp.zeros((B, 1, POOL_WIDTH), jnp.float32), jnp.cumsum(xp, axis=1)], axis=1)
    pos = (pos0 + jnp.arange(L)).astype(jnp.float32)
    uf = u.astype(jnp.float32)
    outs = []
    for gi, w in enumerate(POOL_WINDOWS):
        sl = slice(gi * POOL_GROUP, (gi + 1) * POOL_GROUP)
        total = cs[:, POOL_HIST + 1:POOL_HIST + 1 + L, sl] - cs[:, POOL_HIST + 1 - w:POOL_HIST + 1 - w + L, sl]
        cnt = jnp.minimum(jnp.float32(w), pos + 1.0)
        mean = total / cnt[None, :, None]
        outs.append(jnp.einsum("blc,cd->bld", mean - uf[..., sl], pool_w[gi].astype(jnp.float32)))
    out = jnp.concatenate(outs, axis=-1) * pool_scale.astype(jnp.float32)
    new_hist = xp[:, -POOL_HIST:].astype(hist.dtype)
    return out.astype(u.dtype), new_hist


def hgrn2_scan(q, k, v, log_f, S0):
    B, L, H, DK = q.shape
    DV = v.shape[-1]
    C = min(HGRN_CHUNK, L)
    n = -(-L // C)
    pad = n * C - L

    def blocks(a):
        a = jnp.pad(a, ((0, 0), (0, pad), (0, 0), (0, 0)))
        return a.reshape(B, n, C, H, a.shape[-1]).transpose(1, 0, 3, 2, 4)

    mask = jnp.tril(jnp.ones((C, C), bool))[:, :, None]

    def step(S, blk):
        qc, kc, vc, gc = blk
        A = jnp.cumsum(gc, axis=2)
        decay = jnp.exp(jnp.where(mask, A[:, :, :, None, :] - A[:, :, None, :, :], -jnp.inf))
        scores = jnp.einsum("bhtd,bhsd,bhtsd->bhts", qc, kc, decay)
        o = jnp.einsum("bhts,bhsv->bhtv", scores, vc) + jnp.einsum("bhtd,bhdv->bhtv", qc * jnp.exp(A), S)
        A_end = A[:, :, -1:, :]
        S = jnp.exp(A_end[:, :, 0, :])[..., None] * S + jnp.einsum("bhsd,bhsv->bhdv", kc * jnp.exp(A_end - A), vc)
        return S, o

    S, o = lax.scan(step, S0, (blocks(q), blocks(k), blocks(v), blocks(log_f)))
    o = o.transpose(1, 0, 3, 2, 4).reshape(B, n * C, H, DV)[:, :L]
    return o, S


def hgrn2_mixer(q, fp, i, g, S0, lb, onorm_g):
    B, L, _ = q.shape
    lbf = lb.astype(jnp.float32)
    fpf = fp.astype(jnp.float32)
    log_f = jnp.log(lbf + (1.0 - lbf) * jax.nn.sigmoid(fpf))
    k = (1.0 - lbf) * jax.nn.sigmoid(-fpf)
    qf = jax.nn.silu(q.astype(jnp.float32))
    heads = lambda a, d: a.reshape(B, L, HGRN_HEADS, d)
    o, S = hgrn2_scan(heads(qf, HGRN_DK), heads(k, HGRN_DK), heads(i.astype(jnp.float32), HGRN_DV),
                      heads(log_f, HGRN_DK), S0.astype(jnp.float32))
    o = o * lax.rsqrt(jnp.mean(o * o, axis=-1, keepdims=True) + EPS)
    o = o.reshape(B, L, HGRN_WIDTH) * onorm_g.astype(jnp.float32) * jax.nn.silu(g.astype(jnp.float32))
    return o.astype(q.dtype), S.astype(S0.dtype)


def cross_attn(qx, mem_k, mem_v):
    B, L, _ = qx.shape
    q = qx.reshape(B, L, XATTN_HEADS, XATTN_DH).astype(jnp.float32)
    s = jnp.einsum("blhd,bmhd->bhlm", q, mem_k.astype(jnp.float32)) * (XATTN_DH ** -0.5)
    p = jax.nn.softmax(s, axis=-1)
    o = jnp.einsum("bhlm,bmhd->blhd", p, mem_v.astype(jnp.float32))
    return o.reshape(B, L, XATTN_WIDTH).astype(qx.dtype)


def conv_ffn(x, hist, ln2_g, w_up, conv_w, conv_b, w_down):
    B, L, _ = x.shape
    h = rmsnorm(x, ln2_g)
    ab = h @ w_up
    a, b = ab[..., :D_FF], ab[..., D_FF:]
    ap = jnp.concatenate([hist.astype(a.dtype), a], axis=1)
    conv = conv_b
    for j in range(CONV_W):
        conv = conv + conv_w[j] * ap[:, j:j + L]
    out = (jax.nn.gelu(conv) * b) @ w_down
    return out, ap[:, -(CONV_W - 1):].astype(hist.dtype)


def trunk_layer(x, mem_k, mem_v, pool_hist, pos0, S0, conv_hist, lb,
                ln1_g, w_in, pool_w, pool_scale, onorm_g, w_out, ln2_g, w_up, conv_w, conv_b, w_down):
    h = rmsnorm(x, ln1_g)
    proj = h @ w_in
    u, q, fp, i, g, qx = jnp.split(proj, SPLITS, axis=-1)
    o_pool, new_pool = pool_mixer(u, pool_hist, pos0, pool_w, pool_scale)
    o_hgrn, new_S = hgrn2_mixer(q, fp, i, g, S0, lb, onorm_g)
    o_x = cross_attn(qx, mem_k, mem_v)
    x = x + jnp.concatenate([o_pool, o_hgrn, o_x], axis=-1) @ w_out
    f_out, new_conv = conv_ffn(x, conv_hist, ln2_g, w_up, conv_w, conv_b, w_down)
    x = x + f_out
    return x, new_pool, new_S, new_conv


def setup_inputs(seed: int = 0) -> dict:
    key = jax.random.key(seed)
    ks = jax.random.split(key, 32)
    nrm = lambda k, shape, s: jax.random.normal(k, shape, jnp.float32) * s
    return {
        "x_prompt": nrm(ks[0], (BATCH, SEQ, D_MODEL), 1.0),
        "x_sample": nrm(ks[1], (DEC_BATCH, DEC_SEQ, D_MODEL), 1.0),
        "mem_prompt": nrm(ks[2], (BATCH, N_MEM, D_MODEL), 1.0),
        "state_pool": nrm(ks[3], (DEPTH, DEC_BATCH, POOL_HIST, POOL_WIDTH), 1.0),
        "state_hgrn": nrm(ks[4], (DEPTH, DEC_BATCH, HGRN_HEADS, HGRN_DK, HGRN_DV), 0.5),
        "state_conv": nrm(ks[5], (DEPTH, DEC_BATCH, CONV_W - 1, D_FF), 1.0),
        "cache_mem_k": nrm(ks[6], (DEPTH, DEC_BATCH, N_MEM, XATTN_HEADS, XATTN_DH), 1.0),
        "cache_mem_v": nrm(ks[7], (DEPTH, DEC_BATCH, N_MEM, XATTN_HEADS, XATTN_DH), 1.0),
        "ln1_g": 1.0 + nrm(ks[8], (DEPTH, D_MODEL), 0.02),
        "w_in": nrm(ks[9], (DEPTH, D_MODEL, D_IN), D_MODEL ** -0.5),
        "pool_w": nrm(ks[10], (DEPTH, len(POOL_WINDOWS), POOL_GROUP, POOL_GROUP), POOL_GROUP ** -0.5),
        "pool_scale": 1.0 + nrm(ks[11], (DEPTH, POOL_WIDTH), 0.02),
        "hgrn_lb_logits": nrm(ks[12], (DEPTH + 1, HGRN_FDIM), 0.1),
        "hgrn_onorm_g": 1.0 + nrm(ks[13], (DEPTH, HGRN_WIDTH), 0.02),
        "mem_norm_g": 1.0 + nrm(ks[14], (DEPTH, D_MODEL), 0.02),
        "w_mem_kv": nrm(ks[15], (DEPTH, D_MODEL, 2 * XATTN_WIDTH), D_MODEL ** -0.5),
        "w_out": nrm(ks[16], (DEPTH, D_MIX, D_MODEL), D_MIX ** -0.5),
        "ln2_g": 1.0 + nrm(ks[17], (DEPTH, D_MODEL), 0.02),
        "w_up": nrm(ks[18], (DEPTH, D_MODEL, 2 * D_FF), D_MODEL ** -0.5),
        "conv_w": nrm(ks[19], (DEPTH, CONV_W, D_FF), CONV_W ** -0.5),
        "conv_b": nrm(ks[20], (DEPTH, D_FF), 0.02),
        "w_down": nrm(ks[21], (DEPTH, D_FF, D_MODEL), D_FF ** -0.5),
        "lnf_g": 1.0 + nrm(ks[22], (D_MODEL,), 0.02),
    }


def reference(x_prompt, x_sample, mem_prompt, state_pool, state_hgrn, state_conv, cache_mem_k, cache_mem_v,
              ln1_g, w_in, pool_w, pool_scale, hgrn_lb_logits, hgrn_onorm_g, mem_norm_g, w_mem_kv, w_out,
              ln2_g, w_up, conv_w, conv_b, w_down, lnf_g):
    lb_all = jnp.cumsum(jax.nn.softmax(hgrn_lb_logits.astype(jnp.float32), axis=0), axis=0)
    yp, ys = x_prompt, x_sample
    pp_l, sp_l, cp_l, mk_l, mv_l, ps_l, ss_l, cs_l = [], [], [], [], [], [], [], []
    dt = x_prompt.dtype
    for l in range(DEPTH):
        weights = (ln1_g[l], w_in[l], pool_w[l], pool_scale[l], hgrn_onorm_g[l], w_out[l],
                   ln2_g[l], w_up[l], conv_w[l], conv_b[l], w_down[l])
        kv = rmsnorm(mem_prompt, mem_norm_g[l]) @ w_mem_kv[l]
        mk = kv[..., :XATTN_WIDTH].reshape(BATCH, N_MEM, XATTN_HEADS, XATTN_DH)
        mv = kv[..., XATTN_WIDTH:].reshape(BATCH, N_MEM, XATTN_HEADS, XATTN_DH)
        yp, pp, sp, cp = trunk_layer(
            yp, mk, mv, jnp.zeros((BATCH, POOL_HIST, POOL_WIDTH), dt), 0,
            jnp.zeros((BATCH, HGRN_HEADS, HGRN_DK, HGRN_DV), dt),
            jnp.zeros((BATCH, CONV_W - 1, D_FF), dt), lb_all[l], *weights)
        ys, ps, ss, cs = trunk_layer(
            ys, cache_mem_k[l], cache_mem_v[l], state_pool[l], PAST_LEN, state_hgrn[l], state_conv[l],
            lb_all[l], *weights)
        pp_l.append(pp); sp_l.append(sp); cp_l.append(cp); mk_l.append(mk); mv_l.append(mv)
        ps_l.append(ps); ss_l.append(ss); cs_l.append(cs)
    y_prompt = rmsnorm(yp, lnf_g)
    y_sample = rmsnorm(ys, lnf_g)
    return (y_prompt, y_sample,
            jnp.stack(pp_l), jnp.stack(sp_l), jnp.stack(cp_l), jnp.stack(mk_l), jnp.stack(mv_l),
            jnp.stack(ps_l), jnp.stack(ss_l), jnp.stack(cs_l))
```

```python
import numpy as np
from contextlib import ExitStack
import concourse.bass as bass
import concourse.mybir as mybir
from concourse.bass_utils import run_bass_kernel_spmd

F32, BF16 = mybir.dt.float32, mybir.dt.bfloat16
AF = mybir.ActivationFunctionType
ALU = mybir.AluOpType
EPS = 1e-6
NSLOT = 4
SAME_ENGINE_SYNC = True


import os


class StopBuild(Exception):
    pass


_hits = {}


def chk(name):
    if os.environ.get('KSTOP') == name:
        _hits[name] = _hits.get(name, 0) + 1
        if _hits[name] == int(os.environ.get('KHIT', '1')):
            raise StopBuild(name)


class Sched:
    def __init__(s, nc, es):
        s.nc, s.es = nc, es
        s.E = {'pe': nc.tensor, 'act': nc.scalar, 'dve': nc.vector, 'pool': nc.gpsimd, 'sp': nc.sync}
        s.sem = {k: es.enter_context(nc.semaphore('sem_' + k)) for k in s.E}
        s.cnt = {k: 0 for k in s.E}
        s.seen = {k: {} for k in s.E}
        s.lastw, s.reads, s.dsem = {}, {}, {}
        s.psn = 0
        s.ps_open = {}

    def _semh(s, key):
        return s.sem[key] if key in s.sem else s.dsem[key][0]

    def _wait(s, eng, key, val):
        if key == eng and (eng in ('pe', 'sp') or not SAME_ENGINE_SYNC):
            return
        if s.seen[eng].get(key, 0) >= val:
            return
        s.seen[eng][key] = val
        s.E[eng].wait_ge(s._semh(key), val)

    def deps(s, eng, reads, writes):
        need = {}
        for b in reads:
            if b in s.lastw:
                k, v = s.lastw[b]
                need[k] = max(need.get(k, 0), v)
        for b in writes:
            if b in s.lastw:
                k, v = s.lastw[b]
                need[k] = max(need.get(k, 0), v)
            for (k, v) in s.reads.get(b, ()):
                need[k] = max(need.get(k, 0), v)
        for k, v in need.items():
            s._wait(eng, k, v)

    def _record(s, tok, reads, writes):
        for b in reads:
            s.reads.setdefault(b, []).append(tok)
        for b in writes:
            s.lastw[b] = tok
            s.reads[b] = []

    def op(s, eng, fn, reads=(), writes=(), inc=True):
        psr = [b for b in reads if isinstance(b, tuple) and b[0] == 'ps']
        if psr:
            reads = [b for b in reads if b not in psr]
            writes = list(writes) + psr
            for b in psr:
                s.ps_open[b[1]] -= 1
                if s.ps_open[b[1]] <= 0:
                    del s.ps_open[b[1]]
        s.deps(eng, reads, writes)
        ins = fn(s.E[eng])
        if inc:
            s.cnt[eng] += 1
            ins.then_inc(s.sem[eng], 1)
            tok = (eng, s.cnt[eng])
        else:
            tok = (eng, s.cnt[eng] + 1)
        s._record(tok, reads, writes)
        return ins

    def dma(s, eng, chan, out, in_, reads=(), writes=(), **kw):
        s.deps(eng, reads, writes)
        if chan not in s.dsem:
            s.dsem[chan] = [s.es.enter_context(s.nc.semaphore('d_' + chan)), 0]
        ins = s.E[eng].dma_start(out=out, in_=in_, **kw)
        s.dsem[chan][1] += 16
        ins.then_inc(s.dsem[chan][0], 16)
        s._record((chan, s.dsem[chan][1]), reads, writes)

    def barrier(s):
        for eng in s.E:
            for k in s.sem:
                if s.cnt[k] > 0:
                    s._wait(eng, k, s.cnt[k])
            for k in s.dsem:
                s._wait(eng, k, s.dsem[k][1])

    def final(s):
        for k in s.sem:
            if s.cnt[k] > 0:
                s._wait('sp', k, s.cnt[k])
        for k in s.dsem:
            s._wait('sp', k, s.dsem[k][1])


def make_consts():
    c = np.zeros((128, 576), np.float32)
    c[:, 0:128] = np.eye(128, dtype=np.float32)
    s = np.arange(128)[:, None]
    t = np.arange(128)[None, :]
    c[:, 128:256] = (t >= s)
    c[:, 256:384] = (t >= s) & ((t // 8) == (s // 8))
    c[:, 384:400] = (np.arange(128)[:, None] // 8) == np.arange(16)[None, :]
    c[:, 400:528] = (np.arange(128)[None, :] % 8 != 0)
    for ch in range(2):
        for p in range(128):
            w = [2, 4, 8, 16][2 * ch + p // 64]
            c[p, 528 + ch * 16: 528 + ch * 16 + 16] = 1.0 / np.minimum(w, np.arange(16) + 1.0)
            c[p, 560 + ch] = 1.0 / w
    return c


def build():
    nc = bass.Bass("TRN2", target_bir_lowering=False)
    D = lambda n, sh, k="ExternalInput": nc.dram_tensor(n, sh, F32, kind=k).ap()
    x_tok = D("x_tok", [2176, 1024]); mem = D("mem", [256, 1024])
    spool = D("spool", [240, 256]); shgrn = D("shgrn", [16, 4, 128, 128]); sconv = D("sconv", [32, 2816])
    ck = D("ck", [16, 256, 256]); cv = D("cv", [16, 256, 256])
    lnf = D("lnf", [1024])
    w_in = D("w_in", [1024, 2560]); w_kv = D("w_kv", [1024, 512]); w_out = D("w_out", [1024, 1024])
    w_up = D("w_up", [1024, 5632]); w_dn = D("w_dn", [2816, 1024])
    pool_w = D("pool_w", [4, 64, 64]); prm_in = D("prm_in", [126, 128]); cst_in = D("cst", [128, 576])
    O = lambda n, sh: D(n, sh, "ExternalOutput")
    y_tok = O("y_tok", [2176, 1024]); o_pool_p = O("o_pool_p", [15, 256]); o_hgrn_p = O("o_hgrn_p", [4, 128, 128])
    o_conv_p = O("o_conv_p", [2, 2816]); o_mk = O("o_mk", [256, 256]); o_mv = O("o_mv", [256, 256])
    o_pool_s = O("o_pool_s", [16, 15, 256]); o_hgrn_s = O("o_hgrn_s", [16, 4, 128, 128]); o_conv_s = O("o_conv_s", [32, 2816])

    with ExitStack() as es:
        S = Sched(nc, es)
        uid = [0]
        def sbt(st, n, sh, dt=F32):
            uid[0] += 1
            return st.enter_context(nc.sbuf_tensor("sb%d_%s" % (uid[0], n), sh, dt))
        psb = [es.enter_context(nc.psum_tensor("ps%d" % i, [128, 512], F32)) for i in range(8)]

        def PS(n=1):
            for _ in range(8):
                i = S.psn % 8
                S.psn += 1
                if i not in S.ps_open:
                    break
            else:
                raise RuntimeError('no free PSUM bank')
            S.ps_open[i] = n
            return psb[i], ('ps', i)

        cst = sbt(es, "cst", [128, 576]); prm = sbt(es, "prm", [128, 128]); prm_st = sbt(es, "prm_st", [126, 128])
        idb = sbt(es, "idb", [128, 128], BF16); onesb = sbt(es, "onesb", [128, 128], BF16)
        gf = sbt(es, "gf", [128, 1024])
        wbd = sbt(es, "wbd", [128, 2, 128], BF16)
        lbc = sbt(es, "lbc", [128, 16])
        small = sbt(es, "small", [128, 8])
        ring = [sbt(es, "ring%d" % i, [128, 5632], BF16) for i in range(NSLOT)]
        G = {}
        stat = sbt(es, "stat", [128, 32])
        ident = cst[:, 0:128]; cmask = cst[:, 128:256]; smask = cst[:, 256:384]; seqm = cst[:, 384:400]
        rmask = cst[:, 400:528]; invw = cst[:, 560:562]
        invcnt = cst[:, 528:560]
        epsc = small[:, 0:1]; mhalf = small[:, 1:2]; onec = small[:, 2:3]
        cw = lambda r, j: prm[:, r * 22 + j: r * 22 + j + 1]
        cb = lambda j: prm[:, 66 + j: 67 + j]
        pscale = lambda c: prm[:, 96 + c: 97 + c]
        onorm = lambda h: prm[:, 98 + h: 99 + h]

        wseq = []

        def ld_in(b):
            def f(slot, key, chan):
                S.dma('pool', chan, slot[:, 0:4096].rearrange("p (k n) -> p k n", k=8),
                      w_in[:, b * 512:(b + 1) * 512].rearrange("(k p) n -> p k n", p=128), writes=[key])
            return f

        def ld_kv():
            def f(slot, key, chan):
                S.dma('pool', chan, slot[:, 0:4096].rearrange("p (k n) -> p k n", k=8),
                      w_kv.rearrange("(k p) n -> p k n", p=128), writes=[key])
            return f

        def ld_out(c):
            def f(slot, key, chan):
                S.dma('pool', chan, slot[:, 0:4096].rearrange("p (k n) -> p k n", k=8),
                      w_out[:, c * 512:(c + 1) * 512].rearrange("(k p) n -> p k n", p=128), writes=[key])
            return f

        def ld_up(r):
            def f(slot, key, chan):
                v = slot[:, 0:4096].rearrange("p (k n) -> p k n", k=8)
                S.dma('pool', chan, v[:, :, 0:256],
                      w_up[:, r * 256:(r + 1) * 256].rearrange("(k p) n -> p k n", p=128), writes=[key])
                S.dma('pool', chan, v[:, :, 256:512],
                      w_up[:, 2816 + r * 256:2816 + (r + 1) * 256].rearrange("(k p) n -> p k n", p=128), writes=[key])
            return f

        def ld_dn(q):
            def f(slot, key, chan):
                S.dma('pool', chan, slot[:, 0:5632].rearrange("p (k n) -> p k n", k=22),
                      w_dn[:, q * 256:(q + 1) * 256].rearrange("(k p) n -> p k n", p=128), writes=[key])
            return f

        wscr = nc.dram_tensor("wscr", [22, 128, 5632], BF16, kind="Internal").ap()
        blocks = [('in', b) for b in range(5)] + [('out', c) for c in range(2)] + [('up', r) for r in range(11)] + [('dn', q) for q in range(4)]
        mk = {'in': ld_in, 'out': ld_out, 'up': ld_up, 'dn': ld_dn}
        for ti in range(5):
            for bi, (kind, idx) in enumerate(blocks):
                n = 5632 if kind == 'dn' else 4096
                if ti == 0:
                    wseq.append((mk[kind](idx), bi, n))
                    if bi == 2:
                        wseq.append((ld_kv(), None, 0))
                else:
                    def f(slot, key, chan, bi=bi, n=n):
                        S.dma('pool', chan, slot[:, 0:n], wscr[bi, :, 0:n], reads=[('scr', bi)], writes=[key])
                    wseq.append((f, None, n))
        wstate = {'issued': 0, 'next': 0}

        def wneed(prefetch=True):
            i = wstate['next']
            wstate['next'] += 1
            upto = min(i + NSLOT - 1, len(wseq) - 1) if prefetch else i
            while wstate['issued'] <= upto:
                j = wstate['issued']
                wseq[j][0](ring[j % NSLOT], ('w', j % NSLOT), 'w%d' % (j % NSLOT))
                wstate['issued'] += 1
            sl = ring[i % NSLOT]
            bi, n = wseq[i][1], wseq[i][2]
            if bi is not None:
                S.dma('sp', 'wb%d' % (i % NSLOT), wscr[bi, :, 0:n], sl[:, 0:n], reads=[('w', i % NSLOT)], writes=[('scr', bi)])
            return sl, ('w', i % NSLOT)

        def w8(sl):
            return sl[:, 0:4096].rearrange("p (k n) -> p k n", k=8)

        def w22(sl):
            return sl[:, 0:5632].rearrange("p (k n) -> p k n", k=22)

        def mmg(out_ap, pskey, pairs, reads, per=None):
            n = len(pairs)
            for i, (l, r) in enumerate(pairs):
                S.op('pe', lambda e, l=l, r=r, i=i: e.matmul(out_ap, lhsT=l, rhs=r, start=(i == 0), stop=(i == n - 1)),
                     reads=list(reads) + (list(per[i]) if per else []), writes=[pskey], inc=(i == n - 1))

        pst = ExitStack(); sst = ExitStack()
        es.enter_context(pst); es.enter_context(sst)
        try:
            S.dma('sp', 'c0', cst[:], cst_in[:, :], writes=['cst'])
            S.dma('sp', 'c1', prm_st[:], prm_in[:, :], writes=['prm_st'])
            S.dma('sp', 'c4', gf[:], lnf.partition_broadcast(128), writes=['gf'])
            S.op('dve', lambda e: e.memset(small[:, 0:1], EPS), writes=['small'])
            S.op('dve', lambda e: e.memset(small[:, 1:2], -0.5), writes=['small'])
            S.op('dve', lambda e: e.memset(small[:, 2:3], 1.0), writes=['small'])
            S.op('dve', lambda e: e.memset(onesb[:], 1.0), writes=['onesb'])
            S.op('dve', lambda e: e.tensor_copy(out=idb[:], in_=ident), reads=['cst'], writes=['idb'])
            S.op('pool', lambda e: e.memset(wbd[:], 0.0), writes=['wbd'])
            for gi in range(4):
                c, o = gi // 2, (gi % 2) * 64
                S.dma('pool', 'c5', wbd[o:o + 64, c, o:o + 64], pool_w[gi, :, :], writes=['wbd'])
            p_, pk = PS()
            S.op('pe', lambda e: e.transpose(out=p_[:, 0:126], in_=prm_st[:, :], identity=cst[0:126, 0:126]),
                 reads=['prm_st', 'cst'], writes=[pk])
            S.op('dve', lambda e: e.tensor_copy(out=prm[:, 0:126], in_=p_[:, 0:126]), reads=[pk], writes=['prm'])
            S.op('dve', lambda e: e.tensor_sub(out=lbc[:, 8:12], in0=prm[:, 88:92], in1=prm[:, 92:96]), reads=['prm'], writes=['lbc'])
            S.op('act', lambda e: e.activation(out=lbc[:, 8:12], in_=lbc[:, 8:12], func=AF.Tanh, scale=0.5), reads=['lbc'], writes=['lbc'])
            S.op('dve', lambda e: e.tensor_scalar(out=lbc[:, 0:4], in0=lbc[:, 8:12], scalar1=0.25, scalar2=0.75, op0=ALU.mult, op1=ALU.add), reads=['lbc'], writes=['lbc'])
            S.op('dve', lambda e: e.tensor_scalar(out=lbc[:, 4:8], in0=lbc[:, 8:12], scalar1=-0.25, scalar2=0.25, op0=ALU.mult, op1=ALU.add), reads=['lbc'], writes=['lbc'])
            S.op('dve', lambda e: e.tensor_scalar(out=lbc[:, 12:16], in0=lbc[:, 8:12], scalar1=0.25, scalar2=-0.25, op0=ALU.mult, op1=ALU.add), reads=['lbc'], writes=['lbc'])

            def rms_to_T(src_ap, gcol, dstT, col0, rd, wr_extra=()):
                hb = G['hb']
                S.op('act', lambda e: e.activation(out=hb[:], in_=src_ap, func=AF.Square, accum_out=stat[:, 0:1]),
                     reads=rd, writes=['hb', 'stat'])
                S.op('dve', lambda e: e.tensor_scalar(out=stat[:, 1:2], in0=stat[:, 0:1], scalar1=1.0 / 1024, scalar2=EPS, op0=ALU.mult, op1=ALU.add),
                     reads=['stat'], writes=['stat'])
                S.op('pool', lambda e: e.tensor_tensor(out=stat[:, 2:3], in0=stat[:, 1:2], in1=mhalf, op=ALU.pow),
                     reads=['stat', 'small'], writes=['stat'])
                chk('rms_a')
                S.op('act', lambda e: e.activation(out=hb[:], in_=src_ap, func=AF.Copy, scale=stat[:, 2:3]),
                     reads=list(rd) + ['stat'], writes=['hb'])
                chk('rms_b')
                p, k = PS()
                pb = p[:].bitcast(BF16)
                for kk in range(8):
                    S.op('pe', lambda e, kk=kk: e.transpose(out=pb[:, kk * 128:(kk + 1) * 128], in_=hb[:, kk * 128:(kk + 1) * 128], identity=idb[:]),
                         reads=['hb', 'idb'], writes=[k], inc=(kk == 7))
                chk('rms_c')
                S.op('dve', lambda e: e.tensor_tensor(out=dstT[:, :, col0:col0 + 128], in0=pb.rearrange("p (k n) -> p k n", k=8),
                                                      in1=prm[:, gcol:gcol + 8].unsqueeze(2).to_broadcast([128, 8, 128]), op=ALU.mult),
                     reads=[k, 'prm'], writes=list(wr_extra))
                chk('rms_d')

            def rms_multi(items, gcol, dstT, wr, base=0, phase='all'):
                n = len(items)
                hbs = G['hbs']
                if phase in ('all', 'stats'):
                    for i_, (src, rd, col0) in enumerate(items):
                        S.op('act', lambda e, i_=i_, src=src: e.activation(out=hbs[(base + i_) % 2][:], in_=src, func=AF.Square, accum_out=stat[:, base + i_:base + i_ + 1]),
                             reads=rd, writes=['hb%d' % ((base + i_) % 2), 'stat'])
                    S.op('act', lambda e: e.activation(out=stat[:, 4 + base:4 + base + n], in_=stat[:, base:base + n], func=AF.Ln, scale=1.0 / 1024, bias=epsc), reads=['stat', 'small'], writes=['stat'])
                    S.op('act', lambda e: e.activation(out=stat[:, 8 + base:8 + base + n], in_=stat[:, 4 + base:4 + base + n], func=AF.Exp, scale=-0.5), reads=['stat'], writes=['stat'])
                if phase == 'stats':
                    return
                for i_, (src, rd, col0) in enumerate(items):
                    hb = hbs[(base + i_) % 2]
                    S.op('act', lambda e, i_=i_, src=src, hb=hb: e.activation(out=hb[:], in_=src, func=AF.Copy, scale=stat[:, 8 + base + i_:9 + base + i_]),
                         reads=list(rd) + ['stat'], writes=['hb%d' % ((base + i_) % 2)])
                    p, k = PS()
                    pb = p[:].bitcast(BF16)
                    for kk in range(8):
                        S.op('pe', lambda e, kk=kk, hb=hb, pb=pb: e.transpose(out=pb[:, kk * 128:(kk + 1) * 128], in_=hb[:, kk * 128:(kk + 1) * 128], identity=idb[:]),
                             reads=['hb%d' % ((base + i_) % 2), 'idb'], writes=[k], inc=(kk == 7))
                    S.op('dve', lambda e, col0=col0, pb=pb: e.tensor_tensor(out=dstT[:, :, col0:col0 + 128], in0=pb.rearrange("p (k n) -> p k n", k=8),
                                                                          in1=prm[:, gcol:gcol + 8].unsqueeze(2).to_broadcast([128, 8, 128]), op=ALU.mult),
                         reads=[k, 'prm'], writes=list(wr))

            def alloc_phase(st, T, sample):
                B = {}
                B['T'] = T
                ns = T // 128
                G['xres'] = sbt(st, "xres", [128, ns, 1024]); G['hT'] = sbt(st, "hT", [128, 8, T], BF16)
                G['mixT'] = sbt(st, "mixT", [128, 8, T], BF16); G['mT'] = sbt(st, "mT", [128, 22, T], BF16)
                G['hbs'] = [sbt(st, "hb%d" % i, [128, 1024], BF16) for i in range(2)]
                G['hb'] = G['hbs'][0]
                B['u'] = sbt(st, "u_sb", [128, 2, 15 + T]) if not sample else None
                B['pA'] = sbt(st, "pA", [128, max(16 + T, 368)]); B['pB'] = sbt(st, "pB", [128, max(16 + T, 368)])
                B['d'] = sbt(st, "d_sb", [128, 2, T], BF16)
                B['Ab'] = [sbt(st, "A_sb%d" % i, [128, T]) for i in range(2)]
                B['E2b'] = [sbt(st, "E2_%d" % i, [128, T]) for i in range(2)]
                B['K1b'] = [sbt(st, "K1_%d" % i, [128, T]) for i in range(2)]
                B['QSall'] = sbt(st, "QSall", [128, 4, T]); B['THall'] = sbt(st, "THall", [128, 4, T])
                B['GS'] = sbt(st, "GS", [128, 4, T])
                B['qAT'] = sbt(st, "qAT", [128, 4, T], BF16); B['kAT'] = sbt(st, "kAT", [128, 4, T], BF16)
                B['kAk'] = sbt(st, "kAk", [128, T // 128, 512], BF16); B['vtk'] = sbt(st, "vtk", [128, T // 128, 512], BF16)
                B['eAe'] = sbt(st, "eAe", [128, 4, 16])
                B['PT'] = sbt(st, "PT", [128, 4, 128], BF16)
                B['osb'] = sbt(st, "osb", [128, 4, T]); B['osq'] = sbt(st, "osq", [128, T], BF16)
                B['R'] = sbt(st, "R_sb", [128, T]); B['t1'] = sbt(st, "t1", [128, T])
                B['qxT'] = sbt(st, "qxT", [128, 2, T], BF16)
                B['cbuf'] = [B['QSall'][:, i, :] for i in range(2)]
                B['gbuf'] = [B['QSall'][:, 2 + i, :] for i in range(2)]
                return B

            def do_tile(B, t0, first, sample, X):
                T = B['T']; nsub = T // 128
                xres, hT, mixT, mT = G['xres'], G['hT'], G['mixT'], G['mT']
                for sub in range(nsub):
                    S.dma('sp', 'xin%d' % sub, xres[:, sub, :], x_tok[t0 + sub * 128:t0 + (sub + 1) * 128, :], writes=['xres%d' % sub])
                if first and not sample:
                    X['memdma']()
                if sample:
                    X['pre1']()
                if not X.get('prenormed'):
                    rms_multi([(xres[:, sub, :], ['xres%d' % sub], sub * 128) for sub in range(nsub)], 102, hT, ['hT'])
                X['prenormed'] = False
                chk('norm1')
                QSall, THall, GS = B['QSall'], B['THall'], B['GS']
                done = {'pool': False, 'attn': False, 'inproj': False}

                def g_inproj(b0, b1):
                    for b in range(b0, b1):
                        wsl, wk = wneed()
                        wv = w8(wsl)
                        for jj in range(4):
                            j = b * 4 + jj
                            if 10 <= j < 14:
                                continue
                            yield
                            p, k = PS()
                            mmg(p[:, 0:T], k, [(wv[:, kk, jj * 128:(jj + 1) * 128], hT[:, kk, 0:T]) for kk in range(8)], ['hT', wk])
                            src = p[:, 0:T]
                            if j < 2:
                                if sample:
                                    S.op('act', lambda e, j=j, src=src: e.activation(out=X['xp'][:, j, :, 15:23], in_=src.rearrange("p (b t) -> p b t", t=8), func=AF.Copy),
                                         reads=[k], writes=['xp'])
                                else:
                                    S.op('act', lambda e, j=j, src=src: e.activation(out=B['u'][:, j, 15:15 + T], in_=src, func=AF.Copy), reads=[k], writes=['u'])
                            elif j < 6:
                                S.op('act', lambda e, j=j, src=src: e.activation(out=QSall[:, j - 2, :], in_=src, func=AF.Silu), reads=[k], writes=['QS%d' % (j - 2)])
                            elif j < 10:
                                S.op('act', lambda e, j=j, src=src: e.activation(out=THall[:, j - 6, :], in_=src, func=AF.Tanh, scale=0.5), reads=[k], writes=['TH%d' % (j - 6)])
                            elif j < 18:
                                h = j - 14
                                S.op('act', lambda e, h=h, src=src: e.activation(out=GS[:, h, :], in_=src, func=AF.Copy), reads=[k], writes=['GS%d' % h])
                            else:
                                S.op('act', lambda e, j=j, src=src: e.activation(out=B['qxT'][:, j - 18, :], in_=src, func=AF.Copy), reads=[k], writes=['qxT'])
                        if b in (2, 3):
                            c0 = 256 if b == 2 else 0
                            o0 = 0 if b == 2 else 256
                            for sub in range(nsub):
                                yield
                                p, k = PS()
                                mmg(p[:, 0:256], k, [(hT[:, kk, sub * 128:(sub + 1) * 128], wv[:, kk, c0:c0 + 256]) for kk in range(8)], ['hT', wk])
                                S.op('dve', lambda e, p=p, sub=sub, o0=o0: e.tensor_copy(out=B['vtk'][:, sub, o0:o0 + 256], in_=p[:, 0:256]), reads=[k], writes=['vtk'])

                    if b1 == 5:
                        done['inproj'] = True
                    yield

                for _ in g_inproj(0, 3):
                    pass
                if first and not sample:
                    X['memkv']()
                chk('inproj')
                kgen = X['pre2']() if sample else iter(())

                def kstep(n=1):
                    for _ in range(n):
                        next(kgen, None)
                kstep(2)
                def g_pool():
                    pA, pB, dsb = B['pA'], B['pB'], B['d']
                    Rb = G['hbs'][0][:].bitcast(F32)
                    for c in range(2):
                        kstep(1)
                        if sample:
                            Xv = X['xp'][:, c, :, :]
                            L = 23
                            sl = lambda buf, n: buf[:, 0:16 * n].rearrange("p (b n) -> p b n", b=16)
                            xs = lambda a, b_: Xv[:, :, a:b_]
                            xkey = 'xp'
                        else:
                            u = B['u']
                            if first and c == 0:
                                yield
                                S.op('dve', lambda e: e.memset(u[:, :, 0:15], 0.0), writes=['u'])
                            L = 15 + T
                            sl = lambda buf, n: buf[:, 0:n]
                            xs = lambda a, b_, c=c: u[:, c, a:b_]
                            xkey = 'u'
                        sv = lambda buf, n, a, b_: (sl(buf, n)[:, :, a:b_] if sample else sl(buf, n)[:, a:b_])
                        yield
                        S.op('dve', lambda e: e.tensor_tensor(out=sl(pA, L - 1), in0=xs(1, L), in1=xs(0, L - 1), op=ALU.add), reads=[xkey], writes=['pA'])
                        yield
                        S.op('dve', lambda e: e.tensor_tensor(out=sl(pB, L - 3), in0=sv(pA, L - 1, 2, L - 1), in1=sv(pA, L - 1, 0, L - 3), op=ALU.add), reads=['pA'], writes=['pB'])
                        uview = xs(15, L)
                        dv = dsb[:, c, :].rearrange("p (b t) -> p b t", t=8) if sample else dsb[:, c, :]

                        def comb(plo, phi, buf, n, off, wsel, dv=dv, uview=uview):
                            S.op('dve', lambda e: e.scalar_tensor_tensor(out=dv[plo:phi], in0=sv(buf, n, off, off + (8 if sample else T))[plo:phi], scalar=invw[plo:phi, c:c + 1],
                                                                          in1=uview[plo:phi], op0=ALU.mult, op1=ALU.subtract),
                                 reads=[wsel, xkey, 'cst'], writes=['d'])
                        if c == 0:
                            yield
                            comb(0, 64, pA, L - 1, 14, 'pA')
                            yield
                            comb(64, 128, pB, L - 3, 12, 'pB')
                        else:
                            yield
                            S.op('dve', lambda e: e.tensor_tensor(out=sl(pA, L - 7), in0=sv(pB, L - 3, 4, L - 3), in1=sv(pB, L - 3, 0, L - 7), op=ALU.add), reads=['pB'], writes=['pA'])
                            yield
                            comb(0, 64, pA, L - 7, 8, 'pA')
                            yield
                            S.op('dve', lambda e: e.tensor_tensor(out=sl(pB, L - 15), in0=sv(pA, L - 7, 8, L - 7), in1=sv(pA, L - 7, 0, L - 15), op=ALU.add), reads=['pA'], writes=['pB'])
                            yield
                            comb(64, 128, pB, L - 15, 0, 'pB')
                        if first and not sample:
                            for (plo, phi, buf, off) in ((0, 64, pA, 14 if c == 0 else 8), (64, 128, pB, 12 if c == 0 else 0)):
                                yield
                                S.op('dve', lambda e, plo=plo, phi=phi, buf=buf, off=off: e.tensor_tensor(out=Rb[plo:phi, 0:15], in0=buf[plo:phi, off:off + 15],
                                                                                                          in1=invcnt[plo:phi, c * 16:c * 16 + 15], op=ALU.mult),
                                     reads=['pA', 'pB', 'cst'], writes=['hb0'])
                                yield
                                S.op('dve', lambda e, plo=plo, phi=phi: e.tensor_tensor(out=dsb[plo:phi, c, 0:15], in0=Rb[plo:phi, 0:15], in1=u[plo:phi, c, 15:30], op=ALU.subtract),
                                     reads=['hb0', 'u'], writes=['d'])
                        yield
                        p, k = PS()
                        yield
                        mmg(p[:, 0:T], k, [(wbd[:, c, :], dsb[:, c, :])], ['wbd', 'd'])
                        yield
                        S.op('act', lambda e, p=p, c=c: e.activation(out=mixT[:, c, 0:T], in_=p[:, 0:T], func=AF.Identity, scale=pscale(c)), reads=[k, 'prm'], writes=['mixT%d' % c])
                    if not sample:
                        u = B['u']
                        if X.get('last'):
                            for c in range(2):
                                yield
                                p, k = PS()
                                yield
                                S.op('pe', lambda e, p=p, c=c: e.transpose(out=p[0:15, c * 128:(c + 1) * 128], in_=u[:, c, T:T + 15], identity=ident), reads=['u', 'cst'], writes=[k])
                                yield
                                S.op('dve', lambda e, p=p, c=c: e.tensor_copy(out=X['ppo'][0:15, c * 128:(c + 1) * 128], in_=p[0:15, c * 128:(c + 1) * 128]), reads=[k], writes=['ppo'])
                            S.dma('sp', 'o_pp', o_pool_p[:, :], X['ppo'][0:15, :], reads=['ppo'])
                        else:
                            yield
                            S.op('dve', lambda e: e.tensor_copy(out=u[:, :, 0:15], in_=u[:, :, T:T + 15]), reads=['u'], writes=['u'])

                    yield
                def g_hgrn():
                    qAT, kAT, eAe = B['qAT'], B['kAT'], B['eAe']
                    smk = rmask if sample else X['cm512'][:, :]
                    smkey = 'cst' if sample else 'cm512'

                    def stA(h):
                        par = h % 2
                        K1 = B['K1b'][par]
                        thk = 'TH%d' % h
                        S.op('act', lambda e: e.activation(out=K1[:], in_=THall[:, h, :], func=AF.Identity, scale=lbc[:, 12 + h:13 + h], bias=lbc[:, 4 + h:5 + h]),
                             reads=[thk, 'lbc'], writes=['K1%d' % par])
                        S.op('act', lambda e: e.activation(out=THall[:, h, :], in_=THall[:, h, :], func=AF.Ln, scale=lbc[:, 4 + h:5 + h], bias=lbc[:, h:h + 1]),
                             reads=[thk, 'lbc'], writes=[thk])

                    def stB(h):
                        par = h % 2
                        A = B['Ab'][par]
                        S.op('dve', lambda e: e.tensor_tensor_scan(out=A[:], data0=smk, data1=THall[:, h, :], initial=0.0, op0=ALU.mult, op1=ALU.add),
                             reads=['TH%d' % h, smkey], writes=['A%d' % par])

                    def stC(h):
                        par = h % 2
                        A, E2 = B['Ab'][par], B['E2b'][par]
                        S.op('act', lambda e: e.activation(out=E2[:], in_=A[:], func=AF.Exp, scale=-1.0), reads=['A%d' % par], writes=['E2%d' % par])
                        S.op('act', lambda e: e.activation(out=A[:], in_=A[:], func=AF.Exp), reads=['A%d' % par], writes=['A%d' % par])

                    def stD(h):
                        par = h % 2
                        A, E2, K1 = B['Ab'][par], B['E2b'][par], B['K1b'][par]
                        ka, ke, kk1 = 'A%d' % par, 'E2%d' % par, 'K1%d' % par
                        S.op('dve', lambda e: e.tensor_tensor(out=qAT[:, h, :], in0=QSall[:, h, :], in1=A[:], op=ALU.mult), reads=['QS%d' % h, ka], writes=['qAT'])
                        S.op('dve', lambda e: e.tensor_tensor(out=kAT[:, h, :], in0=K1[:], in1=E2[:], op=ALU.mult), reads=[kk1, ke], writes=['kAT'])
                        if sample:
                            S.op('dve', lambda e: e.tensor_copy(out=eAe[:, h, 0:16], in_=A[:].rearrange("p (b t) -> p b t", t=8)[:, :, 7]), reads=[ka], writes=['eAe'])
                        else:
                            S.op('dve', lambda e: e.tensor_copy(out=eAe[:, h, 0:nsub], in_=A[:].rearrange("p (c t) -> p c t", t=128)[:, :, 127]), reads=[ka], writes=['eAe'])

                    stA(0)
                    yield
                    stB(0)
                    yield
                    kstep()
                    stA(1)
                    yield
                    stC(0)
                    yield
                    stB(1)
                    yield
                    kstep()
                    stD(0)
                    yield
                    stA(2)
                    yield
                    stC(1)
                    yield
                    stB(2)
                    yield
                    kstep()
                    stD(1)
                    yield
                    stA(3)
                    yield
                    stC(2)
                    yield
                    stB(3)
                    yield
                    kstep()
                    stD(2)
                    yield
                    stC(3)
                    yield
                    stD(3)
                    yield
                    kstep(8)
                    chk('hgrn_ew')
                    while not done['inproj']:
                        yield
                    for h in range(4):
                        S.op('act', lambda e, h=h: e.activation(out=GS[:, h, :], in_=GS[:, h, :], func=AF.Silu), reads=['GS%d' % h], writes=['GS%d' % h])
                    for cc in range(nsub):
                        yield
                        p, k = PS()
                        pb = p[:].bitcast(BF16)
                        for h in range(4):
                            yield
                            S.op('pe', lambda e, h=h, cc=cc, pb=pb: e.transpose(out=pb[:, h * 128:(h + 1) * 128], in_=kAT[:, h, cc * 128:(cc + 1) * 128], identity=idb[:]),
                                 reads=['kAT', 'idb'], writes=[k], inc=(h == 3))
                        yield
                        S.op('act', lambda e, cc=cc, pb=pb: e.activation(out=B['kAk'][:, cc, :], in_=pb[:, 0:512], func=AF.Copy), reads=[k], writes=['kAk'])
                    chk('katr')
                    PT, osb, kAk, vtk = B['PT'], B['osb'], B['kAk'], B['vtk']
                    for cc in range(nsub):
                        cs = slice(cc * 128, (cc + 1) * 128)
                        yield
                        pS, kS = PS()
                        for h in range(4):
                            yield
                            mmg(pS[:, h * 128:(h + 1) * 128], kS, [(kAT[:, h, cs], qAT[:, h, cs])], ['kAT', 'qAT'])
                        msk = smask if sample else cmask
                        yield
                        S.op('dve', lambda e, pS=pS, msk=msk: e.tensor_tensor(out=PT[:], in0=pS[:, :].rearrange("p (h t) -> p h t", h=4),
                                                                              in1=msk.unsqueeze(1).to_broadcast([128, 4, 128]), op=ALU.mult),
                             reads=[kS, 'cst'], writes=['PT'])
                        yield
                        pO, kO = PS()
                        if not sample:
                            Sf, Sb = X['Sf'], X['Sb']
                            for h in range(4):
                                yield
                                mmg(pO[:, h * 128:(h + 1) * 128], kO, [(vtk[:, cc, h * 128:(h + 1) * 128], PT[:, h, :]), (Sb[:, h, :], qAT[:, h, cs])],
                                    ['vtk', 'PT', 'Sb', 'qAT'])
                            yield
                            S.op('act', lambda e, pO=pO, cs=cs: e.activation(out=osb[:, :, cs], in_=pO[:, :].rearrange("p (h t) -> p h t", h=4), func=AF.Copy), reads=[kO], writes=['osb'])
                            yield
                            pZ, kZ = PS()
                            for h in range(4):
                                yield
                                mmg(pZ[:, h * 128:(h + 1) * 128], kZ, [(kAk[:, cc, h * 128:(h + 1) * 128], vtk[:, cc, h * 128:(h + 1) * 128])], ['kAk', 'vtk'])
                            yield
                            S.op('dve', lambda e, pZ=pZ: e.tensor_tensor(out=Sf[:], in0=Sf[:], in1=pZ[:, :].rearrange("p (h v) -> p h v", h=4), op=ALU.add), reads=[kZ, 'Sf'], writes=['Sf'])
                            yield
                            S.op('dve', lambda e, cc=cc: e.tensor_tensor(out=Sf[:], in0=Sf[:], in1=eAe[:, :, cc:cc + 1].to_broadcast([128, 4, 128]), op=ALU.mult), reads=['Sf', 'eAe'], writes=['Sf'])
                            yield
                            S.op('act', lambda e: e.activation(out=Sb[:], in_=Sf[:], func=AF.Copy), reads=['Sf'], writes=['Sb'])
                        else:
                            S0f, S0b = X['S0f'], X['S0b']
                            for h in range(4):
                                pairs = [(vtk[:, 0, h * 128:(h + 1) * 128], PT[:, h, :])]
                                n = 17
                                yield
                                S.op('pe', lambda e, h=h: e.matmul(pO[:, h * 128:(h + 1) * 128], lhsT=vtk[:, 0, h * 128:(h + 1) * 128], rhs=PT[:, h, :], start=True, stop=False),
                                     reads=['vtk', 'PT'], writes=[kO], inc=False)
                                for bq in range(16):
                                    yield
                                    S.op('pe', lambda e, h=h, bq=bq: e.matmul(pO[:, h * 128 + bq * 8:h * 128 + bq * 8 + 8], lhsT=S0b[:, bq, h, :], rhs=qAT[:, h, bq * 8:bq * 8 + 8],
                                                                             start=False, stop=(bq == 15)),
                                         reads=['S0b', 'qAT'], writes=[kO], inc=(bq == 15))
                            yield
                            S.op('act', lambda e, pO=pO: e.activation(out=osb[:, :, 0:128], in_=pO[:, :].rearrange("p (h t) -> p h t", h=4), func=AF.Copy), reads=[kO], writes=['osb'])
                            Vb2 = X['Vblk2']

                            def stV(i):
                                h, bg = i // 4, i % 4
                                vb = Vb2[i % 2]
                                S.op('dve', lambda e: e.tensor_tensor(out=vb[:], in0=vtk[:, 0, h * 128:(h + 1) * 128].unsqueeze(1).to_broadcast([128, 4, 128]),
                                                                      in1=seqm[:, bg * 4:bg * 4 + 4].unsqueeze(2).to_broadcast([128, 4, 128]), op=ALU.mult),
                                     reads=['vtk', 'cst'], writes=['Vblk%d' % (i % 2)])

                            def stU(i):
                                h, bg = i // 4, i % 4
                                vb = Vb2[i % 2]
                                pZ, kZ = PS()
                                mmg(pZ[:, :], kZ, [(kAk[:, 0, h * 128:(h + 1) * 128], vb[:].rearrange("p b v -> p (b v)"))], ['kAk', 'Vblk%d' % (i % 2)])
                                S.op('dve', lambda e: e.tensor_tensor(out=S0f[:, bg * 4:bg * 4 + 4, h, :], in0=S0f[:, bg * 4:bg * 4 + 4, h, :],
                                                                      in1=pZ[:, :].rearrange("p (b v) -> p b v", b=4), op=ALU.add),
                                     reads=[kZ, 'S0f'], writes=['S0f'])
                                S.op('dve', lambda e: e.tensor_tensor(out=S0f[:, bg * 4:bg * 4 + 4, h, :], in0=S0f[:, bg * 4:bg * 4 + 4, h, :],
                                                                      in1=eAe[:, h, bg * 4:bg * 4 + 4].unsqueeze(2).to_broadcast([128, 4, 128]), op=ALU.mult),
                                     reads=['S0f', 'eAe'], writes=['S0f'])
                            yield
                            stV(0)
                            for i in range(16):
                                if i + 1 < 16:
                                    yield
                                    stV(i + 1)
                                yield
                                stU(i)
                            S.dma('sp', 'o_hs', o_hgrn_s.rearrange("b h d v -> d b h v"), S0f[:], reads=['S0f'])
                    if (not sample) and X.get('last'):
                        S.dma('sp', 'o_hp', o_hgrn_p.rearrange("h d v -> d h v"), X['Sf'][:], reads=['Sf'])
                    chk('hgrn')
                    osq, R, t1 = B['osq'], B['R'], B['t1']
                    for h in range(4):
                        yield
                        S.op('act', lambda e, h=h: e.activation(out=osq[:], in_=osb[:, h, :], func=AF.Square), reads=['osb'], writes=['osq'])
                        yield
                        p, k = PS()
                        yield
                        mmg(p[:, 0:T], k, [(onesb[:], osq[:])], ['onesb', 'osq'])
                        yield
                        S.op('act', lambda e, p=p: e.activation(out=R[:], in_=p[:, 0:T], func=AF.Ln, scale=1.0 / 128, bias=epsc), reads=[k, 'small'], writes=['R'])
                        yield
                        S.op('act', lambda e: e.activation(out=R[:], in_=R[:], func=AF.Exp, scale=-0.5), reads=['R'], writes=['R'])
                        yield
                        S.op('dve', lambda e, h=h: e.tensor_tensor(out=t1[:], in0=osb[:, h, :], in1=R[:], op=ALU.mult), reads=['osb', 'R'], writes=['t1'])
                        yield
                        S.op('dve', lambda e, h=h: e.scalar_tensor_tensor(out=mixT[:, 2 + h, 0:T], in0=t1[:], scalar=onorm(h), in1=GS[:, h, :], op0=ALU.mult, op1=ALU.mult),
                             reads=['t1', 'GS%d' % h, 'prm'], writes=['mixT%d' % (2 + h)])

                    yield
                def g_attn():
                    qxT = B['qxT']
                    Ra = G['hbs'][1][:].bitcast(F32); Rb = G['hbs'][0][:].bitcast(F32)
                    while not done['inproj']:
                        yield
                    if sample:
                        for _ in range(24):
                            yield
                        kstep(8)
                    if not sample:
                        PTa = X['PTa']
                        for pr in range(2):
                            for hh in range(2):
                                h = pr * 2 + hh
                                rows = slice(hh * 64, hh * 64 + 64)
                                for mc in range(2):
                                    yield
                                    p, k = PS()
                                    yield
                                    mmg(p[:, 0:T], k, [(KT[rows, pr, mc * 128:(mc + 1) * 128], qxT[rows, pr, :])], ['KT', 'qxT'])
                                    yield
                                    S.op('act', lambda e, p=p, hh=hh, mc=mc: e.activation(out=PTa[:, hh, mc, :], in_=p[:, 0:T], func=AF.Exp, scale=0.125), reads=[k], writes=['PTa'])
                            yield
                            pO, kO = PS()
                            yield
                            pD, kD = PS()
                            for hh in range(2):
                                h = pr * 2 + hh
                                rows = slice(hh * 64, hh * 64 + 64)
                                for mc in range(2):
                                    yield
                                    S.op('pe', lambda e, hh=hh, mc=mc, h=h, rows=rows, pO=pO: e.matmul(pO[rows, 0:T], lhsT=Vb[:, mc, h * 64:(h + 1) * 64], rhs=PTa[:, hh, mc, :],
                                                                                                        start=(mc == 0), stop=(mc == 1)),
                                         reads=['Vb', 'PTa'], writes=[kO], inc=(mc == 1))
                                for mc in range(2):
                                    yield
                                    S.op('pe', lambda e, hh=hh, mc=mc, rows=rows, pD=pD: e.matmul(pD[rows, 0:T], lhsT=onesb[:, 0:64], rhs=PTa[:, hh, mc, :],
                                                                                                  start=(mc == 0), stop=(mc == 1)),
                                         reads=['onesb', 'PTa'], writes=[kD], inc=(mc == 1))
                            yield
                            S.op('act', lambda e, pD=pD: e.activation(out=Ra[:, 0:T], in_=pD[:, 0:T], func=AF.Ln), reads=[kD], writes=['hb1'])
                            S.op('act', lambda e: e.activation(out=Ra[:, 0:T], in_=Ra[:, 0:T], func=AF.Exp, scale=-1.0), reads=['hb1'], writes=['hb1'])
                            yield
                            S.op('dve', lambda e, pO=pO, pr=pr: e.tensor_tensor(out=mixT[:, 6 + pr, 0:T], in0=pO[:, 0:T], in1=Ra[:, 0:T], op=ALU.mult), reads=[kO, 'hb1'], writes=['mixT%d' % (6 + pr)])
                    else:
                        KTs, Vs, PTs = X['KTs'], X['Vs'], X['PTs']
                        for g8 in range(2):
                            yield
                            pp = [PS(), PS()]
                            for bi in range(8):
                                bq = g8 * 8 + bi
                                for mc in range(2):
                                    for h in range(4):
                                        par = h % 2
                                        rows = slice(par * 64, par * 64 + 64)
                                        col = bi * 32 + (mc * 2 + h // 2) * 8
                                        last = (bi == 7 and mc == 1 and h >= 2)
                                        p, k = pp[par]
                                        yield
                                        S.op('pe', lambda e, bq=bq, mc=mc, h=h, rows=rows, col=col, p=p: e.matmul(p[:, col:col + 8], lhsT=KTs[rows, bq, h // 2, mc * 128:(mc + 1) * 128],
                                                                                                                 rhs=qxT[rows, h // 2, bq * 8:bq * 8 + 8], start=True, stop=True),
                                             reads=['KTs', 'qxT'], writes=[k], inc=last)
                            for par in range(2):
                                p, k = pp[par]
                                yield
                                S.op('act', lambda e, p=p, g8=g8, par=par: e.activation(out=PTs[:, par, g8 * 256:(g8 + 1) * 256], in_=p[:, 0:256], func=AF.Exp, scale=0.125), reads=[k], writes=['PTs'])
                        yield
                        pO, kO = PS(2)
                        yield
                        pD, kD = PS(2)
                        for bq in range(16):
                            for h in range(4):
                                rows = slice((h % 2) * 64, (h % 2) * 64 + 64)
                                oc = (h // 2) * 128 + bq * 8
                                for mc in range(2):
                                    col = bq * 32 + (mc * 2 + h // 2) * 8
                                    yield
                                    S.op('pe', lambda e, bq=bq, h=h, mc=mc, rows=rows, oc=oc, col=col: e.matmul(pO[rows, oc:oc + 8], lhsT=Vs[:, bq, mc, h * 64:(h + 1) * 64], rhs=PTs[:, h % 2, col:col + 8],
                                                                                                               start=(mc == 0), stop=(mc == 1)),
                                         reads=['Vs', 'PTs'], writes=[kO], inc=(bq == 15 and h == 3 and mc == 1))
                                for mc in range(2):
                                    col = bq * 32 + (mc * 2 + h // 2) * 8
                                    yield
                                    S.op('pe', lambda e, bq=bq, h=h, mc=mc, rows=rows, oc=oc, col=col: e.matmul(pD[rows, oc:oc + 8], lhsT=onesb[:, 0:64], rhs=PTs[:, h % 2, col:col + 8],
                                                                                                               start=(mc == 0), stop=(mc == 1)),
                                         reads=['onesb', 'PTs'], writes=[kD], inc=(bq == 15 and h == 3 and mc == 1))
                        yield
                        S.op('act', lambda e: e.activation(out=Ra[:, 0:128], in_=pD[:, 0:128], func=AF.Ln), reads=[kD], writes=['hb1'])
                        S.op('act', lambda e: e.activation(out=Ra[:, 0:128], in_=Ra[:, 0:128], func=AF.Exp, scale=-1.0), reads=['hb1'], writes=['hb1'])
                        yield
                        S.op('dve', lambda e: e.tensor_tensor(out=mixT[:, 6, 0:128], in0=pO[:, 0:128], in1=Ra[:, 0:128], op=ALU.mult), reads=[kO, 'hb1'], writes=['mixT6'])
                        yield
                        S.op('act', lambda e: e.activation(out=Rb[:, 0:128], in_=pD[:, 128:256], func=AF.Ln), reads=[kD], writes=['hb0'])
                        S.op('act', lambda e: e.activation(out=Rb[:, 0:128], in_=Rb[:, 0:128], func=AF.Exp, scale=-1.0), reads=['hb0'], writes=['hb0'])
                        yield
                        S.op('dve', lambda e: e.tensor_tensor(out=mixT[:, 7, 0:128], in0=pO[:, 128:256], in1=Rb[:, 0:128], op=ALU.mult), reads=[kO, 'hb0'], writes=['mixT7'])

                    yield
                gens = [(g_inproj(3, 5), 2), (g_hgrn(), 3), (g_pool(), 1), (g_attn(), 1)]
                while gens:
                    for ge in list(gens):
                        for _ in range(ge[1]):
                            try:
                                next(ge[0])
                            except StopIteration:
                                gens.remove(ge)
                                break
                chk('pool')
                chk('hgrn_o')
                chk('attn')
                wo = [wneed(), wneed(prefetch=False)]
                for sub in range(nsub):
                    for c in range(2):
                        wsl, wk = wo[c]
                        wv = w8(wsl)
                        p, k = PS()
                        mmg(p[:, :], k, [(mixT[:, kk, sub * 128:(sub + 1) * 128], wv[:, kk, :]) for kk in (0, 1, 6, 7, 2, 3, 4, 5)], [wk], per=[['mixT%d' % kk] for kk in (0, 1, 6, 7, 2, 3, 4, 5)])
                        S.op('dve', lambda e, p=p, sub=sub, c=c: e.tensor_tensor(out=xres[:, sub, c * 512:(c + 1) * 512], in0=p[:, :], in1=xres[:, sub, c * 512:(c + 1) * 512], op=ALU.add),
                             reads=[k, 'xres%d' % sub], writes=['xres%d' % sub])
                    if sub >= 1:
                        rms_multi([(xres[:, sub - 1, :], ['xres%d' % (sub - 1)], (sub - 1) * 128)], 110, hT, ['hT'], base=sub - 1)
                rms_multi([(xres[:, nsub - 1, :], ['xres%d' % (nsub - 1)], (nsub - 1) * 128)], 110, hT, ['hT'], base=nsub - 1)
                chk('outproj')
                chk('norm2')
                pend2 = []
                nxt = X.get('nxt')
                if nxt is not None:
                    GSf = B['GS'][:].rearrange("p h t -> p (h t)"); osf = B['osb'][:].rearrange("p h t -> p (h t)")
                    xn = [GSf[:, 0:1024], GSf[:, 1024:2048], osf[:, 0:1024], osf[:, 1024:2048]]
                    xnk = [['GS0', 'GS1'], ['GS2', 'GS3'], ['osb'], ['osb']]
                    for s_ in range(4):
                        S.dma('sp', 'xn%d' % s_, xn[s_], x_tok[nxt + s_ * 128:nxt + (s_ + 1) * 128, :], writes=xnk[s_])
                for r in range(11):
                    wsl, wk = wneed()
                    wv = w8(wsl)
                    for jj in range(2):
                        j = 2 * r + jj
                        pa, ka = PS(2)
                        mmg(pa[:, 0:T], ka, [(wv[:, kk, jj * 128:(jj + 1) * 128], hT[:, kk, 0:T]) for kk in range(8)], ['hT', wk])
                        pb_, kb = PS()
                        mmg(pb_[:, 0:T], kb, [(wv[:, kk, 256 + jj * 128:256 + (jj + 1) * 128], hT[:, kk, 0:T]) for kk in range(8)], ['hT', wk])
                        cbuf = B['cbuf'][j % 2]; gbuf = B['gbuf'][j % 2]
                        ck_, gk_ = 'cbuf%d' % (j % 2), 'gbuf%d' % (j % 2)
                        if not sample:
                            asb = X['asb'][j % 2]; ak_ = 'asb%d' % (j % 2); carry = X['carry']
                            S.op('dve', lambda e, asb=asb, j=j: e.tensor_copy(out=asb[:, 0:2], in_=carry[:, j, :]), reads=['carry'], writes=[ak_])
                            S.op('act', lambda e, asb=asb, pa=pa: e.activation(out=asb[:, 2:2 + T], in_=pa[:, 0:T], func=AF.Copy), reads=[ka], writes=[ak_])
                            S.op('act', lambda e, pa=pa, j=j, cbuf=cbuf: e.activation(out=cbuf, in_=pa[:, 0:T], func=AF.Identity, scale=cw(2, j), bias=cb(j)), reads=[ka, 'prm'], writes=[ck_])
                            S.op('dve', lambda e, asb=asb, j=j: e.tensor_copy(out=carry[:, j, :], in_=asb[:, T:T + 2]), reads=[ak_], writes=['carry'])
                            S.op('dve', lambda e, asb=asb, j=j, cbuf=cbuf: e.scalar_tensor_tensor(out=cbuf, in0=asb[:, 1:1 + T], scalar=cw(1, j), in1=cbuf, op0=ALU.mult, op1=ALU.add),
                                 reads=[ak_, ck_, 'prm'], writes=[ck_])
                            S.op('dve', lambda e, asb=asb, j=j, cbuf=cbuf: e.scalar_tensor_tensor(out=cbuf, in0=asb[:, 0:T], scalar=cw(0, j), in1=cbuf, op0=ALU.mult, op1=ALU.add),
                                 reads=[ak_, ck_, 'prm'], writes=[ck_])
                        else:
                            a3 = X['a3'][j % 2]; ak_ = 'a3%d' % (j % 2); ahist, anew = X['ahist'], X['anew']
                            c3 = cbuf.rearrange("p (b t) -> p b t", t=8)
                            S.op('dve', lambda e, a3=a3, j=j: e.tensor_copy(out=a3[:, :, 0:2], in_=ahist[:, j, :, :]), reads=['ahist'], writes=[ak_])
                            S.op('act', lambda e, a3=a3, pa=pa: e.activation(out=a3[:, :, 2:10], in_=pa[:, 0:128].rearrange("p (b t) -> p b t", t=8), func=AF.Copy), reads=[ka], writes=[ak_])
                            S.op('act', lambda e, pa=pa, j=j, cbuf=cbuf: e.activation(out=cbuf, in_=pa[:, 0:T], func=AF.Identity, scale=cw(2, j), bias=cb(j)), reads=[ka, 'prm'], writes=[ck_])
                            S.op('dve', lambda e, a3=a3, j=j: e.tensor_copy(out=anew[:, j, :, :], in_=a3[:, :, 8:10]), reads=[ak_], writes=['anew'])
                            S.op('dve', lambda e, a3=a3, j=j, c3=c3: e.scalar_tensor_tensor(out=c3, in0=a3[:, :, 1:9], scalar=cw(1, j), in1=c3, op0=ALU.mult, op1=ALU.add),
                                 reads=[ak_, ck_, 'prm'], writes=[ck_])
                            S.op('dve', lambda e, a3=a3, j=j, c3=c3: e.scalar_tensor_tensor(out=c3, in0=a3[:, :, 0:8], scalar=cw(0, j), in1=c3, op0=ALU.mult, op1=ALU.add),
                                 reads=[ak_, ck_, 'prm'], writes=[ck_])
                        def stage2(cbuf=cbuf, gbuf=gbuf, pb_=pb_, j=j, ck_=ck_, gk_=gk_, kb=kb):
                            S.op('act', lambda e: e.activation(out=gbuf, in_=cbuf, func=AF.Gelu_apprx_tanh), reads=[ck_], writes=[gk_])
                            S.op('dve', lambda e: e.tensor_tensor(out=mT[:, j, 0:T], in0=pb_[:, 0:T], in1=gbuf, op=ALU.mult), reads=[kb, gk_], writes=['mT%d' % j] + (['kvt'] if (first and j < 4) else []))
                        if pend2:
                            pend2.pop()()
                        pend2.append(stage2)
                if pend2:
                    pend2.pop()()
                chk('up')
                if (not sample) and X.get('last'):
                    carry, rowb = X['carry'], X['rowb']
                    for g4 in range(6):
                        p, k = PS()
                        n4 = 4 if g4 < 5 else 2
                        for q in range(n4):
                            j = g4 * 4 + q
                            S.op('pe', lambda e, p=p, q=q, j=j: e.transpose(out=p[0:2, q * 128:(q + 1) * 128], in_=carry[:, j, :], identity=ident), reads=['carry', 'cst'], writes=[k], inc=(q == n4 - 1))
                        S.op('dve', lambda e, p=p, g4=g4, n4=n4: e.tensor_copy(out=rowb[0:2, g4 % 2, 0:n4 * 128], in_=p[0:2, 0:n4 * 128]), reads=[k], writes=['rowb%d' % (g4 % 2)])
                        S.dma('sp', 'o_cp%d' % (g4 % 2), o_conv_p[:, g4 * 512:g4 * 512 + n4 * 128], rowb[0:2, g4 % 2, 0:n4 * 128], reads=['rowb%d' % (g4 % 2)])
                if sample:
                    anew, rowb = X['anew'], X['rowb']
                    for g4 in range(6):
                        p, k = PS()
                        n4 = 4 if g4 < 5 else 2
                        for q in range(n4):
                            j = g4 * 4 + q
                            S.op('pe', lambda e, p=p, q=q, j=j: e.transpose(out=p[0:32, q * 128:(q + 1) * 128], in_=anew[:, j, :, :].rearrange("p b r -> p (b r)"), identity=ident),
                                 reads=['anew', 'cst'], writes=[k], inc=(q == n4 - 1))
                        S.op('dve', lambda e, p=p, g4=g4, n4=n4: e.tensor_copy(out=rowb[0:32, g4 % 2, 0:n4 * 128], in_=p[0:32, 0:n4 * 128]), reads=[k], writes=['rowb%d' % (g4 % 2)])
                        S.dma('sp', 'o_cs%d' % (g4 % 2), o_conv_s[:, g4 * 512:g4 * 512 + n4 * 128], rowb[0:32, g4 % 2, 0:n4 * 128], reads=['rowb%d' % (g4 % 2)])
                chk('convout')
                if nxt is not None:
                    rms_multi([(xn[s_], xnk[s_], s_ * 128) for s_ in range(4)], 102, hT, ['hT'], phase='stats')
                for q in range(4):
                    wsl, wk = wneed()
                    wv = w22(wsl)
                    for sub in range(nsub):
                        p, k = PS()
                        mmg(p[:, 0:256], k, [(mT[:, kk, sub * 128:(sub + 1) * 128], wv[:, kk, :]) for kk in range(22)], ['mT%d' % kk for kk in range(22)] + [wk])
                        S.op('dve', lambda e, p=p, sub=sub, q=q: e.tensor_tensor(out=xres[:, sub, q * 256:(q + 1) * 256], in0=p[:, 0:256], in1=xres[:, sub, q * 256:(q + 1) * 256], op=ALU.add),
                             reads=[k, 'xres%d' % sub], writes=['xres%d' % sub])
                    if q == 1 and nxt is not None:
                        rms_multi([(xn[s_], xnk[s_], s_ * 128) for s_ in range(4)], 102, hT, ['hT'], phase='apply')
                        X['prenormed'] = True
                hbs = G['hbs']
                for sub in range(nsub):
                    S.op('act', lambda e, sub=sub: e.activation(out=hbs[sub % 2][:], in_=xres[:, sub, :], func=AF.Square, accum_out=stat[:, 16 + sub:17 + sub]),
                         reads=['xres%d' % sub], writes=['hb%d' % (sub % 2), 'stat2'])
                S.op('act', lambda e: e.activation(out=stat[:, 20:20 + nsub], in_=stat[:, 16:16 + nsub], func=AF.Ln, scale=1.0 / 1024, bias=epsc), reads=['stat2', 'small'], writes=['stat2'])
                S.op('act', lambda e: e.activation(out=stat[:, 24:24 + nsub], in_=stat[:, 20:20 + nsub], func=AF.Exp, scale=-0.5), reads=['stat2'], writes=['stat2'])
                for sub in range(nsub):
                    xk = 'xres%d' % sub
                    S.op('dve', lambda e, sub=sub: e.scalar_tensor_tensor(out=xres[:, sub, :], in0=xres[:, sub, :], scalar=stat[:, 24 + sub:25 + sub], in1=gf[:], op0=ALU.mult, op1=ALU.mult),
                         reads=[xk, 'stat2', 'gf'], writes=[xk])
                    S.dma('sp', 'yout%d' % sub, y_tok[t0 + sub * 128:t0 + (sub + 1) * 128, :], xres[:, sub, :], reads=[xk])

            chk('prologue')
            Bp = alloc_phase(pst, 512, False)
            xres, hT, mixT, mT = G['xres'], G['hT'], G['mixT'], G['mT']
            memx = Bp['osb'][:].rearrange("p h t -> p (h t)").rearrange("p (s f) -> p s f", s=2); memT = mixT
            kvt = mT[:].rearrange("p k t -> p (k t)")[:, 0:2048].bitcast(F32).rearrange("p (s f) -> p s f", s=2)
            KT = sbt(pst, "KT", [128, 2, 256], BF16); Vb = sbt(pst, "Vb", [128, 2, 256], BF16)

            def memkv():
                rms_multi([(memx[:, sub, :], ['osb'], sub * 128) for sub in range(2)], 118, memT, ['memT'])
                wkv, wk = wneed()
                chk('kv_w')
                for sub in range(2):
                    p, k = PS(2)
                    mmg(p[:, :], k, [(memT[:, kk, sub * 128:(sub + 1) * 128], w8(wkv)[:, kk, :]) for kk in range(8)], ['memT', wk])
                    chk('kv_m')
                    S.op('act', lambda e, p=p, sub=sub: e.activation(out=kvt[:, sub, :], in_=p[:, :], func=AF.Copy), reads=[k], writes=['kvt'])
                    chk('kv_n')
                    S.op('dve', lambda e, p=p, sub=sub: e.tensor_copy(out=Vb[:, sub, :], in_=p[:, 256:512]), reads=[k], writes=['Vb'])
                    chk('kv_a%d' % sub)
                for j in range(2):
                    p, k = PS()
                    mmg(p[:, 0:256], k, [(w8(wkv)[:, kk, j * 128:(j + 1) * 128], memT[:, kk, 0:256]) for kk in range(8)], ['memT', wk])
                    S.op('act', lambda e, p=p, j=j: e.activation(out=KT[:, j, :], in_=p[:, 0:256], func=AF.Copy), reads=[k], writes=['KT'])
                    chk('kv_b%d' % j)
                S.dma('sp', 'o_mk', o_mk.rearrange("(s p) f -> p s f", p=128), kvt[:, :, 0:256], reads=['kvt'])
                chk('kv_c')
                S.dma('sp', 'o_mv', o_mv.rearrange("(s p) f -> p s f", p=128), kvt[:, :, 256:512], reads=['kvt'])


            chk('memkv')
            Xp = {}
            Xp['memkv'] = memkv
            Xp['memdma'] = lambda: S.dma('sp', 'c7', memx[:, 0:2, :], mem.rearrange("(s p) f -> p s f", p=128), writes=['osb'])
            Xp['Sf'] = sbt(pst, "Sf", [128, 4, 128]); Xp['Sb'] = sbt(pst, "Sb", [128, 4, 128], BF16)
            Xp['cm512'] = sbt(pst, "cm512", [128, 512])
            Xp['PTa'] = sbt(pst, "PTa", [128, 2, 2, 512], BF16)
            thf = Bp['THall'][:].rearrange("p h t -> p (h t)")
            Xp['asb'] = [thf[:, 0:514], thf[:, 1024:1538]]
            Xp['carry'] = sbt(pst, "carry", [128, 22, 2]); Xp['rowb'] = sbt(pst, "rowb_p", [2, 2, 512]); Xp['ppo'] = sbt(pst, "ppo", [16, 256])
            S.op('dve', lambda e: e.memset(Xp['Sf'][:], 0.0), writes=['Sf'])
            S.op('dve', lambda e: e.memset(Xp['Sb'][:], 0.0), writes=['Sb'])
            S.op('dve', lambda e: e.memset(Xp['cm512'][:], 1.0), writes=['cm512'])
            S.op('dve', lambda e: e.memset(Xp['cm512'][:].rearrange("p (c t) -> p c t", t=128)[:, :, 0:1], 0.0), writes=['cm512'])
            S.op('dve', lambda e: e.memset(Xp['carry'][:], 0.0), writes=['carry'])
            for ti in range(4):
                Xp['last'] = (ti == 3)
                Xp['nxt'] = (ti + 1) * 512 if ti < 3 else None
                do_tile(Bp, ti * 512, ti == 0, False, Xp)
                chk('tile%d' % ti)
            S.barrier()
            pst.close()

            chk('prompt')
            Bs = alloc_phase(sst, 128, True)
            Xs = {}
            Xs['xp'] = sbt(sst, "xp", [128, 2, 16, 23]); Xs['sp_tok'] = sbt(sst, "sp_tok", [120, 2, 256]); Xs['xpc'] = sbt(sst, "xpc", [128, 2, 16, 15])
            Xs['spo'] = Xs['sp_tok']
            Xs['S0f'] = sbt(sst, "S0f", [128, 16, 4, 128]); Xs['S0b'] = sbt(sst, "S0b", [128, 16, 4, 128], BF16)
            Xs['Vblk2'] = [sbt(sst, "Vblk%d" % i, [128, 4, 128], BF16) for i in range(2)]
            Xs['KTs'] = sbt(sst, "KTs", [128, 16, 2, 256], BF16); Xs['Vs'] = sbt(sst, "Vs", [128, 16, 2, 256], BF16)
            Xs['PTs'] = sbt(sst, "PTs", [128, 2, 512], BF16); kst2 = [sbt(sst, "kst%d" % i, [128, 2, 2, 256], BF16) for i in range(2)]
            thfs = Bs['THall'][:].rearrange("p h t -> p (h t)")
            Xs['a3'] = [thfs[:, 0:160].rearrange("p (b t) -> p b t", t=10), thfs[:, 256:416].rearrange("p (b t) -> p b t", t=10)]
            Xs['ahist'] = sbt(sst, "ahist", [128, 22, 16, 2]); Xs['anew'] = sbt(sst, "anew", [128, 22, 16, 2]); Xs['rowb'] = sbt(sst, "rowb_s", [32, 2, 512])
            cst_tok = sbt(sst, "cst_tok", [32, 2816])
            def pre1():
                S.dma('sp', 's3', Xs['sp_tok'][:], spool.rearrange("(h q) c -> q h c", q=120), writes=['sp_tok'])
                S.dma('sp', 's4', cst_tok[:], sconv[:, :], writes=['cst_tok'])
                kload(0)
                for hh in range(2):
                    for c in range(2):
                        p, k = PS()
                        S.op('pe', lambda e, p=p, hh=hh, c=c: e.transpose(out=p[:, 0:120], in_=Xs['sp_tok'][0:120, hh, c * 128:(c + 1) * 128], identity=cst[0:120, 0:120]),
                             reads=['sp_tok', 'cst'], writes=[k])
                        S.op('dve', lambda e, p=p, hh=hh, c=c: e.tensor_copy(out=Xs['xp'][:, c, hh * 8:(hh + 1) * 8, 0:15], in_=p[:, 0:120].rearrange("p (b r) -> p b r", r=15)),
                             reads=[k], writes=['xp'])
                for j in range(22):
                    p, k = PS()
                    S.op('pe', lambda e, p=p, j=j: e.transpose(out=p[:, 0:32], in_=cst_tok[0:32, j * 128:(j + 1) * 128], identity=cst[0:32, 0:32]), reads=['cst_tok', 'cst'], writes=[k])
                    S.op('dve', lambda e, p=p, j=j: e.tensor_copy(out=Xs['ahist'][:, j, :, :], in_=p[:, 0:32].rearrange("p (b r) -> p b r", r=2)), reads=[k], writes=['ahist'])

            def kload(g4):
                S.dma('pool', 's5%d' % (g4 % 2), kst2[g4 % 2][:], ck[g4 * 2:(g4 + 1) * 2].rearrange("b (mc p) f -> p b mc f", p=128), writes=['kst%d' % (g4 % 2)])

            def pre2():
                kload(1)
                S.dma('pool', 's1', Xs['S0b'][:], shgrn.rearrange("b h d v -> d b h v"), writes=['S0b'])
                for g4 in range(8):
                    kst = kst2[g4 % 2]
                    for bi in range(2):
                        bq = g4 * 2 + bi
                        p, k = PS()
                        pb = p[:].bitcast(BF16)
                        for hc in range(2):
                            for mc in range(2):
                                S.op('pe', lambda e, pb=pb, kst=kst, bi=bi, hc=hc, mc=mc: e.transpose(out=pb[:, (hc * 2 + mc) * 128:(hc * 2 + mc + 1) * 128], in_=kst[:, bi, mc, hc * 128:(hc + 1) * 128], identity=idb[:]),
                                     reads=['kst%d' % (g4 % 2), 'idb'], writes=[k], inc=(hc == 1 and mc == 1))
                        S.op('act', lambda e, pb=pb, bq=bq: e.activation(out=Xs['KTs'][:, bq, :, :], in_=pb[:, 0:512].rearrange("p (hc m) -> p hc m", hc=2), func=AF.Copy), reads=[k], writes=['KTs'])
                    if g4 + 2 < 8:
                        kload(g4 + 2)
                    if g4 == 7:
                        S.dma('pool', 's2', Xs['Vs'][:], cv.rearrange("b (mc p) f -> p b mc f", p=128), writes=['Vs'])
                        S.dma('sp', 's0', Xs['S0f'][:], shgrn.rearrange("b h d v -> d b h v"), writes=['S0f'])
                    yield
            Xs['pre1'] = pre1; Xs['pre2'] = pre2
            chk('sprologue')
            do_tile(Bs, 2048, False, True, Xs)
            S.op('dve', lambda e: e.tensor_copy(out=Xs['xpc'][:], in_=Xs['xp'][:, :, :, 8:23]), reads=['xp'], writes=['xpc'])
            for hh in range(2):
                for c in range(2):
                    p, k = PS()
                    S.op('pe', lambda e, p=p, hh=hh, c=c: e.transpose(out=p[0:120, 0:128], in_=Xs['xpc'][:, c, hh * 8:(hh + 1) * 8, :].rearrange("p b r -> p (b r)"), identity=ident),
                         reads=['xpc', 'cst'], writes=[k])
                    S.op('dve', lambda e, p=p, hh=hh, c=c: e.tensor_copy(out=Xs['spo'][0:120, hh, c * 128:(c + 1) * 128], in_=p[0:120, 0:128]), reads=[k], writes=['spo'])
            S.dma('sp', 'o_ps', o_pool_s.rearrange("(h b) r c -> (b r) h c", h=2), Xs['spo'][:], reads=['spo'])
        except StopBuild as ex:
            print('STOPPED at', ex)
        S.final()
        sst.close()
        import os
        if os.environ.get('KDEBUG'):
            print('CNT', S.cnt, {k: v[1] for k, v in S.dsem.items()})
    return nc


_NC = None


def kernel(**inp):
    global _NC
    f = lambda a: np.ascontiguousarray(np.asarray(a, dtype=np.float32))
    if _NC is None:
        _NC = build()
    cst = make_consts()
    prm = np.concatenate([f(inp['conv_w'][0]).reshape(66, 128), f(inp['conv_b'][0]).reshape(22, 128),
                          f(inp['hgrn_lb_logits']).reshape(8, 128), f(inp['pool_scale'][0]).reshape(2, 128),
                          f(inp['hgrn_onorm_g'][0]).reshape(4, 128), f(inp['ln1_g'][0]).reshape(8, 128),
                          f(inp['ln2_g'][0]).reshape(8, 128), f(inp['mem_norm_g'][0]).reshape(8, 128)], axis=0)
    shared = dict(lnf=f(inp['lnf_g']),
                  w_in=f(inp['w_in'][0]), w_kv=f(inp['w_mem_kv'][0]), w_out=f(inp['w_out'][0]), w_up=f(inp['w_up'][0]),
                  w_dn=f(inp['w_down'][0]), pool_w=f(inp['pool_w'][0]), prm_in=f(prm), cst=cst)
    in_maps = []
    for c in range(8):
        sl = slice(16 * c, 16 * c + 16)
        m = dict(shared)
        m['x_tok'] = f(np.concatenate([inp['x_prompt'][c], np.asarray(inp['x_sample'][sl]).reshape(128, 1024)], axis=0))
        m['mem'] = f(inp['mem_prompt'][c])
        m['spool'] = f(np.asarray(inp['state_pool'][0, sl]).reshape(240, 256))
        m['shgrn'] = f(inp['state_hgrn'][0, sl])
        m['sconv'] = f(np.asarray(inp['state_conv'][0, sl]).reshape(32, 2816))
        m['ck'] = f(np.asarray(inp['cache_mem_k'][0, sl]).reshape(16, 256, 256))
        m['cv'] = f(np.asarray(inp['cache_mem_v'][0, sl]).reshape(16, 256, 256))
        in_maps.append(m)
    res = run_bass_kernel_spmd(_NC, in_maps, core_ids=list(range(8)))
    R = res.results
    g = lambda k: np.stack([np.asarray(R[c][k], dtype=np.float32) for c in range(8)])
    y = g('y_tok')
    y_prompt = np.ascontiguousarray(y[:, :2048, :])
    y_sample = np.ascontiguousarray(y[:, 2048:, :].reshape(128, 8, 1024))
    return (y_prompt, y_sample,
            g('o_pool_p')[None], g('o_hgrn_p')[None], g('o_conv_p')[None],
            g('o_mk').reshape(1, 8, 256, 4, 64), g('o_mv').reshape(1, 8, 256, 4, 64),
            g('o_pool_s').reshape(1, 128, 15, 256), g('o_hgrn_s').reshape(1, 128, 4, 128, 128),
            g('o_conv_s').reshape(1, 128, 2, 2816))
```

```python
import numpy as np
from contextlib import ExitStack
import concourse.bass as bass
import concourse.mybir as mybir
from concourse.bass_utils import run_bass_kernel_spmd

F32, BF16 = mybir.dt.float32, mybir.dt.bfloat16
AF = mybir.ActivationFunctionType
ALU = mybir.AluOpType
EPS = 1e-6
NSLOT = 4
SAME_ENGINE_SYNC = True


import os


class StopBuild(Exception):
    pass


_hits = {}


def chk(name):
    if os.environ.get('KSTOP') == name:
        _hits[name] = _hits.get(name, 0) + 1
        if _hits[name] == int(os.environ.get('KHIT', '1')):
            raise StopBuild(name)


class Sched:
    def __init__(s, nc, es):
        s.nc, s.es = nc, es
        s.E = {'pe': nc.tensor, 'act': nc.scalar, 'dve': nc.vector, 'pool': nc.gpsimd, 'sp': nc.sync}
        s.sem = {k: es.enter_context(nc.semaphore('sem_' + k)) for k in s.E}
        s.cnt = {k: 0 for k in s.E}
        s.seen = {k: {} for k in s.E}
        s.lastw, s.reads, s.dsem = {}, {}, {}
        s.psn = 0
        s.ps_open = {}

    def _semh(s, key):
        return s.sem[key] if key in s.sem else s.dsem[key][0]

    def _wait(s, eng, key, val):
        if key == eng and (eng in ('pe', 'sp') or not SAME_ENGINE_SYNC):
            return
        if s.seen[eng].get(key, 0) >= val:
            return
        s.seen[eng][key] = val
        s.E[eng].wait_ge(s._semh(key), val)

    def deps(s, eng, reads, writes):
        need = {}
        for b in reads:
            if b in s.lastw:
                k, v = s.lastw[b]
                need[k] = max(need.get(k, 0), v)
        for b in writes:
            if b in s.lastw:
                k, v = s.lastw[b]
                need[k] = max(need.get(k, 0), v)
            for (k, v) in s.reads.get(b, ()):
                need[k] = max(need.get(k, 0), v)
        for k, v in need.items():
            s._wait(eng, k, v)

    def _record(s, tok, reads, writes):
        for b in reads:
            s.reads.setdefault(b, []).append(tok)
        for b in writes:
            s.lastw[b] = tok
            s.reads[b] = []

    def op(s, eng, fn, reads=(), writes=(), inc=True):
        psr = [b for b in reads if isinstance(b, tuple) and b[0] == 'ps']
        if psr:
            reads = [b for b in reads if b not in psr]
            writes = list(writes) + psr
            for b in psr:
                s.ps_open[b[1]] -= 1
                if s.ps_open[b[1]] <= 0:
                    del s.ps_open[b[1]]
        s.deps(eng, reads, writes)
        ins = fn(s.E[eng])
        if inc:
            s.cnt[eng] += 1
            ins.then_inc(s.sem[eng], 1)
            tok = (eng, s.cnt[eng])
        else:
            tok = (eng, s.cnt[eng] + 1)
        s._record(tok, reads, writes)
        return ins

    def dma(s, eng, chan, out, in_, reads=(), writes=(), **kw):
        s.deps(eng, reads, writes)
        if chan not in s.dsem:
            s.dsem[chan] = [s.es.enter_context(s.nc.semaphore('d_' + chan)), 0]
        ins = s.E[eng].dma_start(out=out, in_=in_, **kw)
        s.dsem[chan][1] += 16
        ins.then_inc(s.dsem[chan][0], 16)
        s._record((chan, s.dsem[chan][1]), reads, writes)

    def barrier(s):
        for eng in s.E:
            for k in s.sem:
                if s.cnt[k] > 0:
                    s._wait(eng, k, s.cnt[k])
            for k in s.dsem:
                s._wait(eng, k, s.dsem[k][1])

    def final(s):
        for k in s.sem:
            if s.cnt[k] > 0:
                s._wait('sp', k, s.cnt[k])
        for k in s.dsem:
            s._wait('sp', k, s.dsem[k][1])


def make_consts():
    c = np.zeros((128, 576), np.float32)
    c[:, 0:128] = np.eye(128, dtype=np.float32)
    s = np.arange(128)[:, None]
    t = np.arange(128)[None, :]
    c[:, 128:256] = (t >= s)
    c[:, 256:384] = (t >= s) & ((t // 8) == (s // 8))
    c[:, 384:400] = (np.arange(128)[:, None] // 8) == np.arange(16)[None, :]
    c[:, 400:528] = (np.arange(128)[None, :] % 8 != 0)
    for ch in range(2):
        for p in range(128):
            w = [2, 4, 8, 16][2 * ch + p // 64]
            c[p, 528 + ch * 16: 528 + ch * 16 + 16] = 1.0 / np.minimum(w, np.arange(16) + 1.0)
            c[p, 560 + ch] = 1.0 / w
    return c


def build():
    nc = bass.Bass("TRN2", target_bir_lowering=False)
    D = lambda n, sh, k="ExternalInput": nc.dram_tensor(n, sh, F32, kind=k).ap()
    x_tok = D("x_tok", [2176, 1024]); mem = D("mem", [256, 1024])
    spool = D("spool", [240, 256]); shgrn = D("shgrn", [16, 4, 128, 128]); sconv = D("sconv", [32, 2816])
    ck = D("ck", [16, 256, 256]); cv = D("cv", [16, 256, 256])
    lnf = D("lnf", [1024])
    w_in = D("w_in", [1024, 2560]); w_kv = D("w_kv", [1024, 512]); w_out = D("w_out", [1024, 1024])
    w_up = D("w_up", [1024, 5632]); w_dn = D("w_dn", [2816, 1024])
    pool_w = D("pool_w", [4, 64, 64]); prm_in = D("prm_in", [126, 128]); cst_in = D("cst", [128, 576])
    O = lambda n, sh: D(n, sh, "ExternalOutput")
    y_tok = O("y_tok", [2176, 1024]); o_pool_p = O("o_pool_p", [15, 256]); o_hgrn_p = O("o_hgrn_p", [4, 128, 128])
    o_conv_p = O("o_conv_p", [2, 2816]); o_mk = O("o_mk", [256, 256]); o_mv = O("o_mv", [256, 256])
    o_pool_s = O("o_pool_s", [16, 15, 256]); o_hgrn_s = O("o_hgrn_s", [16, 4, 128, 128]); o_conv_s = O("o_conv_s", [32, 2816])

    with ExitStack() as es:
        S = Sched(nc, es)
        uid = [0]
        def sbt(st, n, sh, dt=F32):
            uid[0] += 1
            return st.enter_context(nc.sbuf_tensor("sb%d_%s" % (uid[0], n), sh, dt))
        psb = [es.enter_context(nc.psum_tensor("ps%d" % i, [128, 512], F32)) for i in range(8)]

        def PS(n=1):
            for _ in range(8):
                i = S.psn % 8
                S.psn += 1
                if i not in S.ps_open:
                    break
            else:
                raise RuntimeError('no free PSUM bank')
            S.ps_open[i] = n
            return psb[i], ('ps', i)

        cst = sbt(es, "cst", [128, 576]); prm = sbt(es, "prm", [128, 128]); prm_st = sbt(es, "prm_st", [126, 128])
        idb = sbt(es, "idb", [128, 128], BF16); onesb = sbt(es, "onesb", [128, 128], BF16)
        gf = sbt(es, "gf", [128, 1024])
        wbd = sbt(es, "wbd", [128, 2, 128], BF16)
        lbc = sbt(es, "lbc", [128, 16])
        small = sbt(es, "small", [128, 8])
        ring = [sbt(es, "ring%d" % i, [128, 5632], BF16) for i in range(NSLOT)]
        G = {}
        stat = sbt(es, "stat", [128, 32])
        ident = cst[:, 0:128]; cmask = cst[:, 128:256]; smask = cst[:, 256:384]; seqm = cst[:, 384:400]
        rmask = cst[:, 400:528]; invw = cst[:, 560:562]
        invcnt = cst[:, 528:560]
        epsc = small[:, 0:1]; mhalf = small[:, 1:2]; onec = small[:, 2:3]
        cw = lambda r, j: prm[:, r * 22 + j: r * 22 + j + 1]
        cb = lambda j: prm[:, 66 + j: 67 + j]
        pscale = lambda c: prm[:, 96 + c: 97 + c]
        onorm = lambda h: prm[:, 98 + h: 99 + h]

        wseq = []

        def ld_in(b):
            def f(slot, key, chan):
                S.dma('pool', chan, slot[:, 0:4096].rearrange("p (k n) -> p k n", k=8),
                      w_in[:, b * 512:(b + 1) * 512].rearrange("(k p) n -> p k n", p=128), writes=[key])
            return f

        def ld_kv():
            def f(slot, key, chan):
                S.dma('pool', chan, slot[:, 0:4096].rearrange("p (k n) -> p k n", k=8),
                      w_kv.rearrange("(k p) n -> p k n", p=128), writes=[key])
            return f

        def ld_out(c):
            def f(slot, key, chan):
                S.dma('pool', chan, slot[:, 0:4096].rearrange("p (k n) -> p k n", k=8),
                      w_out[:, c * 512:(c + 1) * 512].rearrange("(k p) n -> p k n", p=128), writes=[key])
            return f

        def ld_up(r):
            def f(slot, key, chan):
                v = slot[:, 0:4096].rearrange("p (k n) -> p k n", k=8)
                S.dma('pool', chan, v[:, :, 0:256],
                      w_up[:, r * 256:(r + 1) * 256].rearrange("(k p) n -> p k n", p=128), writes=[key])
                S.dma('pool', chan, v[:, :, 256:512],
                      w_up[:, 2816 + r * 256:2816 + (r + 1) * 256].rearrange("(k p) n -> p k n", p=128), writes=[key])
            return f

        def ld_dn(q):
            def f(slot, key, chan):
                S.dma('pool', chan, slot[:, 0:5632].rearrange("p (k n) -> p k n", k=22),
                      w_dn[:, q * 256:(q + 1) * 256].rearrange("(k p) n -> p k n", p=128), writes=[key])
            return f

        wscr = nc.dram_tensor("wscr", [22, 128, 5632], BF16, kind="Internal").ap()
        blocks = [('in', b) for b in range(5)] + [('out', c) for c in range(2)] + [('up', r) for r in range(11)] + [('dn', q) for q in range(4)]
        mk = {'in': ld_in, 'out': ld_out, 'up': ld_up, 'dn': ld_dn}
        for ti in range(5):
            for bi, (kind, idx) in enumerate(blocks):
                n = 5632 if kind == 'dn' else 4096
                if ti == 0:
                    wseq.append((mk[kind](idx), bi, n))
                    if bi == 2:
                        wseq.append((ld_kv(), None, 0))
                else:
                    def f(slot, key, chan, bi=bi, n=n):
                        S.dma('pool', chan, slot[:, 0:n], wscr[bi, :, 0:n], reads=[('scr', bi)], writes=[key])
                    wseq.append((f, None, n))
        wstate = {'issued': 0, 'next': 0}

        def wneed(prefetch=True):
            i = wstate['next']
            wstate['next'] += 1
            upto = min(i + NSLOT - 1, len(wseq) - 1) if prefetch else i
            while wstate['issued'] <= upto:
                j = wstate['issued']
                wseq[j][0](ring[j % NSLOT], ('w', j % NSLOT), 'w%d' % (j % NSLOT))
                wstate['issued'] += 1
            sl = ring[i % NSLOT]
            bi, n = wseq[i][1], wseq[i][2]
            if bi is not None:
                S.dma('sp', 'wb%d' % (i % NSLOT), wscr[bi, :, 0:n], sl[:, 0:n], reads=[('w', i % NSLOT)], writes=[('scr', bi)])
            return sl, ('w', i % NSLOT)

        def w8(sl):
            return sl[:, 0:4096].rearrange("p (k n) -> p k n", k=8)

        def w22(sl):
            return sl[:, 0:5632].rearrange("p (k n) -> p k n", k=22)

        def mmg(out_ap, pskey, pairs, reads, per=None):
            n = len(pairs)
            for i, (l, r) in enumerate(pairs):
                S.op('pe', lambda e, l=l, r=r, i=i: e.matmul(out_ap, lhsT=l, rhs=r, start=(i == 0), stop=(i == n - 1)),
                     reads=list(reads) + (list(per[i]) if per else []), writes=[pskey], inc=(i == n - 1))

        pst = ExitStack(); sst = ExitStack()
        es.enter_context(pst); es.enter_context(sst)
        try:
            S.dma('sp', 'c0', cst[:], cst_in[:, :], writes=['cst'])
            S.dma('sp', 'c1', prm_st[:], prm_in[:, :], writes=['prm_st'])
            S.dma('sp', 'c4', gf[:], lnf.partition_broadcast(128), writes=['gf'])
            S.op('dve', lambda e: e.memset(small[:, 0:1], EPS), writes=['small'])
            S.op('dve', lambda e: e.memset(small[:, 1:2], -0.5), writes=['small'])
            S.op('dve', lambda e: e.memset(small[:, 2:3], 1.0), writes=['small'])
            S.op('dve', lambda e: e.memset(onesb[:], 1.0), writes=['onesb'])
            S.op('dve', lambda e: e.tensor_copy(out=idb[:], in_=ident), reads=['cst'], writes=['idb'])
            S.op('pool', lambda e: e.memset(wbd[:], 0.0), writes=['wbd'])
            for gi in range(4):
                c, o = gi // 2, (gi % 2) * 64
                S.dma('pool', 'c5', wbd[o:o + 64, c, o:o + 64], pool_w[gi, :, :], writes=['wbd'])
            p_, pk = PS()
            S.op('pe', lambda e: e.transpose(out=p_[:, 0:126], in_=prm_st[:, :], identity=cst[0:126, 0:126]),
                 reads=['prm_st', 'cst'], writes=[pk])
            S.op('dve', lambda e: e.tensor_copy(out=prm[:, 0:126], in_=p_[:, 0:126]), reads=[pk], writes=['prm'])
            S.op('dve', lambda e: e.tensor_sub(out=lbc[:, 8:12], in0=prm[:, 88:92], in1=prm[:, 92:96]), reads=['prm'], writes=['lbc'])
            S.op('act', lambda e: e.activation(out=lbc[:, 8:12], in_=lbc[:, 8:12], func=AF.Tanh, scale=0.5), reads=['lbc'], writes=['lbc'])
            S.op('dve', lambda e: e.tensor_scalar(out=lbc[:, 0:4], in0=lbc[:, 8:12], scalar1=0.25, scalar2=0.75, op0=ALU.mult, op1=ALU.add), reads=['lbc'], writes=['lbc'])
            S.op('dve', lambda e: e.tensor_scalar(out=lbc[:, 4:8], in0=lbc[:, 8:12], scalar1=-0.25, scalar2=0.25, op0=ALU.mult, op1=ALU.add), reads=['lbc'], writes=['lbc'])
            S.op('dve', lambda e: e.tensor_scalar(out=lbc[:, 12:16], in0=lbc[:, 8:12], scalar1=0.25, scalar2=-0.25, op0=ALU.mult, op1=ALU.add), reads=['lbc'], writes=['lbc'])

            def rms_to_T(src_ap, gcol, dstT, col0, rd, wr_extra=()):
                hb = G['hb']
                S.op('act', lambda e: e.activation(out=hb[:], in_=src_ap, func=AF.Square, accum_out=stat[:, 0:1]),
                     reads=rd, writes=['hb', 'stat'])
                S.op('dve', lambda e: e.tensor_scalar(out=stat[:, 1:2], in0=stat[:, 0:1], scalar1=1.0 / 1024, scalar2=EPS, op0=ALU.mult, op1=ALU.add),
                     reads=['stat'], writes=['stat'])
                S.op('pool', lambda e: e.tensor_tensor(out=stat[:, 2:3], in0=stat[:, 1:2], in1=mhalf, op=ALU.pow),
                     reads=['stat', 'small'], writes=['stat'])
                chk('rms_a')
                S.op('act', lambda e: e.activation(out=hb[:], in_=src_ap, func=AF.Copy, scale=stat[:, 2:3]),
                     reads=list(rd) + ['stat'], writes=['hb'])
                chk('rms_b')
                p, k = PS()
                pb = p[:].bitcast(BF16)
                for kk in range(8):
                    S.op('pe', lambda e, kk=kk: e.transpose(out=pb[:, kk * 128:(kk + 1) * 128], in_=hb[:, kk * 128:(kk + 1) * 128], identity=idb[:]),
                         reads=['hb', 'idb'], writes=[k], inc=(kk == 7))
                chk('rms_c')
                S.op('dve', lambda e: e.tensor_tensor(out=dstT[:, :, col0:col0 + 128], in0=pb.rearrange("p (k n) -> p k n", k=8),
                                                      in1=prm[:, gcol:gcol + 8].unsqueeze(2).to_broadcast([128, 8, 128]), op=ALU.mult),
                     reads=[k, 'prm'], writes=list(wr_extra))
                chk('rms_d')

            def rms_multi(items, gcol, dstT, wr, base=0, phase='all'):
                n = len(items)
                hbs = G['hbs']
                if phase in ('all', 'stats'):
                    for i_, (src, rd, col0) in enumerate(items):
                        S.op('act', lambda e, i_=i_, src=src: e.activation(out=hbs[(base + i_) % 2][:], in_=src, func=AF.Square, accum_out=stat[:, base + i_:base + i_ + 1]),
                             reads=rd, writes=['hb%d' % ((base + i_) % 2), 'stat'])
                    S.op('act', lambda e: e.activation(out=stat[:, 4 + base:4 + base + n], in_=stat[:, base:base + n], func=AF.Ln, scale=1.0 / 1024, bias=epsc), reads=['stat', 'small'], writes=['stat'])
                    S.op('act', lambda e: e.activation(out=stat[:, 8 + base:8 + base + n], in_=stat[:, 4 + base:4 + base + n], func=AF.Exp, scale=-0.5), reads=['stat'], writes=['stat'])
                if phase == 'stats':
                    return
                for i_, (src, rd, col0) in enumerate(items):
                    hb = hbs[(base + i_) % 2]
                    S.op('act', lambda e, i_=i_, src=src, hb=hb: e.activation(out=hb[:], in_=src, func=AF.Copy, scale=stat[:, 8 + base + i_:9 + base + i_]),
                         reads=list(rd) + ['stat'], writes=['hb%d' % ((base + i_) % 2)])
                    p, k = PS()
                    pb = p[:].bitcast(BF16)
                    for kk in range(8):
                        S.op('pe', lambda e, kk=kk, hb=hb, pb=pb: e.transpose(out=pb[:, kk * 128:(kk + 1) * 128], in_=hb[:, kk * 128:(kk + 1) * 128], identity=idb[:]),
                             reads=['hb%d' % ((base + i_) % 2), 'idb'], writes=[k], inc=(kk == 7))
                    S.op('dve', lambda e, col0=col0, pb=pb: e.tensor_tensor(out=dstT[:, :, col0:col0 + 128], in0=pb.rearrange("p (k n) -> p k n", k=8),
                                                                          in1=prm[:, gcol:gcol + 8].unsqueeze(2).to_broadcast([128, 8, 128]), op=ALU.mult),
                         reads=[k, 'prm'], writes=list(wr))

            def alloc_phase(st, T, sample):
                B = {}
                B['T'] = T
                ns = T // 128
                G['xres'] = sbt(st, "xres", [128, ns, 1024]); G['hT'] = sbt(st, "hT", [128, 8, T], BF16)
                G['mixT'] = sbt(st, "mixT", [128, 8, T], BF16); G['mT'] = sbt(st, "mT", [128, 22, T], BF16)
                G['hbs'] = [sbt(st, "hb%d" % i, [128, 1024], BF16) for i in range(2)]
                G['hb'] = G['hbs'][0]
                B['u'] = sbt(st, "u_sb", [128, 2, 15 + T]) if not sample else None
                B['pA'] = sbt(st, "pA", [128, max(16 + T, 368)]); B['pB'] = sbt(st, "pB", [128, max(16 + T, 368)])
                B['d'] = sbt(st, "d_sb", [128, 2, T], BF16)
                B['Ab'] = [sbt(st, "A_sb%d" % i, [128, T]) for i in range(2)]
                B['E2b'] = [sbt(st, "E2_%d" % i, [128, T]) for i in range(2)]
                B['K1b'] = [sbt(st, "K1_%d" % i, [128, T]) for i in range(2)]
                B['QSall'] = sbt(st, "QSall", [128, 4, T]); B['THall'] = sbt(st, "THall", [128, 4, T])
                B['GS'] = sbt(st, "GS", [128, 4, T])
                B['qAT'] = sbt(st, "qAT", [128, 4, T], BF16); B['kAT'] = sbt(st, "kAT", [128, 4, T], BF16)
                B['kAk'] = sbt(st, "kAk", [128, T // 128, 512], BF16); B['vtk'] = sbt(st, "vtk", [128, T // 128, 512], BF16)
                B['eAe'] = sbt(st, "eAe", [128, 4, 16])
                B['PT'] = sbt(st, "PT", [128, 4, 128], BF16)
                B['osb'] = sbt(st, "osb", [128, 4, T]); B['osq'] = sbt(st, "osq", [128, T], BF16)
                B['R'] = sbt(st, "R_sb", [128, T]); B['t1'] = sbt(st, "t1", [128, T])
                B['qxT'] = sbt(st, "qxT", [128, 2, T], BF16)
                B['cbuf'] = [B['QSall'][:, i, :] for i in range(2)]
                B['gbuf'] = [B['QSall'][:, 2 + i, :] for i in range(2)]
                return B

            def do_tile(B, t0, first, sample, X):
                T = B['T']; nsub = T // 128
                xres, hT, mixT, mT = G['xres'], G['hT'], G['mixT'], G['mT']
                for sub in range(nsub):
                    S.dma('sp', 'xin%d' % sub, xres[:, sub, :], x_tok[t0 + sub * 128:t0 + (sub + 1) * 128, :], writes=['xres%d' % sub])
                if first and not sample:
                    X['memdma']()
                if sample:
                    X['pre1']()
                if not X.get('prenormed'):
                    rms_multi([(xres[:, sub, :], ['xres%d' % sub], sub * 128) for sub in range(nsub)], 102, hT, ['hT'])
                X['prenormed'] = False
                chk('norm1')
                QSall, THall, GS = B['QSall'], B['THall'], B['GS']
                done = {'pool': False, 'attn': False, 'inproj': False}

                def g_inproj(b0, b1):
                    for b in range(b0, b1):
                        wsl, wk = wneed()
                        wv = w8(wsl)
                        for jj in range(4):
                            j = b * 4 + jj
                            if 10 <= j < 14:
                                continue
                            yield
                            p, k = PS()
                            mmg(p[:, 0:T], k, [(wv[:, kk, jj * 128:(jj + 1) * 128], hT[:, kk, 0:T]) for kk in range(8)], ['hT', wk])
                            src = p[:, 0:T]
                            if j < 2:
                                if sample:
                                    S.op('act', lambda e, j=j, src=src: e.activation(out=X['xp'][:, j, :, 15:23], in_=src.rearrange("p (b t) -> p b t", t=8), func=AF.Copy),
                                         reads=[k], writes=['xp'])
                                else:
                                    S.op('act', lambda e, j=j, src=src: e.activation(out=B['u'][:, j, 15:15 + T], in_=src, func=AF.Copy), reads=[k], writes=['u'])
                            elif j < 6:
                                S.op('act', lambda e, j=j, src=src: e.activation(out=QSall[:, j - 2, :], in_=src, func=AF.Silu), reads=[k], writes=['QS%d' % (j - 2)])
                            elif j < 10:
                                S.op('act', lambda e, j=j, src=src: e.activation(out=THall[:, j - 6, :], in_=src, func=AF.Tanh, scale=0.5), reads=[k], writes=['TH%d' % (j - 6)])
                            elif j < 18:
                                h = j - 14
                                S.op('act', lambda e, h=h, src=src: e.activation(out=GS[:, h, :], in_=src, func=AF.Copy), reads=[k], writes=['GS%d' % h])
                            else:
                                S.op('act', lambda e, j=j, src=src: e.activation(out=B['qxT'][:, j - 18, :], in_=src, func=AF.Copy), reads=[k], writes=['qxT'])
                        if b in (2, 3):
                            c0 = 256 if b == 2 else 0
                            o0 = 0 if b == 2 else 256
                            for sub in range(nsub):
                                yield
                                p, k = PS()
                                mmg(p[:, 0:256], k, [(hT[:, kk, sub * 128:(sub + 1) * 128], wv[:, kk, c0:c0 + 256]) for kk in range(8)], ['hT', wk])
                                S.op('dve', lambda e, p=p, sub=sub, o0=o0: e.tensor_copy(out=B['vtk'][:, sub, o0:o0 + 256], in_=p[:, 0:256]), reads=[k], writes=['vtk'])

                    if b1 == 5:
                        done['inproj'] = True
                    yield

                for _ in g_inproj(0, 3):
                    pass
                if first and not sample:
                    X['memkv']()
                chk('inproj')
                kgen = X['pre2']() if sample else iter(())

                def kstep(n=1):
                    for _ in range(n):
                        next(kgen, None)
                kstep(2)
                def g_pool():
                    pA, pB, dsb = B['pA'], B['pB'], B['d']
                    Rb = G['hbs'][0][:].bitcast(F32)
                    for c in range(2):
                        kstep(1)
                        if sample:
                            Xv = X['xp'][:, c, :, :]
                            L = 23
                            sl = lambda buf, n: buf[:, 0:16 * n].rearrange("p (b n) -> p b n", b=16)
                            xs = lambda a, b_: Xv[:, :, a:b_]
                            xkey = 'xp'
                        else:
                            u = B['u']
                            if first and c == 0:
                                yield
                                S.op('dve', lambda e: e.memset(u[:, :, 0:15], 0.0), writes=['u'])
                            L = 15 + T
                            sl = lambda buf, n: buf[:, 0:n]
                            xs = lambda a, b_, c=c: u[:, c, a:b_]
                            xkey = 'u'
                        sv = lambda buf, n, a, b_: (sl(buf, n)[:, :, a:b_] if sample else sl(buf, n)[:, a:b_])
                        yield
                        S.op('dve', lambda e: e.tensor_tensor(out=sl(pA, L - 1), in0=xs(1, L), in1=xs(0, L - 1), op=ALU.add), reads=[xkey], writes=['pA'])
                        yield
                        S.op('dve', lambda e: e.tensor_tensor(out=sl(pB, L - 3), in0=sv(pA, L - 1, 2, L - 1), in1=sv(pA, L - 1, 0, L - 3), op=ALU.add), reads=['pA'], writes=['pB'])
                        uview = xs(15, L)
                        dv = dsb[:, c, :].rearrange("p (b t) -> p b t", t=8) if sample else dsb[:, c, :]

                        def comb(plo, phi, buf, n, off, wsel, dv=dv, uview=uview):
                            S.op('dve', lambda e: e.scalar_tensor_tensor(out=dv[plo:phi], in0=sv(buf, n, off, off + (8 if sample else T))[plo:phi], scalar=invw[plo:phi, c:c + 1],
                                                                          in1=uview[plo:phi], op0=ALU.mult, op1=ALU.subtract),
                                 reads=[wsel, xkey, 'cst'], writes=['d'])
                        if c == 0:
                            yield
                            comb(0, 64, pA, L - 1, 14, 'pA')
                            yield
                            comb(64, 128, pB, L - 3, 12, 'pB')
                        else:
                            yield
                            S.op('dve', lambda e: e.tensor_tensor(out=sl(pA, L - 7), in0=sv(pB, L - 3, 4, L - 3), in1=sv(pB, L - 3, 0, L - 7), op=ALU.add), reads=['pB'], writes=['pA'])
                            yield
                            comb(0, 64, pA, L - 7, 8, 'pA')
                            yield
                            S.op('dve', lambda e: e.tensor_tensor(out=sl(pB, L - 15), in0=sv(pA, L - 7, 8, L - 7), in1=sv(pA, L - 7, 0, L - 15), op=ALU.add), reads=['pA'], writes=['pB'])
                            yield
                            comb(64, 128, pB, L - 15, 0, 'pB')
                        if first and not sample:
                            for (plo, phi, buf, off) in ((0, 64, pA, 14 if c == 0 else 8), (64, 128, pB, 12 if c == 0 else 0)):
                                yield
                                S.op('dve', lambda e, plo=plo, phi=phi, buf=buf, off=off: e.tensor_tensor(out=Rb[plo:phi, 0:15], in0=buf[plo:phi, off:off + 15],
                                                                                                          in1=invcnt[plo:phi, c * 16:c * 16 + 15], op=ALU.mult),
                                     reads=['pA', 'pB', 'cst'], writes=['hb0'])
                                yield
                                S.op('dve', lambda e, plo=plo, phi=phi: e.tensor_tensor(out=dsb[plo:phi, c, 0:15], in0=Rb[plo:phi, 0:15], in1=u[plo:phi, c, 15:30], op=ALU.subtract),
                                     reads=['hb0', 'u'], writes=['d'])
                        yield
                        p, k = PS()
                        yield
                        mmg(p[:, 0:T], k, [(wbd[:, c, :], dsb[:, c, :])], ['wbd', 'd'])
                        yield
                        S.op('act', lambda e, p=p, c=c: e.activation(out=mixT[:, c, 0:T], in_=p[:, 0:T], func=AF.Identity, scale=pscale(c)), reads=[k, 'prm'], writes=['mixT%d' % c])
                    if not sample:
                        u = B['u']
                        if X.get('last'):
                            for c in range(2):
                                yield
                                p, k = PS()
                                yield
                                S.op('pe', lambda e, p=p, c=c: e.transpose(out=p[0:15, c * 128:(c + 1) * 128], in_=u[:, c, T:T + 15], identity=ident), reads=['u', 'cst'], writes=[k])
                                yield
                                S.op('dve', lambda e, p=p, c=c: e.tensor_copy(out=X['ppo'][0:15, c * 128:(c + 1) * 128], in_=p[0:15, c * 128:(c + 1) * 128]), reads=[k], writes=['ppo'])
                            S.dma('sp', 'o_pp', o_pool_p[:, :], X['ppo'][0:15, :], reads=['ppo'])
                        else:
                            yield
                            S.op('dve', lambda e: e.tensor_copy(out=u[:, :, 0:15], in_=u[:, :, T:T + 15]), reads=['u'], writes=['u'])

                    yield
                def g_hgrn():
                    qAT, kAT, eAe = B['qAT'], B['kAT'], B['eAe']
                    smk = rmask if sample else X['cm512'][:, :]
                    smkey = 'cst' if sample else 'cm512'

                    def stA(h):
                        par = h % 2
                        K1 = B['K1b'][par]
                        thk = 'TH%d' % h
                        S.op('act', lambda e: e.activation(out=K1[:], in_=THall[:, h, :], func=AF.Identity, scale=lbc[:, 12 + h:13 + h], bias=lbc[:, 4 + h:5 + h]),
                             reads=[thk, 'lbc'], writes=['K1%d' % par])
                        S.op('act', lambda e: e.activation(out=THall[:, h, :], in_=THall[:, h, :], func=AF.Ln, scale=lbc[:, 4 + h:5 + h], bias=lbc[:, h:h + 1]),
                             reads=[thk, 'lbc'], writes=[thk])

                    def stB(h):
                        par = h % 2
                        A = B['Ab'][par]
                        S.op('dve', lambda e: e.tensor_tensor_scan(out=A[:], data0=smk, data1=THall[:, h, :], initial=0.0, op0=ALU.mult, op1=ALU.add),
                             reads=['TH%d' % h, smkey], writes=['A%d' % par])

                    def stC(h):
                        par = h % 2
                        A, E2 = B['Ab'][par], B['E2b'][par]
                        S.op('act', lambda e: e.activation(out=E2[:], in_=A[:], func=AF.Exp, scale=-1.0), reads=['A%d' % par], writes=['E2%d' % par])
                        S.op('act', lambda e: e.activation(out=A[:], in_=A[:], func=AF.Exp), reads=['A%d' % par], writes=['A%d' % par])

                    def stD(h):
                        par = h % 2
                        A, E2, K1 = B['Ab'][par], B['E2b'][par], B['K1b'][par]
                        ka, ke, kk1 = 'A%d' % par, 'E2%d' % par, 'K1%d' % par
                        S.op('dve', lambda e: e.tensor_tensor(out=qAT[:, h, :], in0=QSall[:, h, :], in1=A[:], op=ALU.mult), reads=['QS%d' % h, ka], writes=['qAT'])
                        S.op('dve', lambda e: e.tensor_tensor(out=kAT[:, h, :], in0=K1[:], in1=E2[:], op=ALU.mult), reads=[kk1, ke], writes=['kAT'])
                        if sample:
                            S.op('dve', lambda e: e.tensor_copy(out=eAe[:, h, 0:16], in_=A[:].rearrange("p (b t) -> p b t", t=8)[:, :, 7]), reads=[ka], writes=['eAe'])
                        else:
                            S.op('dve', lambda e: e.tensor_copy(out=eAe[:, h, 0:nsub], in_=A[:].rearrange("p (c t) -> p c t", t=128)[:, :, 127]), reads=[ka], writes=['eAe'])

                    stA(0)
                    yield
                    stB(0)
                    yield
                    kstep()
                    stA(1)
                    yield
                    stC(0)
                    yield
                    stB(1)
                    yield
                    kstep()
                    stD(0)
                    yield
                    stA(2)
                    yield
                    stC(1)
                    yield
                    stB(2)
                    yield
                    kstep()
                    stD(1)
                    yield
                    stA(3)
                    yield
                    stC(2)
                    yield
                    stB(3)
                    yield
                    kstep()
                    stD(2)
                    yield
                    stC(3)
                    yield
                    stD(3)
                    yield
                    kstep(8)
                    chk('hgrn_ew')
                    while not done['inproj']:
                        yield
                    for h in range(4):
                        S.op('act', lambda e, h=h: e.activation(out=GS[:, h, :], in_=GS[:, h, :], func=AF.Silu), reads=['GS%d' % h], writes=['GS%d' % h])
                    for cc in range(nsub):
                        yield
                        p, k = PS()
                        pb = p[:].bitcast(BF16)
                        for h in range(4):
                            yield
                            S.op('pe', lambda e, h=h, cc=cc, pb=pb: e.transpose(out=pb[:, h * 128:(h + 1) * 128], in_=kAT[:, h, cc * 128:(cc + 1) * 128], identity=idb[:]),
                                 reads=['kAT', 'idb'], writes=[k], inc=(h == 3))
                        yield
                        S.op('act', lambda e, cc=cc, pb=pb: e.activation(out=B['kAk'][:, cc, :], in_=pb[:, 0:512], func=AF.Copy), reads=[k], writes=['kAk%d' % cc])
                    chk('katr')
                    PT, osb, kAk, vtk = B['PT'], B['osb'], B['kAk'], B['vtk']
                    for cc in range(nsub):
                        cs = slice(cc * 128, (cc + 1) * 128)
                        yield
                        pS, kS = PS()
                        for h in range(4):
                            yield
                            mmg(pS[:, h * 128:(h + 1) * 128], kS, [(kAT[:, h, cs], qAT[:, h, cs])], ['kAT', 'qAT'])
                        msk = smask if sample else cmask
                        yield
                        S.op('dve', lambda e, pS=pS, msk=msk: e.tensor_tensor(out=PT[:], in0=pS[:, :].rearrange("p (h t) -> p h t", h=4),
                                                                              in1=msk.unsqueeze(1).to_broadcast([128, 4, 128]), op=ALU.mult),
                             reads=[kS, 'cst'], writes=['PT'])
                        yield
                        pO, kO = PS()
                        if not sample:
                            Sf, Sb = X['Sf'], X['Sb']
                            for h in range(4):
                                yield
                                mmg(pO[:, h * 128:(h + 1) * 128], kO, [(vtk[:, cc, h * 128:(h + 1) * 128], PT[:, h, :]), (Sb[:, h, :], qAT[:, h, cs])],
                                    ['vtk', 'PT', 'Sb', 'qAT'])
                            yield
                            S.op('act', lambda e, pO=pO, cs=cs: e.activation(out=osb[:, :, cs], in_=pO[:, :].rearrange("p (h t) -> p h t", h=4), func=AF.Copy), reads=[kO], writes=['osb'])
                            yield
                            pZ, kZ = PS()
                            for h in range(4):
                                yield
                                mmg(pZ[:, h * 128:(h + 1) * 128], kZ, [(kAk[:, cc, h * 128:(h + 1) * 128], vtk[:, cc, h * 128:(h + 1) * 128])], ['kAk%d' % cc, 'vtk'])
                            yield
                            S.op('dve', lambda e, pZ=pZ: e.tensor_tensor(out=Sf[:], in0=Sf[:], in1=pZ[:, :].rearrange("p (h v) -> p h v", h=4), op=ALU.add), reads=[kZ, 'Sf'], writes=['Sf'])
                            yield
                            S.op('dve', lambda e, cc=cc: e.tensor_tensor(out=Sf[:], in0=Sf[:], in1=eAe[:, :, cc:cc + 1].to_broadcast([128, 4, 128]), op=ALU.mult), reads=['Sf', 'eAe'], writes=['Sf'])
                            yield
                            S.op('act', lambda e: e.activation(out=Sb[:], in_=Sf[:], func=AF.Copy), reads=['Sf'], writes=['Sb'])
                        else:
                            S0f, S0b = X['S0f'], X['S0b']
                            for h in range(4):
                                pairs = [(vtk[:, 0, h * 128:(h + 1) * 128], PT[:, h, :])]
                                n = 17
                                yield
                                S.op('pe', lambda e, h=h: e.matmul(pO[:, h * 128:(h + 1) * 128], lhsT=vtk[:, 0, h * 128:(h + 1) * 128], rhs=PT[:, h, :], start=True, stop=False),
                                     reads=['vtk', 'PT'], writes=[kO], inc=False)
                                for bq in range(16):
                                    yield
                                    S.op('pe', lambda e, h=h, bq=bq: e.matmul(pO[:, h * 128 + bq * 8:h * 128 + bq * 8 + 8], lhsT=S0b[:, bq, h, :], rhs=qAT[:, h, bq * 8:bq * 8 + 8],
                                                                             start=False, stop=(bq == 15)),
                                         reads=['S0b', 'qAT'], writes=[kO], inc=(bq == 15))
                            yield
                            S.op('act', lambda e, pO=pO: e.activation(out=osb[:, :, 0:128], in_=pO[:, :].rearrange("p (h t) -> p h t", h=4), func=AF.Copy), reads=[kO], writes=['osb'])
                            Vb2 = X['Vblk2']

                            def stV(i):
                                h, bg = i // 4, i % 4
                                vb = Vb2[i % 2]
                                S.op('dve', lambda e: e.tensor_tensor(out=vb[:], in0=vtk[:, 0, h * 128:(h + 1) * 128].unsqueeze(1).to_broadcast([128, 4, 128]),
                                                                      in1=seqm[:, bg * 4:bg * 4 + 4].unsqueeze(2).to_broadcast([128, 4, 128]), op=ALU.mult),
                                     reads=['vtk', 'cst'], writes=['Vblk%d' % (i % 2)])

                            def stU(i):
                                h, bg = i // 4, i % 4
                                vb = Vb2[i % 2]
                                pZ, kZ = PS()
                                mmg(pZ[:, :], kZ, [(kAk[:, 0, h * 128:(h + 1) * 128], vb[:].rearrange("p b v -> p (b v)"))], ['kAk0', 'Vblk%d' % (i % 2)])
                                S.op('dve', lambda e: e.tensor_tensor(out=S0f[:, bg * 4:bg * 4 + 4, h, :], in0=S0f[:, bg * 4:bg * 4 + 4, h, :],
                                                                      in1=pZ[:, :].rearrange("p (b v) -> p b v", b=4), op=ALU.add),
                                     reads=[kZ, 'S0f'], writes=['S0f'])
                                S.op('dve', lambda e: e.tensor_tensor(out=S0f[:, bg * 4:bg * 4 + 4, h, :], in0=S0f[:, bg * 4:bg * 4 + 4, h, :],
                                                                      in1=eAe[:, h, bg * 4:bg * 4 + 4].unsqueeze(2).to_broadcast([128, 4, 128]), op=ALU.mult),
                                     reads=['S0f', 'eAe'], writes=['S0f'])
                            yield
                            stV(0)
                            for i in range(16):
                                if i + 1 < 16:
                                    yield
                                    stV(i + 1)
                                yield
                                stU(i)
                            S.dma('sp', 'o_hs', o_hgrn_s.rearrange("b h d v -> d b h v"), S0f[:], reads=['S0f'])
                    if (not sample) and X.get('last'):
                        S.dma('sp', 'o_hp', o_hgrn_p.rearrange("h d v -> d h v"), X['Sf'][:], reads=['Sf'])
                    chk('hgrn')
                    osq, R, t1 = B['osq'], B['R'], B['t1']
                    for h in range(4):
                        yield
                        S.op('act', lambda e, h=h: e.activation(out=osq[:], in_=osb[:, h, :], func=AF.Square), reads=['osb'], writes=['osq'])
                        yield
                        p, k = PS()
                        yield
                        mmg(p[:, 0:T], k, [(onesb[:], osq[:])], ['onesb', 'osq'])
                        yield
                        S.op('act', lambda e, p=p: e.activation(out=R[:], in_=p[:, 0:T], func=AF.Ln, scale=1.0 / 128, bias=epsc), reads=[k, 'small'], writes=['R'])
                        yield
                        S.op('act', lambda e: e.activation(out=R[:], in_=R[:], func=AF.Exp, scale=-0.5), reads=['R'], writes=['R'])
                        yield
                        S.op('dve', lambda e, h=h: e.tensor_tensor(out=t1[:], in0=osb[:, h, :], in1=R[:], op=ALU.mult), reads=['osb', 'R'], writes=['t1'])
                        yield
                        S.op('dve', lambda e, h=h: e.scalar_tensor_tensor(out=mixT[:, 2 + h, 0:T], in0=t1[:], scalar=onorm(h), in1=GS[:, h, :], op0=ALU.mult, op1=ALU.mult),
                             reads=['t1', 'GS%d' % h, 'prm'], writes=['mixT%d' % (2 + h)])

                    yield
                def g_attn():
                    qxT = B['qxT']
                    Ra = G['hbs'][1][:].bitcast(F32); Rb = G['hbs'][0][:].bitcast(F32)
                    while not done['inproj']:
                        yield
                    if sample:
                        for _ in range(24):
                            yield
                        kstep(8)
                    if not sample:
                        PTa = X['PTa']
                        for pr in range(2):
                            for hh in range(2):
                                h = pr * 2 + hh
                                rows = slice(hh * 64, hh * 64 + 64)
                                for mc in range(2):
                                    yield
                                    p, k = PS()
                                    yield
                                    mmg(p[:, 0:T], k, [(KT[rows, pr, mc * 128:(mc + 1) * 128], qxT[rows, pr, :])], ['KT', 'qxT'])
                                    yield
                                    S.op('act', lambda e, p=p, hh=hh, mc=mc: e.activation(out=PTa[:, hh, mc, :], in_=p[:, 0:T], func=AF.Exp, scale=0.125), reads=[k], writes=['PTa'])
                            yield
                            pO, kO = PS()
                            yield
                            pD, kD = PS()
                            for hh in range(2):
                                h = pr * 2 + hh
                                rows = slice(hh * 64, hh * 64 + 64)
                                for mc in range(2):
                                    yield
                                    S.op('pe', lambda e, hh=hh, mc=mc, h=h, rows=rows, pO=pO: e.matmul(pO[rows, 0:T], lhsT=Vb[:, mc, h * 64:(h + 1) * 64], rhs=PTa[:, hh, mc, :],
                                                                                                        start=(mc == 0), stop=(mc == 1)),
                                         reads=['Vb', 'PTa'], writes=[kO], inc=(mc == 1))
                                for mc in range(2):
                                    yield
                                    S.op('pe', lambda e, hh=hh, mc=mc, rows=rows, pD=pD: e.matmul(pD[rows, 0:T], lhsT=onesb[:, 0:64], rhs=PTa[:, hh, mc, :],
                                                                                                  start=(mc == 0), stop=(mc == 1)),
                                         reads=['onesb', 'PTa'], writes=[kD], inc=(mc == 1))
                            yield
                            S.op('act', lambda e, pD=pD: e.activation(out=Ra[:, 0:T], in_=pD[:, 0:T], func=AF.Ln), reads=[kD], writes=['hb1'])
                            S.op('act', lambda e: e.activation(out=Ra[:, 0:T], in_=Ra[:, 0:T], func=AF.Exp, scale=-1.0), reads=['hb1'], writes=['hb1'])
                            yield
                            S.op('dve', lambda e, pO=pO, pr=pr: e.tensor_tensor(out=mixT[:, 6 + pr, 0:T], in0=pO[:, 0:T], in1=Ra[:, 0:T], op=ALU.mult), reads=[kO, 'hb1'], writes=['mixT%d' % (6 + pr)])
                    else:
                        KTs, Vs, PTs = X['KTs'], X['Vs'], X['PTs']
                        for g8 in range(2):
                            yield
                            pp = [PS(), PS()]
                            for bi in range(8):
                                bq = g8 * 8 + bi
                                for mc in range(2):
                                    for h in range(4):
                                        par = h % 2
                                        rows = slice(par * 64, par * 64 + 64)
                                        col = bi * 32 + (mc * 2 + h // 2) * 8
                                        last = (bi == 7 and mc == 1 and h >= 2)
                                        p, k = pp[par]
                                        yield
                                        S.op('pe', lambda e, bq=bq, mc=mc, h=h, rows=rows, col=col, p=p: e.matmul(p[:, col:col + 8], lhsT=KTs[rows, bq, h // 2, mc * 128:(mc + 1) * 128],
                                                                                                                 rhs=qxT[rows, h // 2, bq * 8:bq * 8 + 8], start=True, stop=True),
                                             reads=['KTs', 'qxT'], writes=[k], inc=last)
                            for par in range(2):
                                p, k = pp[par]
                                yield
                                S.op('act', lambda e, p=p, g8=g8, par=par: e.activation(out=PTs[:, par, g8 * 256:(g8 + 1) * 256], in_=p[:, 0:256], func=AF.Exp, scale=0.125), reads=[k], writes=['PTs'])
                        yield
                        pO, kO = PS(2)
                        yield
                        pD, kD = PS(2)
                        for bq in range(16):
                            for h in range(4):
                                rows = slice((h % 2) * 64, (h % 2) * 64 + 64)
                                oc = (h // 2) * 128 + bq * 8
                                for mc in range(2):
                                    col = bq * 32 + (mc * 2 + h // 2) * 8
                                    yield
                                    S.op('pe', lambda e, bq=bq, h=h, mc=mc, rows=rows, oc=oc, col=col: e.matmul(pO[rows, oc:oc + 8], lhsT=Vs[:, bq, mc, h * 64:(h + 1) * 64], rhs=PTs[:, h % 2, col:col + 8],
                                                                                                               start=(mc == 0), stop=(mc == 1)),
                                         reads=['Vs', 'PTs'], writes=[kO], inc=(bq == 15 and h == 3 and mc == 1))
                                for mc in range(2):
                                    col = bq * 32 + (mc * 2 + h // 2) * 8
                                    yield
                                    S.op('pe', lambda e, bq=bq, h=h, mc=mc, rows=rows, oc=oc, col=col: e.matmul(pD[rows, oc:oc + 8], lhsT=onesb[:, 0:64], rhs=PTs[:, h % 2, col:col + 8],
                                                                                                               start=(mc == 0), stop=(mc == 1)),
                                         reads=['onesb', 'PTs'], writes=[kD], inc=(bq == 15 and h == 3 and mc == 1))
                        yield
                        S.op('act', lambda e: e.activation(out=Ra[:, 0:128], in_=pD[:, 0:128], func=AF.Ln), reads=[kD], writes=['hb1'])
                        S.op('act', lambda e: e.activation(out=Ra[:, 0:128], in_=Ra[:, 0:128], func=AF.Exp, scale=-1.0), reads=['hb1'], writes=['hb1'])
                        yield
                        S.op('dve', lambda e: e.tensor_tensor(out=mixT[:, 6, 0:128], in0=pO[:, 0:128], in1=Ra[:, 0:128], op=ALU.mult), reads=[kO, 'hb1'], writes=['mixT6'])
                        yield
                        S.op('act', lambda e: e.activation(out=Rb[:, 0:128], in_=pD[:, 128:256], func=AF.Ln), reads=[kD], writes=['hb0'])
                        S.op('act', lambda e: e.activation(out=Rb[:, 0:128], in_=Rb[:, 0:128], func=AF.Exp, scale=-1.0), reads=['hb0'], writes=['hb0'])
                        yield
                        S.op('dve', lambda e: e.tensor_tensor(out=mixT[:, 7, 0:128], in0=pO[:, 128:256], in1=Rb[:, 0:128], op=ALU.mult), reads=[kO, 'hb0'], writes=['mixT7'])

                    yield
                gens = [(g_inproj(3, 5), 2), (g_hgrn(), 3), (g_pool(), 1), (g_attn(), 1)]
                while gens:
                    for ge in list(gens):
                        for _ in range(ge[1]):
                            try:
                                next(ge[0])
                            except StopIteration:
                                gens.remove(ge)
                                break
                chk('pool')
                chk('hgrn_o')
                chk('attn')
                wo = [wneed(), wneed(prefetch=False)]
                for sub in range(nsub):
                    for c in range(2):
                        wsl, wk = wo[c]
                        wv = w8(wsl)
                        p, k = PS()
                        mmg(p[:, :], k, [(mixT[:, kk, sub * 128:(sub + 1) * 128], wv[:, kk, :]) for kk in (0, 1, 6, 7, 2, 3, 4, 5)], [wk], per=[['mixT%d' % kk] for kk in (0, 1, 6, 7, 2, 3, 4, 5)])
                        S.op('dve', lambda e, p=p, sub=sub, c=c: e.tensor_tensor(out=xres[:, sub, c * 512:(c + 1) * 512], in0=p[:, :], in1=xres[:, sub, c * 512:(c + 1) * 512], op=ALU.add),
                             reads=[k, 'xres%d' % sub], writes=['xres%d' % sub])
                    if sub >= 1:
                        rms_multi([(xres[:, sub - 1, :], ['xres%d' % (sub - 1)], (sub - 1) * 128)], 110, hT, ['hT'], base=sub - 1)
                rms_multi([(xres[:, nsub - 1, :], ['xres%d' % (nsub - 1)], (nsub - 1) * 128)], 110, hT, ['hT'], base=nsub - 1)
                chk('outproj')
                chk('norm2')
                pend2 = []
                nxt = X.get('nxt')
                if nxt is not None:
                    GSf = B['GS'][:].rearrange("p h t -> p (h t)"); osf = B['osb'][:].rearrange("p h t -> p (h t)")
                    xn = [GSf[:, 0:1024], GSf[:, 1024:2048], osf[:, 0:1024], osf[:, 1024:2048]]
                    xnk = [['GS0', 'GS1'], ['GS2', 'GS3'], ['osb'], ['osb']]
                    for s_ in range(4):
                        S.dma('sp', 'xn%d' % s_, xn[s_], x_tok[nxt + s_ * 128:nxt + (s_ + 1) * 128, :], writes=xnk[s_])
                for r in range(11):
                    wsl, wk = wneed()
                    wv = w8(wsl)
                    for jj in range(2):
                        j = 2 * r + jj
                        pa, ka = PS(2)
                        mmg(pa[:, 0:T], ka, [(wv[:, kk, jj * 128:(jj + 1) * 128], hT[:, kk, 0:T]) for kk in range(8)], ['hT', wk])
                        pb_, kb = PS()
                        mmg(pb_[:, 0:T], kb, [(wv[:, kk, 256 + jj * 128:256 + (jj + 1) * 128], hT[:, kk, 0:T]) for kk in range(8)], ['hT', wk])
                        cbuf = B['cbuf'][j % 2]; gbuf = B['gbuf'][j % 2]
                        ck_, gk_ = 'cbuf%d' % (j % 2), 'gbuf%d' % (j % 2)
                        if not sample:
                            asb = X['asb'][j % 2]; ak_ = 'asb%d' % (j % 2); carry = X['carry']
                            S.op('dve', lambda e, asb=asb, j=j: e.tensor_copy(out=asb[:, 0:2], in_=carry[:, j, :]), reads=['carry'], writes=[ak_])
                            S.op('act', lambda e, asb=asb, pa=pa: e.activation(out=asb[:, 2:2 + T], in_=pa[:, 0:T], func=AF.Copy), reads=[ka], writes=[ak_])
                            S.op('act', lambda e, pa=pa, j=j, cbuf=cbuf: e.activation(out=cbuf, in_=pa[:, 0:T], func=AF.Identity, scale=cw(2, j), bias=cb(j)), reads=[ka, 'prm'], writes=[ck_])
                            S.op('dve', lambda e, asb=asb, j=j: e.tensor_copy(out=carry[:, j, :], in_=asb[:, T:T + 2]), reads=[ak_], writes=['carry'])
                            S.op('dve', lambda e, asb=asb, j=j, cbuf=cbuf: e.scalar_tensor_tensor(out=cbuf, in0=asb[:, 1:1 + T], scalar=cw(1, j), in1=cbuf, op0=ALU.mult, op1=ALU.add),
                                 reads=[ak_, ck_, 'prm'], writes=[ck_])
                            S.op('dve', lambda e, asb=asb, j=j, cbuf=cbuf: e.scalar_tensor_tensor(out=cbuf, in0=asb[:, 0:T], scalar=cw(0, j), in1=cbuf, op0=ALU.mult, op1=ALU.add),
                                 reads=[ak_, ck_, 'prm'], writes=[ck_])
                        else:
                            a3 = X['a3'][j % 2]; ak_ = 'a3%d' % (j % 2); ahist, anew = X['ahist'], X['anew']
                            c3 = cbuf.rearrange("p (b t) -> p b t", t=8)
                            S.op('dve', lambda e, a3=a3, j=j: e.tensor_copy(out=a3[:, :, 0:2], in_=ahist[:, j, :, :]), reads=['ahist'], writes=[ak_])
                            S.op('act', lambda e, a3=a3, pa=pa: e.activation(out=a3[:, :, 2:10], in_=pa[:, 0:128].rearrange("p (b t) -> p b t", t=8), func=AF.Copy), reads=[ka], writes=[ak_])
                            S.op('act', lambda e, pa=pa, j=j, cbuf=cbuf: e.activation(out=cbuf, in_=pa[:, 0:T], func=AF.Identity, scale=cw(2, j), bias=cb(j)), reads=[ka, 'prm'], writes=[ck_])
                            S.op('dve', lambda e, a3=a3, j=j: e.tensor_copy(out=anew[:, j, :, :], in_=a3[:, :, 8:10]), reads=[ak_], writes=['anew'])
                            S.op('dve', lambda e, a3=a3, j=j, c3=c3: e.scalar_tensor_tensor(out=c3, in0=a3[:, :, 1:9], scalar=cw(1, j), in1=c3, op0=ALU.mult, op1=ALU.add),
                                 reads=[ak_, ck_, 'prm'], writes=[ck_])
                            S.op('dve', lambda e, a3=a3, j=j, c3=c3: e.scalar_tensor_tensor(out=c3, in0=a3[:, :, 0:8], scalar=cw(0, j), in1=c3, op0=ALU.mult, op1=ALU.add),
                                 reads=[ak_, ck_, 'prm'], writes=[ck_])
                        def stage2(cbuf=cbuf, gbuf=gbuf, pb_=pb_, j=j, ck_=ck_, gk_=gk_, kb=kb):
                            S.op('act', lambda e: e.activation(out=gbuf, in_=cbuf, func=AF.Gelu_apprx_tanh), reads=[ck_], writes=[gk_])
                            S.op('dve', lambda e: e.tensor_tensor(out=mT[:, j, 0:T], in0=pb_[:, 0:T], in1=gbuf, op=ALU.mult), reads=[kb, gk_], writes=['mT%d' % j] + (['kvt'] if (first and j < 4) else []))
                        if pend2:
                            pend2.pop()()
                        pend2.append(stage2)
                if pend2:
                    pend2.pop()()
                chk('up')
                if (not sample) and X.get('last'):
                    carry, rowb = X['carry'], X['rowb']
                    for g4 in range(6):
                        p, k = PS()
                        n4 = 4 if g4 < 5 else 2
                        for q in range(n4):
                            j = g4 * 4 + q
                            S.op('pe', lambda e, p=p, q=q, j=j: e.transpose(out=p[0:2, q * 128:(q + 1) * 128], in_=carry[:, j, :], identity=ident), reads=['carry', 'cst'], writes=[k], inc=(q == n4 - 1))
                        S.op('dve', lambda e, p=p, g4=g4, n4=n4: e.tensor_copy(out=rowb[0:2, g4 % 2, 0:n4 * 128], in_=p[0:2, 0:n4 * 128]), reads=[k], writes=['rowb%d' % (g4 % 2)])
                        S.dma('sp', 'o_cp%d' % (g4 % 2), o_conv_p[:, g4 * 512:g4 * 512 + n4 * 128], rowb[0:2, g4 % 2, 0:n4 * 128], reads=['rowb%d' % (g4 % 2)])
                if sample:
                    anew, rowb = X['anew'], X['rowb']
                    for g4 in range(6):
                        p, k = PS()
                        n4 = 4 if g4 < 5 else 2
                        for q in range(n4):
                            j = g4 * 4 + q
                            S.op('pe', lambda e, p=p, q=q, j=j: e.transpose(out=p[0:32, q * 128:(q + 1) * 128], in_=anew[:, j, :, :].rearrange("p b r -> p (b r)"), identity=ident),
                                 reads=['anew', 'cst'], writes=[k], inc=(q == n4 - 1))
                        S.op('dve', lambda e, p=p, g4=g4, n4=n4: e.tensor_copy(out=rowb[0:32, g4 % 2, 0:n4 * 128], in_=p[0:32, 0:n4 * 128]), reads=[k], writes=['rowb%d' % (g4 % 2)])
                        S.dma('sp', 'o_cs%d' % (g4 % 2), o_conv_s[:, g4 * 512:g4 * 512 + n4 * 128], rowb[0:32, g4 % 2, 0:n4 * 128], reads=['rowb%d' % (g4 % 2)])
                chk('convout')
                if nxt is not None:
                    rms_multi([(xn[s_], xnk[s_], s_ * 128) for s_ in range(4)], 102, hT, ['hT'], phase='stats')
                for q in range(4):
                    wsl, wk = wneed()
                    wv = w22(wsl)
                    for sub in range(nsub):
                        p, k = PS()
                        mmg(p[:, 0:256], k, [(mT[:, kk, sub * 128:(sub + 1) * 128], wv[:, kk, :]) for kk in range(22)], ['mT%d' % kk for kk in range(22)] + [wk])
                        S.op('dve', lambda e, p=p, sub=sub, q=q: e.tensor_tensor(out=xres[:, sub, q * 256:(q + 1) * 256], in0=p[:, 0:256], in1=xres[:, sub, q * 256:(q + 1) * 256], op=ALU.add),
                             reads=[k, 'xres%d' % sub], writes=['xres%d' % sub])
                    if q == 1 and nxt is not None:
                        rms_multi([(xn[s_], xnk[s_], s_ * 128) for s_ in range(4)], 102, hT, ['hT'], phase='apply')
                        X['prenormed'] = True
                hbs = G['hbs']
                for sub in range(nsub):
                    S.op('act', lambda e, sub=sub: e.activation(out=hbs[sub % 2][:], in_=xres[:, sub, :], func=AF.Square, accum_out=stat[:, 16 + sub:17 + sub]),
                         reads=['xres%d' % sub], writes=['hb%d' % (sub % 2), 'stat2'])
                S.op('act', lambda e: e.activation(out=stat[:, 20:20 + nsub], in_=stat[:, 16:16 + nsub], func=AF.Ln, scale=1.0 / 1024, bias=epsc), reads=['stat2', 'small'], writes=['stat2'])
                S.op('act', lambda e: e.activation(out=stat[:, 24:24 + nsub], in_=stat[:, 20:20 + nsub], func=AF.Exp, scale=-0.5), reads=['stat2'], writes=['stat2'])
                for sub in range(nsub):
                    xk = 'xres%d' % sub
                    S.op('dve', lambda e, sub=sub: e.scalar_tensor_tensor(out=xres[:, sub, :], in0=xres[:, sub, :], scalar=stat[:, 24 + sub:25 + sub], in1=gf[:], op0=ALU.mult, op1=ALU.mult),
                         reads=[xk, 'stat2', 'gf'], writes=[xk])
                    S.dma('sp', 'yout%d' % sub, y_tok[t0 + sub * 128:t0 + (sub + 1) * 128, :], xres[:, sub, :], reads=[xk])

            chk('prologue')
            Bp = alloc_phase(pst, 512, False)
            xres, hT, mixT, mT = G['xres'], G['hT'], G['mixT'], G['mT']
            memx = Bp['osb'][:].rearrange("p h t -> p (h t)").rearrange("p (s f) -> p s f", s=2); memT = mixT
            kvt = mT[:].rearrange("p k t -> p (k t)")[:, 0:2048].bitcast(F32).rearrange("p (s f) -> p s f", s=2)
            KT = sbt(pst, "KT", [128, 2, 256], BF16); Vb = sbt(pst, "Vb", [128, 2, 256], BF16)

            def memkv():
                rms_multi([(memx[:, sub, :], ['osb'], sub * 128) for sub in range(2)], 118, memT, ['memT'])
                wkv, wk = wneed()
                chk('kv_w')
                for sub in range(2):
                    p, k = PS(2)
                    mmg(p[:, :], k, [(memT[:, kk, sub * 128:(sub + 1) * 128], w8(wkv)[:, kk, :]) for kk in range(8)], ['memT', wk])
                    chk('kv_m')
                    S.op('act', lambda e, p=p, sub=sub: e.activation(out=kvt[:, sub, :], in_=p[:, :], func=AF.Copy), reads=[k], writes=['kvt'])
                    chk('kv_n')
                    S.op('dve', lambda e, p=p, sub=sub: e.tensor_copy(out=Vb[:, sub, :], in_=p[:, 256:512]), reads=[k], writes=['Vb'])
                    chk('kv_a%d' % sub)
                for j in range(2):
                    p, k = PS()
                    mmg(p[:, 0:256], k, [(w8(wkv)[:, kk, j * 128:(j + 1) * 128], memT[:, kk, 0:256]) for kk in range(8)], ['memT', wk])
                    S.op('act', lambda e, p=p, j=j: e.activation(out=KT[:, j, :], in_=p[:, 0:256], func=AF.Copy), reads=[k], writes=['KT'])
                    chk('kv_b%d' % j)
                S.dma('sp', 'o_mk', o_mk.rearrange("(s p) f -> p s f", p=128), kvt[:, :, 0:256], reads=['kvt'])
                chk('kv_c')
                S.dma('sp', 'o_mv', o_mv.rearrange("(s p) f -> p s f", p=128), kvt[:, :, 256:512], reads=['kvt'])


            chk('memkv')
            Xp = {}
            Xp['memkv'] = memkv
            Xp['memdma'] = lambda: S.dma('sp', 'c7', memx[:, 0:2, :], mem.rearrange("(s p) f -> p s f", p=128), writes=['osb'])
            Xp['Sf'] = sbt(pst, "Sf", [128, 4, 128]); Xp['Sb'] = sbt(pst, "Sb", [128, 4, 128], BF16)
            Xp['cm512'] = sbt(pst, "cm512", [128, 512])
            Xp['PTa'] = sbt(pst, "PTa", [128, 2, 2, 512], BF16)
            thf = Bp['THall'][:].rearrange("p h t -> p (h t)")
            Xp['asb'] = [thf[:, 0:514], thf[:, 1024:1538]]
            Xp['carry'] = sbt(pst, "carry", [128, 22, 2]); Xp['rowb'] = sbt(pst, "rowb_p", [2, 2, 512]); Xp['ppo'] = sbt(pst, "ppo", [16, 256])
            S.op('dve', lambda e: e.memset(Xp['Sf'][:], 0.0), writes=['Sf'])
            S.op('dve', lambda e: e.memset(Xp['Sb'][:], 0.0), writes=['Sb'])
            S.op('dve', lambda e: e.memset(Xp['cm512'][:], 1.0), writes=['cm512'])
            S.op('dve', lambda e: e.memset(Xp['cm512'][:].rearrange("p (c t) -> p c t", t=128)[:, :, 0:1], 0.0), writes=['cm512'])
            S.op('dve', lambda e: e.memset(Xp['carry'][:], 0.0), writes=['carry'])
            for ti in range(4):
                Xp['last'] = (ti == 3)
                Xp['nxt'] = (ti + 1) * 512 if ti < 3 else None
                do_tile(Bp, ti * 512, ti == 0, False, Xp)
                chk('tile%d' % ti)
            S.barrier()
            pst.close()

            chk('prompt')
            Bs = alloc_phase(sst, 128, True)
            Xs = {}
            Xs['xp'] = sbt(sst, "xp", [128, 2, 16, 23]); Xs['sp_tok'] = sbt(sst, "sp_tok", [120, 2, 256]); Xs['xpc'] = sbt(sst, "xpc", [128, 2, 16, 15])
            Xs['spo'] = Xs['sp_tok']
            Xs['S0f'] = sbt(sst, "S0f", [128, 16, 4, 128]); Xs['S0b'] = sbt(sst, "S0b", [128, 16, 4, 128], BF16)
            Xs['Vblk2'] = [sbt(sst, "Vblk%d" % i, [128, 4, 128], BF16) for i in range(2)]
            Xs['KTs'] = sbt(sst, "KTs", [128, 16, 2, 256], BF16); Xs['Vs'] = sbt(sst, "Vs", [128, 16, 2, 256], BF16)
            Xs['PTs'] = sbt(sst, "PTs", [128, 2, 512], BF16); kst2 = [sbt(sst, "kst%d" % i, [128, 2, 2, 256], BF16) for i in range(2)]
            thfs = Bs['THall'][:].rearrange("p h t -> p (h t)")
            Xs['a3'] = [thfs[:, 0:160].rearrange("p (b t) -> p b t", t=10), thfs[:, 256:416].rearrange("p (b t) -> p b t", t=10)]
            Xs['ahist'] = sbt(sst, "ahist", [128, 22, 16, 2]); Xs['anew'] = sbt(sst, "anew", [128, 22, 16, 2]); Xs['rowb'] = sbt(sst, "rowb_s", [32, 2, 512])
            cst_tok = sbt(sst, "cst_tok", [32, 2816])
            def pre1():
                S.dma('sp', 's3', Xs['sp_tok'][:], spool.rearrange("(h q) c -> q h c", q=120), writes=['sp_tok'])
                S.dma('sp', 's4', cst_tok[:], sconv[:, :], writes=['cst_tok'])
                kload(0)
                for hh in range(2):
                    for c in range(2):
                        p, k = PS()
                        S.op('pe', lambda e, p=p, hh=hh, c=c: e.transpose(out=p[:, 0:120], in_=Xs['sp_tok'][0:120, hh, c * 128:(c + 1) * 128], identity=cst[0:120, 0:120]),
                             reads=['sp_tok', 'cst'], writes=[k])
                        S.op('dve', lambda e, p=p, hh=hh, c=c: e.tensor_copy(out=Xs['xp'][:, c, hh * 8:(hh + 1) * 8, 0:15], in_=p[:, 0:120].rearrange("p (b r) -> p b r", r=15)),
                             reads=[k], writes=['xp'])
                for j in range(22):
                    p, k = PS()
                    S.op('pe', lambda e, p=p, j=j: e.transpose(out=p[:, 0:32], in_=cst_tok[0:32, j * 128:(j + 1) * 128], identity=cst[0:32, 0:32]), reads=['cst_tok', 'cst'], writes=[k])
                    S.op('dve', lambda e, p=p, j=j: e.tensor_copy(out=Xs['ahist'][:, j, :, :], in_=p[:, 0:32].rearrange("p (b r) -> p b r", r=2)), reads=[k], writes=['ahist'])

            def kload(g4):
                S.dma('pool', 's5%d' % (g4 % 2), kst2[g4 % 2][:], ck[g4 * 2:(g4 + 1) * 2].rearrange("b (mc p) f -> p b mc f", p=128), writes=['kst%d' % (g4 % 2)])

            def pre2():
                kload(1)
                S.dma('pool', 's1', Xs['S0b'][:], shgrn.rearrange("b h d v -> d b h v"), writes=['S0b'])
                for g4 in range(8):
                    kst = kst2[g4 % 2]
                    for bi in range(2):
                        bq = g4 * 2 + bi
                        p, k = PS()
                        pb = p[:].bitcast(BF16)
                        for hc in range(2):
                            for mc in range(2):
                                S.op('pe', lambda e, pb=pb, kst=kst, bi=bi, hc=hc, mc=mc: e.transpose(out=pb[:, (hc * 2 + mc) * 128:(hc * 2 + mc + 1) * 128], in_=kst[:, bi, mc, hc * 128:(hc + 1) * 128], identity=idb[:]),
                                     reads=['kst%d' % (g4 % 2), 'idb'], writes=[k], inc=(hc == 1 and mc == 1))
                        S.op('act', lambda e, pb=pb, bq=bq: e.activation(out=Xs['KTs'][:, bq, :, :], in_=pb[:, 0:512].rearrange("p (hc m) -> p hc m", hc=2), func=AF.Copy), reads=[k], writes=['KTs'])
                    if g4 + 2 < 8:
                        kload(g4 + 2)
                    if g4 == 7:
                        S.dma('pool', 's2', Xs['Vs'][:], cv.rearrange("b (mc p) f -> p b mc f", p=128), writes=['Vs'])
                        S.dma('sp', 's0', Xs['S0f'][:], shgrn.rearrange("b h d v -> d b h v"), writes=['S0f'])
                    yield
            Xs['pre1'] = pre1; Xs['pre2'] = pre2
            chk('sprologue')
            do_tile(Bs, 2048, False, True, Xs)
            S.op('dve', lambda e: e.tensor_copy(out=Xs['xpc'][:], in_=Xs['xp'][:, :, :, 8:23]), reads=['xp'], writes=['xpc'])
            for hh in range(2):
                for c in range(2):
                    p, k = PS()
                    S.op('pe', lambda e, p=p, hh=hh, c=c: e.transpose(out=p[0:120, 0:128], in_=Xs['xpc'][:, c, hh * 8:(hh + 1) * 8, :].rearrange("p b r -> p (b r)"), identity=ident),
                         reads=['xpc', 'cst'], writes=[k])
                    S.op('dve', lambda e, p=p, hh=hh, c=c: e.tensor_copy(out=Xs['spo'][0:120, hh, c * 128:(c + 1) * 128], in_=p[0:120, 0:128]), reads=[k], writes=['spo'])
            S.dma('sp', 'o_ps', o_pool_s.rearrange("(h b) r c -> (b r) h c", h=2), Xs['spo'][:], reads=['spo'])
        except StopBuild as ex:
            print('STOPPED at', ex)
        S.final()
        sst.close()
        import os
        if os.environ.get('KDEBUG'):
            print('CNT', S.cnt, {k: v[1] for k, v in S.dsem.items()})
    return nc


_NC = None


def kernel(**inp):
    global _NC
    f = lambda a: np.ascontiguousarray(np.asarray(a, dtype=np.float32))
    if _NC is None:
        _NC = build()
    cst = make_consts()
    prm = np.concatenate([f(inp['conv_w'][0]).reshape(66, 128), f(inp['conv_b'][0]).reshape(22, 128),
                          f(inp['hgrn_lb_logits']).reshape(8, 128), f(inp['pool_scale'][0]).reshape(2, 128),
                          f(inp['hgrn_onorm_g'][0]).reshape(4, 128), f(inp['ln1_g'][0]).reshape(8, 128),
                          f(inp['ln2_g'][0]).reshape(8, 128), f(inp['mem_norm_g'][0]).reshape(8, 128)], axis=0)
    shared = dict(lnf=f(inp['lnf_g']),
                  w_in=f(inp['w_in'][0]), w_kv=f(inp['w_mem_kv'][0]), w_out=f(inp['w_out'][0]), w_up=f(inp['w_up'][0]),
                  w_dn=f(inp['w_down'][0]), pool_w=f(inp['pool_w'][0]), prm_in=f(prm), cst=cst)
    in_maps = []
    for c in range(8):
        sl = slice(16 * c, 16 * c + 16)
        m = dict(shared)
        m['x_tok'] = f(np.concatenate([inp['x_prompt'][c], np.asarray(inp['x_sample'][sl]).reshape(128, 1024)], axis=0))
        m['mem'] = f(inp['mem_prompt'][c])
        m['spool'] = f(np.asarray(inp['state_pool'][0, sl]).reshape(240, 256))
        m['shgrn'] = f(inp['state_hgrn'][0, sl])
        m['sconv'] = f(np.asarray(inp['state_conv'][0, sl]).reshape(32, 2816))
        m['ck'] = f(np.asarray(inp['cache_mem_k'][0, sl]).reshape(16, 256, 256))
        m['cv'] = f(np.asarray(inp['cache_mem_v'][0, sl]).reshape(16, 256, 256))
        in_maps.append(m)
    res = run_bass_kernel_spmd(_NC, in_maps, core_ids=list(range(8)))
    R = res.results
    g = lambda k: np.stack([np.asarray(R[c][k], dtype=np.float32) for c in range(8)])
    y = g('y_tok')
    y_prompt = np.ascontiguousarray(y[:, :2048, :])
    y_sample = np.ascontiguousarray(y[:, 2048:, :].reshape(128, 8, 1024))
    return (y_prompt, y_sample,
            g('o_pool_p')[None], g('o_hgrn_p')[None], g('o_conv_p')[None],
            g('o_mk').reshape(1, 8, 256, 4, 64), g('o_mv').reshape(1, 8, 256, 4, 64),
            g('o_pool_s').reshape(1, 128, 15, 256), g('o_hgrn_s').reshape(1, 128, 4, 128, 128),
            g('o_conv_s').reshape(1, 128, 2, 2816))
```

```python
import numpy as np
from contextlib import ExitStack
import concourse.bass as bass
import concourse.mybir as mybir
from concourse.bass_utils import run_bass_kernel_spmd

F32, BF16 = mybir.dt.float32, mybir.dt.bfloat16
AF = mybir.ActivationFunctionType
ALU = mybir.AluOpType
EPS = 1e-6
NSLOT = 4
SAME_ENGINE_SYNC = True


import os


class StopBuild(Exception):
    pass


_hits = {}


def chk(name):
    if os.environ.get('KSTOP') == name:
        _hits[name] = _hits.get(name, 0) + 1
        if _hits[name] == int(os.environ.get('KHIT', '1')):
            raise StopBuild(name)


class Sched:
    def __init__(s, nc, es):
        s.nc, s.es = nc, es
        s.E = {'pe': nc.tensor, 'act': nc.scalar, 'dve': nc.vector, 'pool': nc.gpsimd, 'sp': nc.sync}
        s.sem = {k: es.enter_context(nc.semaphore('sem_' + k)) for k in s.E}
        s.cnt = {k: 0 for k in s.E}
        s.seen = {k: {} for k in s.E}
        s.lastw, s.reads, s.dsem = {}, {}, {}
        s.psn = 0
        s.ps_open = {}

    def _semh(s, key):
        return s.sem[key] if key in s.sem else s.dsem[key][0]

    def _wait(s, eng, key, val):
        if key == eng and (eng in ('pe', 'sp') or not SAME_ENGINE_SYNC):
            return
        if s.seen[eng].get(key, 0) >= val:
            return
        s.seen[eng][key] = val
        s.E[eng].wait_ge(s._semh(key), val)

    def deps(s, eng, reads, writes):
        need = {}
        for b in reads:
            if b in s.lastw:
                k, v = s.lastw[b]
                need[k] = max(need.get(k, 0), v)
        for b in writes:
            if b in s.lastw:
                k, v = s.lastw[b]
                need[k] = max(need.get(k, 0), v)
            for (k, v) in s.reads.get(b, ()):
                need[k] = max(need.get(k, 0), v)
        for k, v in need.items():
            s._wait(eng, k, v)

    def _record(s, tok, reads, writes):
        for b in reads:
            s.reads.setdefault(b, []).append(tok)
        for b in writes:
            s.lastw[b] = tok
            s.reads[b] = []

    def op(s, eng, fn, reads=(), writes=(), inc=True):
        psr = [b for b in reads if isinstance(b, tuple) and b[0] == 'ps']
        if psr:
            reads = [b for b in reads if b not in psr]
            writes = list(writes) + psr
            for b in psr:
                s.ps_open[b[1]] -= 1
                if s.ps_open[b[1]] <= 0:
                    del s.ps_open[b[1]]
        s.deps(eng, reads, writes)
        ins = fn(s.E[eng])
        if inc:
            s.cnt[eng] += 1
            ins.then_inc(s.sem[eng], 1)
            tok = (eng, s.cnt[eng])
        else:
            tok = (eng, s.cnt[eng] + 1)
        s._record(tok, reads, writes)
        return ins

    def dma(s, eng, chan, out, in_, reads=(), writes=(), **kw):
        s.deps(eng, reads, writes)
        if chan not in s.dsem:
            s.dsem[chan] = [s.es.enter_context(s.nc.semaphore('d_' + chan)), 0]
        ins = s.E[eng].dma_start(out=out, in_=in_, **kw)
        s.dsem[chan][1] += 16
        ins.then_inc(s.dsem[chan][0], 16)
        s._record((chan, s.dsem[chan][1]), reads, writes)

    def barrier(s):
        for eng in s.E:
            for k in s.sem:
                if s.cnt[k] > 0:
                    s._wait(eng, k, s.cnt[k])
            for k in s.dsem:
                s._wait(eng, k, s.dsem[k][1])

    def final(s):
        for k in s.sem:
            if s.cnt[k] > 0:
                s._wait('sp', k, s.cnt[k])
        for k in s.dsem:
            s._wait('sp', k, s.dsem[k][1])


def make_consts():
    c = np.zeros((128, 576), np.float32)
    c[:, 0:128] = np.eye(128, dtype=np.float32)
    s = np.arange(128)[:, None]
    t = np.arange(128)[None, :]
    c[:, 128:256] = (t >= s)
    c[:, 256:384] = (t >= s) & ((t // 8) == (s // 8))
    c[:, 384:400] = (np.arange(128)[:, None] // 8) == np.arange(16)[None, :]
    c[:, 400:528] = (np.arange(128)[None, :] % 8 != 0)
    for ch in range(2):
        for p in range(128):
            w = [2, 4, 8, 16][2 * ch + p // 64]
            c[p, 528 + ch * 16: 528 + ch * 16 + 16] = 1.0 / np.minimum(w, np.arange(16) + 1.0)
            c[p, 560 + ch] = 1.0 / w
    return c


def build():
    nc = bass.Bass("TRN2", target_bir_lowering=False)
    D = lambda n, sh, k="ExternalInput": nc.dram_tensor(n, sh, F32, kind=k).ap()
    x_tok = D("x_tok", [2176, 1024]); mem = D("mem", [256, 1024])
    spool = D("spool", [240, 256]); shgrn = D("shgrn", [16, 4, 128, 128]); sconv = D("sconv", [32, 2816])
    ck = D("ck", [16, 256, 256]); cv = D("cv", [16, 256, 256])
    lnf = D("lnf", [1024])
    w_in = D("w_in", [1024, 2560]); w_kv = D("w_kv", [1024, 512]); w_out = D("w_out", [1024, 1024])
    w_up = D("w_up", [1024, 5632]); w_dn = D("w_dn", [2816, 1024])
    pool_w = D("pool_w", [4, 64, 64]); prm_in = D("prm_in", [126, 128]); cst_in = D("cst", [128, 576])
    O = lambda n, sh: D(n, sh, "ExternalOutput")
    y_tok = O("y_tok", [2176, 1024]); o_pool_p = O("o_pool_p", [15, 256]); o_hgrn_p = O("o_hgrn_p", [4, 128, 128])
    o_conv_p = O("o_conv_p", [2, 2816]); o_mk = O("o_mk", [256, 256]); o_mv = O("o_mv", [256, 256])
    o_pool_s = O("o_pool_s", [16, 15, 256]); o_hgrn_s = O("o_hgrn_s", [16, 4, 128, 128]); o_conv_s = O("o_conv_s", [32, 2816])

    with ExitStack() as es:
        S = Sched(nc, es)
        uid = [0]
        def sbt(st, n, sh, dt=F32):
            uid[0] += 1
            return st.enter_context(nc.sbuf_tensor("sb%d_%s" % (uid[0], n), sh, dt))
        psb = [es.enter_context(nc.psum_tensor("ps%d" % i, [128, 512], F32)) for i in range(8)]

        def PS(n=1):
            for _ in range(8):
                i = S.psn % 8
                S.psn += 1
                if i not in S.ps_open:
                    break
            else:
                raise RuntimeError('no free PSUM bank')
            S.ps_open[i] = n
            return psb[i], ('ps', i)

        cst = sbt(es, "cst", [128, 576]); prm = sbt(es, "prm", [128, 128]); prm_st = sbt(es, "prm_st", [126, 128])
        idb = sbt(es, "idb", [128, 128], BF16); onesb = sbt(es, "onesb", [128, 128], BF16)
        gf = sbt(es, "gf", [128, 1024])
        wbd = sbt(es, "wbd", [128, 2, 128], BF16)
        lbc = sbt(es, "lbc", [128, 16])
        small = sbt(es, "small", [128, 8])
        ring = [sbt(es, "ring%d" % i, [128, 5632], BF16) for i in range(NSLOT)]
        G = {}
        stat = sbt(es, "stat", [128, 32])
        ident = cst[:, 0:128]; cmask = cst[:, 128:256]; smask = cst[:, 256:384]; seqm = cst[:, 384:400]
        rmask = cst[:, 400:528]; invw = cst[:, 560:562]
        invcnt = cst[:, 528:560]
        epsc = small[:, 0:1]; mhalf = small[:, 1:2]; onec = small[:, 2:3]
        cw = lambda r, j: prm[:, r * 22 + j: r * 22 + j + 1]
        cb = lambda j: prm[:, 66 + j: 67 + j]
        pscale = lambda c: prm[:, 96 + c: 97 + c]
        onorm = lambda h: prm[:, 98 + h: 99 + h]

        wseq = []

        def ld_in(b):
            def f(slot, key, chan):
                S.dma('pool', chan, slot[:, 0:4096].rearrange("p (k n) -> p k n", k=8),
                      w_in[:, b * 512:(b + 1) * 512].rearrange("(k p) n -> p k n", p=128), writes=[key])
            return f

        def ld_kv():
            def f(slot, key, chan):
                S.dma('pool', chan, slot[:, 0:4096].rearrange("p (k n) -> p k n", k=8),
                      w_kv.rearrange("(k p) n -> p k n", p=128), writes=[key])
            return f

        def ld_out(c):
            def f(slot, key, chan):
                S.dma('pool', chan, slot[:, 0:4096].rearrange("p (k n) -> p k n", k=8),
                      w_out[:, c * 512:(c + 1) * 512].rearrange("(k p) n -> p k n", p=128), writes=[key])
            return f

        def ld_up(r):
            def f(slot, key, chan):
                v = slot[:, 0:4096].rearrange("p (k n) -> p k n", k=8)
                S.dma('pool', chan, v[:, :, 0:256],
                      w_up[:, r * 256:(r + 1) * 256].rearrange("(k p) n -> p k n", p=128), writes=[key])
                S.dma('pool', chan, v[:, :, 256:512],
                      w_up[:, 2816 + r * 256:2816 + (r + 1) * 256].rearrange("(k p) n -> p k n", p=128), writes=[key])
            return f

        def ld_dn(q):
            def f(slot, key, chan):
                S.dma('pool', chan, slot[:, 0:5632].rearrange("p (k n) -> p k n", k=22),
                      w_dn[:, q * 256:(q + 1) * 256].rearrange("(k p) n -> p k n", p=128), writes=[key])
            return f

        wscr = nc.dram_tensor("wscr", [22, 128, 5632], BF16, kind="Internal").ap()
        blocks = [('in', b) for b in range(5)] + [('out', c) for c in range(2)] + [('up', r) for r in range(11)] + [('dn', q) for q in range(4)]
        mk = {'in': ld_in, 'out': ld_out, 'up': ld_up, 'dn': ld_dn}
        for ti in range(5):
            for bi, (kind, idx) in enumerate(blocks):
                n = 5632 if kind == 'dn' else 4096
                if ti == 0:
                    wseq.append((mk[kind](idx), bi, n))
                    if bi == 2:
                        wseq.append((ld_kv(), None, 0))
                else:
                    def f(slot, key, chan, bi=bi, n=n):
                        S.dma('pool', chan, slot[:, 0:n], wscr[bi, :, 0:n], reads=[('scr', bi)], writes=[key])
                    wseq.append((f, None, n))
        wstate = {'issued': 0, 'next': 0}

        def wneed(prefetch=True):
            i = wstate['next']
            wstate['next'] += 1
            upto = min(i + NSLOT - 1, len(wseq) - 1) if prefetch else i
            while wstate['issued'] <= upto:
                j = wstate['issued']
                wseq[j][0](ring[j % NSLOT], ('w', j % NSLOT), 'w%d' % (j % NSLOT))
                wstate['issued'] += 1
            sl = ring[i % NSLOT]
            bi, n = wseq[i][1], wseq[i][2]
            if bi is not None:
                S.dma('sp', 'wb%d' % (i % NSLOT), wscr[bi, :, 0:n], sl[:, 0:n], reads=[('w', i % NSLOT)], writes=[('scr', bi)])
            return sl, ('w', i % NSLOT)

        def w8(sl):
            return sl[:, 0:4096].rearrange("p (k n) -> p k n", k=8)

        def w22(sl):
            return sl[:, 0:5632].rearrange("p (k n) -> p k n", k=22)

        def mmg(out_ap, pskey, pairs, reads, per=None):
            n = len(pairs)
            for i, (l, r) in enumerate(pairs):
                S.op('pe', lambda e, l=l, r=r, i=i: e.matmul(out_ap, lhsT=l, rhs=r, start=(i == 0), stop=(i == n - 1)),
                     reads=list(reads) + (list(per[i]) if per else []), writes=[pskey], inc=(i == n - 1))

        pst = ExitStack(); sst = ExitStack()
        es.enter_context(pst); es.enter_context(sst)
        try:
            S.dma('sp', 'c0', cst[:], cst_in[:, :], writes=['cst'])
            S.dma('sp', 'c1', prm_st[:], prm_in[:, :], writes=['prm_st'])
            S.dma('sp', 'c4', gf[:], lnf.partition_broadcast(128), writes=['gf'])
            S.op('dve', lambda e: e.memset(small[:, 0:1], EPS), writes=['small'])
            S.op('dve', lambda e: e.memset(small[:, 1:2], -0.5), writes=['small'])
            S.op('dve', lambda e: e.memset(small[:, 2:3], 1.0), writes=['small'])
            S.op('dve', lambda e: e.memset(onesb[:], 1.0), writes=['onesb'])
            S.op('dve', lambda e: e.tensor_copy(out=idb[:], in_=ident), reads=['cst'], writes=['idb'])
            S.op('pool', lambda e: e.memset(wbd[:], 0.0), writes=['wbd'])
            for gi in range(4):
                c, o = gi // 2, (gi % 2) * 64
                S.dma('pool', 'c5', wbd[o:o + 64, c, o:o + 64], pool_w[gi, :, :], writes=['wbd'])
            p_, pk = PS()
            S.op('pe', lambda e: e.transpose(out=p_[:, 0:126], in_=prm_st[:, :], identity=cst[0:126, 0:126]),
                 reads=['prm_st', 'cst'], writes=[pk])
            S.op('dve', lambda e: e.tensor_copy(out=prm[:, 0:126], in_=p_[:, 0:126]), reads=[pk], writes=['prm'])
            S.op('dve', lambda e: e.tensor_sub(out=lbc[:, 8:12], in0=prm[:, 88:92], in1=prm[:, 92:96]), reads=['prm'], writes=['lbc'])
            S.op('act', lambda e: e.activation(out=lbc[:, 8:12], in_=lbc[:, 8:12], func=AF.Tanh, scale=0.5), reads=['lbc'], writes=['lbc'])
            S.op('dve', lambda e: e.tensor_scalar(out=lbc[:, 0:4], in0=lbc[:, 8:12], scalar1=0.25, scalar2=0.75, op0=ALU.mult, op1=ALU.add), reads=['lbc'], writes=['lbc'])
            S.op('dve', lambda e: e.tensor_scalar(out=lbc[:, 4:8], in0=lbc[:, 8:12], scalar1=-0.25, scalar2=0.25, op0=ALU.mult, op1=ALU.add), reads=['lbc'], writes=['lbc'])
            S.op('dve', lambda e: e.tensor_scalar(out=lbc[:, 12:16], in0=lbc[:, 8:12], scalar1=0.25, scalar2=-0.25, op0=ALU.mult, op1=ALU.add), reads=['lbc'], writes=['lbc'])

            def rms_to_T(src_ap, gcol, dstT, col0, rd, wr_extra=()):
                hb = G['hb']
                S.op('act', lambda e: e.activation(out=hb[:], in_=src_ap, func=AF.Square, accum_out=stat[:, 0:1]),
                     reads=rd, writes=['hb', 'stat'])
                S.op('dve', lambda e: e.tensor_scalar(out=stat[:, 1:2], in0=stat[:, 0:1], scalar1=1.0 / 1024, scalar2=EPS, op0=ALU.mult, op1=ALU.add),
                     reads=['stat'], writes=['stat'])
                S.op('pool', lambda e: e.tensor_tensor(out=stat[:, 2:3], in0=stat[:, 1:2], in1=mhalf, op=ALU.pow),
                     reads=['stat', 'small'], writes=['stat'])
                chk('rms_a')
                S.op('act', lambda e: e.activation(out=hb[:], in_=src_ap, func=AF.Copy, scale=stat[:, 2:3]),
                     reads=list(rd) + ['stat'], writes=['hb'])
                chk('rms_b')
                p, k = PS()
                pb = p[:].bitcast(BF16)
                for kk in range(8):
                    S.op('pe', lambda e, kk=kk: e.transpose(out=pb[:, kk * 128:(kk + 1) * 128], in_=hb[:, kk * 128:(kk + 1) * 128], identity=idb[:]),
                         reads=['hb', 'idb'], writes=[k], inc=(kk == 7))
                chk('rms_c')
                S.op('dve', lambda e: e.tensor_tensor(out=dstT[:, :, col0:col0 + 128], in0=pb.rearrange("p (k n) -> p k n", k=8),
                                                      in1=prm[:, gcol:gcol + 8].unsqueeze(2).to_broadcast([128, 8, 128]), op=ALU.mult),
                     reads=[k, 'prm'], writes=list(wr_extra))
                chk('rms_d')

            def rms_multi(items, gcol, dstT, wr, base=0, phase='all'):
                n = len(items)
                hbs = G['hbs']
                if phase in ('all', 'stats'):
                    for i_, (src, rd, col0) in enumerate(items):
                        S.op('act', lambda e, i_=i_, src=src: e.activation(out=hbs[(base + i_) % 2][:], in_=src, func=AF.Square, accum_out=stat[:, base + i_:base + i_ + 1]),
                             reads=rd, writes=['hb%d' % ((base + i_) % 2), 'stat'])
                    S.op('act', lambda e: e.activation(out=stat[:, 4 + base:4 + base + n], in_=stat[:, base:base + n], func=AF.Ln, scale=1.0 / 1024, bias=epsc), reads=['stat', 'small'], writes=['stat'])
                    S.op('act', lambda e: e.activation(out=stat[:, 8 + base:8 + base + n], in_=stat[:, 4 + base:4 + base + n], func=AF.Exp, scale=-0.5), reads=['stat'], writes=['stat'])
                if phase == 'stats':
                    return
                for i_, (src, rd, col0) in enumerate(items):
                    hb = hbs[(base + i_) % 2]
                    S.op('act', lambda e, i_=i_, src=src, hb=hb: e.activation(out=hb[:], in_=src, func=AF.Copy, scale=stat[:, 8 + base + i_:9 + base + i_]),
                         reads=list(rd) + ['stat'], writes=['hb%d' % ((base + i_) % 2)])
                    p, k = PS()
                    pb = p[:].bitcast(BF16)
                    for kk in range(8):
                        S.op('pe', lambda e, kk=kk, hb=hb, pb=pb: e.transpose(out=pb[:, kk * 128:(kk + 1) * 128], in_=hb[:, kk * 128:(kk + 1) * 128], identity=idb[:]),
                             reads=['hb%d' % ((base + i_) % 2), 'idb'], writes=[k], inc=(kk == 7))
                    S.op('dve', lambda e, col0=col0, pb=pb: e.tensor_tensor(out=dstT[:, :, col0:col0 + 128], in0=pb.rearrange("p (k n) -> p k n", k=8),
                                                                          in1=prm[:, gcol:gcol + 8].unsqueeze(2).to_broadcast([128, 8, 128]), op=ALU.mult),
                         reads=[k, 'prm'], writes=list(wr))

            def alloc_phase(st, T, sample):
                B = {}
                B['T'] = T
                ns = T // 128
                G['xres'] = sbt(st, "xres", [128, ns, 1024]); G['hT'] = sbt(st, "hT", [128, 8, T], BF16)
                G['mixT'] = sbt(st, "mixT", [128, 8, T], BF16); G['mT'] = sbt(st, "mT", [128, 22, T], BF16)
                G['hbs'] = [sbt(st, "hb%d" % i, [128, 1024], BF16) for i in range(2)]
                G['hb'] = G['hbs'][0]
                B['u'] = sbt(st, "u_sb", [128, 2, 15 + T]) if not sample else None
                B['pA'] = sbt(st, "pA", [128, max(16 + T, 368)]); B['pB'] = sbt(st, "pB", [128, max(16 + T, 368)])
                B['d'] = sbt(st, "d_sb", [128, 2, T], BF16)
                B['Ab'] = [sbt(st, "A_sb%d" % i, [128, T]) for i in range(2)]
                B['E2b'] = [sbt(st, "E2_%d" % i, [128, T]) for i in range(2)]
                B['K1b'] = [sbt(st, "K1_%d" % i, [128, T]) for i in range(2)]
                B['QSall'] = sbt(st, "QSall", [128, 4, T]); B['THall'] = sbt(st, "THall", [128, 4, T])
                B['GS'] = sbt(st, "GS", [128, 4, T])
                B['qAT'] = sbt(st, "qAT", [128, 4, T], BF16); B['kAT'] = sbt(st, "kAT", [128, 4, T], BF16)
                B['kAk'] = sbt(st, "kAk", [128, T // 128, 512], BF16); B['vtk'] = sbt(st, "vtk", [128, T // 128, 512], BF16)
                B['eAe'] = sbt(st, "eAe", [128, 4, 16])
                B['PT'] = sbt(st, "PT", [128, 4, 128], BF16)
                B['osb'] = sbt(st, "osb", [128, 4, T]); B['osq'] = sbt(st, "osq", [128, T], BF16)
                B['R'] = sbt(st, "R_sb", [128, T]); B['t1'] = sbt(st, "t1", [128, T])
                B['qxT'] = sbt(st, "qxT", [128, 2, T], BF16)
                B['cbuf'] = [B['QSall'][:, i, :] for i in range(2)]
                B['gbuf'] = [B['QSall'][:, 2 + i, :] for i in range(2)]
                return B

            def do_tile(B, t0, first, sample, X):
                T = B['T']; nsub = T // 128
                xres, hT, mixT, mT = G['xres'], G['hT'], G['mixT'], G['mT']
                for sub in range(nsub):
                    S.dma('sp', 'xin%d' % sub, xres[:, sub, :], x_tok[t0 + sub * 128:t0 + (sub + 1) * 128, :], writes=['xres%d' % sub])
                if first and not sample:
                    X['memdma']()
                if sample:
                    X['pre1']()
                if not X.get('prenormed'):
                    rms_multi([(xres[:, sub, :], ['xres%d' % sub], sub * 128) for sub in range(nsub)], 102, hT, ['hT'])
                X['prenormed'] = False
                chk('norm1')
                QSall, THall, GS = B['QSall'], B['THall'], B['GS']
                done = {'pool': False, 'attn': False, 'inproj': False}

                def g_inproj(b0, b1):
                    for b in range(b0, b1):
                        wsl, wk = wneed()
                        wv = w8(wsl)
                        for jj in range(4):
                            j = b * 4 + jj
                            if 10 <= j < 14:
                                continue
                            yield
                            p, k = PS()
                            mmg(p[:, 0:T], k, [(wv[:, kk, jj * 128:(jj + 1) * 128], hT[:, kk, 0:T]) for kk in range(8)], ['hT', wk])
                            src = p[:, 0:T]
                            if j < 2:
                                if sample:
                                    S.op('act', lambda e, j=j, src=src: e.activation(out=X['xp'][:, j, :, 15:23], in_=src.rearrange("p (b t) -> p b t", t=8), func=AF.Copy),
                                         reads=[k], writes=['xp'])
                                else:
                                    S.op('act', lambda e, j=j, src=src: e.activation(out=B['u'][:, j, 15:15 + T], in_=src, func=AF.Copy), reads=[k], writes=['u'])
                            elif j < 6:
                                S.op('act', lambda e, j=j, src=src: e.activation(out=QSall[:, j - 2, :], in_=src, func=AF.Silu), reads=[k], writes=['QS%d' % (j - 2)])
                            elif j < 10:
                                S.op('act', lambda e, j=j, src=src: e.activation(out=THall[:, j - 6, :], in_=src, func=AF.Tanh, scale=0.5), reads=[k], writes=['TH%d' % (j - 6)])
                            elif j < 18:
                                h = j - 14
                                S.op('act', lambda e, h=h, src=src: e.activation(out=GS[:, h, :], in_=src, func=AF.Copy), reads=[k], writes=['GS%d' % h])
                            else:
                                S.op('act', lambda e, j=j, src=src: e.activation(out=B['qxT'][:, j - 18, :], in_=src, func=AF.Copy), reads=[k], writes=['qxT'])
                        if b in (2, 3):
                            c0 = 256 if b == 2 else 0
                            o0 = 0 if b == 2 else 256
                            for sub in range(nsub):
                                yield
                                p, k = PS()
                                mmg(p[:, 0:256], k, [(hT[:, kk, sub * 128:(sub + 1) * 128], wv[:, kk, c0:c0 + 256]) for kk in range(8)], ['hT', wk])
                                S.op('dve', lambda e, p=p, sub=sub, o0=o0: e.tensor_copy(out=B['vtk'][:, sub, o0:o0 + 256], in_=p[:, 0:256]), reads=[k], writes=['vtk'])

                    if b1 == 5:
                        done['inproj'] = True
                    yield

                for _ in g_inproj(0, 3):
                    pass
                if first and not sample:
                    X['memkv']()
                chk('inproj')
                kgen = X['pre2']() if sample else iter(())

                def kstep(n=1):
                    for _ in range(n):
                        next(kgen, None)
                kstep(2)
                def g_pool():
                    pA, pB, dsb = B['pA'], B['pB'], B['d']
                    Rb = G['hbs'][0][:].bitcast(F32)
                    for c in range(2):
                        kstep(1)
                        if sample:
                            Xv = X['xp'][:, c, :, :]
                            L = 23
                            sl = lambda buf, n: buf[:, 0:16 * n].rearrange("p (b n) -> p b n", b=16)
                            xs = lambda a, b_: Xv[:, :, a:b_]
                            xkey = 'xp'
                        else:
                            u = B['u']
                            if first and c == 0:
                                yield
                                S.op('dve', lambda e: e.memset(u[:, :, 0:15], 0.0), writes=['u'])
                            L = 15 + T
                            sl = lambda buf, n: buf[:, 0:n]
                            xs = lambda a, b_, c=c: u[:, c, a:b_]
                            xkey = 'u'
                        sv = lambda buf, n, a, b_: (sl(buf, n)[:, :, a:b_] if sample else sl(buf, n)[:, a:b_])
                        yield
                        S.op('dve', lambda e: e.tensor_tensor(out=sl(pA, L - 1), in0=xs(1, L), in1=xs(0, L - 1), op=ALU.add), reads=[xkey], writes=['pA'])
                        yield
                        S.op('dve', lambda e: e.tensor_tensor(out=sl(pB, L - 3), in0=sv(pA, L - 1, 2, L - 1), in1=sv(pA, L - 1, 0, L - 3), op=ALU.add), reads=['pA'], writes=['pB'])
                        uview = xs(15, L)
                        dv = dsb[:, c, :].rearrange("p (b t) -> p b t", t=8) if sample else dsb[:, c, :]

                        def comb(plo, phi, buf, n, off, wsel, dv=dv, uview=uview):
                            S.op('dve', lambda e: e.scalar_tensor_tensor(out=dv[plo:phi], in0=sv(buf, n, off, off + (8 if sample else T))[plo:phi], scalar=invw[plo:phi, c:c + 1],
                                                                          in1=uview[plo:phi], op0=ALU.mult, op1=ALU.subtract),
                                 reads=[wsel, xkey, 'cst'], writes=['d'])
                        if c == 0:
                            yield
                            comb(0, 64, pA, L - 1, 14, 'pA')
                            yield
                            comb(64, 128, pB, L - 3, 12, 'pB')
                        else:
                            yield
                            S.op('dve', lambda e: e.tensor_tensor(out=sl(pA, L - 7), in0=sv(pB, L - 3, 4, L - 3), in1=sv(pB, L - 3, 0, L - 7), op=ALU.add), reads=['pB'], writes=['pA'])
                            yield
                            comb(0, 64, pA, L - 7, 8, 'pA')
                            yield
                            S.op('dve', lambda e: e.tensor_tensor(out=sl(pB, L - 15), in0=sv(pA, L - 7, 8, L - 7), in1=sv(pA, L - 7, 0, L - 15), op=ALU.add), reads=['pA'], writes=['pB'])
                            yield
                            comb(64, 128, pB, L - 15, 0, 'pB')
                        if first and not sample:
                            for (plo, phi, buf, off) in ((0, 64, pA, 14 if c == 0 else 8), (64, 128, pB, 12 if c == 0 else 0)):
                                yield
                                S.op('dve', lambda e, plo=plo, phi=phi, buf=buf, off=off: e.tensor_tensor(out=Rb[plo:phi, 0:15], in0=buf[plo:phi, off:off + 15],
                                                                                                          in1=invcnt[plo:phi, c * 16:c * 16 + 15], op=ALU.mult),
                                     reads=['pA', 'pB', 'cst'], writes=['hb0'])
                                yield
                                S.op('dve', lambda e, plo=plo, phi=phi: e.tensor_tensor(out=dsb[plo:phi, c, 0:15], in0=Rb[plo:phi, 0:15], in1=u[plo:phi, c, 15:30], op=ALU.subtract),
                                     reads=['hb0', 'u'], writes=['d'])
                        yield
                        p, k = PS()
                        yield
                        mmg(p[:, 0:T], k, [(wbd[:, c, :], dsb[:, c, :])], ['wbd', 'd'])
                        yield
                        S.op('act', lambda e, p=p, c=c: e.activation(out=mixT[:, c, 0:T], in_=p[:, 0:T], func=AF.Identity, scale=pscale(c)), reads=[k, 'prm'], writes=['mixT%d' % c])
                    if not sample:
                        u = B['u']
                        if X.get('last'):
                            for c in range(2):
                                yield
                                p, k = PS()
                                yield
                                S.op('pe', lambda e, p=p, c=c: e.transpose(out=p[0:15, c * 128:(c + 1) * 128], in_=u[:, c, T:T + 15], identity=ident), reads=['u', 'cst'], writes=[k])
                                yield
                                S.op('dve', lambda e, p=p, c=c: e.tensor_copy(out=X['ppo'][0:15, c * 128:(c + 1) * 128], in_=p[0:15, c * 128:(c + 1) * 128]), reads=[k], writes=['ppo'])
                            S.dma('sp', 'o_pp', o_pool_p[:, :], X['ppo'][0:15, :], reads=['ppo'])
                        else:
                            yield
                            S.op('dve', lambda e: e.tensor_copy(out=u[:, :, 0:15], in_=u[:, :, T:T + 15]), reads=['u'], writes=['u'])

                    yield
                def g_hgrn():
                    qAT, kAT, eAe = B['qAT'], B['kAT'], B['eAe']
                    smk = rmask if sample else X['cm512'][:, :]
                    smkey = 'cst' if sample else 'cm512'

                    def stA(h):
                        par = h % 2
                        K1 = B['K1b'][par]
                        thk = 'TH%d' % h
                        S.op('act', lambda e: e.activation(out=K1[:], in_=THall[:, h, :], func=AF.Identity, scale=lbc[:, 12 + h:13 + h], bias=lbc[:, 4 + h:5 + h]),
                             reads=[thk, 'lbc'], writes=['K1%d' % par])
                        S.op('act', lambda e: e.activation(out=THall[:, h, :], in_=THall[:, h, :], func=AF.Ln, scale=lbc[:, 4 + h:5 + h], bias=lbc[:, h:h + 1]),
                             reads=[thk, 'lbc'], writes=[thk])

                    def stB(h):
                        par = h % 2
                        A = B['Ab'][par]
                        S.op('dve', lambda e: e.tensor_tensor_scan(out=A[:], data0=smk, data1=THall[:, h, :], initial=0.0, op0=ALU.mult, op1=ALU.add),
                             reads=['TH%d' % h, smkey], writes=['A%d' % par])

                    def stC(h):
                        par = h % 2
                        A, E2 = B['Ab'][par], B['E2b'][par]
                        S.op('act', lambda e: e.activation(out=E2[:], in_=A[:], func=AF.Exp, scale=-1.0), reads=['A%d' % par], writes=['E2%d' % par])
                        S.op('act', lambda e: e.activation(out=A[:], in_=A[:], func=AF.Exp), reads=['A%d' % par], writes=['A%d' % par])

                    def stD(h):
                        par = h % 2
                        A, E2, K1 = B['Ab'][par], B['E2b'][par], B['K1b'][par]
                        ka, ke, kk1 = 'A%d' % par, 'E2%d' % par, 'K1%d' % par
                        S.op('dve', lambda e: e.tensor_tensor(out=qAT[:, h, :], in0=QSall[:, h, :], in1=A[:], op=ALU.mult), reads=['QS%d' % h, ka], writes=['qAT'])
                        S.op('dve', lambda e: e.tensor_tensor(out=kAT[:, h, :], in0=K1[:], in1=E2[:], op=ALU.mult), reads=[kk1, ke], writes=['kAT'])
                        if sample:
                            S.op('dve', lambda e: e.tensor_copy(out=eAe[:, h, 0:16], in_=A[:].rearrange("p (b t) -> p b t", t=8)[:, :, 7]), reads=[ka], writes=['eAe'])
                        else:
                            S.op('dve', lambda e: e.tensor_copy(out=eAe[:, h, 0:nsub], in_=A[:].rearrange("p (c t) -> p c t", t=128)[:, :, 127]), reads=[ka], writes=['eAe'])

                    stA(0)
                    yield
                    stB(0)
                    yield
                    kstep()
                    stA(1)
                    yield
                    stC(0)
                    yield
                    stB(1)
                    yield
                    kstep()
                    stD(0)
                    yield
                    stA(2)
                    yield
                    stC(1)
                    yield
                    stB(2)
                    yield
                    kstep()
                    stD(1)
                    yield
                    stA(3)
                    yield
                    stC(2)
                    yield
                    stB(3)
                    yield
                    kstep()
                    stD(2)
                    yield
                    stC(3)
                    yield
                    stD(3)
                    yield
                    kstep(8)
                    chk('hgrn_ew')
                    while not done['inproj']:
                        yield
                    for h in range(4):
                        S.op('act', lambda e, h=h: e.activation(out=GS[:, h, :], in_=GS[:, h, :], func=AF.Silu), reads=['GS%d' % h], writes=['GS%d' % h])
                    for cc in range(nsub):
                        yield
                        p, k = PS()
                        pb = p[:].bitcast(BF16)
                        for h in range(4):
                            yield
                            S.op('pe', lambda e, h=h, cc=cc, pb=pb: e.transpose(out=pb[:, h * 128:(h + 1) * 128], in_=kAT[:, h, cc * 128:(cc + 1) * 128], identity=idb[:]),
                                 reads=['kAT', 'idb'], writes=[k], inc=(h == 3))
                        yield
                        S.op('act', lambda e, cc=cc, pb=pb: e.activation(out=B['kAk'][:, cc, :], in_=pb[:, 0:512], func=AF.Copy), reads=[k], writes=['kAk%d' % cc])
                    chk('katr')
                    PT, osb, kAk, vtk = B['PT'], B['osb'], B['kAk'], B['vtk']
                    for cc in range(nsub):
                        cs = slice(cc * 128, (cc + 1) * 128)
                        yield
                        pS, kS = PS()
                        for h in range(4):
                            yield
                            mmg(pS[:, h * 128:(h + 1) * 128], kS, [(kAT[:, h, cs], qAT[:, h, cs])], ['kAT', 'qAT'])
                        msk = smask if sample else cmask
                        yield
                        S.op('dve', lambda e, pS=pS, msk=msk: e.tensor_tensor(out=PT[:], in0=pS[:, :].rearrange("p (h t) -> p h t", h=4),
                                                                              in1=msk.unsqueeze(1).to_broadcast([128, 4, 128]), op=ALU.mult),
                             reads=[kS, 'cst'], writes=['PT'])
                        yield
                        pO, kO = PS()
                        if not sample:
                            Sf, Sb = X['Sf'], X['Sb']
                            for h in range(4):
                                yield
                                mmg(pO[:, h * 128:(h + 1) * 128], kO, [(vtk[:, cc, h * 128:(h + 1) * 128], PT[:, h, :]), (Sb[:, h, :], qAT[:, h, cs])],
                                    ['vtk', 'PT', 'Sb', 'qAT'])
                            yield
                            S.op('act', lambda e, pO=pO, cs=cs: e.activation(out=osb[:, :, cs], in_=pO[:, :].rearrange("p (h t) -> p h t", h=4), func=AF.Copy), reads=[kO], writes=['osb'])
                            yield
                            pZ, kZ = PS()
                            for h in range(4):
                                yield
                                mmg(pZ[:, h * 128:(h + 1) * 128], kZ, [(kAk[:, cc, h * 128:(h + 1) * 128], vtk[:, cc, h * 128:(h + 1) * 128])], ['kAk%d' % cc, 'vtk'])
                            yield
                            S.op('dve', lambda e, pZ=pZ: e.tensor_tensor(out=Sf[:], in0=Sf[:], in1=pZ[:, :].rearrange("p (h v) -> p h v", h=4), op=ALU.add), reads=[kZ, 'Sf'], writes=['Sf'])
                            yield
                            S.op('dve', lambda e, cc=cc: e.tensor_tensor(out=Sf[:], in0=Sf[:], in1=eAe[:, :, cc:cc + 1].to_broadcast([128, 4, 128]), op=ALU.mult), reads=['Sf', 'eAe'], writes=['Sf'])
                            yield
                            S.op('act', lambda e: e.activation(out=Sb[:], in_=Sf[:], func=AF.Copy), reads=['Sf'], writes=['Sb'])
                        else:
                            S0f, S0b = X['S0f'], X['S0b']
                            for h in range(4):
                                pairs = [(vtk[:, 0, h * 128:(h + 1) * 128], PT[:, h, :])]
                                n = 17
                                yield
                                S.op('pe', lambda e, h=h: e.matmul(pO[:, h * 128:(h + 1) * 128], lhsT=vtk[:, 0, h * 128:(h + 1) * 128], rhs=PT[:, h, :], start=True, stop=False),
                                     reads=['vtk', 'PT'], writes=[kO], inc=False)
                                for bq in range(16):
                                    yield
                                    S.op('pe', lambda e, h=h, bq=bq: e.matmul(pO[:, h * 128 + bq * 8:h * 128 + bq * 8 + 8], lhsT=S0b[:, bq, h, :], rhs=qAT[:, h, bq * 8:bq * 8 + 8],
                                                                             start=False, stop=(bq == 15)),
                                         reads=['S0b', 'qAT'], writes=[kO], inc=(bq == 15))
                            yield
                            S.op('act', lambda e, pO=pO: e.activation(out=osb[:, :, 0:128], in_=pO[:, :].rearrange("p (h t) -> p h t", h=4), func=AF.Copy), reads=[kO], writes=['osb'])
                            Vb2 = X['Vblk2']

                            def stV(i):
                                h, bg = i // 4, i % 4
                                vb = Vb2[i % 2]
                                S.op('dve', lambda e: e.tensor_tensor(out=vb[:], in0=vtk[:, 0, h * 128:(h + 1) * 128].unsqueeze(1).to_broadcast([128, 4, 128]),
                                                                      in1=seqm[:, bg * 4:bg * 4 + 4].unsqueeze(2).to_broadcast([128, 4, 128]), op=ALU.mult),
                                     reads=['vtk', 'cst'], writes=['Vblk%d' % (i % 2)])

                            def stU(i):
                                h, bg = i // 4, i % 4
                                vb = Vb2[i % 2]
                                pZ, kZ = PS()
                                mmg(pZ[:, :], kZ, [(kAk[:, 0, h * 128:(h + 1) * 128], vb[:].rearrange("p b v -> p (b v)"))], ['kAk0', 'Vblk%d' % (i % 2)])
                                S.op('dve', lambda e: e.tensor_tensor(out=S0f[:, bg * 4:bg * 4 + 4, h, :], in0=S0f[:, bg * 4:bg * 4 + 4, h, :],
                                                                      in1=pZ[:, :].rearrange("p (b v) -> p b v", b=4), op=ALU.add),
                                     reads=[kZ, 'S0f'], writes=['S0f'])
                                S.op('dve', lambda e: e.tensor_tensor(out=S0f[:, bg * 4:bg * 4 + 4, h, :], in0=S0f[:, bg * 4:bg * 4 + 4, h, :],
                                                                      in1=eAe[:, h, bg * 4:bg * 4 + 4].unsqueeze(2).to_broadcast([128, 4, 128]), op=ALU.mult),
                                     reads=['S0f', 'eAe'], writes=['S0f'])
                            yield
                            stV(0)
                            for i in range(16):
                                if i + 1 < 16:
                                    yield
                                    stV(i + 1)
                                yield
                                stU(i)
                            S.dma('sp', 'o_hs', o_hgrn_s.rearrange("b h d v -> d b h v"), S0f[:], reads=['S0f'])
                    if (not sample) and X.get('last'):
                        S.dma('sp', 'o_hp', o_hgrn_p.rearrange("h d v -> d h v"), X['Sf'][:], reads=['Sf'])
                    chk('hgrn')
                    osq, R, t1 = B['osq'], B['R'], B['t1']
                    for h in range(4):
                        yield
                        S.op('act', lambda e, h=h: e.activation(out=osq[:], in_=osb[:, h, :], func=AF.Square), reads=['osb'], writes=['osq'])
                        yield
                        p, k = PS()
                        yield
                        mmg(p[:, 0:T], k, [(onesb[:], osq[:])], ['onesb', 'osq'])
                        yield
                        S.op('act', lambda e, p=p: e.activation(out=R[:], in_=p[:, 0:T], func=AF.Ln, scale=1.0 / 128, bias=epsc), reads=[k, 'small'], writes=['R'])
                        yield
                        S.op('act', lambda e: e.activation(out=R[:], in_=R[:], func=AF.Exp, scale=-0.5), reads=['R'], writes=['R'])
                        yield
                        S.op('dve', lambda e, h=h: e.tensor_tensor(out=t1[:], in0=osb[:, h, :], in1=R[:], op=ALU.mult), reads=['osb', 'R'], writes=['t1'])
                        yield
                        S.op('dve', lambda e, h=h: e.scalar_tensor_tensor(out=mixT[:, 2 + h, 0:T], in0=t1[:], scalar=onorm(h), in1=GS[:, h, :], op0=ALU.mult, op1=ALU.mult),
                             reads=['t1', 'GS%d' % h, 'prm'], writes=['mixT%d' % (2 + h)])

                    yield
                def g_attn():
                    qxT = B['qxT']
                    Ra = G['hbs'][1][:].bitcast(F32); Rb = G['hbs'][0][:].bitcast(F32)
                    while not done['inproj']:
                        yield
                    if sample:
                        for _ in range(24):
                            yield
                        kstep(8)
                    if not sample:
                        PTa = X['PTa']
                        for pr in range(2):
                            for hh in range(2):
                                h = pr * 2 + hh
                                rows = slice(hh * 64, hh * 64 + 64)
                                for mc in range(2):
                                    yield
                                    p, k = PS()
                                    yield
                                    mmg(p[:, 0:T], k, [(KT[rows, pr, mc * 128:(mc + 1) * 128], qxT[rows, pr, :])], ['KT', 'qxT'])
                                    yield
                                    S.op('act', lambda e, p=p, hh=hh, mc=mc: e.activation(out=PTa[:, hh, mc, :], in_=p[:, 0:T], func=AF.Exp, scale=0.125), reads=[k], writes=['PTa'])
                            yield
                            pO, kO = PS()
                            yield
                            pD, kD = PS()
                            for hh in range(2):
                                h = pr * 2 + hh
                                rows = slice(hh * 64, hh * 64 + 64)
                                for mc in range(2):
                                    yield
                                    S.op('pe', lambda e, hh=hh, mc=mc, h=h, rows=rows, pO=pO: e.matmul(pO[rows, 0:T], lhsT=Vb[:, mc, h * 64:(h + 1) * 64], rhs=PTa[:, hh, mc, :],
                                                                                                        start=(mc == 0), stop=(mc == 1)),
                                         reads=['Vb', 'PTa'], writes=[kO], inc=(mc == 1))
                                for mc in range(2):
                                    yield
                                    S.op('pe', lambda e, hh=hh, mc=mc, rows=rows, pD=pD: e.matmul(pD[rows, 0:T], lhsT=onesb[:, 0:64], rhs=PTa[:, hh, mc, :],
                                                                                                  start=(mc == 0), stop=(mc == 1)),
                                         reads=['onesb', 'PTa'], writes=[kD], inc=(mc == 1))
                            yield
                            S.op('act', lambda e, pD=pD: e.activation(out=Ra[:, 0:T], in_=pD[:, 0:T], func=AF.Ln), reads=[kD], writes=['hb1'])
                            S.op('act', lambda e: e.activation(out=Ra[:, 0:T], in_=Ra[:, 0:T], func=AF.Exp, scale=-1.0), reads=['hb1'], writes=['hb1'])
                            yield
                            S.op('dve', lambda e, pO=pO, pr=pr: e.tensor_tensor(out=mixT[:, 6 + pr, 0:T], in0=pO[:, 0:T], in1=Ra[:, 0:T], op=ALU.mult), reads=[kO, 'hb1'], writes=['mixT%d' % (6 + pr)])
                    else:
                        KTs, Vs, PTs = X['KTs'], X['Vs'], X['PTs']
                        for g8 in range(2):
                            yield
                            pp = [PS(), PS()]
                            for bi in range(8):
                                bq = g8 * 8 + bi
                                for mc in range(2):
                                    for h in range(4):
                                        par = h % 2
                                        rows = slice(par * 64, par * 64 + 64)
                                        col = bi * 32 + (mc * 2 + h // 2) * 8
                                        last = (bi == 7 and mc == 1 and h >= 2)
                                        p, k = pp[par]
                                        yield
                                        S.op('pe', lambda e, bq=bq, mc=mc, h=h, rows=rows, col=col, p=p: e.matmul(p[:, col:col + 8], lhsT=KTs[rows, bq, h // 2, mc * 128:(mc + 1) * 128],
                                                                                                                 rhs=qxT[rows, h // 2, bq * 8:bq * 8 + 8], start=True, stop=True),
                                             reads=['KTs', 'qxT'], writes=[k], inc=last)
                            for par in range(2):
                                p, k = pp[par]
                                yield
                                S.op('act', lambda e, p=p, g8=g8, par=par: e.activation(out=PTs[:, par, g8 * 256:(g8 + 1) * 256], in_=p[:, 0:256], func=AF.Exp, scale=0.125), reads=[k], writes=['PTs'])
                        yield
                        pO, kO = PS(2)
                        yield
                        pD, kD = PS(2)
                        for bq in range(16):
                            for h in range(4):
                                rows = slice((h % 2) * 64, (h % 2) * 64 + 64)
                                oc = (h // 2) * 128 + bq * 8
                                for mc in range(2):
                                    col = bq * 32 + (mc * 2 + h // 2) * 8
                                    yield
                                    S.op('pe', lambda e, bq=bq, h=h, mc=mc, rows=rows, oc=oc, col=col: e.matmul(pO[rows, oc:oc + 8], lhsT=Vs[:, bq, mc, h * 64:(h + 1) * 64], rhs=PTs[:, h % 2, col:col + 8],
                                                                                                               start=(mc == 0), stop=(mc == 1)),
                                         reads=['Vs', 'PTs'], writes=[kO], inc=(bq == 15 and h == 3 and mc == 1))
                                for mc in range(2):
                                    col = bq * 32 + (mc * 2 + h // 2) * 8
                                    yield
                                    S.op('pe', lambda e, bq=bq, h=h, mc=mc, rows=rows, oc=oc, col=col: e.matmul(pD[rows, oc:oc + 8], lhsT=onesb[:, 0:64], rhs=PTs[:, h % 2, col:col + 8],
                                                                                                               start=(mc == 0), stop=(mc == 1)),
                                         reads=['onesb', 'PTs'], writes=[kD], inc=(bq == 15 and h == 3 and mc == 1))
                        yield
                        S.op('act', lambda e: e.activation(out=Ra[:, 0:128], in_=pD[:, 0:128], func=AF.Ln), reads=[kD], writes=['hb1'])
                        S.op('act', lambda e: e.activation(out=Ra[:, 0:128], in_=Ra[:, 0:128], func=AF.Exp, scale=-1.0), reads=['hb1'], writes=['hb1'])
                        yield
                        S.op('dve', lambda e: e.tensor_tensor(out=mixT[:, 6, 0:128], in0=pO[:, 0:128], in1=Ra[:, 0:128], op=ALU.mult), reads=[kO, 'hb1'], writes=['mixT6'])
                        yield
                        S.op('act', lambda e: e.activation(out=Rb[:, 0:128], in_=pD[:, 128:256], func=AF.Ln), reads=[kD], writes=['hb0'])
                        S.op('act', lambda e: e.activation(out=Rb[:, 0:128], in_=Rb[:, 0:128], func=AF.Exp, scale=-1.0), reads=['hb0'], writes=['hb0'])
                        yield
                        S.op('dve', lambda e: e.tensor_tensor(out=mixT[:, 7, 0:128], in0=pO[:, 128:256], in1=Rb[:, 0:128], op=ALU.mult), reads=[kO, 'hb0'], writes=['mixT7'])

                    yield
                gens = [(g_inproj(3, 5), 2), (g_hgrn(), 2), (g_pool(), 1), (g_attn(), 1)]
                while gens:
                    for ge in list(gens):
                        for _ in range(ge[1]):
                            try:
                                next(ge[0])
                            except StopIteration:
                                gens.remove(ge)
                                break
                chk('pool')
                chk('hgrn_o')
                chk('attn')
                wo = [wneed(), wneed(prefetch=False)]
                for sub in range(nsub):
                    for c in range(2):
                        wsl, wk = wo[c]
                        wv = w8(wsl)
                        p, k = PS()
                        mmg(p[:, :], k, [(mixT[:, kk, sub * 128:(sub + 1) * 128], wv[:, kk, :]) for kk in (0, 1, 6, 7, 2, 3, 4, 5)], [wk], per=[['mixT%d' % kk] for kk in (0, 1, 6, 7, 2, 3, 4, 5)])
                        S.op('dve', lambda e, p=p, sub=sub, c=c: e.tensor_tensor(out=xres[:, sub, c * 512:(c + 1) * 512], in0=p[:, :], in1=xres[:, sub, c * 512:(c + 1) * 512], op=ALU.add),
                             reads=[k, 'xres%d' % sub], writes=['xres%d' % sub])
                    if sub >= 1:
                        rms_multi([(xres[:, sub - 1, :], ['xres%d' % (sub - 1)], (sub - 1) * 128)], 110, hT, ['hT'], base=sub - 1)
                rms_multi([(xres[:, nsub - 1, :], ['xres%d' % (nsub - 1)], (nsub - 1) * 128)], 110, hT, ['hT'], base=nsub - 1)
                chk('outproj')
                chk('norm2')
                pend2 = []
                nxt = X.get('nxt')
                if nxt is not None:
                    GSf = B['GS'][:].rearrange("p h t -> p (h t)"); osf = B['osb'][:].rearrange("p h t -> p (h t)")
                    xn = [GSf[:, 0:1024], GSf[:, 1024:2048], osf[:, 0:1024], osf[:, 1024:2048]]
                    xnk = [['GS0', 'GS1'], ['GS2', 'GS3'], ['osb'], ['osb']]
                    for s_ in range(4):
                        S.dma('sp', 'xn%d' % s_, xn[s_], x_tok[nxt + s_ * 128:nxt + (s_ + 1) * 128, :], writes=xnk[s_])
                for r in range(11):
                    wsl, wk = wneed()
                    wv = w8(wsl)
                    for jj in range(2):
                        j = 2 * r + jj
                        pa, ka = PS(2)
                        mmg(pa[:, 0:T], ka, [(wv[:, kk, jj * 128:(jj + 1) * 128], hT[:, kk, 0:T]) for kk in range(8)], ['hT', wk])
                        pb_, kb = PS()
                        mmg(pb_[:, 0:T], kb, [(wv[:, kk, 256 + jj * 128:256 + (jj + 1) * 128], hT[:, kk, 0:T]) for kk in range(8)], ['hT', wk])
                        cbuf = B['cbuf'][j % 2]; gbuf = B['gbuf'][j % 2]
                        ck_, gk_ = 'cbuf%d' % (j % 2), 'gbuf%d' % (j % 2)
                        if not sample:
                            asb = X['asb'][j % 2]; ak_ = 'asb%d' % (j % 2); carry = X['carry']
                            S.op('dve', lambda e, asb=asb, j=j: e.tensor_copy(out=asb[:, 0:2], in_=carry[:, j, :]), reads=['carry'], writes=[ak_])
                            S.op('act', lambda e, asb=asb, pa=pa: e.activation(out=asb[:, 2:2 + T], in_=pa[:, 0:T], func=AF.Copy), reads=[ka], writes=[ak_])
                            S.op('act', lambda e, pa=pa, j=j, cbuf=cbuf: e.activation(out=cbuf, in_=pa[:, 0:T], func=AF.Identity, scale=cw(2, j), bias=cb(j)), reads=[ka, 'prm'], writes=[ck_])
                            S.op('dve', lambda e, asb=asb, j=j: e.tensor_copy(out=carry[:, j, :], in_=asb[:, T:T + 2]), reads=[ak_], writes=['carry'])
                            S.op('dve', lambda e, asb=asb, j=j, cbuf=cbuf: e.scalar_tensor_tensor(out=cbuf, in0=asb[:, 1:1 + T], scalar=cw(1, j), in1=cbuf, op0=ALU.mult, op1=ALU.add),
                                 reads=[ak_, ck_, 'prm'], writes=[ck_])
                            S.op('dve', lambda e, asb=asb, j=j, cbuf=cbuf: e.scalar_tensor_tensor(out=cbuf, in0=asb[:, 0:T], scalar=cw(0, j), in1=cbuf, op0=ALU.mult, op1=ALU.add),
                                 reads=[ak_, ck_, 'prm'], writes=[ck_])
                        else:
                            a3 = X['a3'][j % 2]; ak_ = 'a3%d' % (j % 2); ahist, anew = X['ahist'], X['anew']
                            c3 = cbuf.rearrange("p (b t) -> p b t", t=8)
                            S.op('dve', lambda e, a3=a3, j=j: e.tensor_copy(out=a3[:, :, 0:2], in_=ahist[:, j, :, :]), reads=['ahist'], writes=[ak_])
                            S.op('act', lambda e, a3=a3, pa=pa: e.activation(out=a3[:, :, 2:10], in_=pa[:, 0:128].rearrange("p (b t) -> p b t", t=8), func=AF.Copy), reads=[ka], writes=[ak_])
                            S.op('act', lambda e, pa=pa, j=j, cbuf=cbuf: e.activation(out=cbuf, in_=pa[:, 0:T], func=AF.Identity, scale=cw(2, j), bias=cb(j)), reads=[ka, 'prm'], writes=[ck_])
                            S.op('dve', lambda e, a3=a3, j=j: e.tensor_copy(out=anew[:, j, :, :], in_=a3[:, :, 8:10]), reads=[ak_], writes=['anew'])
                            S.op('dve', lambda e, a3=a3, j=j, c3=c3: e.scalar_tensor_tensor(out=c3, in0=a3[:, :, 1:9], scalar=cw(1, j), in1=c3, op0=ALU.mult, op1=ALU.add),
                                 reads=[ak_, ck_, 'prm'], writes=[ck_])
                            S.op('dve', lambda e, a3=a3, j=j, c3=c3: e.scalar_tensor_tensor(out=c3, in0=a3[:, :, 0:8], scalar=cw(0, j), in1=c3, op0=ALU.mult, op1=ALU.add),
                                 reads=[ak_, ck_, 'prm'], writes=[ck_])
                        def stage2(cbuf=cbuf, gbuf=gbuf, pb_=pb_, j=j, ck_=ck_, gk_=gk_, kb=kb):
                            S.op('act', lambda e: e.activation(out=gbuf, in_=cbuf, func=AF.Gelu_apprx_tanh), reads=[ck_], writes=[gk_])
                            S.op('dve', lambda e: e.tensor_tensor(out=mT[:, j, 0:T], in0=pb_[:, 0:T], in1=gbuf, op=ALU.mult), reads=[kb, gk_], writes=['mT%d' % j] + (['kvt'] if (first and j < 4) else []))
                        if pend2:
                            pend2.pop()()
                        pend2.append(stage2)
                if pend2:
                    pend2.pop()()
                chk('up')
                if (not sample) and X.get('last'):
                    carry, rowb = X['carry'], X['rowb']
                    for g4 in range(6):
                        p, k = PS()
                        n4 = 4 if g4 < 5 else 2
                        for q in range(n4):
                            j = g4 * 4 + q
                            S.op('pe', lambda e, p=p, q=q, j=j: e.transpose(out=p[0:2, q * 128:(q + 1) * 128], in_=carry[:, j, :], identity=ident), reads=['carry', 'cst'], writes=[k], inc=(q == n4 - 1))
                        S.op('dve', lambda e, p=p, g4=g4, n4=n4: e.tensor_copy(out=rowb[0:2, g4 % 2, 0:n4 * 128], in_=p[0:2, 0:n4 * 128]), reads=[k], writes=['rowb%d' % (g4 % 2)])
                        S.dma('sp', 'o_cp%d' % (g4 % 2), o_conv_p[:, g4 * 512:g4 * 512 + n4 * 128], rowb[0:2, g4 % 2, 0:n4 * 128], reads=['rowb%d' % (g4 % 2)])
                if sample:
                    anew, rowb = X['anew'], X['rowb']
                    for g4 in range(6):
                        p, k = PS()
                        n4 = 4 if g4 < 5 else 2
                        for q in range(n4):
                            j = g4 * 4 + q
                            S.op('pe', lambda e, p=p, q=q, j=j: e.transpose(out=p[0:32, q * 128:(q + 1) * 128], in_=anew[:, j, :, :].rearrange("p b r -> p (b r)"), identity=ident),
                                 reads=['anew', 'cst'], writes=[k], inc=(q == n4 - 1))
                        S.op('dve', lambda e, p=p, g4=g4, n4=n4: e.tensor_copy(out=rowb[0:32, g4 % 2, 0:n4 * 128], in_=p[0:32, 0:n4 * 128]), reads=[k], writes=['rowb%d' % (g4 % 2)])
                        S.dma('sp', 'o_cs%d' % (g4 % 2), o_conv_s[:, g4 * 512:g4 * 512 + n4 * 128], rowb[0:32, g4 % 2, 0:n4 * 128], reads=['rowb%d' % (g4 % 2)])
                chk('convout')
                if nxt is not None:
                    rms_multi([(xn[s_], xnk[s_], s_ * 128) for s_ in range(4)], 102, hT, ['hT'], phase='stats')
                for q in range(4):
                    wsl, wk = wneed()
                    wv = w22(wsl)
                    for sub in range(nsub):
                        p, k = PS()
                        mmg(p[:, 0:256], k, [(mT[:, kk, sub * 128:(sub + 1) * 128], wv[:, kk, :]) for kk in range(22)], ['mT%d' % kk for kk in range(22)] + [wk])
                        S.op('dve', lambda e, p=p, sub=sub, q=q: e.tensor_tensor(out=xres[:, sub, q * 256:(q + 1) * 256], in0=p[:, 0:256], in1=xres[:, sub, q * 256:(q + 1) * 256], op=ALU.add),
                             reads=[k, 'xres%d' % sub], writes=['xres%d' % sub])
                    if q == 1 and nxt is not None:
                        rms_multi([(xn[s_], xnk[s_], s_ * 128) for s_ in range(4)], 102, hT, ['hT'], phase='apply')
                        X['prenormed'] = True
                hbs = G['hbs']
                for sub in range(nsub):
                    S.op('act', lambda e, sub=sub: e.activation(out=hbs[sub % 2][:], in_=xres[:, sub, :], func=AF.Square, accum_out=stat[:, 16 + sub:17 + sub]),
                         reads=['xres%d' % sub], writes=['hb%d' % (sub % 2), 'stat2'])
                S.op('act', lambda e: e.activation(out=stat[:, 20:20 + nsub], in_=stat[:, 16:16 + nsub], func=AF.Ln, scale=1.0 / 1024, bias=epsc), reads=['stat2', 'small'], writes=['stat2'])
                S.op('act', lambda e: e.activation(out=stat[:, 24:24 + nsub], in_=stat[:, 20:20 + nsub], func=AF.Exp, scale=-0.5), reads=['stat2'], writes=['stat2'])
                for sub in range(nsub):
                    xk = 'xres%d' % sub
                    S.op('dve', lambda e, sub=sub: e.scalar_tensor_tensor(out=xres[:, sub, :], in0=xres[:, sub, :], scalar=stat[:, 24 + sub:25 + sub], in1=gf[:], op0=ALU.mult, op1=ALU.mult),
                         reads=[xk, 'stat2', 'gf'], writes=[xk])
                    S.dma('sp', 'yout%d' % sub, y_tok[t0 + sub * 128:t0 + (sub + 1) * 128, :], xres[:, sub, :], reads=[xk])

            chk('prologue')
            Bp = alloc_phase(pst, 512, False)
            xres, hT, mixT, mT = G['xres'], G['hT'], G['mixT'], G['mT']
            memx = Bp['osb'][:].rearrange("p h t -> p (h t)").rearrange("p (s f) -> p s f", s=2); memT = mixT
            kvt = mT[:].rearrange("p k t -> p (k t)")[:, 0:2048].bitcast(F32).rearrange("p (s f) -> p s f", s=2)
            KT = sbt(pst, "KT", [128, 2, 256], BF16); Vb = sbt(pst, "Vb", [128, 2, 256], BF16)

            def memkv():
                rms_multi([(memx[:, sub, :], ['osb'], sub * 128) for sub in range(2)], 118, memT, ['memT'])
                wkv, wk = wneed()
                chk('kv_w')
                for sub in range(2):
                    p, k = PS(2)
                    mmg(p[:, :], k, [(memT[:, kk, sub * 128:(sub + 1) * 128], w8(wkv)[:, kk, :]) for kk in range(8)], ['memT', wk])
                    chk('kv_m')
                    S.op('act', lambda e, p=p, sub=sub: e.activation(out=kvt[:, sub, :], in_=p[:, :], func=AF.Copy), reads=[k], writes=['kvt'])
                    chk('kv_n')
                    S.op('dve', lambda e, p=p, sub=sub: e.tensor_copy(out=Vb[:, sub, :], in_=p[:, 256:512]), reads=[k], writes=['Vb'])
                    chk('kv_a%d' % sub)
                for j in range(2):
                    p, k = PS()
                    mmg(p[:, 0:256], k, [(w8(wkv)[:, kk, j * 128:(j + 1) * 128], memT[:, kk, 0:256]) for kk in range(8)], ['memT', wk])
                    S.op('act', lambda e, p=p, j=j: e.activation(out=KT[:, j, :], in_=p[:, 0:256], func=AF.Copy), reads=[k], writes=['KT'])
                    chk('kv_b%d' % j)
                S.dma('sp', 'o_mk', o_mk.rearrange("(s p) f -> p s f", p=128), kvt[:, :, 0:256], reads=['kvt'])
                chk('kv_c')
                S.dma('sp', 'o_mv', o_mv.rearrange("(s p) f -> p s f", p=128), kvt[:, :, 256:512], reads=['kvt'])


            chk('memkv')
            Xp = {}
            Xp['memkv'] = memkv
            Xp['memdma'] = lambda: S.dma('sp', 'c7', memx[:, 0:2, :], mem.rearrange("(s p) f -> p s f", p=128), writes=['osb'])
            Xp['Sf'] = sbt(pst, "Sf", [128, 4, 128]); Xp['Sb'] = sbt(pst, "Sb", [128, 4, 128], BF16)
            Xp['cm512'] = sbt(pst, "cm512", [128, 512])
            Xp['PTa'] = sbt(pst, "PTa", [128, 2, 2, 512], BF16)
            thf = Bp['THall'][:].rearrange("p h t -> p (h t)")
            Xp['asb'] = [thf[:, 0:514], thf[:, 1024:1538]]
            Xp['carry'] = sbt(pst, "carry", [128, 22, 2]); Xp['rowb'] = sbt(pst, "rowb_p", [2, 2, 512]); Xp['ppo'] = sbt(pst, "ppo", [16, 256])
            S.op('dve', lambda e: e.memset(Xp['Sf'][:], 0.0), writes=['Sf'])
            S.op('dve', lambda e: e.memset(Xp['Sb'][:], 0.0), writes=['Sb'])
            S.op('dve', lambda e: e.memset(Xp['cm512'][:], 1.0), writes=['cm512'])
            S.op('dve', lambda e: e.memset(Xp['cm512'][:].rearrange("p (c t) -> p c t", t=128)[:, :, 0:1], 0.0), writes=['cm512'])
            S.op('dve', lambda e: e.memset(Xp['carry'][:], 0.0), writes=['carry'])
            for ti in range(4):
                Xp['last'] = (ti == 3)
                Xp['nxt'] = (ti + 1) * 512 if ti < 3 else None
                do_tile(Bp, ti * 512, ti == 0, False, Xp)
                chk('tile%d' % ti)
            S.barrier()
            pst.close()

            chk('prompt')
            Bs = alloc_phase(sst, 128, True)
            Xs = {}
            Xs['xp'] = sbt(sst, "xp", [128, 2, 16, 23]); Xs['sp_tok'] = sbt(sst, "sp_tok", [120, 2, 256]); Xs['xpc'] = sbt(sst, "xpc", [128, 2, 16, 15])
            Xs['spo'] = Xs['sp_tok']
            Xs['S0f'] = sbt(sst, "S0f", [128, 16, 4, 128]); Xs['S0b'] = sbt(sst, "S0b", [128, 16, 4, 128], BF16)
            Xs['Vblk2'] = [sbt(sst, "Vblk%d" % i, [128, 4, 128], BF16) for i in range(2)]
            Xs['KTs'] = sbt(sst, "KTs", [128, 16, 2, 256], BF16); Xs['Vs'] = sbt(sst, "Vs", [128, 16, 2, 256], BF16)
            Xs['PTs'] = sbt(sst, "PTs", [128, 2, 512], BF16); kst2 = [sbt(sst, "kst%d" % i, [128, 2, 2, 256], BF16) for i in range(2)]
            thfs = Bs['THall'][:].rearrange("p h t -> p (h t)")
            Xs['a3'] = [thfs[:, 0:160].rearrange("p (b t) -> p b t", t=10), thfs[:, 256:416].rearrange("p (b t) -> p b t", t=10)]
            Xs['ahist'] = sbt(sst, "ahist", [128, 22, 16, 2]); Xs['anew'] = sbt(sst, "anew", [128, 22, 16, 2]); Xs['rowb'] = sbt(sst, "rowb_s", [32, 2, 512])
            cst_tok = sbt(sst, "cst_tok", [32, 2816])
            def pre1():
                S.dma('sp', 's3', Xs['sp_tok'][:], spool.rearrange("(h q) c -> q h c", q=120), writes=['sp_tok'])
                S.dma('sp', 's4', cst_tok[:], sconv[:, :], writes=['cst_tok'])
                kload(0)
                for hh in range(2):
                    for c in range(2):
                        p, k = PS()
                        S.op('pe', lambda e, p=p, hh=hh, c=c: e.transpose(out=p[:, 0:120], in_=Xs['sp_tok'][0:120, hh, c * 128:(c + 1) * 128], identity=cst[0:120, 0:120]),
                             reads=['sp_tok', 'cst'], writes=[k])
                        S.op('dve', lambda e, p=p, hh=hh, c=c: e.tensor_copy(out=Xs['xp'][:, c, hh * 8:(hh + 1) * 8, 0:15], in_=p[:, 0:120].rearrange("p (b r) -> p b r", r=15)),
                             reads=[k], writes=['xp'])
                for j in range(22):
                    p, k = PS()
                    S.op('pe', lambda e, p=p, j=j: e.transpose(out=p[:, 0:32], in_=cst_tok[0:32, j * 128:(j + 1) * 128], identity=cst[0:32, 0:32]), reads=['cst_tok', 'cst'], writes=[k])
                    S.op('dve', lambda e, p=p, j=j: e.tensor_copy(out=Xs['ahist'][:, j, :, :], in_=p[:, 0:32].rearrange("p (b r) -> p b r", r=2)), reads=[k], writes=['ahist'])

            def kload(g4):
                S.dma('pool', 's5%d' % (g4 % 2), kst2[g4 % 2][:], ck[g4 * 2:(g4 + 1) * 2].rearrange("b (mc p) f -> p b mc f", p=128), writes=['kst%d' % (g4 % 2)])

            def pre2():
                kload(1)
                S.dma('pool', 's1', Xs['S0b'][:], shgrn.rearrange("b h d v -> d b h v"), writes=['S0b'])
                for g4 in range(8):
                    kst = kst2[g4 % 2]
                    for bi in range(2):
                        bq = g4 * 2 + bi
                        p, k = PS()
                        pb = p[:].bitcast(BF16)
                        for hc in range(2):
                            for mc in range(2):
                                S.op('pe', lambda e, pb=pb, kst=kst, bi=bi, hc=hc, mc=mc: e.transpose(out=pb[:, (hc * 2 + mc) * 128:(hc * 2 + mc + 1) * 128], in_=kst[:, bi, mc, hc * 128:(hc + 1) * 128], identity=idb[:]),
                                     reads=['kst%d' % (g4 % 2), 'idb'], writes=[k], inc=(hc == 1 and mc == 1))
                        S.op('act', lambda e, pb=pb, bq=bq: e.activation(out=Xs['KTs'][:, bq, :, :], in_=pb[:, 0:512].rearrange("p (hc m) -> p hc m", hc=2), func=AF.Copy), reads=[k], writes=['KTs'])
                    if g4 + 2 < 8:
                        kload(g4 + 2)
                    if g4 == 7:
                        S.dma('pool', 's2', Xs['Vs'][:], cv.rearrange("b (mc p) f -> p b mc f", p=128), writes=['Vs'])
                        S.dma('sp', 's0', Xs['S0f'][:], shgrn.rearrange("b h d v -> d b h v"), writes=['S0f'])
                    yield
            Xs['pre1'] = pre1; Xs['pre2'] = pre2
            chk('sprologue')
            do_tile(Bs, 2048, False, True, Xs)
            S.op('dve', lambda e: e.tensor_copy(out=Xs['xpc'][:], in_=Xs['xp'][:, :, :, 8:23]), reads=['xp'], writes=['xpc'])
            for hh in range(2):
                for c in range(2):
                    p, k = PS()
                    S.op('pe', lambda e, p=p, hh=hh, c=c: e.transpose(out=p[0:120, 0:128], in_=Xs['xpc'][:, c, hh * 8:(hh + 1) * 8, :].rearrange("p b r -> p (b r)"), identity=ident),
                         reads=['xpc', 'cst'], writes=[k])
                    S.op('dve', lambda e, p=p, hh=hh, c=c: e.tensor_copy(out=Xs['spo'][0:120, hh, c * 128:(c + 1) * 128], in_=p[0:120, 0:128]), reads=[k], writes=['spo'])
            S.dma('sp', 'o_ps', o_pool_s.rearrange("(h b) r c -> (b r) h c", h=2), Xs['spo'][:], reads=['spo'])
        except StopBuild as ex:
            print('STOPPED at', ex)
        S.final()
        sst.close()
        import os
        if os.environ.get('KDEBUG'):
            print('CNT', S.cnt, {k: v[1] for k, v in S.dsem.items()})
    return nc


_NC = None


def kernel(**inp):
    global _NC
    f = lambda a: np.ascontiguousarray(np.asarray(a, dtype=np.float32))
    if _NC is None:
        _NC = build()
    cst = make_consts()
    prm = np.concatenate([f(inp['conv_w'][0]).reshape(66, 128), f(inp['conv_b'][0]).reshape(22, 128),
                          f(inp['hgrn_lb_logits']).reshape(8, 128), f(inp['pool_scale'][0]).reshape(2, 128),
                          f(inp['hgrn_onorm_g'][0]).reshape(4, 128), f(inp['ln1_g'][0]).reshape(8, 128),
                          f(inp['ln2_g'][0]).reshape(8, 128), f(inp['mem_norm_g'][0]).reshape(8, 128)], axis=0)
    shared = dict(lnf=f(inp['lnf_g']),
                  w_in=f(inp['w_in'][0]), w_kv=f(inp['w_mem_kv'][0]), w_out=f(inp['w_out'][0]), w_up=f(inp['w_up'][0]),
                  w_dn=f(inp['w_down'][0]), pool_w=f(inp['pool_w'][0]), prm_in=f(prm), cst=cst)
    in_maps = []
    for c in range(8):
        sl = slice(16 * c, 16 * c + 16)
        m = dict(shared)
        m['x_tok'] = f(np.concatenate([inp['x_prompt'][c], np.asarray(inp['x_sample'][sl]).reshape(128, 1024)], axis=0))
        m['mem'] = f(inp['mem_prompt'][c])
        m['spool'] = f(np.asarray(inp['state_pool'][0, sl]).reshape(240, 256))
        m['shgrn'] = f(inp['state_hgrn'][0, sl])
        m['sconv'] = f(np.asarray(inp['state_conv'][0, sl]).reshape(32, 2816))
        m['ck'] = f(np.asarray(inp['cache_mem_k'][0, sl]).reshape(16, 256, 256))
        m['cv'] = f(np.asarray(inp['cache_mem_v'][0, sl]).reshape(16, 256, 256))
        in_maps.append(m)
    res = run_bass_kernel_spmd(_NC, in_maps, core_ids=list(range(8)))
    R = res.results
    g = lambda k: np.stack([np.asarray(R[c][k], dtype=np.float32) for c in range(8)])
    y = g('y_tok')
    y_prompt = np.ascontiguousarray(y[:, :2048, :])
    y_sample = np.ascontiguousarray(y[:, 2048:, :].reshape(128, 8, 1024))
    return (y_prompt, y_sample,
            g('o_pool_p')[None], g('o_hgrn_p')[None], g('o_conv_p')[None],
            g('o_mk').reshape(1, 8, 256, 4, 64), g('o_mv').reshape(1, 8, 256, 4, 64),
            g('o_pool_s').reshape(1, 128, 15, 256), g('o_hgrn_s').reshape(1, 128, 4, 128, 128),
            g('o_conv_s').reshape(1, 128, 2, 2816))
```

```python
import numpy as np
from contextlib import ExitStack
import concourse.bass as bass
import concourse.mybir as mybir
from concourse.bass_utils import run_bass_kernel_spmd

F32, BF16 = mybir.dt.float32, mybir.dt.bfloat16
AF = mybir.ActivationFunctionType
ALU = mybir.AluOpType
EPS = 1e-6
NSLOT = 4
SAME_ENGINE_SYNC = True


import os


class StopBuild(Exception):
    pass


_hits = {}


def chk(name):
    if os.environ.get('KSTOP') == name:
        _hits[name] = _hits.get(name, 0) + 1
        if _hits[name] == int(os.environ.get('KHIT', '1')):
            raise StopBuild(name)


class Sched:
    def __init__(s, nc, es):
        s.nc, s.es = nc, es
        s.E = {'pe': nc.tensor, 'act': nc.scalar, 'dve': nc.vector, 'pool': nc.gpsimd, 'sp': nc.sync}
        s.sem = {k: es.enter_context(nc.semaphore('sem_' + k)) for k in s.E}
        s.cnt = {k: 0 for k in s.E}
        s.seen = {k: {} for k in s.E}
        s.lastw, s.reads, s.dsem = {}, {}, {}
        s.psn = 0
        s.ps_open = {}

    def _semh(s, key):
        return s.sem[key] if key in s.sem else s.dsem[key][0]

    def _wait(s, eng, key, val):
        if key == eng and (eng in ('pe', 'sp') or not SAME_ENGINE_SYNC):
            return
        if s.seen[eng].get(key, 0) >= val:
            return
        s.seen[eng][key] = val
        s.E[eng].wait_ge(s._semh(key), val)

    def deps(s, eng, reads, writes):
        need = {}
        for b in reads:
            if b in s.lastw:
                k, v = s.lastw[b]
                need[k] = max(need.get(k, 0), v)
        for b in writes:
            if b in s.lastw:
                k, v = s.lastw[b]
                need[k] = max(need.get(k, 0), v)
            for (k, v) in s.reads.get(b, ()):
                need[k] = max(need.get(k, 0), v)
        for k, v in need.items():
            s._wait(eng, k, v)

    def _record(s, tok, reads, writes):
        for b in reads:
            s.reads.setdefault(b, []).append(tok)
        for b in writes:
            s.lastw[b] = tok
            s.reads[b] = []

    def op(s, eng, fn, reads=(), writes=(), inc=True):
        psr = [b for b in reads if isinstance(b, tuple) and b[0] == 'ps']
        if psr:
            reads = [b for b in reads if b not in psr]
            writes = list(writes) + psr
            for b in psr:
                s.ps_open[b[1]] -= 1
                if s.ps_open[b[1]] <= 0:
                    del s.ps_open[b[1]]
        s.deps(eng, reads, writes)
        ins = fn(s.E[eng])
        if inc:
            s.cnt[eng] += 1
            ins.then_inc(s.sem[eng], 1)
            tok = (eng, s.cnt[eng])
        else:
            tok = (eng, s.cnt[eng] + 1)
        s._record(tok, reads, writes)
        return ins

    def dma(s, eng, chan, out, in_, reads=(), writes=(), **kw):
        s.deps(eng, reads, writes)
        if chan not in s.dsem:
            s.dsem[chan] = [s.es.enter_context(s.nc.semaphore('d_' + chan)), 0]
        ins = s.E[eng].dma_start(out=out, in_=in_, **kw)
        s.dsem[chan][1] += 16
        ins.then_inc(s.dsem[chan][0], 16)
        s._record((chan, s.dsem[chan][1]), reads, writes)

    def barrier(s):
        for eng in s.E:
            for k in s.sem:
                if s.cnt[k] > 0:
                    s._wait(eng, k, s.cnt[k])
            for k in s.dsem:
                s._wait(eng, k, s.dsem[k][1])

    def final(s):
        for k in s.sem:
            if s.cnt[k] > 0:
                s._wait('sp', k, s.cnt[k])
        for k in s.dsem:
            s._wait('sp', k, s.dsem[k][1])


def make_consts():
    c = np.zeros((128, 576), np.float32)
    c[:, 0:128] = np.eye(128, dtype=np.float32)
    s = np.arange(128)[:, None]
    t = np.arange(128)[None, :]
    c[:, 128:256] = (t >= s)
    c[:, 256:384] = (t >= s) & ((t // 8) == (s // 8))
    c[:, 384:400] = (np.arange(128)[:, None] // 8) == np.arange(16)[None, :]
    c[:, 400:528] = (np.arange(128)[None, :] % 8 != 0)
    for ch in range(2):
        for p in range(128):
            w = [2, 4, 8, 16][2 * ch + p // 64]
            c[p, 528 + ch * 16: 528 + ch * 16 + 16] = 1.0 / np.minimum(w, np.arange(16) + 1.0)
            c[p, 560 + ch] = 1.0 / w
    return c


def build():
    nc = bass.Bass("TRN2", target_bir_lowering=False)
    D = lambda n, sh, k="ExternalInput": nc.dram_tensor(n, sh, F32, kind=k).ap()
    x_tok = D("x_tok", [2176, 1024]); mem = D("mem", [256, 1024])
    spool = D("spool", [240, 256]); shgrn = D("shgrn", [16, 4, 128, 128]); sconv = D("sconv", [32, 2816])
    ck = D("ck", [16, 256, 256]); cv = D("cv", [16, 256, 256])
    lnf = D("lnf", [1024])
    w_in = D("w_in", [1024, 2560]); w_kv = D("w_kv", [1024, 512]); w_out = D("w_out", [1024, 1024])
    w_up = D("w_up", [1024, 5632]); w_dn = D("w_dn", [2816, 1024])
    pool_w = D("pool_w", [4, 64, 64]); prm_in = D("prm_in", [126, 128]); cst_in = D("cst", [128, 576])
    O = lambda n, sh: D(n, sh, "ExternalOutput")
    y_tok = O("y_tok", [2176, 1024]); o_pool_p = O("o_pool_p", [15, 256]); o_hgrn_p = O("o_hgrn_p", [4, 128, 128])
    o_conv_p = O("o_conv_p", [2, 2816]); o_mk = O("o_mk", [256, 256]); o_mv = O("o_mv", [256, 256])
    o_pool_s = O("o_pool_s", [16, 15, 256]); o_hgrn_s = O("o_hgrn_s", [16, 4, 128, 128]); o_conv_s = O("o_conv_s", [32, 2816])

    with ExitStack() as es:
        S = Sched(nc, es)
        uid = [0]
        def sbt(st, n, sh, dt=F32):
            uid[0] += 1
            return st.enter_context(nc.sbuf_tensor("sb%d_%s" % (uid[0], n), sh, dt))
        psb = [es.enter_context(nc.psum_tensor("ps%d" % i, [128, 512], F32)) for i in range(8)]

        def PS(n=1):
            for _ in range(8):
                i = S.psn % 8
                S.psn += 1
                if i not in S.ps_open:
                    break
            else:
                raise RuntimeError('no free PSUM bank')
            S.ps_open[i] = n
            return psb[i], ('ps', i)

        cst = sbt(es, "cst", [128, 576]); prm = sbt(es, "prm", [128, 128]); prm_st = sbt(es, "prm_st", [126, 128])
        idb = sbt(es, "idb", [128, 128], BF16); onesb = sbt(es, "onesb", [128, 128], BF16)
        gf = sbt(es, "gf", [128, 1024])
        wbd = sbt(es, "wbd", [128, 2, 128], BF16)
        lbc = sbt(es, "lbc", [128, 16])
        small = sbt(es, "small", [128, 8])
        ring = [sbt(es, "ring%d" % i, [128, 5632], BF16) for i in range(NSLOT)]
        G = {}
        stat = sbt(es, "stat", [128, 32])
        ident = cst[:, 0:128]; cmask = cst[:, 128:256]; smask = cst[:, 256:384]; seqm = cst[:, 384:400]
        rmask = cst[:, 400:528]; invw = cst[:, 560:562]
        invcnt = cst[:, 528:560]
        epsc = small[:, 0:1]; mhalf = small[:, 1:2]; onec = small[:, 2:3]
        cw = lambda r, j: prm[:, r * 22 + j: r * 22 + j + 1]
        cb = lambda j: prm[:, 66 + j: 67 + j]
        pscale = lambda c: prm[:, 96 + c: 97 + c]
        onorm = lambda h: prm[:, 98 + h: 99 + h]

        wseq = []

        def ld_in(b):
            def f(slot, key, chan):
                S.dma('pool', chan, slot[:, 0:4096].rearrange("p (k n) -> p k n", k=8),
                      w_in[:, b * 512:(b + 1) * 512].rearrange("(k p) n -> p k n", p=128), writes=[key])
            return f

        def ld_kv():
            def f(slot, key, chan):
                S.dma('pool', chan, slot[:, 0:4096].rearrange("p (k n) -> p k n", k=8),
                      w_kv.rearrange("(k p) n -> p k n", p=128), writes=[key])
            return f

        def ld_out(c):
            def f(slot, key, chan):
                S.dma('pool', chan, slot[:, 0:4096].rearrange("p (k n) -> p k n", k=8),
                      w_out[:, c * 512:(c + 1) * 512].rearrange("(k p) n -> p k n", p=128), writes=[key])
            return f

        def ld_up(r):
            def f(slot, key, chan):
                v = slot[:, 0:4096].rearrange("p (k n) -> p k n", k=8)
                S.dma('pool', chan, v[:, :, 0:256],
                      w_up[:, r * 256:(r + 1) * 256].rearrange("(k p) n -> p k n", p=128), writes=[key])
                S.dma('pool', chan, v[:, :, 256:512],
                      w_up[:, 2816 + r * 256:2816 + (r + 1) * 256].rearrange("(k p) n -> p k n", p=128), writes=[key])
            return f

        def ld_dn(q):
            def f(slot, key, chan):
                S.dma('pool', chan, slot[:, 0:5632].rearrange("p (k n) -> p k n", k=22),
                      w_dn[:, q * 256:(q + 1) * 256].rearrange("(k p) n -> p k n", p=128), writes=[key])
            return f

        wscr = nc.dram_tensor("wscr", [22, 128, 5632], BF16, kind="Internal").ap()
        blocks = [('in', b) for b in range(5)] + [('out', c) for c in range(2)] + [('up', r) for r in range(11)] + [('dn', q) for q in range(4)]
        mk = {'in': ld_in, 'out': ld_out, 'up': ld_up, 'dn': ld_dn}
        for ti in range(5):
            for bi, (kind, idx) in enumerate(blocks):
                n = 5632 if kind == 'dn' else 4096
                if ti == 0:
                    wseq.append((mk[kind](idx), bi, n))
                    if bi == 2:
                        wseq.append((ld_kv(), None, 0))
                else:
                    def f(slot, key, chan, bi=bi, n=n):
                        S.dma('pool', chan, slot[:, 0:n], wscr[bi, :, 0:n], reads=[('scr', bi)], writes=[key])
                    wseq.append((f, None, n))
        wstate = {'issued': 0, 'next': 0}

        def wneed(prefetch=True):
            i = wstate['next']
            wstate['next'] += 1
            upto = min(i + NSLOT - 1, len(wseq) - 1) if prefetch else i
            while wstate['issued'] <= upto:
                j = wstate['issued']
                wseq[j][0](ring[j % NSLOT], ('w', j % NSLOT), 'w%d' % (j % NSLOT))
                wstate['issued'] += 1
            sl = ring[i % NSLOT]
            bi, n = wseq[i][1], wseq[i][2]
            if bi is not None:
                S.dma('sp', 'wb%d' % (i % NSLOT), wscr[bi, :, 0:n], sl[:, 0:n], reads=[('w', i % NSLOT)], writes=[('scr', bi)])
            return sl, ('w', i % NSLOT)

        def w8(sl):
            return sl[:, 0:4096].rearrange("p (k n) -> p k n", k=8)

        def w22(sl):
            return sl[:, 0:5632].rearrange("p (k n) -> p k n", k=22)

        def mmg(out_ap, pskey, pairs, reads, per=None):
            n = len(pairs)
            for i, (l, r) in enumerate(pairs):
                S.op('pe', lambda e, l=l, r=r, i=i: e.matmul(out_ap, lhsT=l, rhs=r, start=(i == 0), stop=(i == n - 1)),
                     reads=list(reads) + (list(per[i]) if per else []), writes=[pskey], inc=(i == n - 1))

        pst = ExitStack(); sst = ExitStack()
        es.enter_context(pst); es.enter_context(sst)
        try:
            S.dma('sp', 'c0', cst[:], cst_in[:, :], writes=['cst'])
            S.dma('sp', 'c1', prm_st[:], prm_in[:, :], writes=['prm_st'])
            S.dma('sp', 'c4', gf[:], lnf.partition_broadcast(128), writes=['gf'])
            S.op('dve', lambda e: e.memset(small[:, 0:1], EPS), writes=['small'])
            S.op('dve', lambda e: e.memset(small[:, 1:2], -0.5), writes=['small'])
            S.op('dve', lambda e: e.memset(small[:, 2:3], 1.0), writes=['small'])
            S.op('dve', lambda e: e.memset(onesb[:], 1.0), writes=['onesb'])
            S.op('dve', lambda e: e.tensor_copy(out=idb[:], in_=ident), reads=['cst'], writes=['idb'])
            S.op('pool', lambda e: e.memset(wbd[:], 0.0), writes=['wbd'])
            for gi in range(4):
                c, o = gi // 2, (gi % 2) * 64
                S.dma('pool', 'c5', wbd[o:o + 64, c, o:o + 64], pool_w[gi, :, :], writes=['wbd'])
            p_, pk = PS()
            S.op('pe', lambda e: e.transpose(out=p_[:, 0:126], in_=prm_st[:, :], identity=cst[0:126, 0:126]),
                 reads=['prm_st', 'cst'], writes=[pk])
            S.op('dve', lambda e: e.tensor_copy(out=prm[:, 0:126], in_=p_[:, 0:126]), reads=[pk], writes=['prm'])
            S.op('dve', lambda e: e.tensor_sub(out=lbc[:, 8:12], in0=prm[:, 88:92], in1=prm[:, 92:96]), reads=['prm'], writes=['lbc'])
            S.op('act', lambda e: e.activation(out=lbc[:, 8:12], in_=lbc[:, 8:12], func=AF.Tanh, scale=0.5), reads=['lbc'], writes=['lbc'])
            S.op('dve', lambda e: e.tensor_scalar(out=lbc[:, 0:4], in0=lbc[:, 8:12], scalar1=0.25, scalar2=0.75, op0=ALU.mult, op1=ALU.add), reads=['lbc'], writes=['lbc'])
            S.op('dve', lambda e: e.tensor_scalar(out=lbc[:, 4:8], in0=lbc[:, 8:12], scalar1=-0.25, scalar2=0.25, op0=ALU.mult, op1=ALU.add), reads=['lbc'], writes=['lbc'])
            S.op('dve', lambda e: e.tensor_scalar(out=lbc[:, 12:16], in0=lbc[:, 8:12], scalar1=0.25, scalar2=-0.25, op0=ALU.mult, op1=ALU.add), reads=['lbc'], writes=['lbc'])

            def rms_to_T(src_ap, gcol, dstT, col0, rd, wr_extra=()):
                hb = G['hb']
                S.op('act', lambda e: e.activation(out=hb[:], in_=src_ap, func=AF.Square, accum_out=stat[:, 0:1]),
                     reads=rd, writes=['hb', 'stat'])
                S.op('dve', lambda e: e.tensor_scalar(out=stat[:, 1:2], in0=stat[:, 0:1], scalar1=1.0 / 1024, scalar2=EPS, op0=ALU.mult, op1=ALU.add),
                     reads=['stat'], writes=['stat'])
                S.op('pool', lambda e: e.tensor_tensor(out=stat[:, 2:3], in0=stat[:, 1:2], in1=mhalf, op=ALU.pow),
                     reads=['stat', 'small'], writes=['stat'])
                chk('rms_a')
                S.op('act', lambda e: e.activation(out=hb[:], in_=src_ap, func=AF.Copy, scale=stat[:, 2:3]),
                     reads=list(rd) + ['stat'], writes=['hb'])
                chk('rms_b')
                p, k = PS()
                pb = p[:].bitcast(BF16)
                for kk in range(8):
                    S.op('pe', lambda e, kk=kk: e.transpose(out=pb[:, kk * 128:(kk + 1) * 128], in_=hb[:, kk * 128:(kk + 1) * 128], identity=idb[:]),
                         reads=['hb', 'idb'], writes=[k], inc=(kk == 7))
                chk('rms_c')
                S.op('dve', lambda e: e.tensor_tensor(out=dstT[:, :, col0:col0 + 128], in0=pb.rearrange("p (k n) -> p k n", k=8),
                                                      in1=prm[:, gcol:gcol + 8].unsqueeze(2).to_broadcast([128, 8, 128]), op=ALU.mult),
                     reads=[k, 'prm'], writes=list(wr_extra))
                chk('rms_d')

            def rms_multi(items, gcol, dstT, wr, base=0, phase='all'):
                n = len(items)
                hbs = G['hbs']
                if phase in ('all', 'stats'):
                    for i_, (src, rd, col0) in enumerate(items):
                        S.op('act', lambda e, i_=i_, src=src: e.activation(out=hbs[(base + i_) % 2][:], in_=src, func=AF.Square, accum_out=stat[:, base + i_:base + i_ + 1]),
                             reads=rd, writes=['hb%d' % ((base + i_) % 2), 'stat'])
                    S.op('act', lambda e: e.activation(out=stat[:, 4 + base:4 + base + n], in_=stat[:, base:base + n], func=AF.Ln, scale=1.0 / 1024, bias=epsc), reads=['stat', 'small'], writes=['stat'])
                    S.op('act', lambda e: e.activation(out=stat[:, 8 + base:8 + base + n], in_=stat[:, 4 + base:4 + base + n], func=AF.Exp, scale=-0.5), reads=['stat'], writes=['stat'])
                if phase == 'stats':
                    return
                for i_, (src, rd, col0) in enumerate(items):
                    hb = hbs[(base + i_) % 2]
                    S.op('act', lambda e, i_=i_, src=src, hb=hb: e.activation(out=hb[:], in_=src, func=AF.Copy, scale=stat[:, 8 + base + i_:9 + base + i_]),
                         reads=list(rd) + ['stat'], writes=['hb%d' % ((base + i_) % 2)])
                    p, k = PS()
                    pb = p[:].bitcast(BF16)
                    for kk in range(8):
                        S.op('pe', lambda e, kk=kk, hb=hb, pb=pb: e.transpose(out=pb[:, kk * 128:(kk + 1) * 128], in_=hb[:, kk * 128:(kk + 1) * 128], identity=idb[:]),
                             reads=['hb%d' % ((base + i_) % 2), 'idb'], writes=[k], inc=(kk == 7))
                    S.op('dve', lambda e, col0=col0, pb=pb: e.tensor_tensor(out=dstT[:, :, col0:col0 + 128], in0=pb.rearrange("p (k n) -> p k n", k=8),
                                                                          in1=prm[:, gcol:gcol + 8].unsqueeze(2).to_broadcast([128, 8, 128]), op=ALU.mult),
                         reads=[k, 'prm'], writes=list(wr))

            def alloc_phase(st, T, sample):
                B = {}
                B['T'] = T
                ns = T // 128
                G['xres'] = sbt(st, "xres", [128, ns, 1024]); G['hT'] = sbt(st, "hT", [128, 8, T], BF16)
                G['mixT'] = sbt(st, "mixT", [128, 8, T], BF16); G['mT'] = sbt(st, "mT", [128, 22, T], BF16)
                G['hbs'] = [sbt(st, "hb%d" % i, [128, 1024], BF16) for i in range(2)]
                G['hb'] = G['hbs'][0]
                B['u'] = sbt(st, "u_sb", [128, 2, 15 + T]) if not sample else None
                B['pA'] = sbt(st, "pA", [128, max(16 + T, 368)]); B['pB'] = sbt(st, "pB", [128, max(16 + T, 368)])
                B['d'] = sbt(st, "d_sb", [128, 2, T], BF16)
                B['Ab'] = [sbt(st, "A_sb%d" % i, [128, T]) for i in range(2)]
                B['E2b'] = [sbt(st, "E2_%d" % i, [128, T]) for i in range(2)]
                B['K1b'] = [sbt(st, "K1_%d" % i, [128, T]) for i in range(2)]
                B['QSall'] = sbt(st, "QSall", [128, 4, T]); B['THall'] = sbt(st, "THall", [128, 4, T])
                B['GS'] = sbt(st, "GS", [128, 4, T])
                B['qAT'] = sbt(st, "qAT", [128, 4, T], BF16); B['kAT'] = sbt(st, "kAT", [128, 4, T], BF16)
                B['kAk'] = sbt(st, "kAk", [128, T // 128, 512], BF16); B['vtk'] = sbt(st, "vtk", [128, T // 128, 512], BF16)
                B['eAe'] = sbt(st, "eAe", [128, 4, 16])
                B['PT'] = sbt(st, "PT", [128, 4, 128], BF16)
                B['osb'] = sbt(st, "osb", [128, 4, T]); B['osq'] = sbt(st, "osq", [128, T], BF16)
                B['R'] = sbt(st, "R_sb", [128, T]); B['t1'] = sbt(st, "t1", [128, T])
                B['qxT'] = sbt(st, "qxT", [128, 2, T], BF16)
                B['cbuf'] = [B['QSall'][:, i, :] for i in range(2)]
                B['gbuf'] = [B['QSall'][:, 2 + i, :] for i in range(2)]
                return B

            def do_tile(B, t0, first, sample, X):
                T = B['T']; nsub = T // 128
                xres, hT, mixT, mT = G['xres'], G['hT'], G['mixT'], G['mT']
                for sub in range(nsub):
                    S.dma('sp', 'xin%d' % sub, xres[:, sub, :], x_tok[t0 + sub * 128:t0 + (sub + 1) * 128, :], writes=['xres%d' % sub])
                if first and not sample:
                    X['memdma']()
                if sample:
                    X['pre1']()
                if not X.get('prenormed'):
                    rms_multi([(xres[:, sub, :], ['xres%d' % sub], sub * 128) for sub in range(nsub)], 102, hT, ['hT'])
                X['prenormed'] = False
                chk('norm1')
                QSall, THall, GS = B['QSall'], B['THall'], B['GS']
                done = {'pool': False, 'attn': False, 'inproj': False}

                def g_inproj(b0, b1):
                    for b in range(b0, b1):
                        wsl, wk = wneed()
                        wv = w8(wsl)
                        for jj in range(4):
                            j = b * 4 + jj
                            if 10 <= j < 14:
                                continue
                            yield
                            p, k = PS()
                            mmg(p[:, 0:T], k, [(wv[:, kk, jj * 128:(jj + 1) * 128], hT[:, kk, 0:T]) for kk in range(8)], ['hT', wk])
                            src = p[:, 0:T]
                            if j < 2:
                                if sample:
                                    S.op('act', lambda e, j=j, src=src: e.activation(out=X['xp'][:, j, :, 15:23], in_=src.rearrange("p (b t) -> p b t", t=8), func=AF.Copy),
                                         reads=[k], writes=['xp'])
                                else:
                                    S.op('act', lambda e, j=j, src=src: e.activation(out=B['u'][:, j, 15:15 + T], in_=src, func=AF.Copy), reads=[k], writes=['u'])
                            elif j < 6:
                                S.op('act', lambda e, j=j, src=src: e.activation(out=QSall[:, j - 2, :], in_=src, func=AF.Silu), reads=[k], writes=['QS%d' % (j - 2)])
                            elif j < 10:
                                S.op('act', lambda e, j=j, src=src: e.activation(out=THall[:, j - 6, :], in_=src, func=AF.Tanh, scale=0.5), reads=[k], writes=['TH%d' % (j - 6)])
                            elif j < 18:
                                h = j - 14
                                S.op('act', lambda e, h=h, src=src: e.activation(out=GS[:, h, :], in_=src, func=AF.Copy), reads=[k], writes=['GS%d' % h])
                            else:
                                S.op('act', lambda e, j=j, src=src: e.activation(out=B['qxT'][:, j - 18, :], in_=src, func=AF.Copy), reads=[k], writes=['qxT'])
                        if b in (2, 3):
                            c0 = 256 if b == 2 else 0
                            o0 = 0 if b == 2 else 256
                            for sub in range(nsub):
                                yield
                                p, k = PS()
                                mmg(p[:, 0:256], k, [(hT[:, kk, sub * 128:(sub + 1) * 128], wv[:, kk, c0:c0 + 256]) for kk in range(8)], ['hT', wk])
                                S.op('dve', lambda e, p=p, sub=sub, o0=o0: e.tensor_copy(out=B['vtk'][:, sub, o0:o0 + 256], in_=p[:, 0:256]), reads=[k], writes=['vtk'])

                    if b1 == 5:
                        done['inproj'] = True
                    yield

                for _ in g_inproj(0, 3):
                    pass
                if first and not sample:
                    X['memkv']()
                chk('inproj')
                if sample:
                    X['pre1b']()
                kgen = X['pre2']() if sample else iter(())

                def kstep(n=1):
                    for _ in range(n):
                        next(kgen, None)
                kstep(2)
                def g_pool():
                    pA, pB, dsb = B['pA'], B['pB'], B['d']
                    Rb = G['hbs'][0][:].bitcast(F32)
                    for c in range(2):
                        kstep(1)
                        if sample:
                            Xv = X['xp'][:, c, :, :]
                            L = 23
                            sl = lambda buf, n: buf[:, 0:16 * n].rearrange("p (b n) -> p b n", b=16)
                            xs = lambda a, b_: Xv[:, :, a:b_]
                            xkey = 'xp'
                        else:
                            u = B['u']
                            if first and c == 0:
                                yield
                                S.op('dve', lambda e: e.memset(u[:, :, 0:15], 0.0), writes=['u'])
                            L = 15 + T
                            sl = lambda buf, n: buf[:, 0:n]
                            xs = lambda a, b_, c=c: u[:, c, a:b_]
                            xkey = 'u'
                        sv = lambda buf, n, a, b_: (sl(buf, n)[:, :, a:b_] if sample else sl(buf, n)[:, a:b_])
                        yield
                        S.op('dve', lambda e: e.tensor_tensor(out=sl(pA, L - 1), in0=xs(1, L), in1=xs(0, L - 1), op=ALU.add), reads=[xkey], writes=['pA'])
                        yield
                        S.op('dve', lambda e: e.tensor_tensor(out=sl(pB, L - 3), in0=sv(pA, L - 1, 2, L - 1), in1=sv(pA, L - 1, 0, L - 3), op=ALU.add), reads=['pA'], writes=['pB'])
                        uview = xs(15, L)
                        dv = dsb[:, c, :].rearrange("p (b t) -> p b t", t=8) if sample else dsb[:, c, :]

                        def comb(plo, phi, buf, n, off, wsel, dv=dv, uview=uview):
                            S.op('dve', lambda e: e.scalar_tensor_tensor(out=dv[plo:phi], in0=sv(buf, n, off, off + (8 if sample else T))[plo:phi], scalar=invw[plo:phi, c:c + 1],
                                                                          in1=uview[plo:phi], op0=ALU.mult, op1=ALU.subtract),
                                 reads=[wsel, xkey, 'cst'], writes=['d'])
                        if c == 0:
                            yield
                            comb(0, 64, pA, L - 1, 14, 'pA')
                            yield
                            comb(64, 128, pB, L - 3, 12, 'pB')
                        else:
                            yield
                            S.op('dve', lambda e: e.tensor_tensor(out=sl(pA, L - 7), in0=sv(pB, L - 3, 4, L - 3), in1=sv(pB, L - 3, 0, L - 7), op=ALU.add), reads=['pB'], writes=['pA'])
                            yield
                            comb(0, 64, pA, L - 7, 8, 'pA')
                            yield
                            S.op('dve', lambda e: e.tensor_tensor(out=sl(pB, L - 15), in0=sv(pA, L - 7, 8, L - 7), in1=sv(pA, L - 7, 0, L - 15), op=ALU.add), reads=['pA'], writes=['pB'])
                            yield
                            comb(64, 128, pB, L - 15, 0, 'pB')
                        if first and not sample:
                            for (plo, phi, buf, off) in ((0, 64, pA, 14 if c == 0 else 8), (64, 128, pB, 12 if c == 0 else 0)):
                                yield
                                S.op('dve', lambda e, plo=plo, phi=phi, buf=buf, off=off: e.tensor_tensor(out=Rb[plo:phi, 0:15], in0=buf[plo:phi, off:off + 15],
                                                                                                          in1=invcnt[plo:phi, c * 16:c * 16 + 15], op=ALU.mult),
                                     reads=['pA', 'pB', 'cst'], writes=['hb0'])
                                yield
                                S.op('dve', lambda e, plo=plo, phi=phi: e.tensor_tensor(out=dsb[plo:phi, c, 0:15], in0=Rb[plo:phi, 0:15], in1=u[plo:phi, c, 15:30], op=ALU.subtract),
                                     reads=['hb0', 'u'], writes=['d'])
                        yield
                        p, k = PS()
                        yield
                        mmg(p[:, 0:T], k, [(wbd[:, c, :], dsb[:, c, :])], ['wbd', 'd'])
                        yield
                        S.op('act', lambda e, p=p, c=c: e.activation(out=mixT[:, c, 0:T], in_=p[:, 0:T], func=AF.Identity, scale=pscale(c)), reads=[k, 'prm'], writes=['mixT%d' % c])
                    if not sample:
                        u = B['u']
                        if X.get('last'):
                            for c in range(2):
                                yield
                                p, k = PS()
                                yield
                                S.op('pe', lambda e, p=p, c=c: e.transpose(out=p[0:15, c * 128:(c + 1) * 128], in_=u[:, c, T:T + 15], identity=ident), reads=['u', 'cst'], writes=[k])
                                yield
                                S.op('dve', lambda e, p=p, c=c: e.tensor_copy(out=X['ppo'][0:15, c * 128:(c + 1) * 128], in_=p[0:15, c * 128:(c + 1) * 128]), reads=[k], writes=['ppo'])
                            S.dma('sp', 'o_pp', o_pool_p[:, :], X['ppo'][0:15, :], reads=['ppo'])
                        else:
                            yield
                            S.op('dve', lambda e: e.tensor_copy(out=u[:, :, 0:15], in_=u[:, :, T:T + 15]), reads=['u'], writes=['u'])

                    yield
                def g_hgrn():
                    qAT, kAT, eAe = B['qAT'], B['kAT'], B['eAe']
                    smk = rmask if sample else X['cm512'][:, :]
                    smkey = 'cst' if sample else 'cm512'

                    def stA(h):
                        par = h % 2
                        K1 = B['K1b'][par]
                        thk = 'TH%d' % h
                        S.op('act', lambda e: e.activation(out=K1[:], in_=THall[:, h, :], func=AF.Identity, scale=lbc[:, 12 + h:13 + h], bias=lbc[:, 4 + h:5 + h]),
                             reads=[thk, 'lbc'], writes=['K1%d' % par])
                        S.op('act', lambda e: e.activation(out=THall[:, h, :], in_=THall[:, h, :], func=AF.Ln, scale=lbc[:, 4 + h:5 + h], bias=lbc[:, h:h + 1]),
                             reads=[thk, 'lbc'], writes=[thk])

                    def stB(h):
                        par = h % 2
                        A = B['Ab'][par]
                        S.op('dve', lambda e: e.tensor_tensor_scan(out=A[:], data0=smk, data1=THall[:, h, :], initial=0.0, op0=ALU.mult, op1=ALU.add),
                             reads=['TH%d' % h, smkey], writes=['A%d' % par])

                    def stC(h):
                        par = h % 2
                        A, E2 = B['Ab'][par], B['E2b'][par]
                        S.op('act', lambda e: e.activation(out=E2[:], in_=A[:], func=AF.Exp, scale=-1.0), reads=['A%d' % par], writes=['E2%d' % par])
                        S.op('act', lambda e: e.activation(out=A[:], in_=A[:], func=AF.Exp), reads=['A%d' % par], writes=['A%d' % par])

                    def stD(h):
                        par = h % 2
                        A, E2, K1 = B['Ab'][par], B['E2b'][par], B['K1b'][par]
                        ka, ke, kk1 = 'A%d' % par, 'E2%d' % par, 'K1%d' % par
                        S.op('dve', lambda e: e.tensor_tensor(out=qAT[:, h, :], in0=QSall[:, h, :], in1=A[:], op=ALU.mult), reads=['QS%d' % h, ka], writes=['qAT'])
                        S.op('dve', lambda e: e.tensor_tensor(out=kAT[:, h, :], in0=K1[:], in1=E2[:], op=ALU.mult), reads=[kk1, ke], writes=['kAT'])
                        if sample:
                            S.op('dve', lambda e: e.tensor_copy(out=eAe[:, h, 0:16], in_=A[:].rearrange("p (b t) -> p b t", t=8)[:, :, 7]), reads=[ka], writes=['eAe'])
                        else:
                            S.op('dve', lambda e: e.tensor_copy(out=eAe[:, h, 0:nsub], in_=A[:].rearrange("p (c t) -> p c t", t=128)[:, :, 127]), reads=[ka], writes=['eAe'])

                    stA(0)
                    yield
                    stB(0)
                    yield
                    kstep()
                    stA(1)
                    yield
                    stC(0)
                    yield
                    stB(1)
                    yield
                    kstep()
                    stD(0)
                    yield
                    stA(2)
                    yield
                    stC(1)
                    yield
                    stB(2)
                    yield
                    kstep()
                    stD(1)
                    yield
                    stA(3)
                    yield
                    stC(2)
                    yield
                    stB(3)
                    yield
                    kstep()
                    stD(2)
                    yield
                    stC(3)
                    yield
                    stD(3)
                    yield
                    kstep(8)
                    chk('hgrn_ew')
                    while not done['inproj']:
                        yield
                    for h in range(4):
                        S.op('act', lambda e, h=h: e.activation(out=GS[:, h, :], in_=GS[:, h, :], func=AF.Silu), reads=['GS%d' % h], writes=['GS%d' % h])
                    for cc in range(nsub):
                        yield
                        p, k = PS()
                        pb = p[:].bitcast(BF16)
                        for h in range(4):
                            yield
                            S.op('pe', lambda e, h=h, cc=cc, pb=pb: e.transpose(out=pb[:, h * 128:(h + 1) * 128], in_=kAT[:, h, cc * 128:(cc + 1) * 128], identity=idb[:]),
                                 reads=['kAT', 'idb'], writes=[k], inc=(h == 3))
                        yield
                        S.op('act', lambda e, cc=cc, pb=pb: e.activation(out=B['kAk'][:, cc, :], in_=pb[:, 0:512], func=AF.Copy), reads=[k], writes=['kAk%d' % cc])
                    chk('katr')
                    PT, osb, kAk, vtk = B['PT'], B['osb'], B['kAk'], B['vtk']
                    for cc in range(nsub):
                        cs = slice(cc * 128, (cc + 1) * 128)
                        yield
                        pS, kS = PS()
                        for h in range(4):
                            yield
                            mmg(pS[:, h * 128:(h + 1) * 128], kS, [(kAT[:, h, cs], qAT[:, h, cs])], ['kAT', 'qAT'])
                        msk = smask if sample else cmask
                        yield
                        S.op('dve', lambda e, pS=pS, msk=msk: e.tensor_tensor(out=PT[:], in0=pS[:, :].rearrange("p (h t) -> p h t", h=4),
                                                                              in1=msk.unsqueeze(1).to_broadcast([128, 4, 128]), op=ALU.mult),
                             reads=[kS, 'cst'], writes=['PT'])
                        yield
                        pO, kO = PS()
                        if not sample:
                            Sf, Sb = X['Sf'], X['Sb']
                            for h in range(4):
                                yield
                                mmg(pO[:, h * 128:(h + 1) * 128], kO, [(vtk[:, cc, h * 128:(h + 1) * 128], PT[:, h, :]), (Sb[:, h, :], qAT[:, h, cs])],
                                    ['vtk', 'PT', 'Sb', 'qAT'])
                            yield
                            S.op('act', lambda e, pO=pO, cs=cs: e.activation(out=osb[:, :, cs], in_=pO[:, :].rearrange("p (h t) -> p h t", h=4), func=AF.Copy), reads=[kO], writes=['osb'])
                            yield
                            pZ, kZ = PS()
                            for h in range(4):
                                yield
                                mmg(pZ[:, h * 128:(h + 1) * 128], kZ, [(kAk[:, cc, h * 128:(h + 1) * 128], vtk[:, cc, h * 128:(h + 1) * 128])], ['kAk%d' % cc, 'vtk'])
                            yield
                            S.op('dve', lambda e, pZ=pZ: e.tensor_tensor(out=Sf[:], in0=Sf[:], in1=pZ[:, :].rearrange("p (h v) -> p h v", h=4), op=ALU.add), reads=[kZ, 'Sf'], writes=['Sf'])
                            yield
                            S.op('dve', lambda e, cc=cc: e.tensor_tensor(out=Sf[:], in0=Sf[:], in1=eAe[:, :, cc:cc + 1].to_broadcast([128, 4, 128]), op=ALU.mult), reads=['Sf', 'eAe'], writes=['Sf'])
                            yield
                            S.op('act', lambda e: e.activation(out=Sb[:], in_=Sf[:], func=AF.Copy), reads=['Sf'], writes=['Sb'])
                        else:
                            S0f, S0b = X['S0f'], X['S0b']
                            for h in range(4):
                                pairs = [(vtk[:, 0, h * 128:(h + 1) * 128], PT[:, h, :])]
                                n = 17
                                yield
                                S.op('pe', lambda e, h=h: e.matmul(pO[:, h * 128:(h + 1) * 128], lhsT=vtk[:, 0, h * 128:(h + 1) * 128], rhs=PT[:, h, :], start=True, stop=False),
                                     reads=['vtk', 'PT'], writes=[kO], inc=False)
                                for bq in range(16):
                                    yield
                                    S.op('pe', lambda e, h=h, bq=bq: e.matmul(pO[:, h * 128 + bq * 8:h * 128 + bq * 8 + 8], lhsT=S0b[:, bq, h, :], rhs=qAT[:, h, bq * 8:bq * 8 + 8],
                                                                             start=False, stop=(bq == 15)),
                                         reads=['S0b', 'qAT'], writes=[kO], inc=(bq == 15))
                            yield
                            S.op('act', lambda e, pO=pO: e.activation(out=osb[:, :, 0:128], in_=pO[:, :].rearrange("p (h t) -> p h t", h=4), func=AF.Copy), reads=[kO], writes=['osb'])
                            Vb2 = X['Vblk2']

                            def stV(i):
                                h, bg = i // 4, i % 4
                                vb = Vb2[i % 2]
                                S.op('dve', lambda e: e.tensor_tensor(out=vb[:], in0=vtk[:, 0, h * 128:(h + 1) * 128].unsqueeze(1).to_broadcast([128, 4, 128]),
                                                                      in1=seqm[:, bg * 4:bg * 4 + 4].unsqueeze(2).to_broadcast([128, 4, 128]), op=ALU.mult),
                                     reads=['vtk', 'cst'], writes=['Vblk%d' % (i % 2)])

                            def stU(i):
                                h, bg = i // 4, i % 4
                                vb = Vb2[i % 2]
                                pZ, kZ = PS()
                                mmg(pZ[:, :], kZ, [(kAk[:, 0, h * 128:(h + 1) * 128], vb[:].rearrange("p b v -> p (b v)"))], ['kAk0', 'Vblk%d' % (i % 2)])
                                S.op('dve', lambda e: e.tensor_tensor(out=S0f[:, bg * 4:bg * 4 + 4, h, :], in0=S0f[:, bg * 4:bg * 4 + 4, h, :],
                                                                      in1=pZ[:, :].rearrange("p (b v) -> p b v", b=4), op=ALU.add),
                                     reads=[kZ, 'S0f'], writes=['S0f'])
                                S.op('dve', lambda e: e.tensor_tensor(out=S0f[:, bg * 4:bg * 4 + 4, h, :], in0=S0f[:, bg * 4:bg * 4 + 4, h, :],
                                                                      in1=eAe[:, h, bg * 4:bg * 4 + 4].unsqueeze(2).to_broadcast([128, 4, 128]), op=ALU.mult),
                                     reads=['S0f', 'eAe'], writes=['S0f'])
                            yield
                            stV(0)
                            for i in range(16):
                                if i + 1 < 16:
                                    yield
                                    stV(i + 1)
                                yield
                                stU(i)
                            S.dma('sp', 'o_hs', o_hgrn_s.rearrange("b h d v -> d b h v"), S0f[:], reads=['S0f'])
                    if (not sample) and X.get('last'):
                        S.dma('sp', 'o_hp', o_hgrn_p.rearrange("h d v -> d h v"), X['Sf'][:], reads=['Sf'])
                    chk('hgrn')
                    osq, R, t1 = B['osq'], B['R'], B['t1']
                    for h in range(4):
                        yield
                        S.op('act', lambda e, h=h: e.activation(out=osq[:], in_=osb[:, h, :], func=AF.Square), reads=['osb'], writes=['osq'])
                        yield
                        p, k = PS()
                        yield
                        mmg(p[:, 0:T], k, [(onesb[:], osq[:])], ['onesb', 'osq'])
                        yield
                        S.op('act', lambda e, p=p: e.activation(out=R[:], in_=p[:, 0:T], func=AF.Ln, scale=1.0 / 128, bias=epsc), reads=[k, 'small'], writes=['R'])
                        yield
                        S.op('act', lambda e: e.activation(out=R[:], in_=R[:], func=AF.Exp, scale=-0.5), reads=['R'], writes=['R'])
                        yield
                        S.op('dve', lambda e, h=h: e.tensor_tensor(out=t1[:], in0=osb[:, h, :], in1=R[:], op=ALU.mult), reads=['osb', 'R'], writes=['t1'])
                        yield
                        S.op('dve', lambda e, h=h: e.scalar_tensor_tensor(out=mixT[:, 2 + h, 0:T], in0=t1[:], scalar=onorm(h), in1=GS[:, h, :], op0=ALU.mult, op1=ALU.mult),
                             reads=['t1', 'GS%d' % h, 'prm'], writes=['mixT%d' % (2 + h)])

                    yield
                def g_attn():
                    qxT = B['qxT']
                    Ra = G['hbs'][1][:].bitcast(F32); Rb = G['hbs'][0][:].bitcast(F32)
                    while not done['inproj']:
                        yield
                    if sample:
                        for _ in range(24):
                            yield
                        kstep(8)
                    if not sample:
                        PTa = X['PTa']
                        for pr in range(2):
                            for hh in range(2):
                                h = pr * 2 + hh
                                rows = slice(hh * 64, hh * 64 + 64)
                                for mc in range(2):
                                    yield
                                    p, k = PS()
                                    yield
                                    mmg(p[:, 0:T], k, [(KT[rows, pr, mc * 128:(mc + 1) * 128], qxT[rows, pr, :])], ['KT', 'qxT'])
                                    yield
                                    S.op('act', lambda e, p=p, hh=hh, mc=mc: e.activation(out=PTa[:, hh, mc, :], in_=p[:, 0:T], func=AF.Exp, scale=0.125), reads=[k], writes=['PTa'])
                            yield
                            pO, kO = PS()
                            yield
                            pD, kD = PS()
                            for hh in range(2):
                                h = pr * 2 + hh
                                rows = slice(hh * 64, hh * 64 + 64)
                                for mc in range(2):
                                    yield
                                    S.op('pe', lambda e, hh=hh, mc=mc, h=h, rows=rows, pO=pO: e.matmul(pO[rows, 0:T], lhsT=Vb[:, mc, h * 64:(h + 1) * 64], rhs=PTa[:, hh, mc, :],
                                                                                                        start=(mc == 0), stop=(mc == 1)),
                                         reads=['Vb', 'PTa'], writes=[kO], inc=(mc == 1))
                                for mc in range(2):
                                    yield
                                    S.op('pe', lambda e, hh=hh, mc=mc, rows=rows, pD=pD: e.matmul(pD[rows, 0:T], lhsT=onesb[:, 0:64], rhs=PTa[:, hh, mc, :],
                                                                                                  start=(mc == 0), stop=(mc == 1)),
                                         reads=['onesb', 'PTa'], writes=[kD], inc=(mc == 1))
                            yield
                            S.op('act', lambda e, pD=pD: e.activation(out=Ra[:, 0:T], in_=pD[:, 0:T], func=AF.Ln), reads=[kD], writes=['hb1'])
                            S.op('act', lambda e: e.activation(out=Ra[:, 0:T], in_=Ra[:, 0:T], func=AF.Exp, scale=-1.0), reads=['hb1'], writes=['hb1'])
                            yield
                            S.op('dve', lambda e, pO=pO, pr=pr: e.tensor_tensor(out=mixT[:, 6 + pr, 0:T], in0=pO[:, 0:T], in1=Ra[:, 0:T], op=ALU.mult), reads=[kO, 'hb1'], writes=['mixT%d' % (6 + pr)])
                    else:
                        KTs, Vs, PTs = X['KTs'], X['Vs'], X['PTs']
                        for g8 in range(2):
                            yield
                            pp = [PS(), PS()]
                            for bi in range(8):
                                bq = g8 * 8 + bi
                                for mc in range(2):
                                    for h in range(4):
                                        par = h % 2
                                        rows = slice(par * 64, par * 64 + 64)
                                        col = bi * 32 + (mc * 2 + h // 2) * 8
                                        last = (bi == 7 and mc == 1 and h >= 2)
                                        p, k = pp[par]
                                        yield
                                        S.op('pe', lambda e, bq=bq, mc=mc, h=h, rows=rows, col=col, p=p: e.matmul(p[:, col:col + 8], lhsT=KTs[rows, bq, h // 2, mc * 128:(mc + 1) * 128],
                                                                                                                 rhs=qxT[rows, h // 2, bq * 8:bq * 8 + 8], start=True, stop=True),
                                             reads=['KTs', 'qxT'], writes=[k], inc=last)
                            for par in range(2):
                                p, k = pp[par]
                                yield
                                S.op('act', lambda e, p=p, g8=g8, par=par: e.activation(out=PTs[:, par, g8 * 256:(g8 + 1) * 256], in_=p[:, 0:256], func=AF.Exp, scale=0.125), reads=[k], writes=['PTs'])
                        yield
                        pO, kO = PS(2)
                        yield
                        pD, kD = PS(2)
                        for bq in range(16):
                            for h in range(4):
                                rows = slice((h % 2) * 64, (h % 2) * 64 + 64)
                                oc = (h // 2) * 128 + bq * 8
                                for mc in range(2):
                                    col = bq * 32 + (mc * 2 + h // 2) * 8
                                    yield
                                    S.op('pe', lambda e, bq=bq, h=h, mc=mc, rows=rows, oc=oc, col=col: e.matmul(pO[rows, oc:oc + 8], lhsT=Vs[:, bq, mc, h * 64:(h + 1) * 64], rhs=PTs[:, h % 2, col:col + 8],
                                                                                                               start=(mc == 0), stop=(mc == 1)),
                                         reads=['Vs', 'PTs'], writes=[kO], inc=(bq == 15 and h == 3 and mc == 1))
                                for mc in range(2):
                                    col = bq * 32 + (mc * 2 + h // 2) * 8
                                    yield
                                    S.op('pe', lambda e, bq=bq, h=h, mc=mc, rows=rows, oc=oc, col=col: e.matmul(pD[rows, oc:oc + 8], lhsT=onesb[:, 0:64], rhs=PTs[:, h % 2, col:col + 8],
                                                                                                               start=(mc == 0), stop=(mc == 1)),
                                         reads=['onesb', 'PTs'], writes=[kD], inc=(bq == 15 and h == 3 and mc == 1))
                        yield
                        S.op('act', lambda e: e.activation(out=Ra[:, 0:128], in_=pD[:, 0:128], func=AF.Ln), reads=[kD], writes=['hb1'])
                        S.op('act', lambda e: e.activation(out=Ra[:, 0:128], in_=Ra[:, 0:128], func=AF.Exp, scale=-1.0), reads=['hb1'], writes=['hb1'])
                        yield
                        S.op('dve', lambda e: e.tensor_tensor(out=mixT[:, 6, 0:128], in0=pO[:, 0:128], in1=Ra[:, 0:128], op=ALU.mult), reads=[kO, 'hb1'], writes=['mixT6'])
                        yield
                        S.op('act', lambda e: e.activation(out=Rb[:, 0:128], in_=pD[:, 128:256], func=AF.Ln), reads=[kD], writes=['hb0'])
                        S.op('act', lambda e: e.activation(out=Rb[:, 0:128], in_=Rb[:, 0:128], func=AF.Exp, scale=-1.0), reads=['hb0'], writes=['hb0'])
                        yield
                        S.op('dve', lambda e: e.tensor_tensor(out=mixT[:, 7, 0:128], in0=pO[:, 128:256], in1=Rb[:, 0:128], op=ALU.mult), reads=[kO, 'hb0'], writes=['mixT7'])

                    yield
                gens = [(g_inproj(3, 5), 2), (g_hgrn(), 3), (g_pool(), 1), (g_attn(), 1)]
                while gens:
                    for ge in list(gens):
                        for _ in range(ge[1]):
                            try:
                                next(ge[0])
                            except StopIteration:
                                gens.remove(ge)
                                break
                chk('pool')
                chk('hgrn_o')
                chk('attn')
                wo = [wneed(), wneed(prefetch=False)]
                for sub in range(nsub):
                    for c in range(2):
                        wsl, wk = wo[c]
                        wv = w8(wsl)
                        p, k = PS()
                        mmg(p[:, :], k, [(mixT[:, kk, sub * 128:(sub + 1) * 128], wv[:, kk, :]) for kk in (0, 1, 6, 7, 2, 3, 4, 5)], [wk], per=[['mixT%d' % kk] for kk in (0, 1, 6, 7, 2, 3, 4, 5)])
                        S.op('dve', lambda e, p=p, sub=sub, c=c: e.tensor_tensor(out=xres[:, sub, c * 512:(c + 1) * 512], in0=p[:, :], in1=xres[:, sub, c * 512:(c + 1) * 512], op=ALU.add),
                             reads=[k, 'xres%d' % sub], writes=['xres%d' % sub])
                    if sub >= 1:
                        rms_multi([(xres[:, sub - 1, :], ['xres%d' % (sub - 1)], (sub - 1) * 128)], 110, hT, ['hT'], base=sub - 1)
                rms_multi([(xres[:, nsub - 1, :], ['xres%d' % (nsub - 1)], (nsub - 1) * 128)], 110, hT, ['hT'], base=nsub - 1)
                chk('outproj')
                chk('norm2')
                pend2 = []
                nxt = X.get('nxt')
                if nxt is not None:
                    GSf = B['GS'][:].rearrange("p h t -> p (h t)"); osf = B['osb'][:].rearrange("p h t -> p (h t)")
                    xn = [GSf[:, 0:1024], GSf[:, 1024:2048], osf[:, 0:1024], osf[:, 1024:2048]]
                    xnk = [['GS0', 'GS1'], ['GS2', 'GS3'], ['osb'], ['osb']]
                    for s_ in range(4):
                        S.dma('sp', 'xn%d' % s_, xn[s_], x_tok[nxt + s_ * 128:nxt + (s_ + 1) * 128, :], writes=xnk[s_])
                for r in range(11):
                    wsl, wk = wneed()
                    wv = w8(wsl)
                    for jj in range(2):
                        j = 2 * r + jj
                        pa, ka = PS(2)
                        mmg(pa[:, 0:T], ka, [(wv[:, kk, jj * 128:(jj + 1) * 128], hT[:, kk, 0:T]) for kk in range(8)], ['hT', wk])
                        pb_, kb = PS()
                        mmg(pb_[:, 0:T], kb, [(wv[:, kk, 256 + jj * 128:256 + (jj + 1) * 128], hT[:, kk, 0:T]) for kk in range(8)], ['hT', wk])
                        cbuf = B['cbuf'][j % 2]; gbuf = B['gbuf'][j % 2]
                        ck_, gk_ = 'cbuf%d' % (j % 2), 'gbuf%d' % (j % 2)
                        if not sample:
                            asb = X['asb'][j % 2]; ak_ = 'asb%d' % (j % 2); carry = X['carry']
                            S.op('dve', lambda e, asb=asb, j=j: e.tensor_copy(out=asb[:, 0:2], in_=carry[:, j, :]), reads=['carry'], writes=[ak_])
                            S.op('act', lambda e, asb=asb, pa=pa: e.activation(out=asb[:, 2:2 + T], in_=pa[:, 0:T], func=AF.Copy), reads=[ka], writes=[ak_])
                            S.op('act', lambda e, pa=pa, j=j, cbuf=cbuf: e.activation(out=cbuf, in_=pa[:, 0:T], func=AF.Identity, scale=cw(2, j), bias=cb(j)), reads=[ka, 'prm'], writes=[ck_])
                            S.op('dve', lambda e, asb=asb, j=j: e.tensor_copy(out=carry[:, j, :], in_=asb[:, T:T + 2]), reads=[ak_], writes=['carry'])
                            S.op('dve', lambda e, asb=asb, j=j, cbuf=cbuf: e.scalar_tensor_tensor(out=cbuf, in0=asb[:, 1:1 + T], scalar=cw(1, j), in1=cbuf, op0=ALU.mult, op1=ALU.add),
                                 reads=[ak_, ck_, 'prm'], writes=[ck_])
                            S.op('dve', lambda e, asb=asb, j=j, cbuf=cbuf: e.scalar_tensor_tensor(out=cbuf, in0=asb[:, 0:T], scalar=cw(0, j), in1=cbuf, op0=ALU.mult, op1=ALU.add),
                                 reads=[ak_, ck_, 'prm'], writes=[ck_])
                        else:
                            a3 = X['a3'][j % 2]; ak_ = 'a3%d' % (j % 2); ahist, anew = X['ahist'], X['anew']
                            c3 = cbuf.rearrange("p (b t) -> p b t", t=8)
                            S.op('dve', lambda e, a3=a3, j=j: e.tensor_copy(out=a3[:, :, 0:2], in_=ahist[:, j, :, :]), reads=['ahist'], writes=[ak_])
                            S.op('act', lambda e, a3=a3, pa=pa: e.activation(out=a3[:, :, 2:10], in_=pa[:, 0:128].rearrange("p (b t) -> p b t", t=8), func=AF.Copy), reads=[ka], writes=[ak_])
                            S.op('act', lambda e, pa=pa, j=j, cbuf=cbuf: e.activation(out=cbuf, in_=pa[:, 0:T], func=AF.Identity, scale=cw(2, j), bias=cb(j)), reads=[ka, 'prm'], writes=[ck_])
                            S.op('dve', lambda e, a3=a3, j=j: e.tensor_copy(out=anew[:, j, :, :], in_=a3[:, :, 8:10]), reads=[ak_], writes=['anew'])
                            S.op('dve', lambda e, a3=a3, j=j, c3=c3: e.scalar_tensor_tensor(out=c3, in0=a3[:, :, 1:9], scalar=cw(1, j), in1=c3, op0=ALU.mult, op1=ALU.add),
                                 reads=[ak_, ck_, 'prm'], writes=[ck_])
                            S.op('dve', lambda e, a3=a3, j=j, c3=c3: e.scalar_tensor_tensor(out=c3, in0=a3[:, :, 0:8], scalar=cw(0, j), in1=c3, op0=ALU.mult, op1=ALU.add),
                                 reads=[ak_, ck_, 'prm'], writes=[ck_])
                        def stage2(cbuf=cbuf, gbuf=gbuf, pb_=pb_, j=j, ck_=ck_, gk_=gk_, kb=kb):
                            S.op('act', lambda e: e.activation(out=gbuf, in_=cbuf, func=AF.Gelu_apprx_tanh), reads=[ck_], writes=[gk_])
                            S.op('dve', lambda e: e.tensor_tensor(out=mT[:, j, 0:T], in0=pb_[:, 0:T], in1=gbuf, op=ALU.mult), reads=[kb, gk_], writes=['mT%d' % j] + (['kvt'] if (first and j < 4) else []))
                        if pend2:
                            pend2.pop()()
                        pend2.append(stage2)
                if pend2:
                    pend2.pop()()
                chk('up')
                if (not sample) and X.get('last'):
                    carry, rowb = X['carry'], X['rowb']
                    for g4 in range(6):
                        p, k = PS()
                        n4 = 4 if g4 < 5 else 2
                        for q in range(n4):
                            j = g4 * 4 + q
                            S.op('pe', lambda e, p=p, q=q, j=j: e.transpose(out=p[0:2, q * 128:(q + 1) * 128], in_=carry[:, j, :], identity=ident), reads=['carry', 'cst'], writes=[k], inc=(q == n4 - 1))
                        S.op('dve', lambda e, p=p, g4=g4, n4=n4: e.tensor_copy(out=rowb[0:2, g4 % 2, 0:n4 * 128], in_=p[0:2, 0:n4 * 128]), reads=[k], writes=['rowb%d' % (g4 % 2)])
                        S.dma('sp', 'o_cp%d' % (g4 % 2), o_conv_p[:, g4 * 512:g4 * 512 + n4 * 128], rowb[0:2, g4 % 2, 0:n4 * 128], reads=['rowb%d' % (g4 % 2)])
                if sample:
                    anew, rowb = X['anew'], X['rowb']
                    for g4 in range(6):
                        p, k = PS()
                        n4 = 4 if g4 < 5 else 2
                        for q in range(n4):
                            j = g4 * 4 + q
                            S.op('pe', lambda e, p=p, q=q, j=j: e.transpose(out=p[0:32, q * 128:(q + 1) * 128], in_=anew[:, j, :, :].rearrange("p b r -> p (b r)"), identity=ident),
                                 reads=['anew', 'cst'], writes=[k], inc=(q == n4 - 1))
                        S.op('dve', lambda e, p=p, g4=g4, n4=n4: e.tensor_copy(out=rowb[0:32, g4 % 2, 0:n4 * 128], in_=p[0:32, 0:n4 * 128]), reads=[k], writes=['rowb%d' % (g4 % 2)])
                        S.dma('sp', 'o_cs%d' % (g4 % 2), o_conv_s[:, g4 * 512:g4 * 512 + n4 * 128], rowb[0:32, g4 % 2, 0:n4 * 128], reads=['rowb%d' % (g4 % 2)])
                chk('convout')
                if nxt is not None:
                    rms_multi([(xn[s_], xnk[s_], s_ * 128) for s_ in range(4)], 102, hT, ['hT'], phase='stats')
                for q in range(4):
                    wsl, wk = wneed()
                    wv = w22(wsl)
                    for sub in range(nsub):
                        p, k = PS()
                        mmg(p[:, 0:256], k, [(mT[:, kk, sub * 128:(sub + 1) * 128], wv[:, kk, :]) for kk in range(22)], ['mT%d' % kk for kk in range(22)] + [wk])
                        S.op('dve', lambda e, p=p, sub=sub, q=q: e.tensor_tensor(out=xres[:, sub, q * 256:(q + 1) * 256], in0=p[:, 0:256], in1=xres[:, sub, q * 256:(q + 1) * 256], op=ALU.add),
                             reads=[k, 'xres%d' % sub], writes=['xres%d' % sub])
                    if q == 1 and nxt is not None:
                        rms_multi([(xn[s_], xnk[s_], s_ * 128) for s_ in range(4)], 102, hT, ['hT'], phase='apply')
                        X['prenormed'] = True
                hbs = G['hbs']
                for sub in range(nsub):
                    S.op('act', lambda e, sub=sub: e.activation(out=hbs[sub % 2][:], in_=xres[:, sub, :], func=AF.Square, accum_out=stat[:, 16 + sub:17 + sub]),
                         reads=['xres%d' % sub], writes=['hb%d' % (sub % 2), 'stat2'])
                S.op('act', lambda e: e.activation(out=stat[:, 20:20 + nsub], in_=stat[:, 16:16 + nsub], func=AF.Ln, scale=1.0 / 1024, bias=epsc), reads=['stat2', 'small'], writes=['stat2'])
                S.op('act', lambda e: e.activation(out=stat[:, 24:24 + nsub], in_=stat[:, 20:20 + nsub], func=AF.Exp, scale=-0.5), reads=['stat2'], writes=['stat2'])
                for sub in range(nsub):
                    xk = 'xres%d' % sub
                    S.op('dve', lambda e, sub=sub: e.scalar_tensor_tensor(out=xres[:, sub, :], in0=xres[:, sub, :], scalar=stat[:, 24 + sub:25 + sub], in1=gf[:], op0=ALU.mult, op1=ALU.mult),
                         reads=[xk, 'stat2', 'gf'], writes=[xk])
                    S.dma('sp', 'yout%d' % sub, y_tok[t0 + sub * 128:t0 + (sub + 1) * 128, :], xres[:, sub, :], reads=[xk])

            chk('prologue')
            Bp = alloc_phase(pst, 512, False)
            xres, hT, mixT, mT = G['xres'], G['hT'], G['mixT'], G['mT']
            memx = Bp['osb'][:].rearrange("p h t -> p (h t)").rearrange("p (s f) -> p s f", s=2); memT = mixT
            kvt = mT[:].rearrange("p k t -> p (k t)")[:, 0:2048].bitcast(F32).rearrange("p (s f) -> p s f", s=2)
            KT = sbt(pst, "KT", [128, 2, 256], BF16); Vb = sbt(pst, "Vb", [128, 2, 256], BF16)

            def memkv():
                rms_multi([(memx[:, sub, :], ['osb'], sub * 128) for sub in range(2)], 118, memT, ['memT'])
                wkv, wk = wneed()
                chk('kv_w')
                for sub in range(2):
                    p, k = PS(2)
                    mmg(p[:, :], k, [(memT[:, kk, sub * 128:(sub + 1) * 128], w8(wkv)[:, kk, :]) for kk in range(8)], ['memT', wk])
                    chk('kv_m')
                    S.op('act', lambda e, p=p, sub=sub: e.activation(out=kvt[:, sub, :], in_=p[:, :], func=AF.Copy), reads=[k], writes=['kvt'])
                    chk('kv_n')
                    S.op('dve', lambda e, p=p, sub=sub: e.tensor_copy(out=Vb[:, sub, :], in_=p[:, 256:512]), reads=[k], writes=['Vb'])
                    chk('kv_a%d' % sub)
                for j in range(2):
                    p, k = PS()
                    mmg(p[:, 0:256], k, [(w8(wkv)[:, kk, j * 128:(j + 1) * 128], memT[:, kk, 0:256]) for kk in range(8)], ['memT', wk])
                    S.op('act', lambda e, p=p, j=j: e.activation(out=KT[:, j, :], in_=p[:, 0:256], func=AF.Copy), reads=[k], writes=['KT'])
                    chk('kv_b%d' % j)
                S.dma('sp', 'o_mk', o_mk.rearrange("(s p) f -> p s f", p=128), kvt[:, :, 0:256], reads=['kvt'])
                chk('kv_c')
                S.dma('sp', 'o_mv', o_mv.rearrange("(s p) f -> p s f", p=128), kvt[:, :, 256:512], reads=['kvt'])


            chk('memkv')
            Xp = {}
            Xp['memkv'] = memkv
            Xp['memdma'] = lambda: S.dma('sp', 'c7', memx[:, 0:2, :], mem.rearrange("(s p) f -> p s f", p=128), writes=['osb'])
            Xp['Sf'] = sbt(pst, "Sf", [128, 4, 128]); Xp['Sb'] = sbt(pst, "Sb", [128, 4, 128], BF16)
            Xp['cm512'] = sbt(pst, "cm512", [128, 512])
            Xp['PTa'] = sbt(pst, "PTa", [128, 2, 2, 512], BF16)
            thf = Bp['THall'][:].rearrange("p h t -> p (h t)")
            Xp['asb'] = [thf[:, 0:514], thf[:, 1024:1538]]
            Xp['carry'] = sbt(pst, "carry", [128, 22, 2]); Xp['rowb'] = sbt(pst, "rowb_p", [2, 2, 512]); Xp['ppo'] = sbt(pst, "ppo", [16, 256])
            S.op('dve', lambda e: e.memset(Xp['Sf'][:], 0.0), writes=['Sf'])
            S.op('dve', lambda e: e.memset(Xp['Sb'][:], 0.0), writes=['Sb'])
            S.op('dve', lambda e: e.memset(Xp['cm512'][:], 1.0), writes=['cm512'])
            S.op('dve', lambda e: e.memset(Xp['cm512'][:].rearrange("p (c t) -> p c t", t=128)[:, :, 0:1], 0.0), writes=['cm512'])
            S.op('dve', lambda e: e.memset(Xp['carry'][:], 0.0), writes=['carry'])
            for ti in range(4):
                Xp['last'] = (ti == 3)
                Xp['nxt'] = (ti + 1) * 512 if ti < 3 else None
                do_tile(Bp, ti * 512, ti == 0, False, Xp)
                chk('tile%d' % ti)
            S.barrier()
            pst.close()

            chk('prompt')
            Bs = alloc_phase(sst, 128, True)
            Xs = {}
            Xs['xp'] = sbt(sst, "xp", [128, 2, 16, 23]); Xs['sp_tok'] = sbt(sst, "sp_tok", [120, 2, 256]); Xs['xpc'] = sbt(sst, "xpc", [128, 2, 16, 15])
            Xs['spo'] = Xs['sp_tok']
            Xs['S0f'] = sbt(sst, "S0f", [128, 16, 4, 128]); Xs['S0b'] = sbt(sst, "S0b", [128, 16, 4, 128], BF16)
            Xs['Vblk2'] = [sbt(sst, "Vblk%d" % i, [128, 4, 128], BF16) for i in range(2)]
            Xs['KTs'] = sbt(sst, "KTs", [128, 16, 2, 256], BF16); Xs['Vs'] = sbt(sst, "Vs", [128, 16, 2, 256], BF16)
            Xs['PTs'] = sbt(sst, "PTs", [128, 2, 512], BF16); kst2 = [sbt(sst, "kst%d" % i, [128, 2, 2, 256], BF16) for i in range(2)]
            thfs = Bs['THall'][:].rearrange("p h t -> p (h t)")
            Xs['a3'] = [thfs[:, 0:160].rearrange("p (b t) -> p b t", t=10), thfs[:, 256:416].rearrange("p (b t) -> p b t", t=10)]
            Xs['ahist'] = sbt(sst, "ahist", [128, 22, 16, 2]); Xs['anew'] = sbt(sst, "anew", [128, 22, 16, 2]); Xs['rowb'] = sbt(sst, "rowb_s", [32, 2, 512])
            cst_tok = sbt(sst, "cst_tok", [32, 2816])
            def pre1():
                S.dma('sp', 's3', Xs['sp_tok'][:], spool.rearrange("(h q) c -> q h c", q=120), writes=['sp_tok'])
                S.dma('sp', 's4', cst_tok[:], sconv[:, :], writes=['cst_tok'])
                kload(0)

            def pre1b():
                for hh in range(2):
                    for c in range(2):
                        p, k = PS()
                        S.op('pe', lambda e, p=p, hh=hh, c=c: e.transpose(out=p[:, 0:120], in_=Xs['sp_tok'][0:120, hh, c * 128:(c + 1) * 128], identity=cst[0:120, 0:120]),
                             reads=['sp_tok', 'cst'], writes=[k])
                        S.op('dve', lambda e, p=p, hh=hh, c=c: e.tensor_copy(out=Xs['xp'][:, c, hh * 8:(hh + 1) * 8, 0:15], in_=p[:, 0:120].rearrange("p (b r) -> p b r", r=15)),
                             reads=[k], writes=['xp'])
                for j in range(22):
                    p, k = PS()
                    S.op('pe', lambda e, p=p, j=j: e.transpose(out=p[:, 0:32], in_=cst_tok[0:32, j * 128:(j + 1) * 128], identity=cst[0:32, 0:32]), reads=['cst_tok', 'cst'], writes=[k])
                    S.op('dve', lambda e, p=p, j=j: e.tensor_copy(out=Xs['ahist'][:, j, :, :], in_=p[:, 0:32].rearrange("p (b r) -> p b r", r=2)), reads=[k], writes=['ahist'])

            def kload(g4):
                S.dma('pool', 's5%d' % (g4 % 2), kst2[g4 % 2][:], ck[g4 * 2:(g4 + 1) * 2].rearrange("b (mc p) f -> p b mc f", p=128), writes=['kst%d' % (g4 % 2)])

            def pre2():
                kload(1)
                S.dma('pool', 's1', Xs['S0b'][:], shgrn.rearrange("b h d v -> d b h v"), writes=['S0b'])
                for g4 in range(8):
                    kst = kst2[g4 % 2]
                    for bi in range(2):
                        bq = g4 * 2 + bi
                        p, k = PS()
                        pb = p[:].bitcast(BF16)
                        for hc in range(2):
                            for mc in range(2):
                                S.op('pe', lambda e, pb=pb, kst=kst, bi=bi, hc=hc, mc=mc: e.transpose(out=pb[:, (hc * 2 + mc) * 128:(hc * 2 + mc + 1) * 128], in_=kst[:, bi, mc, hc * 128:(hc + 1) * 128], identity=idb[:]),
                                     reads=['kst%d' % (g4 % 2), 'idb'], writes=[k], inc=(hc == 1 and mc == 1))
                        S.op('act', lambda e, pb=pb, bq=bq: e.activation(out=Xs['KTs'][:, bq, :, :], in_=pb[:, 0:512].rearrange("p (hc m) -> p hc m", hc=2), func=AF.Copy), reads=[k], writes=['KTs'])
                    if g4 + 2 < 8:
                        kload(g4 + 2)
                    if g4 == 7:
                        S.dma('pool', 's2', Xs['Vs'][:], cv.rearrange("b (mc p) f -> p b mc f", p=128), writes=['Vs'])
                        S.dma('sp', 's0', Xs['S0f'][:], shgrn.rearrange("b h d v -> d b h v"), writes=['S0f'])
                    yield
            Xs['pre1'] = pre1; Xs['pre1b'] = pre1b; Xs['pre2'] = pre2
            chk('sprologue')
            do_tile(Bs, 2048, False, True, Xs)
            S.op('dve', lambda e: e.tensor_copy(out=Xs['xpc'][:], in_=Xs['xp'][:, :, :, 8:23]), reads=['xp'], writes=['xpc'])
            for hh in range(2):
                for c in range(2):
                    p, k = PS()
                    S.op('pe', lambda e, p=p, hh=hh, c=c: e.transpose(out=p[0:120, 0:128], in_=Xs['xpc'][:, c, hh * 8:(hh + 1) * 8, :].rearrange("p b r -> p (b r)"), identity=ident),
                         reads=['xpc', 'cst'], writes=[k])
                    S.op('dve', lambda e, p=p, hh=hh, c=c: e.tensor_copy(out=Xs['spo'][0:120, hh, c * 128:(c + 1) * 128], in_=p[0:120, 0:128]), reads=[k], writes=['spo'])
            S.dma('sp', 'o_ps', o_pool_s.rearrange("(h b) r c -> (b r) h c", h=2), Xs['spo'][:], reads=['spo'])
        except StopBuild as ex:
            print('STOPPED at', ex)
        S.final()
        sst.close()
        import os
        if os.environ.get('KDEBUG'):
            print('CNT', S.cnt, {k: v[1] for k, v in S.dsem.items()})
    return nc


_NC = None


def kernel(**inp):
    global _NC
    f = lambda a: np.ascontiguousarray(np.asarray(a, dtype=np.float32))
    if _NC is None:
        _NC = build()
    cst = make_consts()
    prm = np.concatenate([f(inp['conv_w'][0]).reshape(66, 128), f(inp['conv_b'][0]).reshape(22, 128),
                          f(inp['hgrn_lb_logits']).reshape(8, 128), f(inp['pool_scale'][0]).reshape(2, 128),
                          f(inp['hgrn_onorm_g'][0]).reshape(4, 128), f(inp['ln1_g'][0]).reshape(8, 128),
                          f(inp['ln2_g'][0]).reshape(8, 128), f(inp['mem_norm_g'][0]).reshape(8, 128)], axis=0)
    shared = dict(lnf=f(inp['lnf_g']),
                  w_in=f(inp['w_in'][0]), w_kv=f(inp['w_mem_kv'][0]), w_out=f(inp['w_out'][0]), w_up=f(inp['w_up'][0]),
                  w_dn=f(inp['w_down'][0]), pool_w=f(inp['pool_w'][0]), prm_in=f(prm), cst=cst)
    in_maps = []
    for c in range(8):
        sl = slice(16 * c, 16 * c + 16)
        m = dict(shared)
        m['x_tok'] = f(np.concatenate([inp['x_prompt'][c], np.asarray(inp['x_sample'][sl]).reshape(128, 1024)], axis=0))
        m['mem'] = f(inp['mem_prompt'][c])
        m['spool'] = f(np.asarray(inp['state_pool'][0, sl]).reshape(240, 256))
        m['shgrn'] = f(inp['state_hgrn'][0, sl])
        m['sconv'] = f(np.asarray(inp['state_conv'][0, sl]).reshape(32, 2816))
        m['ck'] = f(np.asarray(inp['cache_mem_k'][0, sl]).reshape(16, 256, 256))
        m['cv'] = f(np.asarray(inp['cache_mem_v'][0, sl]).reshape(16, 256, 256))
        in_maps.append(m)
    res = run_bass_kernel_spmd(_NC, in_maps, core_ids=list(range(8)))
    R = res.results
    g = lambda k: np.stack([np.asarray(R[c][k], dtype=np.float32) for c in range(8)])
    y = g('y_tok')
    y_prompt = np.ascontiguousarray(y[:, :2048, :])
    y_sample = np.ascontiguousarray(y[:, 2048:, :].reshape(128, 8, 1024))
    return (y_prompt, y_sample,
            g('o_pool_p')[None], g('o_hgrn_p')[None], g('o_conv_p')[None],
            g('o_mk').reshape(1, 8, 256, 4, 64), g('o_mv').reshape(1, 8, 256, 4, 64),
            g('o_pool_s').reshape(1, 128, 15, 256), g('o_hgrn_s').reshape(1, 128, 4, 128, 128),
            g('o_conv_s').reshape(1, 128, 2, 2816))
```

```python
import numpy as np
from contextlib import ExitStack
import concourse.bass as bass
import concourse.mybir as mybir
from concourse.bass_utils import run_bass_kernel_spmd

F32, BF16 = mybir.dt.float32, mybir.dt.bfloat16
AF = mybir.ActivationFunctionType
ALU = mybir.AluOpType
EPS = 1e-6
NSLOT = 4
SAME_ENGINE_SYNC = True


import os


class StopBuild(Exception):
    pass


_hits = {}


def chk(name):
    if os.environ.get('KSTOP') == name:
        _hits[name] = _hits.get(name, 0) + 1
        if _hits[name] == int(os.environ.get('KHIT', '1')):
            raise StopBuild(name)


class Sched:
    def __init__(s, nc, es):
        s.nc, s.es = nc, es
        s.E = {'pe': nc.tensor, 'act': nc.scalar, 'dve': nc.vector, 'pool': nc.gpsimd, 'sp': nc.sync}
        s.sem = {k: es.enter_context(nc.semaphore('sem_' + k)) for k in s.E}
        s.cnt = {k: 0 for k in s.E}
        s.seen = {k: {} for k in s.E}
        s.lastw, s.reads, s.dsem = {}, {}, {}
        s.psn = 0
        s.ps_open = {}

    def _semh(s, key):
        return s.sem[key] if key in s.sem else s.dsem[key][0]

    def _wait(s, eng, key, val):
        if key == eng and (eng in ('pe', 'sp') or not SAME_ENGINE_SYNC):
            return
        if s.seen[eng].get(key, 0) >= val:
            return
        s.seen[eng][key] = val
        s.E[eng].wait_ge(s._semh(key), val)

    def deps(s, eng, reads, writes):
        need = {}
        for b in reads:
            if b in s.lastw:
                k, v = s.lastw[b]
                need[k] = max(need.get(k, 0), v)
        for b in writes:
            if b in s.lastw:
                k, v = s.lastw[b]
                need[k] = max(need.get(k, 0), v)
            for (k, v) in s.reads.get(b, ()):
                need[k] = max(need.get(k, 0), v)
        for k, v in need.items():
            s._wait(eng, k, v)

    def _record(s, tok, reads, writes):
        for b in reads:
            s.reads.setdefault(b, []).append(tok)
        for b in writes:
            s.lastw[b] = tok
            s.reads[b] = []

    def op(s, eng, fn, reads=(), writes=(), inc=True):
        psr = [b for b in reads if isinstance(b, tuple) and b[0] == 'ps']
        if psr:
            reads = [b for b in reads if b not in psr]
            writes = list(writes) + psr
            for b in psr:
                s.ps_open[b[1]] -= 1
                if s.ps_open[b[1]] <= 0:
                    del s.ps_open[b[1]]
        s.deps(eng, reads, writes)
        ins = fn(s.E[eng])
        if inc:
            s.cnt[eng] += 1
            ins.then_inc(s.sem[eng], 1)
            tok = (eng, s.cnt[eng])
        else:
            tok = (eng, s.cnt[eng] + 1)
        s._record(tok, reads, writes)
        return ins

    def dma(s, eng, chan, out, in_, reads=(), writes=(), **kw):
        s.deps(eng, reads, writes)
        if chan not in s.dsem:
            s.dsem[chan] = [s.es.enter_context(s.nc.semaphore('d_' + chan)), 0]
        ins = s.E[eng].dma_start(out=out, in_=in_, **kw)
        s.dsem[chan][1] += 16
        ins.then_inc(s.dsem[chan][0], 16)
        s._record((chan, s.dsem[chan][1]), reads, writes)

    def barrier(s):
        for eng in s.E:
            for k in s.sem:
                if s.cnt[k] > 0:
                    s._wait(eng, k, s.cnt[k])
            for k in s.dsem:
                s._wait(eng, k, s.dsem[k][1])

    def final(s):
        for k in s.sem:
            if s.cnt[k] > 0:
                s._wait('sp', k, s.cnt[k])
        for k in s.dsem:
            s._wait('sp', k, s.dsem[k][1])


def make_consts():
    c = np.zeros((128, 576), np.float32)
    c[:, 0:128] = np.eye(128, dtype=np.float32)
    s = np.arange(128)[:, None]
    t = np.arange(128)[None, :]
    c[:, 128:256] = (t >= s)
    c[:, 256:384] = (t >= s) & ((t // 8) == (s // 8))
    c[:, 384:400] = (np.arange(128)[:, None] // 8) == np.arange(16)[None, :]
    c[:, 400:528] = (np.arange(128)[None, :] % 8 != 0)
    for ch in range(2):
        for p in range(128):
            w = [2, 4, 8, 16][2 * ch + p // 64]
            c[p, 528 + ch * 16: 528 + ch * 16 + 16] = 1.0 / np.minimum(w, np.arange(16) + 1.0)
            c[p, 560 + ch] = 1.0 / w
    return c


def build():
    nc = bass.Bass("TRN2", target_bir_lowering=False)
    D = lambda n, sh, k="ExternalInput": nc.dram_tensor(n, sh, F32, kind=k).ap()
    x_tok = D("x_tok", [2176, 1024]); mem = D("mem", [256, 1024])
    spool = D("spool", [240, 256]); shgrn = D("shgrn", [16, 4, 128, 128]); sconv = D("sconv", [32, 2816])
    ck = D("ck", [16, 256, 256]); cv = D("cv", [16, 256, 256])
    lnf = D("lnf", [1024])
    w_in = D("w_in", [1024, 2560]); w_kv = D("w_kv", [1024, 512]); w_out = D("w_out", [1024, 1024])
    w_up = D("w_up", [1024, 5632]); w_dn = D("w_dn", [2816, 1024])
    pool_w = D("pool_w", [4, 64, 64]); prm_in = D("prm_in", [126, 128]); cst_in = D("cst", [128, 576])
    O = lambda n, sh: D(n, sh, "ExternalOutput")
    y_tok = O("y_tok", [2176, 1024]); o_pool_p = O("o_pool_p", [15, 256]); o_hgrn_p = O("o_hgrn_p", [4, 128, 128])
    o_conv_p = O("o_conv_p", [2, 2816]); o_mk = O("o_mk", [256, 256]); o_mv = O("o_mv", [256, 256])
    o_pool_s = O("o_pool_s", [16, 15, 256]); o_hgrn_s = O("o_hgrn_s", [16, 4, 128, 128]); o_conv_s = O("o_conv_s", [32, 2816])

    with ExitStack() as es:
        S = Sched(nc, es)
        uid = [0]
        def sbt(st, n, sh, dt=F32):
            uid[0] += 1
            return st.enter_context(nc.sbuf_tensor("sb%d_%s" % (uid[0], n), sh, dt))
        psb = [es.enter_context(nc.psum_tensor("ps%d" % i, [128, 512], F32)) for i in range(8)]

        def PS(n=1):
            for _ in range(8):
                i = S.psn % 8
                S.psn += 1
                if i not in S.ps_open:
                    break
            else:
                raise RuntimeError('no free PSUM bank')
            S.ps_open[i] = n
            return psb[i], ('ps', i)

        cst = sbt(es, "cst", [128, 576]); prm = sbt(es, "prm", [128, 128]); prm_st = sbt(es, "prm_st", [126, 128])
        idb = sbt(es, "idb", [128, 128], BF16); onesb = sbt(es, "onesb", [128, 128], BF16)
        gf = sbt(es, "gf", [128, 1024])
        wbd = sbt(es, "wbd", [128, 2, 128], BF16)
        lbc = sbt(es, "lbc", [128, 16])
        small = sbt(es, "small", [128, 8])
        ring = [sbt(es, "ring%d" % i, [128, 5632], BF16) for i in range(NSLOT)]
        G = {}
        stat = sbt(es, "stat", [128, 32])
        ident = cst[:, 0:128]; cmask = cst[:, 128:256]; smask = cst[:, 256:384]; seqm = cst[:, 384:400]
        rmask = cst[:, 400:528]; invw = cst[:, 560:562]
        invcnt = cst[:, 528:560]
        epsc = small[:, 0:1]; mhalf = small[:, 1:2]; onec = small[:, 2:3]
        cw = lambda r, j: prm[:, r * 22 + j: r * 22 + j + 1]
        cb = lambda j: prm[:, 66 + j: 67 + j]
        pscale = lambda c: prm[:, 96 + c: 97 + c]
        onorm = lambda h: prm[:, 98 + h: 99 + h]

        wseq = []

        def ld_in(b):
            def f(slot, key, chan):
                S.dma('pool', chan, slot[:, 0:4096].rearrange("p (k n) -> p k n", k=8),
                      w_in[:, b * 512:(b + 1) * 512].rearrange("(k p) n -> p k n", p=128), writes=[key])
            return f

        def ld_kv():
            def f(slot, key, chan):
                S.dma('pool', chan, slot[:, 0:4096].rearrange("p (k n) -> p k n", k=8),
                      w_kv.rearrange("(k p) n -> p k n", p=128), writes=[key])
            return f

        def ld_out(c):
            def f(slot, key, chan):
                S.dma('pool', chan, slot[:, 0:4096].rearrange("p (k n) -> p k n", k=8),
                      w_out[:, c * 512:(c + 1) * 512].rearrange("(k p) n -> p k n", p=128), writes=[key])
            return f

        def ld_up(r):
            def f(slot, key, chan):
                v = slot[:, 0:4096].rearrange("p (k n) -> p k n", k=8)
                S.dma('pool', chan, v[:, :, 0:256],
                      w_up[:, r * 256:(r + 1) * 256].rearrange("(k p) n -> p k n", p=128), writes=[key])
                S.dma('pool', chan, v[:, :, 256:512],
                      w_up[:, 2816 + r * 256:2816 + (r + 1) * 256].rearrange("(k p) n -> p k n", p=128), writes=[key])
            return f

        def ld_dn(q):
            def f(slot, key, chan):
                S.dma('pool', chan, slot[:, 0:5632].rearrange("p (k n) -> p k n", k=22),
                      w_dn[:, q * 256:(q + 1) * 256].rearrange("(k p) n -> p k n", p=128), writes=[key])
            return f

        wscr = nc.dram_tensor("wscr", [22, 128, 5632], BF16, kind="Internal").ap()
        blocks = [('in', b) for b in range(5)] + [('out', c) for c in range(2)] + [('up', r) for r in range(11)] + [('dn', q) for q in range(4)]
        mk = {'in': ld_in, 'out': ld_out, 'up': ld_up, 'dn': ld_dn}
        for ti in range(5):
            for bi, (kind, idx) in enumerate(blocks):
                n = 5632 if kind == 'dn' else 4096
                if ti == 0:
                    wseq.append((mk[kind](idx), bi, n))
                    if bi == 2:
                        wseq.append((ld_kv(), None, 0))
                else:
                    def f(slot, key, chan, bi=bi, n=n):
                        S.dma('pool', chan, slot[:, 0:n], wscr[bi, :, 0:n], reads=[('scr', bi)], writes=[key])
                    wseq.append((f, None, n))
        wstate = {'issued': 0, 'next': 0}

        def wneed(prefetch=True):
            i = wstate['next']
            wstate['next'] += 1
            upto = min(i + NSLOT - 1, len(wseq) - 1) if prefetch else i
            while wstate['issued'] <= upto:
                j = wstate['issued']
                wseq[j][0](ring[j % NSLOT], ('w', j % NSLOT), 'w%d' % (j % NSLOT))
                wstate['issued'] += 1
            sl = ring[i % NSLOT]
            bi, n = wseq[i][1], wseq[i][2]
            if bi is not None:
                S.dma('sp', 'wb%d' % (i % NSLOT), wscr[bi, :, 0:n], sl[:, 0:n], reads=[('w', i % NSLOT)], writes=[('scr', bi)])
            return sl, ('w', i % NSLOT)

        def w8(sl):
            return sl[:, 0:4096].rearrange("p (k n) -> p k n", k=8)

        def w22(sl):
            return sl[:, 0:5632].rearrange("p (k n) -> p k n", k=22)

        def mmg(out_ap, pskey, pairs, reads, per=None):
            n = len(pairs)
            for i, (l, r) in enumerate(pairs):
                S.op('pe', lambda e, l=l, r=r, i=i: e.matmul(out_ap, lhsT=l, rhs=r, start=(i == 0), stop=(i == n - 1)),
                     reads=list(reads) + (list(per[i]) if per else []), writes=[pskey], inc=(i == n - 1))

        pst = ExitStack(); sst = ExitStack()
        es.enter_context(pst); es.enter_context(sst)
        try:
            S.dma('sp', 'c0', cst[:], cst_in[:, :], writes=['cst'])
            S.dma('sp', 'c1', prm_st[:], prm_in[:, :], writes=['prm_st'])
            S.dma('sp', 'c4', gf[:], lnf.partition_broadcast(128), writes=['gf'])
            S.op('dve', lambda e: e.memset(small[:, 0:1], EPS), writes=['small'])
            S.op('dve', lambda e: e.memset(small[:, 1:2], -0.5), writes=['small'])
            S.op('dve', lambda e: e.memset(small[:, 2:3], 1.0), writes=['small'])
            S.op('dve', lambda e: e.memset(onesb[:], 1.0), writes=['onesb'])
            S.op('dve', lambda e: e.tensor_copy(out=idb[:], in_=ident), reads=['cst'], writes=['idb'])
            S.op('pool', lambda e: e.memset(wbd[:], 0.0), writes=['wbd'])
            for gi in range(4):
                c, o = gi // 2, (gi % 2) * 64
                S.dma('pool', 'c5', wbd[o:o + 64, c, o:o + 64], pool_w[gi, :, :], writes=['wbd'])
            p_, pk = PS()
            S.op('pe', lambda e: e.transpose(out=p_[:, 0:126], in_=prm_st[:, :], identity=cst[0:126, 0:126]),
                 reads=['prm_st', 'cst'], writes=[pk])
            S.op('dve', lambda e: e.tensor_copy(out=prm[:, 0:126], in_=p_[:, 0:126]), reads=[pk], writes=['prm'])
            S.op('dve', lambda e: e.tensor_sub(out=lbc[:, 8:12], in0=prm[:, 88:92], in1=prm[:, 92:96]), reads=['prm'], writes=['lbc'])
            S.op('act', lambda e: e.activation(out=lbc[:, 8:12], in_=lbc[:, 8:12], func=AF.Tanh, scale=0.5), reads=['lbc'], writes=['lbc'])
            S.op('dve', lambda e: e.tensor_scalar(out=lbc[:, 0:4], in0=lbc[:, 8:12], scalar1=0.25, scalar2=0.75, op0=ALU.mult, op1=ALU.add), reads=['lbc'], writes=['lbc'])
            S.op('dve', lambda e: e.tensor_scalar(out=lbc[:, 4:8], in0=lbc[:, 8:12], scalar1=-0.25, scalar2=0.25, op0=ALU.mult, op1=ALU.add), reads=['lbc'], writes=['lbc'])
            S.op('dve', lambda e: e.tensor_scalar(out=lbc[:, 12:16], in0=lbc[:, 8:12], scalar1=0.25, scalar2=-0.25, op0=ALU.mult, op1=ALU.add), reads=['lbc'], writes=['lbc'])

            def rms_to_T(src_ap, gcol, dstT, col0, rd, wr_extra=()):
                hb = G['hb']
                S.op('act', lambda e: e.activation(out=hb[:], in_=src_ap, func=AF.Square, accum_out=stat[:, 0:1]),
                     reads=rd, writes=['hb', 'stat'])
                S.op('dve', lambda e: e.tensor_scalar(out=stat[:, 1:2], in0=stat[:, 0:1], scalar1=1.0 / 1024, scalar2=EPS, op0=ALU.mult, op1=ALU.add),
                     reads=['stat'], writes=['stat'])
                S.op('pool', lambda e: e.tensor_tensor(out=stat[:, 2:3], in0=stat[:, 1:2], in1=mhalf, op=ALU.pow),
                     reads=['stat', 'small'], writes=['stat'])
                chk('rms_a')
                S.op('act', lambda e: e.activation(out=hb[:], in_=src_ap, func=AF.Copy, scale=stat[:, 2:3]),
                     reads=list(rd) + ['stat'], writes=['hb'])
                chk('rms_b')
                p, k = PS()
                pb = p[:].bitcast(BF16)
                for kk in range(8):
                    S.op('pe', lambda e, kk=kk: e.transpose(out=pb[:, kk * 128:(kk + 1) * 128], in_=hb[:, kk * 128:(kk + 1) * 128], identity=idb[:]),
                         reads=['hb', 'idb'], writes=[k], inc=(kk == 7))
                chk('rms_c')
                S.op('dve', lambda e: e.tensor_tensor(out=dstT[:, :, col0:col0 + 128], in0=pb.rearrange("p (k n) -> p k n", k=8),
                                                      in1=prm[:, gcol:gcol + 8].unsqueeze(2).to_broadcast([128, 8, 128]), op=ALU.mult),
                     reads=[k, 'prm'], writes=list(wr_extra))
                chk('rms_d')

            def rms_multi(items, gcol, dstT, wr, base=0, phase='all'):
                n = len(items)
                hbs = G['hbs']
                if phase in ('all', 'stats'):
                    for i_, (src, rd, col0) in enumerate(items):
                        S.op('act', lambda e, i_=i_, src=src: e.activation(out=hbs[(base + i_) % 2][:], in_=src, func=AF.Square, accum_out=stat[:, base + i_:base + i_ + 1]),
                             reads=rd, writes=['hb%d' % ((base + i_) % 2), 'stat'])
                    S.op('act', lambda e: e.activation(out=stat[:, 4 + base:4 + base + n], in_=stat[:, base:base + n], func=AF.Ln, scale=1.0 / 1024, bias=epsc), reads=['stat', 'small'], writes=['stat'])
                    S.op('act', lambda e: e.activation(out=stat[:, 8 + base:8 + base + n], in_=stat[:, 4 + base:4 + base + n], func=AF.Exp, scale=-0.5), reads=['stat'], writes=['stat'])
                if phase == 'stats':
                    return
                for i_, (src, rd, col0) in enumerate(items):
                    hb = hbs[(base + i_) % 2]
                    S.op('act', lambda e, i_=i_, src=src, hb=hb: e.activation(out=hb[:], in_=src, func=AF.Copy, scale=stat[:, 8 + base + i_:9 + base + i_]),
                         reads=list(rd) + ['stat'], writes=['hb%d' % ((base + i_) % 2)])
                    p, k = PS()
                    pb = p[:].bitcast(BF16)
                    for kk in range(8):
                        S.op('pe', lambda e, kk=kk, hb=hb, pb=pb: e.transpose(out=pb[:, kk * 128:(kk + 1) * 128], in_=hb[:, kk * 128:(kk + 1) * 128], identity=idb[:]),
                             reads=['hb%d' % ((base + i_) % 2), 'idb'], writes=[k], inc=(kk == 7))
                    S.op('dve', lambda e, col0=col0, pb=pb: e.tensor_tensor(out=dstT[:, :, col0:col0 + 128], in0=pb.rearrange("p (k n) -> p k n", k=8),
                                                                          in1=prm[:, gcol:gcol + 8].unsqueeze(2).to_broadcast([128, 8, 128]), op=ALU.mult),
                         reads=[k, 'prm'], writes=list(wr))

            def alloc_phase(st, T, sample):
                B = {}
                B['T'] = T
                ns = T // 128
                G['xres'] = sbt(st, "xres", [128, ns, 1024]); G['hT'] = sbt(st, "hT", [128, 8, T], BF16)
                G['mixT'] = sbt(st, "mixT", [128, 8, T], BF16); G['mT'] = sbt(st, "mT", [128, 22, T], BF16)
                G['hbs'] = [sbt(st, "hb%d" % i, [128, 1024], BF16) for i in range(2)]
                G['hb'] = G['hbs'][0]
                B['u'] = sbt(st, "u_sb", [128, 2, 15 + T]) if not sample else None
                B['pA'] = sbt(st, "pA", [128, max(16 + T, 368)]); B['pB'] = sbt(st, "pB", [128, max(16 + T, 368)])
                B['d'] = sbt(st, "d_sb", [128, 2, T], BF16)
                B['Ab'] = [sbt(st, "A_sb%d" % i, [128, T]) for i in range(2)]
                B['E2b'] = [sbt(st, "E2_%d" % i, [128, T]) for i in range(2)]
                B['K1b'] = [sbt(st, "K1_%d" % i, [128, T]) for i in range(2)]
                B['QSall'] = sbt(st, "QSall", [128, 4, T]); B['THall'] = sbt(st, "THall", [128, 4, T])
                B['GS'] = sbt(st, "GS", [128, 4, T])
                B['qAT'] = sbt(st, "qAT", [128, 4, T], BF16); B['kAT'] = sbt(st, "kAT", [128, 4, T], BF16)
                B['kAk'] = sbt(st, "kAk", [128, T // 128, 512], BF16); B['vtk'] = sbt(st, "vtk", [128, T // 128, 512], BF16)
                B['eAe'] = sbt(st, "eAe", [128, 4, 16])
                B['PT'] = sbt(st, "PT", [128, 4, 128], BF16)
                B['osb'] = sbt(st, "osb", [128, 4, T]); B['osq'] = sbt(st, "osq", [128, T], BF16)
                B['R'] = sbt(st, "R_sb", [128, T]); B['t1'] = sbt(st, "t1", [128, T])
                B['qxT'] = sbt(st, "qxT", [128, 2, T], BF16)
                B['cbuf'] = [B['QSall'][:, i, :] for i in range(2)]
                B['gbuf'] = [B['QSall'][:, 2 + i, :] for i in range(2)]
                return B

            def do_tile(B, t0, first, sample, X):
                T = B['T']; nsub = T // 128
                xres, hT, mixT, mT = G['xres'], G['hT'], G['mixT'], G['mT']
                for sub in range(nsub):
                    S.dma('sp', 'xin%d' % sub, xres[:, sub, :], x_tok[t0 + sub * 128:t0 + (sub + 1) * 128, :], writes=['xres%d' % sub])
                if first and not sample:
                    X['memdma']()
                if sample:
                    X['pre1']()
                if not X.get('prenormed'):
                    rms_multi([(xres[:, sub, :], ['xres%d' % sub], sub * 128) for sub in range(nsub)], 102, hT, ['hT'])
                X['prenormed'] = False
                chk('norm1')
                QSall, THall, GS = B['QSall'], B['THall'], B['GS']
                done = {'pool': False, 'attn': False, 'inproj': False}

                def g_inproj(b0, b1):
                    for b in range(b0, b1):
                        wsl, wk = wneed()
                        wv = w8(wsl)
                        for jj in range(4):
                            j = b * 4 + jj
                            if 10 <= j < 14:
                                continue
                            yield
                            p, k = PS()
                            mmg(p[:, 0:T], k, [(wv[:, kk, jj * 128:(jj + 1) * 128], hT[:, kk, 0:T]) for kk in range(8)], ['hT', wk])
                            src = p[:, 0:T]
                            if j < 2:
                                if sample:
                                    S.op('act', lambda e, j=j, src=src: e.activation(out=X['xp'][:, j, :, 15:23], in_=src.rearrange("p (b t) -> p b t", t=8), func=AF.Copy),
                                         reads=[k], writes=['xp'])
                                else:
                                    S.op('act', lambda e, j=j, src=src: e.activation(out=B['u'][:, j, 15:15 + T], in_=src, func=AF.Copy), reads=[k], writes=['u'])
                            elif j < 6:
                                S.op('act', lambda e, j=j, src=src: e.activation(out=QSall[:, j - 2, :], in_=src, func=AF.Silu), reads=[k], writes=['QS%d' % (j - 2)])
                            elif j < 10:
                                S.op('act', lambda e, j=j, src=src: e.activation(out=THall[:, j - 6, :], in_=src, func=AF.Tanh, scale=0.5), reads=[k], writes=['TH%d' % (j - 6)])
                            elif j < 18:
                                h = j - 14
                                S.op('act', lambda e, h=h, src=src: e.activation(out=GS[:, h, :], in_=src, func=AF.Copy), reads=[k], writes=['GS%d' % h])
                            else:
                                S.op('act', lambda e, j=j, src=src: e.activation(out=B['qxT'][:, j - 18, :], in_=src, func=AF.Copy), reads=[k], writes=['qxT'])
                        if b in (2, 3):
                            c0 = 256 if b == 2 else 0
                            o0 = 0 if b == 2 else 256
                            for sub in range(nsub):
                                yield
                                p, k = PS()
                                mmg(p[:, 0:256], k, [(hT[:, kk, sub * 128:(sub + 1) * 128], wv[:, kk, c0:c0 + 256]) for kk in range(8)], ['hT', wk])
                                S.op('dve', lambda e, p=p, sub=sub, o0=o0: e.tensor_copy(out=B['vtk'][:, sub, o0:o0 + 256], in_=p[:, 0:256]), reads=[k], writes=['vtk'])

                    if b1 == 5:
                        done['inproj'] = True
                    yield

                for _ in g_inproj(0, 3):
                    pass
                if first and not sample:
                    X['memkv']()
                chk('inproj')
                kgen = X['pre2']() if sample else iter(())

                def kstep(n=1):
                    for _ in range(n):
                        next(kgen, None)
                kstep(2)
                def g_pool():
                    pA, pB, dsb = B['pA'], B['pB'], B['d']
                    Rb = G['hbs'][0][:].bitcast(F32)
                    for c in range(2):
                        kstep(1)
                        if sample:
                            Xv = X['xp'][:, c, :, :]
                            L = 23
                            sl = lambda buf, n: buf[:, 0:16 * n].rearrange("p (b n) -> p b n", b=16)
                            xs = lambda a, b_: Xv[:, :, a:b_]
                            xkey = 'xp'
                        else:
                            u = B['u']
                            if first and c == 0:
                                yield
                                S.op('dve', lambda e: e.memset(u[:, :, 0:15], 0.0), writes=['u'])
                            L = 15 + T
                            sl = lambda buf, n: buf[:, 0:n]
                            xs = lambda a, b_, c=c: u[:, c, a:b_]
                            xkey = 'u'
                        sv = lambda buf, n, a, b_: (sl(buf, n)[:, :, a:b_] if sample else sl(buf, n)[:, a:b_])
                        yield
                        S.op('dve', lambda e: e.tensor_tensor(out=sl(pA, L - 1), in0=xs(1, L), in1=xs(0, L - 1), op=ALU.add), reads=[xkey], writes=['pA'])
                        yield
                        S.op('dve', lambda e: e.tensor_tensor(out=sl(pB, L - 3), in0=sv(pA, L - 1, 2, L - 1), in1=sv(pA, L - 1, 0, L - 3), op=ALU.add), reads=['pA'], writes=['pB'])
                        uview = xs(15, L)
                        dv = dsb[:, c, :].rearrange("p (b t) -> p b t", t=8) if sample else dsb[:, c, :]

                        def comb(plo, phi, buf, n, off, wsel, dv=dv, uview=uview):
                            S.op('dve', lambda e: e.scalar_tensor_tensor(out=dv[plo:phi], in0=sv(buf, n, off, off + (8 if sample else T))[plo:phi], scalar=invw[plo:phi, c:c + 1],
                                                                          in1=uview[plo:phi], op0=ALU.mult, op1=ALU.subtract),
                                 reads=[wsel, xkey, 'cst'], writes=['d'])
                        if c == 0:
                            yield
                            comb(0, 64, pA, L - 1, 14, 'pA')
                            yield
                            comb(64, 128, pB, L - 3, 12, 'pB')
                        else:
                            yield
                            S.op('dve', lambda e: e.tensor_tensor(out=sl(pA, L - 7), in0=sv(pB, L - 3, 4, L - 3), in1=sv(pB, L - 3, 0, L - 7), op=ALU.add), reads=['pB'], writes=['pA'])
                            yield
                            comb(0, 64, pA, L - 7, 8, 'pA')
                            yield
                            S.op('dve', lambda e: e.tensor_tensor(out=sl(pB, L - 15), in0=sv(pA, L - 7, 8, L - 7), in1=sv(pA, L - 7, 0, L - 15), op=ALU.add), reads=['pA'], writes=['pB'])
                            yield
                            comb(64, 128, pB, L - 15, 0, 'pB')
                        if first and not sample:
                            for (plo, phi, buf, off) in ((0, 64, pA, 14 if c == 0 else 8), (64, 128, pB, 12 if c == 0 else 0)):
                                yield
                                S.op('dve', lambda e, plo=plo, phi=phi, buf=buf, off=off: e.tensor_tensor(out=Rb[plo:phi, 0:15], in0=buf[plo:phi, off:off + 15],
                                                                                                          in1=invcnt[plo:phi, c * 16:c * 16 + 15], op=ALU.mult),
                                     reads=['pA', 'pB', 'cst'], writes=['hb0'])
                                yield
                                S.op('dve', lambda e, plo=plo, phi=phi: e.tensor_tensor(out=dsb[plo:phi, c, 0:15], in0=Rb[plo:phi, 0:15], in1=u[plo:phi, c, 15:30], op=ALU.subtract),
                                     reads=['hb0', 'u'], writes=['d'])
                        yield
                        p, k = PS()
                        yield
                        mmg(p[:, 0:T], k, [(wbd[:, c, :], dsb[:, c, :])], ['wbd', 'd'])
                        yield
                        S.op('act', lambda e, p=p, c=c: e.activation(out=mixT[:, c, 0:T], in_=p[:, 0:T], func=AF.Identity, scale=pscale(c)), reads=[k, 'prm'], writes=['mixT%d' % c])
                    if not sample:
                        u = B['u']
                        if X.get('last'):
                            for c in range(2):
                                yield
                                p, k = PS()
                                yield
                                S.op('pe', lambda e, p=p, c=c: e.transpose(out=p[0:15, c * 128:(c + 1) * 128], in_=u[:, c, T:T + 15], identity=ident), reads=['u', 'cst'], writes=[k])
                                yield
                                S.op('dve', lambda e, p=p, c=c: e.tensor_copy(out=X['ppo'][0:15, c * 128:(c + 1) * 128], in_=p[0:15, c * 128:(c + 1) * 128]), reads=[k], writes=['ppo'])
                            S.dma('sp', 'o_pp', o_pool_p[:, :], X['ppo'][0:15, :], reads=['ppo'])
                        else:
                            yield
                            S.op('dve', lambda e: e.tensor_copy(out=u[:, :, 0:15], in_=u[:, :, T:T + 15]), reads=['u'], writes=['u'])

                    yield
                def g_hgrn():
                    qAT, kAT, eAe = B['qAT'], B['kAT'], B['eAe']
                    smk = rmask if sample else X['cm512'][:, :]
                    smkey = 'cst' if sample else 'cm512'

                    def stA(h):
                        par = h % 2
                        K1 = B['K1b'][par]
                        thk = 'TH%d' % h
                        S.op('act', lambda e: e.activation(out=K1[:], in_=THall[:, h, :], func=AF.Identity, scale=lbc[:, 12 + h:13 + h], bias=lbc[:, 4 + h:5 + h]),
                             reads=[thk, 'lbc'], writes=['K1%d' % par])
                        S.op('act', lambda e: e.activation(out=THall[:, h, :], in_=THall[:, h, :], func=AF.Ln, scale=lbc[:, 4 + h:5 + h], bias=lbc[:, h:h + 1]),
                             reads=[thk, 'lbc'], writes=[thk])

                    def stB(h):
                        par = h % 2
                        A = B['Ab'][par]
                        S.op('dve', lambda e: e.tensor_tensor_scan(out=A[:], data0=smk, data1=THall[:, h, :], initial=0.0, op0=ALU.mult, op1=ALU.add),
                             reads=['TH%d' % h, smkey], writes=['A%d' % par])

                    def stC(h):
                        par = h % 2
                        A, E2 = B['Ab'][par], B['E2b'][par]
                        S.op('act', lambda e: e.activation(out=E2[:], in_=A[:], func=AF.Exp, scale=-1.0), reads=['A%d' % par], writes=['E2%d' % par])
                        S.op('act', lambda e: e.activation(out=A[:], in_=A[:], func=AF.Exp), reads=['A%d' % par], writes=['A%d' % par])

                    def stD(h):
                        par = h % 2
                        A, E2, K1 = B['Ab'][par], B['E2b'][par], B['K1b'][par]
                        ka, ke, kk1 = 'A%d' % par, 'E2%d' % par, 'K1%d' % par
                        S.op('dve', lambda e: e.tensor_tensor(out=qAT[:, h, :], in0=QSall[:, h, :], in1=A[:], op=ALU.mult), reads=['QS%d' % h, ka], writes=['qAT'])
                        S.op('dve', lambda e: e.tensor_tensor(out=kAT[:, h, :], in0=K1[:], in1=E2[:], op=ALU.mult), reads=[kk1, ke], writes=['kAT'])
                        if sample:
                            S.op('dve', lambda e: e.tensor_copy(out=eAe[:, h, 0:16], in_=A[:].rearrange("p (b t) -> p b t", t=8)[:, :, 7]), reads=[ka], writes=['eAe'])
                        else:
                            S.op('dve', lambda e: e.tensor_copy(out=eAe[:, h, 0:nsub], in_=A[:].rearrange("p (c t) -> p c t", t=128)[:, :, 127]), reads=[ka], writes=['eAe'])

                    stA(0)
                    yield
                    stB(0)
                    yield
                    kstep()
                    stA(1)
                    yield
                    stC(0)
                    yield
                    stB(1)
                    yield
                    kstep()
                    stD(0)
                    yield
                    stA(2)
                    yield
                    stC(1)
                    yield
                    stB(2)
                    yield
                    kstep()
                    stD(1)
                    yield
                    stA(3)
                    yield
                    stC(2)
                    yield
                    stB(3)
                    yield
                    kstep()
                    stD(2)
                    yield
                    stC(3)
                    yield
                    stD(3)
                    yield
                    kstep(8)
                    chk('hgrn_ew')
                    while not done['inproj']:
                        yield
                    for h in range(4):
                        S.op('act', lambda e, h=h: e.activation(out=GS[:, h, :], in_=GS[:, h, :], func=AF.Silu), reads=['GS%d' % h], writes=['GS%d' % h])
                    for cc in range(nsub):
                        yield
                        p, k = PS()
                        pb = p[:].bitcast(BF16)
                        for h in range(4):
                            yield
                            S.op('pe', lambda e, h=h, cc=cc, pb=pb: e.transpose(out=pb[:, h * 128:(h + 1) * 128], in_=kAT[:, h, cc * 128:(cc + 1) * 128], identity=idb[:]),
                                 reads=['kAT', 'idb'], writes=[k], inc=(h == 3))
                        yield
                        S.op('act', lambda e, cc=cc, pb=pb: e.activation(out=B['kAk'][:, cc, :], in_=pb[:, 0:512], func=AF.Copy), reads=[k], writes=['kAk%d' % cc])
                    chk('katr')
                    PT, osb, kAk, vtk = B['PT'], B['osb'], B['kAk'], B['vtk']
                    for cc in range(nsub):
                        cs = slice(cc * 128, (cc + 1) * 128)
                        yield
                        pS, kS = PS()
                        for h in range(4):
                            yield
                            mmg(pS[:, h * 128:(h + 1) * 128], kS, [(kAT[:, h, cs], qAT[:, h, cs])], ['kAT', 'qAT'])
                        msk = smask if sample else cmask
                        yield
                        S.op('dve', lambda e, pS=pS, msk=msk: e.tensor_tensor(out=PT[:], in0=pS[:, :].rearrange("p (h t) -> p h t", h=4),
                                                                              in1=msk.unsqueeze(1).to_broadcast([128, 4, 128]), op=ALU.mult),
                             reads=[kS, 'cst'], writes=['PT'])
                        yield
                        pO, kO = PS()
                        if not sample:
                            Sf, Sb = X['Sf'], X['Sb']
                            for h in range(4):
                                yield
                                mmg(pO[:, h * 128:(h + 1) * 128], kO, [(vtk[:, cc, h * 128:(h + 1) * 128], PT[:, h, :]), (Sb[:, h, :], qAT[:, h, cs])],
                                    ['vtk', 'PT', 'Sb', 'qAT'])
                            yield
                            S.op('act', lambda e, pO=pO, cs=cs: e.activation(out=osb[:, :, cs], in_=pO[:, :].rearrange("p (h t) -> p h t", h=4), func=AF.Copy), reads=[kO], writes=['osb'])
                            yield
                            pZ, kZ = PS()
                            for h in range(4):
                                yield
                                mmg(pZ[:, h * 128:(h + 1) * 128], kZ, [(kAk[:, cc, h * 128:(h + 1) * 128], vtk[:, cc, h * 128:(h + 1) * 128])], ['kAk%d' % cc, 'vtk'])
                            yield
                            S.op('dve', lambda e, pZ=pZ: e.tensor_tensor(out=Sf[:], in0=Sf[:], in1=pZ[:, :].rearrange("p (h v) -> p h v", h=4), op=ALU.add), reads=[kZ, 'Sf'], writes=['Sf'])
                            yield
                            S.op('dve', lambda e, cc=cc: e.tensor_tensor(out=Sf[:], in0=Sf[:], in1=eAe[:, :, cc:cc + 1].to_broadcast([128, 4, 128]), op=ALU.mult), reads=['Sf', 'eAe'], writes=['Sf'])
                            yield
                            S.op('act', lambda e: e.activation(out=Sb[:], in_=Sf[:], func=AF.Copy), reads=['Sf'], writes=['Sb'])
                        else:
                            S0f, S0b = X['S0f'], X['S0b']
                            for h in range(4):
                                pairs = [(vtk[:, 0, h * 128:(h + 1) * 128], PT[:, h, :])]
                                n = 17
                                yield
                                S.op('pe', lambda e, h=h: e.matmul(pO[:, h * 128:(h + 1) * 128], lhsT=vtk[:, 0, h * 128:(h + 1) * 128], rhs=PT[:, h, :], start=True, stop=False),
                                     reads=['vtk', 'PT'], writes=[kO], inc=False)
                                for bq in range(16):
                                    yield
                                    S.op('pe', lambda e, h=h, bq=bq: e.matmul(pO[:, h * 128 + bq * 8:h * 128 + bq * 8 + 8], lhsT=S0b[:, bq, h, :], rhs=qAT[:, h, bq * 8:bq * 8 + 8],
                                                                             start=False, stop=(bq == 15)),
                                         reads=['S0b', 'qAT'], writes=[kO], inc=(bq == 15))
                            yield
                            S.op('act', lambda e, pO=pO: e.activation(out=osb[:, :, 0:128], in_=pO[:, :].rearrange("p (h t) -> p h t", h=4), func=AF.Copy), reads=[kO], writes=['osb'])
                            Vb2 = X['Vblk2']

                            def stV(i):
                                h, bg = i // 4, i % 4
                                vb = Vb2[i % 2]
                                S.op('dve', lambda e: e.tensor_tensor(out=vb[:], in0=vtk[:, 0, h * 128:(h + 1) * 128].unsqueeze(1).to_broadcast([128, 4, 128]),
                                                                      in1=seqm[:, bg * 4:bg * 4 + 4].unsqueeze(2).to_broadcast([128, 4, 128]), op=ALU.mult),
                                     reads=['vtk', 'cst'], writes=['Vblk%d' % (i % 2)])

                            def stU(i):
                                h, bg = i // 4, i % 4
                                vb = Vb2[i % 2]
                                pZ, kZ = PS()
                                mmg(pZ[:, :], kZ, [(kAk[:, 0, h * 128:(h + 1) * 128], vb[:].rearrange("p b v -> p (b v)"))], ['kAk0', 'Vblk%d' % (i % 2)])
                                S.op('dve', lambda e: e.tensor_tensor(out=S0f[:, bg * 4:bg * 4 + 4, h, :], in0=S0f[:, bg * 4:bg * 4 + 4, h, :],
                                                                      in1=pZ[:, :].rearrange("p (b v) -> p b v", b=4), op=ALU.add),
                                     reads=[kZ, 'S0f'], writes=['S0f'])
                                S.op('dve', lambda e: e.tensor_tensor(out=S0f[:, bg * 4:bg * 4 + 4, h, :], in0=S0f[:, bg * 4:bg * 4 + 4, h, :],
                                                                      in1=eAe[:, h, bg * 4:bg * 4 + 4].unsqueeze(2).to_broadcast([128, 4, 128]), op=ALU.mult),
                                     reads=['S0f', 'eAe'], writes=['S0f'])
                            yield
                            stV(0)
                            for i in range(16):
                                if i + 1 < 16:
                                    yield
                                    stV(i + 1)
                                yield
                                stU(i)
                            S.dma('sp', 'o_hs', o_hgrn_s.rearrange("b h d v -> d b h v"), S0f[:], reads=['S0f'])
                    if (not sample) and X.get('last'):
                        S.dma('sp', 'o_hp', o_hgrn_p.rearrange("h d v -> d h v"), X['Sf'][:], reads=['Sf'])
                    chk('hgrn')
                    osq, R, t1 = B['osq'], B['R'], B['t1']
                    for h in range(4):
                        yield
                        S.op('act', lambda e, h=h: e.activation(out=osq[:], in_=osb[:, h, :], func=AF.Square), reads=['osb'], writes=['osq'])
                        yield
                        p, k = PS()
                        yield
                        mmg(p[:, 0:T], k, [(onesb[:], osq[:])], ['onesb', 'osq'])
                        yield
                        S.op('act', lambda e, p=p: e.activation(out=R[:], in_=p[:, 0:T], func=AF.Ln, scale=1.0 / 128, bias=epsc), reads=[k, 'small'], writes=['R'])
                        yield
                        S.op('act', lambda e: e.activation(out=R[:], in_=R[:], func=AF.Exp, scale=-0.5), reads=['R'], writes=['R'])
                        yield
                        S.op('dve', lambda e, h=h: e.tensor_tensor(out=t1[:], in0=osb[:, h, :], in1=R[:], op=ALU.mult), reads=['osb', 'R'], writes=['t1'])
                        yield
                        S.op('dve', lambda e, h=h: e.scalar_tensor_tensor(out=mixT[:, 2 + h, 0:T], in0=t1[:], scalar=onorm(h), in1=GS[:, h, :], op0=ALU.mult, op1=ALU.mult),
                             reads=['t1', 'GS%d' % h, 'prm'], writes=['mixT%d' % (2 + h)])

                    yield
                def g_attn():
                    qxT = B['qxT']
                    Ra = G['hbs'][1][:].bitcast(F32); Rb = G['hbs'][0][:].bitcast(F32)
                    while not done['inproj']:
                        yield
                    if sample:
                        for _ in range(24):
                            yield
                        kstep(8)
                    if not sample:
                        PTa = X['PTa']
                        for pr in range(2):
                            for hh in range(2):
                                h = pr * 2 + hh
                                rows = slice(hh * 64, hh * 64 + 64)
                                for mc in range(2):
                                    yield
                                    p, k = PS()
                                    yield
                                    mmg(p[:, 0:T], k, [(KT[rows, pr, mc * 128:(mc + 1) * 128], qxT[rows, pr, :])], ['KT', 'qxT'])
                                    yield
                                    S.op('act', lambda e, p=p, hh=hh, mc=mc: e.activation(out=PTa[:, hh, mc, :], in_=p[:, 0:T], func=AF.Exp, scale=0.125), reads=[k], writes=['PTa'])
                            yield
                            pO, kO = PS()
                            yield
                            pD, kD = PS()
                            for hh in range(2):
                                h = pr * 2 + hh
                                rows = slice(hh * 64, hh * 64 + 64)
                                for mc in range(2):
                                    yield
                                    S.op('pe', lambda e, hh=hh, mc=mc, h=h, rows=rows, pO=pO: e.matmul(pO[rows, 0:T], lhsT=Vb[:, mc, h * 64:(h + 1) * 64], rhs=PTa[:, hh, mc, :],
                                                                                                        start=(mc == 0), stop=(mc == 1)),
                                         reads=['Vb', 'PTa'], writes=[kO], inc=(mc == 1))
                                for mc in range(2):
                                    yield
                                    S.op('pe', lambda e, hh=hh, mc=mc, rows=rows, pD=pD: e.matmul(pD[rows, 0:T], lhsT=onesb[:, 0:64], rhs=PTa[:, hh, mc, :],
                                                                                                  start=(mc == 0), stop=(mc == 1)),
                                         reads=['onesb', 'PTa'], writes=[kD], inc=(mc == 1))
                            yield
                            S.op('act', lambda e, pD=pD: e.activation(out=Ra[:, 0:T], in_=pD[:, 0:T], func=AF.Ln), reads=[kD], writes=['hb1'])
                            S.op('act', lambda e: e.activation(out=Ra[:, 0:T], in_=Ra[:, 0:T], func=AF.Exp, scale=-1.0), reads=['hb1'], writes=['hb1'])
                            yield
                            S.op('dve', lambda e, pO=pO, pr=pr: e.tensor_tensor(out=mixT[:, 6 + pr, 0:T], in0=pO[:, 0:T], in1=Ra[:, 0:T], op=ALU.mult), reads=[kO, 'hb1'], writes=['mixT%d' % (6 + pr)])
                    else:
                        KTs, Vs, PTs = X['KTs'], X['Vs'], X['PTs']
                        for g8 in range(2):
                            yield
                            pp = [PS(), PS()]
                            for bi in range(8):
                                bq = g8 * 8 + bi
                                for mc in range(2):
                                    for h in range(4):
                                        par = h % 2
                                        rows = slice(par * 64, par * 64 + 64)
                                        col = bi * 32 + (mc * 2 + h // 2) * 8
                                        last = (bi == 7 and mc == 1 and h >= 2)
                                        p, k = pp[par]
                                        yield
                                        S.op('pe', lambda e, bq=bq, mc=mc, h=h, rows=rows, col=col, p=p: e.matmul(p[:, col:col + 8], lhsT=KTs[rows, bq, h // 2, mc * 128:(mc + 1) * 128],
                                                                                                                 rhs=qxT[rows, h // 2, bq * 8:bq * 8 + 8], start=True, stop=True),
                                             reads=['KTs', 'qxT'], writes=[k], inc=last)
                            for par in range(2):
                                p, k = pp[par]
                                yield
                                S.op('act', lambda e, p=p, g8=g8, par=par: e.activation(out=PTs[:, par, g8 * 256:(g8 + 1) * 256], in_=p[:, 0:256], func=AF.Exp, scale=0.125), reads=[k], writes=['PTs'])
                        yield
                        pO, kO = PS(2)
                        yield
                        pD, kD = PS(2)
                        for bq in range(16):
                            for h in range(4):
                                rows = slice((h % 2) * 64, (h % 2) * 64 + 64)
                                oc = (h // 2) * 128 + bq * 8
                                for mc in range(2):
                                    col = bq * 32 + (mc * 2 + h // 2) * 8
                                    yield
                                    S.op('pe', lambda e, bq=bq, h=h, mc=mc, rows=rows, oc=oc, col=col: e.matmul(pO[rows, oc:oc + 8], lhsT=Vs[:, bq, mc, h * 64:(h + 1) * 64], rhs=PTs[:, h % 2, col:col + 8],
                                                                                                               start=(mc == 0), stop=(mc == 1)),
                                         reads=['Vs', 'PTs'], writes=[kO], inc=(bq == 15 and h == 3 and mc == 1))
                                for mc in range(2):
                                    col = bq * 32 + (mc * 2 + h // 2) * 8
                                    yield
                                    S.op('pe', lambda e, bq=bq, h=h, mc=mc, rows=rows, oc=oc, col=col: e.matmul(pD[rows, oc:oc + 8], lhsT=onesb[:, 0:64], rhs=PTs[:, h % 2, col:col + 8],
                                                                                                               start=(mc == 0), stop=(mc == 1)),
                                         reads=['onesb', 'PTs'], writes=[kD], inc=(bq == 15 and h == 3 and mc == 1))
                        yield
                        S.op('act', lambda e: e.activation(out=Ra[:, 0:128], in_=pD[:, 0:128], func=AF.Ln), reads=[kD], writes=['hb1'])
                        S.op('act', lambda e: e.activation(out=Ra[:, 0:128], in_=Ra[:, 0:128], func=AF.Exp, scale=-1.0), reads=['hb1'], writes=['hb1'])
                        yield
                        S.op('dve', lambda e: e.tensor_tensor(out=mixT[:, 6, 0:128], in0=pO[:, 0:128], in1=Ra[:, 0:128], op=ALU.mult), reads=[kO, 'hb1'], writes=['mixT6'])
                        yield
                        S.op('act', lambda e: e.activation(out=Rb[:, 0:128], in_=pD[:, 128:256], func=AF.Ln), reads=[kD], writes=['hb0'])
                        S.op('act', lambda e: e.activation(out=Rb[:, 0:128], in_=Rb[:, 0:128], func=AF.Exp, scale=-1.0), reads=['hb0'], writes=['hb0'])
                        yield
                        S.op('dve', lambda e: e.tensor_tensor(out=mixT[:, 7, 0:128], in0=pO[:, 128:256], in1=Rb[:, 0:128], op=ALU.mult), reads=[kO, 'hb0'], writes=['mixT7'])

                    yield
                gens = [(g_inproj(3, 5), 2), (g_hgrn(), 3), (g_pool(), 1), (g_attn(), 1)]
                while gens:
                    for ge in list(gens):
                        for _ in range(ge[1]):
                            try:
                                next(ge[0])
                            except StopIteration:
                                gens.remove(ge)
                                break
                chk('pool')
                chk('hgrn_o')
                chk('attn')
                wo = [wneed(), wneed(prefetch=False)]
                for sub in range(nsub):
                    for c in range(2):
                        wsl, wk = wo[c]
                        wv = w8(wsl)
                        p, k = PS()
                        mmg(p[:, :], k, [(mixT[:, kk, sub * 128:(sub + 1) * 128], wv[:, kk, :]) for kk in (0, 1, 6, 7, 2, 3, 4, 5)], [wk], per=[['mixT%d' % kk] for kk in (0, 1, 6, 7, 2, 3, 4, 5)])
                        S.op('dve', lambda e, p=p, sub=sub, c=c: e.tensor_tensor(out=xres[:, sub, c * 512:(c + 1) * 512], in0=p[:, :], in1=xres[:, sub, c * 512:(c + 1) * 512], op=ALU.add),
                             reads=[k, 'xres%d' % sub], writes=['xres%d' % sub])
                    if sub >= 1:
                        rms_multi([(xres[:, sub - 1, :], ['xres%d' % (sub - 1)], (sub - 1) * 128)], 110, hT, ['hT'], base=sub - 1)
                rms_multi([(xres[:, nsub - 1, :], ['xres%d' % (nsub - 1)], (nsub - 1) * 128)], 110, hT, ['hT'], base=nsub - 1)
                chk('outproj')
                chk('norm2')
                pend2 = []
                nxt = X.get('nxt')
                if nxt is not None:
                    GSf = B['GS'][:].rearrange("p h t -> p (h t)"); osf = B['osb'][:].rearrange("p h t -> p (h t)")
                    xn = [GSf[:, 0:1024], GSf[:, 1024:2048], osf[:, 0:1024], osf[:, 1024:2048]]
                    xnk = [['GS0', 'GS1'], ['GS2', 'GS3'], ['osb'], ['osb']]
                    for s_ in range(4):
                        S.dma('sp', 'xn%d' % s_, xn[s_], x_tok[nxt + s_ * 128:nxt + (s_ + 1) * 128, :], writes=xnk[s_])
                for r in range(11):
                    wsl, wk = wneed()
                    wv = w8(wsl)
                    for jj in range(2):
                        j = 2 * r + jj
                        pa, ka = PS(2)
                        mmg(pa[:, 0:T], ka, [(wv[:, kk, jj * 128:(jj + 1) * 128], hT[:, kk, 0:T]) for kk in range(8)], ['hT', wk])
                        pb_, kb = PS()
                        mmg(pb_[:, 0:T], kb, [(wv[:, kk, 256 + jj * 128:256 + (jj + 1) * 128], hT[:, kk, 0:T]) for kk in range(8)], ['hT', wk])
                        cbuf = B['cbuf'][j % 2]; gbuf = B['gbuf'][j % 2]
                        ck_, gk_ = 'cbuf%d' % (j % 2), 'gbuf%d' % (j % 2)
                        if not sample:
                            asb = X['asb'][j % 2]; ak_ = 'asb%d' % (j % 2); carry = X['carry']
                            S.op('dve', lambda e, asb=asb, j=j: e.tensor_copy(out=asb[:, 0:2], in_=carry[:, j, :]), reads=['carry'], writes=[ak_])
                            S.op('act', lambda e, asb=asb, pa=pa: e.activation(out=asb[:, 2:2 + T], in_=pa[:, 0:T], func=AF.Copy), reads=[ka], writes=[ak_])
                            S.op('act', lambda e, pa=pa, j=j, cbuf=cbuf: e.activation(out=cbuf, in_=pa[:, 0:T], func=AF.Identity, scale=cw(2, j), bias=cb(j)), reads=[ka, 'prm'], writes=[ck_])
                            S.op('dve', lambda e, asb=asb, j=j: e.tensor_copy(out=carry[:, j, :], in_=asb[:, T:T + 2]), reads=[ak_], writes=['carry'])
                            S.op('dve', lambda e, asb=asb, j=j, cbuf=cbuf: e.scalar_tensor_tensor(out=cbuf, in0=asb[:, 1:1 + T], scalar=cw(1, j), in1=cbuf, op0=ALU.mult, op1=ALU.add),
                                 reads=[ak_, ck_, 'prm'], writes=[ck_])
                            S.op('dve', lambda e, asb=asb, j=j, cbuf=cbuf: e.scalar_tensor_tensor(out=cbuf, in0=asb[:, 0:T], scalar=cw(0, j), in1=cbuf, op0=ALU.mult, op1=ALU.add),
                                 reads=[ak_, ck_, 'prm'], writes=[ck_])
                        else:
                            a3 = X['a3'][j % 2]; ak_ = 'a3%d' % (j % 2); ahist, anew = X['ahist'], X['anew']
                            c3 = cbuf.rearrange("p (b t) -> p b t", t=8)
                            S.op('dve', lambda e, a3=a3, j=j: e.tensor_copy(out=a3[:, :, 0:2], in_=ahist[:, j, :, :]), reads=['ahist'], writes=[ak_])
                            S.op('act', lambda e, a3=a3, pa=pa: e.activation(out=a3[:, :, 2:10], in_=pa[:, 0:128].rearrange("p (b t) -> p b t", t=8), func=AF.Copy), reads=[ka], writes=[ak_])
                            S.op('act', lambda e, pa=pa, j=j, cbuf=cbuf: e.activation(out=cbuf, in_=pa[:, 0:T], func=AF.Identity, scale=cw(2, j), bias=cb(j)), reads=[ka, 'prm'], writes=[ck_])
                            S.op('dve', lambda e, a3=a3, j=j: e.tensor_copy(out=anew[:, j, :, :], in_=a3[:, :, 8:10]), reads=[ak_], writes=['anew'])
                            S.op('dve', lambda e, a3=a3, j=j, c3=c3: e.scalar_tensor_tensor(out=c3, in0=a3[:, :, 1:9], scalar=cw(1, j), in1=c3, op0=ALU.mult, op1=ALU.add),
                                 reads=[ak_, ck_, 'prm'], writes=[ck_])
                            S.op('dve', lambda e, a3=a3, j=j, c3=c3: e.scalar_tensor_tensor(out=c3, in0=a3[:, :, 0:8], scalar=cw(0, j), in1=c3, op0=ALU.mult, op1=ALU.add),
                                 reads=[ak_, ck_, 'prm'], writes=[ck_])
                        def stage2(cbuf=cbuf, gbuf=gbuf, pb_=pb_, j=j, ck_=ck_, gk_=gk_, kb=kb):
                            S.op('act', lambda e: e.activation(out=gbuf, in_=cbuf, func=AF.Gelu_apprx_tanh), reads=[ck_], writes=[gk_])
                            S.op('dve', lambda e: e.tensor_tensor(out=mT[:, j, 0:T], in0=pb_[:, 0:T], in1=gbuf, op=ALU.mult), reads=[kb, gk_], writes=['mT%d' % j] + (['kvt'] if (first and j < 4) else []))
                        if pend2:
                            pend2.pop()()
                        pend2.append(stage2)
                if pend2:
                    pend2.pop()()
                chk('up')
                if (not sample) and X.get('last'):
                    carry, rowb = X['carry'], X['rowb']
                    for g4 in range(6):
                        p, k = PS()
                        n4 = 4 if g4 < 5 else 2
                        for q in range(n4):
                            j = g4 * 4 + q
                            S.op('pe', lambda e, p=p, q=q, j=j: e.transpose(out=p[0:2, q * 128:(q + 1) * 128], in_=carry[:, j, :], identity=ident), reads=['carry', 'cst'], writes=[k], inc=(q == n4 - 1))
                        S.op('dve', lambda e, p=p, g4=g4, n4=n4: e.tensor_copy(out=rowb[0:2, g4 % 2, 0:n4 * 128], in_=p[0:2, 0:n4 * 128]), reads=[k], writes=['rowb%d' % (g4 % 2)])
                        S.dma('sp', 'o_cp%d' % (g4 % 2), o_conv_p[:, g4 * 512:g4 * 512 + n4 * 128], rowb[0:2, g4 % 2, 0:n4 * 128], reads=['rowb%d' % (g4 % 2)])
                if sample:
                    anew, rowb = X['anew'], X['rowb']
                    for g4 in range(6):
                        p, k = PS()
                        n4 = 4 if g4 < 5 else 2
                        for q in range(n4):
                            j = g4 * 4 + q
                            S.op('pe', lambda e, p=p, q=q, j=j: e.transpose(out=p[0:32, q * 128:(q + 1) * 128], in_=anew[:, j, :, :].rearrange("p b r -> p (b r)"), identity=ident),
                                 reads=['anew', 'cst'], writes=[k], inc=(q == n4 - 1))
                        S.op('dve', lambda e, p=p, g4=g4, n4=n4: e.tensor_copy(out=rowb[0:32, g4 % 2, 0:n4 * 128], in_=p[0:32, 0:n4 * 128]), reads=[k], writes=['rowb%d' % (g4 % 2)])
                        S.dma('sp', 'o_cs%d' % (g4 % 2), o_conv_s[:, g4 * 512:g4 * 512 + n4 * 128], rowb[0:32, g4 % 2, 0:n4 * 128], reads=['rowb%d' % (g4 % 2)])
                chk('convout')
                if nxt is not None:
                    rms_multi([(xn[s_], xnk[s_], s_ * 128) for s_ in range(4)], 102, hT, ['hT'], phase='stats')
                for q in range(4):
                    wsl, wk = wneed()
                    wv = w22(wsl)
                    for sub in range(nsub):
                        p, k = PS()
                        mmg(p[:, 0:256], k, [(mT[:, kk, sub * 128:(sub + 1) * 128], wv[:, kk, :]) for kk in range(22)], [wk], per=[['mT%d' % kk] for kk in range(22)])
                        S.op('dve', lambda e, p=p, sub=sub, q=q: e.tensor_tensor(out=xres[:, sub, q * 256:(q + 1) * 256], in0=p[:, 0:256], in1=xres[:, sub, q * 256:(q + 1) * 256], op=ALU.add),
                             reads=[k, 'xres%d' % sub], writes=['xres%d' % sub])
                    if q == 1 and nxt is not None:
                        rms_multi([(xn[s_], xnk[s_], s_ * 128) for s_ in range(4)], 102, hT, ['hT'], phase='apply')
                        X['prenormed'] = True
                hbs = G['hbs']
                for sub in range(nsub):
                    S.op('act', lambda e, sub=sub: e.activation(out=hbs[sub % 2][:], in_=xres[:, sub, :], func=AF.Square, accum_out=stat[:, 16 + sub:17 + sub]),
                         reads=['xres%d' % sub], writes=['hb%d' % (sub % 2), 'stat2'])
                S.op('act', lambda e: e.activation(out=stat[:, 20:20 + nsub], in_=stat[:, 16:16 + nsub], func=AF.Ln, scale=1.0 / 1024, bias=epsc), reads=['stat2', 'small'], writes=['stat2'])
                S.op('act', lambda e: e.activation(out=stat[:, 24:24 + nsub], in_=stat[:, 20:20 + nsub], func=AF.Exp, scale=-0.5), reads=['stat2'], writes=['stat2'])
                for sub in range(nsub):
                    xk = 'xres%d' % sub
                    S.op('dve', lambda e, sub=sub: e.scalar_tensor_tensor(out=xres[:, sub, :], in0=xres[:, sub, :], scalar=stat[:, 24 + sub:25 + sub], in1=gf[:], op0=ALU.mult, op1=ALU.mult),
                         reads=[xk, 'stat2', 'gf'], writes=[xk])
                    S.dma('sp', 'yout%d' % sub, y_tok[t0 + sub * 128:t0 + (sub + 1) * 128, :], xres[:, sub, :], reads=[xk])

            chk('prologue')
            Bp = alloc_phase(pst, 512, False)
            xres, hT, mixT, mT = G['xres'], G['hT'], G['mixT'], G['mT']
            memx = Bp['osb'][:].rearrange("p h t -> p (h t)").rearrange("p (s f) -> p s f", s=2); memT = mixT
            kvt = mT[:].rearrange("p k t -> p (k t)")[:, 0:2048].bitcast(F32).rearrange("p (s f) -> p s f", s=2)
            KT = sbt(pst, "KT", [128, 2, 256], BF16); Vb = sbt(pst, "Vb", [128, 2, 256], BF16)

            def memkv():
                rms_multi([(memx[:, sub, :], ['osb'], sub * 128) for sub in range(2)], 118, memT, ['memT'])
                wkv, wk = wneed()
                chk('kv_w')
                for sub in range(2):
                    p, k = PS(2)
                    mmg(p[:, :], k, [(memT[:, kk, sub * 128:(sub + 1) * 128], w8(wkv)[:, kk, :]) for kk in range(8)], ['memT', wk])
                    chk('kv_m')
                    S.op('act', lambda e, p=p, sub=sub: e.activation(out=kvt[:, sub, :], in_=p[:, :], func=AF.Copy), reads=[k], writes=['kvt'])
                    chk('kv_n')
                    S.op('dve', lambda e, p=p, sub=sub: e.tensor_copy(out=Vb[:, sub, :], in_=p[:, 256:512]), reads=[k], writes=['Vb'])
                    chk('kv_a%d' % sub)
                for j in range(2):
                    p, k = PS()
                    mmg(p[:, 0:256], k, [(w8(wkv)[:, kk, j * 128:(j + 1) * 128], memT[:, kk, 0:256]) for kk in range(8)], ['memT', wk])
                    S.op('act', lambda e, p=p, j=j: e.activation(out=KT[:, j, :], in_=p[:, 0:256], func=AF.Copy), reads=[k], writes=['KT'])
                    chk('kv_b%d' % j)
                S.dma('sp', 'o_mk', o_mk.rearrange("(s p) f -> p s f", p=128), kvt[:, :, 0:256], reads=['kvt'])
                chk('kv_c')
                S.dma('sp', 'o_mv', o_mv.rearrange("(s p) f -> p s f", p=128), kvt[:, :, 256:512], reads=['kvt'])


            chk('memkv')
            Xp = {}
            Xp['memkv'] = memkv
            Xp['memdma'] = lambda: S.dma('sp', 'c7', memx[:, 0:2, :], mem.rearrange("(s p) f -> p s f", p=128), writes=['osb'])
            Xp['Sf'] = sbt(pst, "Sf", [128, 4, 128]); Xp['Sb'] = sbt(pst, "Sb", [128, 4, 128], BF16)
            Xp['cm512'] = sbt(pst, "cm512", [128, 512])
            Xp['PTa'] = sbt(pst, "PTa", [128, 2, 2, 512], BF16)
            thf = Bp['THall'][:].rearrange("p h t -> p (h t)")
            Xp['asb'] = [thf[:, 0:514], thf[:, 1024:1538]]
            Xp['carry'] = sbt(pst, "carry", [128, 22, 2]); Xp['rowb'] = sbt(pst, "rowb_p", [2, 2, 512]); Xp['ppo'] = sbt(pst, "ppo", [16, 256])
            S.op('dve', lambda e: e.memset(Xp['Sf'][:], 0.0), writes=['Sf'])
            S.op('dve', lambda e: e.memset(Xp['Sb'][:], 0.0), writes=['Sb'])
            S.op('dve', lambda e: e.memset(Xp['cm512'][:], 1.0), writes=['cm512'])
            S.op('dve', lambda e: e.memset(Xp['cm512'][:].rearrange("p (c t) -> p c t", t=128)[:, :, 0:1], 0.0), writes=['cm512'])
            S.op('dve', lambda e: e.memset(Xp['carry'][:], 0.0), writes=['carry'])
            for ti in range(4):
                Xp['last'] = (ti == 3)
                Xp['nxt'] = (ti + 1) * 512 if ti < 3 else None
                do_tile(Bp, ti * 512, ti == 0, False, Xp)
                chk('tile%d' % ti)
            S.barrier()
            pst.close()

            chk('prompt')
            Bs = alloc_phase(sst, 128, True)
            Xs = {}
            Xs['xp'] = sbt(sst, "xp", [128, 2, 16, 23]); Xs['sp_tok'] = sbt(sst, "sp_tok", [120, 2, 256]); Xs['xpc'] = sbt(sst, "xpc", [128, 2, 16, 15])
            Xs['spo'] = Xs['sp_tok']
            Xs['S0f'] = sbt(sst, "S0f", [128, 16, 4, 128]); Xs['S0b'] = sbt(sst, "S0b", [128, 16, 4, 128], BF16)
            Xs['Vblk2'] = [sbt(sst, "Vblk%d" % i, [128, 4, 128], BF16) for i in range(2)]
            Xs['KTs'] = sbt(sst, "KTs", [128, 16, 2, 256], BF16); Xs['Vs'] = sbt(sst, "Vs", [128, 16, 2, 256], BF16)
            Xs['PTs'] = sbt(sst, "PTs", [128, 2, 512], BF16); kst2 = [sbt(sst, "kst%d" % i, [128, 2, 2, 256], BF16) for i in range(2)]
            thfs = Bs['THall'][:].rearrange("p h t -> p (h t)")
            Xs['a3'] = [thfs[:, 0:160].rearrange("p (b t) -> p b t", t=10), thfs[:, 256:416].rearrange("p (b t) -> p b t", t=10)]
            Xs['ahist'] = sbt(sst, "ahist", [128, 22, 16, 2]); Xs['anew'] = sbt(sst, "anew", [128, 22, 16, 2]); Xs['rowb'] = sbt(sst, "rowb_s", [32, 2, 512])
            cst_tok = sbt(sst, "cst_tok", [32, 2816])
            def pre1():
                S.dma('sp', 's3', Xs['sp_tok'][:], spool.rearrange("(h q) c -> q h c", q=120), writes=['sp_tok'])
                S.dma('sp', 's4', cst_tok[:], sconv[:, :], writes=['cst_tok'])
                kload(0)
                for hh in range(2):
                    for c in range(2):
                        p, k = PS()
                        S.op('pe', lambda e, p=p, hh=hh, c=c: e.transpose(out=p[:, 0:120], in_=Xs['sp_tok'][0:120, hh, c * 128:(c + 1) * 128], identity=cst[0:120, 0:120]),
                             reads=['sp_tok', 'cst'], writes=[k])
                        S.op('dve', lambda e, p=p, hh=hh, c=c: e.tensor_copy(out=Xs['xp'][:, c, hh * 8:(hh + 1) * 8, 0:15], in_=p[:, 0:120].rearrange("p (b r) -> p b r", r=15)),
                             reads=[k], writes=['xp'])
                for j in range(22):
                    p, k = PS()
                    S.op('pe', lambda e, p=p, j=j: e.transpose(out=p[:, 0:32], in_=cst_tok[0:32, j * 128:(j + 1) * 128], identity=cst[0:32, 0:32]), reads=['cst_tok', 'cst'], writes=[k])
                    S.op('dve', lambda e, p=p, j=j: e.tensor_copy(out=Xs['ahist'][:, j, :, :], in_=p[:, 0:32].rearrange("p (b r) -> p b r", r=2)), reads=[k], writes=['ahist'])

            def kload(g4):
                S.dma('pool', 's5%d' % (g4 % 2), kst2[g4 % 2][:], ck[g4 * 2:(g4 + 1) * 2].rearrange("b (mc p) f -> p b mc f", p=128), writes=['kst%d' % (g4 % 2)])

            def pre2():
                kload(1)
                S.dma('pool', 's1', Xs['S0b'][:], shgrn.rearrange("b h d v -> d b h v"), writes=['S0b'])
                for g4 in range(8):
                    kst = kst2[g4 % 2]
                    for bi in range(2):
                        bq = g4 * 2 + bi
                        p, k = PS()
                        pb = p[:].bitcast(BF16)
                        for hc in range(2):
                            for mc in range(2):
                                S.op('pe', lambda e, pb=pb, kst=kst, bi=bi, hc=hc, mc=mc: e.transpose(out=pb[:, (hc * 2 + mc) * 128:(hc * 2 + mc + 1) * 128], in_=kst[:, bi, mc, hc * 128:(hc + 1) * 128], identity=idb[:]),
                                     reads=['kst%d' % (g4 % 2), 'idb'], writes=[k], inc=(hc == 1 and mc == 1))
                        S.op('act', lambda e, pb=pb, bq=bq: e.activation(out=Xs['KTs'][:, bq, :, :], in_=pb[:, 0:512].rearrange("p (hc m) -> p hc m", hc=2), func=AF.Copy), reads=[k], writes=['KTs'])
                    if g4 + 2 < 8:
                        kload(g4 + 2)
                    if g4 == 7:
                        S.dma('pool', 's2', Xs['Vs'][:], cv.rearrange("b (mc p) f -> p b mc f", p=128), writes=['Vs'])
                        S.dma('sp', 's0', Xs['S0f'][:], shgrn.rearrange("b h d v -> d b h v"), writes=['S0f'])
                    yield
            Xs['pre1'] = pre1; Xs['pre2'] = pre2
            chk('sprologue')
            do_tile(Bs, 2048, False, True, Xs)
            S.op('dve', lambda e: e.tensor_copy(out=Xs['xpc'][:], in_=Xs['xp'][:, :, :, 8:23]), reads=['xp'], writes=['xpc'])
            for hh in range(2):
                for c in range(2):
                    p, k = PS()
                    S.op('pe', lambda e, p=p, hh=hh, c=c: e.transpose(out=p[0:120, 0:128], in_=Xs['xpc'][:, c, hh * 8:(hh + 1) * 8, :].rearrange("p b r -> p (b r)"), identity=ident),
                         reads=['xpc', 'cst'], writes=[k])
                    S.op('dve', lambda e, p=p, hh=hh, c=c: e.tensor_copy(out=Xs['spo'][0:120, hh, c * 128:(c + 1) * 128], in_=p[0:120, 0:128]), reads=[k], writes=['spo'])
            S.dma('sp', 'o_ps', o_pool_s.rearrange("(h b) r c -> (b r) h c", h=2), Xs['spo'][:], reads=['spo'])
        except StopBuild as ex:
            print('STOPPED at', ex)
        S.final()
        sst.close()
        import os
        if os.environ.get('KDEBUG'):
            print('CNT', S.cnt, {k: v[1] for k, v in S.dsem.items()})
    return nc


_NC = None


def kernel(**inp):
    global _NC
    f = lambda a: np.ascontiguousarray(np.asarray(a, dtype=np.float32))
    if _NC is None:
        _NC = build()
    cst = make_consts()
    prm = np.concatenate([f(inp['conv_w'][0]).reshape(66, 128), f(inp['conv_b'][0]).reshape(22, 128),
                          f(inp['hgrn_lb_logits']).reshape(8, 128), f(inp['pool_scale'][0]).reshape(2, 128),
                          f(inp['hgrn_onorm_g'][0]).reshape(4, 128), f(inp['ln1_g'][0]).reshape(8, 128),
                          f(inp['ln2_g'][0]).reshape(8, 128), f(inp['mem_norm_g'][0]).reshape(8, 128)], axis=0)
    shared = dict(lnf=f(inp['lnf_g']),
                  w_in=f(inp['w_in'][0]), w_kv=f(inp['w_mem_kv'][0]), w_out=f(inp['w_out'][0]), w_up=f(inp['w_up'][0]),
                  w_dn=f(inp['w_down'][0]), pool_w=f(inp['pool_w'][0]), prm_in=f(prm), cst=cst)
    in_maps = []
    for c in range(8):
        sl = slice(16 * c, 16 * c + 16)
        m = dict(shared)
        m['x_tok'] = f(np.concatenate([inp['x_prompt'][c], np.asarray(inp['x_sample'][sl]).reshape(128, 1024)], axis=0))
        m['mem'] = f(inp['mem_prompt'][c])
        m['spool'] = f(np.asarray(inp['state_pool'][0, sl]).reshape(240, 256))
        m['shgrn'] = f(inp['state_hgrn'][0, sl])
        m['sconv'] = f(np.asarray(inp['state_conv'][0, sl]).reshape(32, 2816))
        m['ck'] = f(np.asarray(inp['cache_mem_k'][0, sl]).reshape(16, 256, 256))
        m['cv'] = f(np.asarray(inp['cache_mem_v'][0, sl]).reshape(16, 256, 256))
        in_maps.append(m)
    res = run_bass_kernel_spmd(_NC, in_maps, core_ids=list(range(8)))
    R = res.results
    g = lambda k: np.stack([np.asarray(R[c][k], dtype=np.float32) for c in range(8)])
    y = g('y_tok')
    y_prompt = np.ascontiguousarray(y[:, :2048, :])
    y_sample = np.ascontiguousarray(y[:, 2048:, :].reshape(128, 8, 1024))
    return (y_prompt, y_sample,
            g('o_pool_p')[None], g('o_hgrn_p')[None], g('o_conv_p')[None],
            g('o_mk').reshape(1, 8, 256, 4, 64), g('o_mv').reshape(1, 8, 256, 4, 64),
            g('o_pool_s').reshape(1, 128, 15, 256), g('o_hgrn_s').reshape(1, 128, 4, 128, 128),
            g('o_conv_s').reshape(1, 128, 2, 2816))
```

```python
import numpy as np
from contextlib import ExitStack
import concourse.bass as bass
import concourse.mybir as mybir
from concourse.bass_utils import run_bass_kernel_spmd

F32, BF16 = mybir.dt.float32, mybir.dt.bfloat16
AF = mybir.ActivationFunctionType
ALU = mybir.AluOpType
EPS = 1e-6
NSLOT = 4
SAME_ENGINE_SYNC = True


import os


class StopBuild(Exception):
    pass


_hits = {}


def chk(name):
    if os.environ.get('KSTOP') == name:
        _hits[name] = _hits.get(name, 0) + 1
        if _hits[name] == int(os.environ.get('KHIT', '1')):
            raise StopBuild(name)


class Sched:
    def __init__(s, nc, es):
        s.nc, s.es = nc, es
        s.E = {'pe': nc.tensor, 'act': nc.scalar, 'dve': nc.vector, 'pool': nc.gpsimd, 'sp': nc.sync}
        s.sem = {k: es.enter_context(nc.semaphore('sem_' + k)) for k in s.E}
        s.cnt = {k: 0 for k in s.E}
        s.seen = {k: {} for k in s.E}
        s.lastw, s.reads, s.dsem = {}, {}, {}
        s.psn = 0
        s.ps_open = {}

    def _semh(s, key):
        return s.sem[key] if key in s.sem else s.dsem[key][0]

    def _wait(s, eng, key, val):
        if key == eng and (eng in ('pe', 'sp') or not SAME_ENGINE_SYNC):
            return
        if s.seen[eng].get(key, 0) >= val:
            return
        s.seen[eng][key] = val
        s.E[eng].wait_ge(s._semh(key), val)

    def deps(s, eng, reads, writes):
        need = {}
        for b in reads:
            if b in s.lastw:
                k, v = s.lastw[b]
                need[k] = max(need.get(k, 0), v)
        for b in writes:
            if b in s.lastw:
                k, v = s.lastw[b]
                need[k] = max(need.get(k, 0), v)
            for (k, v) in s.reads.get(b, ()):
                need[k] = max(need.get(k, 0), v)
        for k, v in need.items():
            s._wait(eng, k, v)

    def _record(s, tok, reads, writes):
        for b in reads:
            s.reads.setdefault(b, []).append(tok)
        for b in writes:
            s.lastw[b] = tok
            s.reads[b] = []

    def op(s, eng, fn, reads=(), writes=(), inc=True):
        psr = [b for b in reads if isinstance(b, tuple) and b[0] == 'ps']
        if psr:
            reads = [b for b in reads if b not in psr]
            writes = list(writes) + psr
            for b in psr:
                s.ps_open[b[1]] -= 1
                if s.ps_open[b[1]] <= 0:
                    del s.ps_open[b[1]]
        s.deps(eng, reads, writes)
        ins = fn(s.E[eng])
        if inc:
            s.cnt[eng] += 1
            ins.then_inc(s.sem[eng], 1)
            tok = (eng, s.cnt[eng])
        else:
            tok = (eng, s.cnt[eng] + 1)
        s._record(tok, reads, writes)
        return ins

    def dma(s, eng, chan, out, in_, reads=(), writes=(), **kw):
        s.deps(eng, reads, writes)
        if chan not in s.dsem:
            s.dsem[chan] = [s.es.enter_context(s.nc.semaphore('d_' + chan)), 0]
        ins = s.E[eng].dma_start(out=out, in_=in_, **kw)
        s.dsem[chan][1] += 16
        ins.then_inc(s.dsem[chan][0], 16)
        s._record((chan, s.dsem[chan][1]), reads, writes)

    def barrier(s):
        for eng in s.E:
            for k in s.sem:
                if s.cnt[k] > 0:
                    s._wait(eng, k, s.cnt[k])
            for k in s.dsem:
                s._wait(eng, k, s.dsem[k][1])

    def final(s):
        for k in s.sem:
            if s.cnt[k] > 0:
                s._wait('sp', k, s.cnt[k])
        for k in s.dsem:
            s._wait('sp', k, s.dsem[k][1])


def make_consts():
    c = np.zeros((128, 576), np.float32)
    c[:, 0:128] = np.eye(128, dtype=np.float32)
    s = np.arange(128)[:, None]
    t = np.arange(128)[None, :]
    c[:, 128:256] = (t >= s)
    c[:, 256:384] = (t >= s) & ((t // 8) == (s // 8))
    c[:, 384:400] = (np.arange(128)[:, None] // 8) == np.arange(16)[None, :]
    c[:, 400:528] = (np.arange(128)[None, :] % 8 != 0)
    for ch in range(2):
        for p in range(128):
            w = [2, 4, 8, 16][2 * ch + p // 64]
            c[p, 528 + ch * 16: 528 + ch * 16 + 16] = 1.0 / np.minimum(w, np.arange(16) + 1.0)
            c[p, 560 + ch] = 1.0 / w
    return c


def build():
    nc = bass.Bass("TRN2", target_bir_lowering=False)
    D = lambda n, sh, k="ExternalInput": nc.dram_tensor(n, sh, F32, kind=k).ap()
    x_tok = D("x_tok", [2176, 1024]); mem = D("mem", [256, 1024])
    spool = D("spool", [240, 256]); shgrn = D("shgrn", [16, 4, 128, 128]); sconv = D("sconv", [32, 2816])
    ck = D("ck", [16, 256, 256]); cv = D("cv", [16, 256, 256])
    lnf = D("lnf", [1024])
    w_in = D("w_in", [1024, 2560]); w_kv = D("w_kv", [1024, 512]); w_out = D("w_out", [1024, 1024])
    w_up = D("w_up", [1024, 5632]); w_dn = D("w_dn", [2816, 1024])
    pool_w = D("pool_w", [4, 64, 64]); prm_in = D("prm_in", [126, 128]); cst_in = D("cst", [128, 576])
    O = lambda n, sh: D(n, sh, "ExternalOutput")
    y_tok = O("y_tok", [2176, 1024]); o_pool_p = O("o_pool_p", [15, 256]); o_hgrn_p = O("o_hgrn_p", [4, 128, 128])
    o_conv_p = O("o_conv_p", [2, 2816]); o_mk = O("o_mk", [256, 256]); o_mv = O("o_mv", [256, 256])
    o_pool_s = O("o_pool_s", [16, 15, 256]); o_hgrn_s = O("o_hgrn_s", [16, 4, 128, 128]); o_conv_s = O("o_conv_s", [32, 2816])

    with ExitStack() as es:
        S = Sched(nc, es)
        uid = [0]
        def sbt(st, n, sh, dt=F32):
            uid[0] += 1
            return st.enter_context(nc.sbuf_tensor("sb%d_%s" % (uid[0], n), sh, dt))
        psb = [es.enter_context(nc.psum_tensor("ps%d" % i, [128, 512], F32)) for i in range(8)]

        def PS(n=1):
            for _ in range(8):
                i = S.psn % 8
                S.psn += 1
                if i not in S.ps_open:
                    break
            else:
                raise RuntimeError('no free PSUM bank')
            S.ps_open[i] = n
            return psb[i], ('ps', i)

        cst = sbt(es, "cst", [128, 576]); prm = sbt(es, "prm", [128, 128]); prm_st = sbt(es, "prm_st", [126, 128])
        idb = sbt(es, "idb", [128, 128], BF16); onesb = sbt(es, "onesb", [128, 128], BF16)
        gf = sbt(es, "gf", [128, 1024])
        wbd = sbt(es, "wbd", [128, 2, 128], BF16)
        lbc = sbt(es, "lbc", [128, 16])
        small = sbt(es, "small", [128, 8])
        ring = [sbt(es, "ring%d" % i, [128, 5632], BF16) for i in range(NSLOT)]
        G = {}
        stat = sbt(es, "stat", [128, 32])
        ident = cst[:, 0:128]; cmask = cst[:, 128:256]; smask = cst[:, 256:384]; seqm = cst[:, 384:400]
        rmask = cst[:, 400:528]; invw = cst[:, 560:562]
        invcnt = cst[:, 528:560]
        epsc = small[:, 0:1]; mhalf = small[:, 1:2]; onec = small[:, 2:3]
        cw = lambda r, j: prm[:, r * 22 + j: r * 22 + j + 1]
        cb = lambda j: prm[:, 66 + j: 67 + j]
        pscale = lambda c: prm[:, 96 + c: 97 + c]
        onorm = lambda h: prm[:, 98 + h: 99 + h]

        wseq = []

        def ld_in(b):
            def f(slot, key, chan):
                S.dma('pool', chan, slot[:, 0:4096].rearrange("p (k n) -> p k n", k=8),
                      w_in[:, b * 512:(b + 1) * 512].rearrange("(k p) n -> p k n", p=128), writes=[key])
            return f

        def ld_kv():
            def f(slot, key, chan):
                S.dma('pool', chan, slot[:, 0:4096].rearrange("p (k n) -> p k n", k=8),
                      w_kv.rearrange("(k p) n -> p k n", p=128), writes=[key])
            return f

        def ld_out(c):
            def f(slot, key, chan):
                S.dma('pool', chan, slot[:, 0:4096].rearrange("p (k n) -> p k n", k=8),
                      w_out[:, c * 512:(c + 1) * 512].rearrange("(k p) n -> p k n", p=128), writes=[key])
            return f

        def ld_up(r):
            def f(slot, key, chan):
                v = slot[:, 0:4096].rearrange("p (k n) -> p k n", k=8)
                S.dma('pool', chan, v[:, :, 0:256],
                      w_up[:, r * 256:(r + 1) * 256].rearrange("(k p) n -> p k n", p=128), writes=[key])
                S.dma('pool', chan, v[:, :, 256:512],
                      w_up[:, 2816 + r * 256:2816 + (r + 1) * 256].rearrange("(k p) n -> p k n", p=128), writes=[key])
            return f

        def ld_dn(q):
            def f(slot, key, chan):
                S.dma('pool', chan, slot[:, 0:5632].rearrange("p (k n) -> p k n", k=22),
                      w_dn[:, q * 256:(q + 1) * 256].rearrange("(k p) n -> p k n", p=128), writes=[key])
            return f

        wscr = nc.dram_tensor("wscr", [22, 128, 5632], BF16, kind="Internal").ap()
        blocks = [('in', b) for b in range(5)] + [('out', c) for c in range(2)] + [('up', r) for r in range(11)] + [('dn', q) for q in range(4)]
        mk = {'in': ld_in, 'out': ld_out, 'up': ld_up, 'dn': ld_dn}
        for ti in range(5):
            for bi, (kind, idx) in enumerate(blocks):
                n = 5632 if kind == 'dn' else 4096
                if ti == 0:
                    wseq.append((mk[kind](idx), bi, n))
                    if bi == 2:
                        wseq.append((ld_kv(), None, 0))
                else:
                    def f(slot, key, chan, bi=bi, n=n):
                        S.dma('pool', chan, slot[:, 0:n], wscr[bi, :, 0:n], reads=[('scr', bi)], writes=[key])
                    wseq.append((f, None, n))
        wstate = {'issued': 0, 'next': 0}

        def wneed(prefetch=True):
            i = wstate['next']
            wstate['next'] += 1
            upto = min(i + NSLOT - 1, len(wseq) - 1) if prefetch else i
            while wstate['issued'] <= upto:
                j = wstate['issued']
                wseq[j][0](ring[j % NSLOT], ('w', j % NSLOT), 'w%d' % (j % NSLOT))
                wstate['issued'] += 1
            sl = ring[i % NSLOT]
            bi, n = wseq[i][1], wseq[i][2]
            if bi is not None:
                S.dma('sp', 'wb%d' % (i % NSLOT), wscr[bi, :, 0:n], sl[:, 0:n], reads=[('w', i % NSLOT)], writes=[('scr', bi)])
            return sl, ('w', i % NSLOT)

        def w8(sl):
            return sl[:, 0:4096].rearrange("p (k n) -> p k n", k=8)

        def w22(sl):
            return sl[:, 0:5632].rearrange("p (k n) -> p k n", k=22)

        def mmg(out_ap, pskey, pairs, reads, per=None):
            n = len(pairs)
            for i, (l, r) in enumerate(pairs):
                S.op('pe', lambda e, l=l, r=r, i=i: e.matmul(out_ap, lhsT=l, rhs=r, start=(i == 0), stop=(i == n - 1)),
                     reads=list(reads) + (list(per[i]) if per else []), writes=[pskey], inc=(i == n - 1))

        pst = ExitStack(); sst = ExitStack()
        es.enter_context(pst); es.enter_context(sst)
        try:
            S.dma('sp', 'c0', cst[:], cst_in[:, :], writes=['cst'])
            S.dma('sp', 'c1', prm_st[:], prm_in[:, :], writes=['prm_st'])
            S.dma('sp', 'c4', gf[:], lnf.partition_broadcast(128), writes=['gf'])
            S.op('dve', lambda e: e.memset(small[:, 0:1], EPS), writes=['small'])
            S.op('dve', lambda e: e.memset(small[:, 1:2], -0.5), writes=['small'])
            S.op('dve', lambda e: e.memset(small[:, 2:3], 1.0), writes=['small'])
            S.op('dve', lambda e: e.memset(onesb[:], 1.0), writes=['onesb'])
            S.op('dve', lambda e: e.tensor_copy(out=idb[:], in_=ident), reads=['cst'], writes=['idb'])
            S.op('pool', lambda e: e.memset(wbd[:], 0.0), writes=['wbd'])
            for gi in range(4):
                c, o = gi // 2, (gi % 2) * 64
                S.dma('pool', 'c5', wbd[o:o + 64, c, o:o + 64], pool_w[gi, :, :], writes=['wbd'])
            p_, pk = PS()
            S.op('pe', lambda e: e.transpose(out=p_[:, 0:126], in_=prm_st[:, :], identity=cst[0:126, 0:126]),
                 reads=['prm_st', 'cst'], writes=[pk])
            S.op('dve', lambda e: e.tensor_copy(out=prm[:, 0:126], in_=p_[:, 0:126]), reads=[pk], writes=['prm'])
            S.op('dve', lambda e: e.tensor_sub(out=lbc[:, 8:12], in0=prm[:, 88:92], in1=prm[:, 92:96]), reads=['prm'], writes=['lbc'])
            S.op('act', lambda e: e.activation(out=lbc[:, 8:12], in_=lbc[:, 8:12], func=AF.Tanh, scale=0.5), reads=['lbc'], writes=['lbc'])
            S.op('dve', lambda e: e.tensor_scalar(out=lbc[:, 0:4], in0=lbc[:, 8:12], scalar1=0.25, scalar2=0.75, op0=ALU.mult, op1=ALU.add), reads=['lbc'], writes=['lbc'])
            S.op('dve', lambda e: e.tensor_scalar(out=lbc[:, 4:8], in0=lbc[:, 8:12], scalar1=-0.25, scalar2=0.25, op0=ALU.mult, op1=ALU.add), reads=['lbc'], writes=['lbc'])
            S.op('dve', lambda e: e.tensor_scalar(out=lbc[:, 12:16], in0=lbc[:, 8:12], scalar1=0.25, scalar2=-0.25, op0=ALU.mult, op1=ALU.add), reads=['lbc'], writes=['lbc'])

            def rms_to_T(src_ap, gcol, dstT, col0, rd, wr_extra=()):
                hb = G['hb']
                S.op('act', lambda e: e.activation(out=hb[:], in_=src_ap, func=AF.Square, accum_out=stat[:, 0:1]),
                     reads=rd, writes=['hb', 'stat'])
                S.op('dve', lambda e: e.tensor_scalar(out=stat[:, 1:2], in0=stat[:, 0:1], scalar1=1.0 / 1024, scalar2=EPS, op0=ALU.mult, op1=ALU.add),
                     reads=['stat'], writes=['stat'])
                S.op('pool', lambda e: e.tensor_tensor(out=stat[:, 2:3], in0=stat[:, 1:2], in1=mhalf, op=ALU.pow),
                     reads=['stat', 'small'], writes=['stat'])
                chk('rms_a')
                S.op('act', lambda e: e.activation(out=hb[:], in_=src_ap, func=AF.Copy, scale=stat[:, 2:3]),
                     reads=list(rd) + ['stat'], writes=['hb'])
                chk('rms_b')
                p, k = PS()
                pb = p[:].bitcast(BF16)
                for kk in range(8):
                    S.op('pe', lambda e, kk=kk: e.transpose(out=pb[:, kk * 128:(kk + 1) * 128], in_=hb[:, kk * 128:(kk + 1) * 128], identity=idb[:]),
                         reads=['hb', 'idb'], writes=[k], inc=(kk == 7))
                chk('rms_c')
                S.op('dve', lambda e: e.tensor_tensor(out=dstT[:, :, col0:col0 + 128], in0=pb.rearrange("p (k n) -> p k n", k=8),
                                                      in1=prm[:, gcol:gcol + 8].unsqueeze(2).to_broadcast([128, 8, 128]), op=ALU.mult),
                     reads=[k, 'prm'], writes=list(wr_extra))
                chk('rms_d')

            def rms_multi(items, gcol, dstT, wr, base=0, phase='all'):
                n = len(items)
                hbs = G['hbs']
                if phase in ('all', 'stats'):
                    for i_, (src, rd, col0) in enumerate(items):
                        S.op('act', lambda e, i_=i_, src=src: e.activation(out=hbs[(base + i_) % 2][:], in_=src, func=AF.Square, accum_out=stat[:, base + i_:base + i_ + 1]),
                             reads=rd, writes=['hb%d' % ((base + i_) % 2), 'stat'])
                    S.op('act', lambda e: e.activation(out=stat[:, 4 + base:4 + base + n], in_=stat[:, base:base + n], func=AF.Ln, scale=1.0 / 1024, bias=epsc), reads=['stat', 'small'], writes=['stat'])
                    S.op('act', lambda e: e.activation(out=stat[:, 8 + base:8 + base + n], in_=stat[:, 4 + base:4 + base + n], func=AF.Exp, scale=-0.5), reads=['stat'], writes=['stat'])
                if phase == 'stats':
                    return
                for i_, (src, rd, col0) in enumerate(items):
                    hb = hbs[(base + i_) % 2]
                    S.op('act', lambda e, i_=i_, src=src, hb=hb: e.activation(out=hb[:], in_=src, func=AF.Copy, scale=stat[:, 8 + base + i_:9 + base + i_]),
                         reads=list(rd) + ['stat'], writes=['hb%d' % ((base + i_) % 2)])
                    p, k = PS()
                    pb = p[:].bitcast(BF16)
                    for kk in range(8):
                        S.op('pe', lambda e, kk=kk, hb=hb, pb=pb: e.transpose(out=pb[:, kk * 128:(kk + 1) * 128], in_=hb[:, kk * 128:(kk + 1) * 128], identity=idb[:]),
                             reads=['hb%d' % ((base + i_) % 2), 'idb'], writes=[k], inc=(kk == 7))
                    S.op('dve', lambda e, col0=col0, pb=pb: e.tensor_tensor(out=dstT[:, :, col0:col0 + 128], in0=pb.rearrange("p (k n) -> p k n", k=8),
                                                                          in1=prm[:, gcol:gcol + 8].unsqueeze(2).to_broadcast([128, 8, 128]), op=ALU.mult),
                         reads=[k, 'prm'], writes=list(wr))

            def alloc_phase(st, T, sample):
                B = {}
                B['T'] = T
                ns = T // 128
                G['xres'] = sbt(st, "xres", [128, ns, 1024]); G['hT'] = sbt(st, "hT", [128, 8, T], BF16)
                G['mixT'] = sbt(st, "mixT", [128, 8, T], BF16); G['mT'] = sbt(st, "mT", [128, 22, T], BF16)
                G['hbs'] = [sbt(st, "hb%d" % i, [128, 1024], BF16) for i in range(2)]
                G['hb'] = G['hbs'][0]
                B['u'] = sbt(st, "u_sb", [128, 2, 15 + T]) if not sample else None
                B['pA'] = sbt(st, "pA", [128, max(16 + T, 368)]); B['pB'] = sbt(st, "pB", [128, max(16 + T, 368)])
                B['d'] = sbt(st, "d_sb", [128, 2, T], BF16)
                B['Ab'] = [sbt(st, "A_sb%d" % i, [128, T]) for i in range(2)]
                B['E2b'] = [sbt(st, "E2_%d" % i, [128, T]) for i in range(2)]
                B['K1b'] = [sbt(st, "K1_%d" % i, [128, T]) for i in range(2)]
                B['QSall'] = sbt(st, "QSall", [128, 4, T]); B['THall'] = sbt(st, "THall", [128, 4, T])
                B['GS'] = sbt(st, "GS", [128, 4, T])
                B['qAT'] = sbt(st, "qAT", [128, 4, T], BF16); B['kAT'] = sbt(st, "kAT", [128, 4, T], BF16)
                B['kAk'] = sbt(st, "kAk", [128, T // 128, 512], BF16); B['vtk'] = sbt(st, "vtk", [128, T // 128, 512], BF16)
                B['eAe'] = sbt(st, "eAe", [128, 4, 16])
                B['PT'] = sbt(st, "PT", [128, 4, 128], BF16)
                B['osb'] = sbt(st, "osb", [128, 4, T]); B['osq'] = sbt(st, "osq", [128, T], BF16)
                B['R'] = sbt(st, "R_sb", [128, T]); B['t1'] = sbt(st, "t1", [128, T])
                B['qxT'] = sbt(st, "qxT", [128, 2, T], BF16)
                B['cbuf'] = [B['QSall'][:, i, :] for i in range(2)]
                B['gbuf'] = [B['QSall'][:, 2 + i, :] for i in range(2)]
                return B

            def do_tile(B, t0, first, sample, X):
                T = B['T']; nsub = T // 128
                xres, hT, mixT, mT = G['xres'], G['hT'], G['mixT'], G['mT']
                for sub in range(nsub):
                    S.dma('sp', 'xin%d' % sub, xres[:, sub, :], x_tok[t0 + sub * 128:t0 + (sub + 1) * 128, :], writes=['xres%d' % sub])
                if first and not sample:
                    X['memdma']()
                if sample:
                    X['pre1']()
                if not X.get('prenormed'):
                    rms_multi([(xres[:, sub, :], ['xres%d' % sub], sub * 128) for sub in range(nsub)], 102, hT, ['hT'])
                X['prenormed'] = False
                chk('norm1')
                QSall, THall, GS = B['QSall'], B['THall'], B['GS']
                done = {'pool': False, 'attn': False, 'inproj': False}

                def g_inproj(b0, b1):
                    for b in range(b0, b1):
                        wsl, wk = wneed()
                        wv = w8(wsl)
                        for jj in range(4):
                            j = b * 4 + jj
                            if 10 <= j < 14:
                                continue
                            yield
                            p, k = PS()
                            mmg(p[:, 0:T], k, [(wv[:, kk, jj * 128:(jj + 1) * 128], hT[:, kk, 0:T]) for kk in range(8)], ['hT', wk])
                            src = p[:, 0:T]
                            if j < 2:
                                if sample:
                                    S.op('act', lambda e, j=j, src=src: e.activation(out=X['xp'][:, j, :, 15:23], in_=src.rearrange("p (b t) -> p b t", t=8), func=AF.Copy),
                                         reads=[k], writes=['xp'])
                                else:
                                    S.op('act', lambda e, j=j, src=src: e.activation(out=B['u'][:, j, 15:15 + T], in_=src, func=AF.Copy), reads=[k], writes=['u'])
                            elif j < 6:
                                S.op('act', lambda e, j=j, src=src: e.activation(out=QSall[:, j - 2, :], in_=src, func=AF.Silu), reads=[k], writes=['QS%d' % (j - 2)])
                            elif j < 10:
                                S.op('act', lambda e, j=j, src=src: e.activation(out=THall[:, j - 6, :], in_=src, func=AF.Tanh, scale=0.5), reads=[k], writes=['TH%d' % (j - 6)])
                            elif j < 18:
                                h = j - 14
                                S.op('act', lambda e, h=h, src=src: e.activation(out=GS[:, h, :], in_=src, func=AF.Copy), reads=[k], writes=['GS%d' % h])
                            else:
                                S.op('act', lambda e, j=j, src=src: e.activation(out=B['qxT'][:, j - 18, :], in_=src, func=AF.Copy), reads=[k], writes=['qxT'])
                        if b in (2, 3):
                            c0 = 256 if b == 2 else 0
                            o0 = 0 if b == 2 else 256
                            for sub in range(nsub):
                                yield
                                p, k = PS()
                                mmg(p[:, 0:256], k, [(hT[:, kk, sub * 128:(sub + 1) * 128], wv[:, kk, c0:c0 + 256]) for kk in range(8)], ['hT', wk])
                                S.op('dve', lambda e, p=p, sub=sub, o0=o0: e.tensor_copy(out=B['vtk'][:, sub, o0:o0 + 256], in_=p[:, 0:256]), reads=[k], writes=['vtk'])

                    if b1 == 5:
                        done['inproj'] = True
                    yield

                for _ in g_inproj(0, 3):
                    pass
                if first and not sample:
                    X['memkv']()
                chk('inproj')
                kgen = X['pre2']() if sample else iter(())

                def kstep(n=1):
                    for _ in range(n):
                        next(kgen, None)
                kstep(2)
                def g_pool():
                    pA, pB, dsb = B['pA'], B['pB'], B['d']
                    Rb = G['hbs'][0][:].bitcast(F32)
                    for c in range(2):
                        kstep(1)
                        if sample:
                            Xv = X['xp'][:, c, :, :]
                            L = 23
                            sl = lambda buf, n: buf[:, 0:16 * n].rearrange("p (b n) -> p b n", b=16)
                            xs = lambda a, b_: Xv[:, :, a:b_]
                            xkey = 'xp'
                        else:
                            u = B['u']
                            if first and c == 0:
                                yield
                                S.op('dve', lambda e: e.memset(u[:, :, 0:15], 0.0), writes=['u'])
                            L = 15 + T
                            sl = lambda buf, n: buf[:, 0:n]
                            xs = lambda a, b_, c=c: u[:, c, a:b_]
                            xkey = 'u'
                        sv = lambda buf, n, a, b_: (sl(buf, n)[:, :, a:b_] if sample else sl(buf, n)[:, a:b_])
                        yield
                        S.op('dve', lambda e: e.tensor_tensor(out=sl(pA, L - 1), in0=xs(1, L), in1=xs(0, L - 1), op=ALU.add), reads=[xkey], writes=['pA'])
                        yield
                        S.op('dve', lambda e: e.tensor_tensor(out=sl(pB, L - 3), in0=sv(pA, L - 1, 2, L - 1), in1=sv(pA, L - 1, 0, L - 3), op=ALU.add), reads=['pA'], writes=['pB'])
                        uview = xs(15, L)
                        dv = dsb[:, c, :].rearrange("p (b t) -> p b t", t=8) if sample else dsb[:, c, :]

                        def comb(plo, phi, buf, n, off, wsel, dv=dv, uview=uview):
                            S.op('dve', lambda e: e.scalar_tensor_tensor(out=dv[plo:phi], in0=sv(buf, n, off, off + (8 if sample else T))[plo:phi], scalar=invw[plo:phi, c:c + 1],
                                                                          in1=uview[plo:phi], op0=ALU.mult, op1=ALU.subtract),
                                 reads=[wsel, xkey, 'cst'], writes=['d'])
                        if c == 0:
                            yield
                            comb(0, 64, pA, L - 1, 14, 'pA')
                            yield
                            comb(64, 128, pB, L - 3, 12, 'pB')
                        else:
                            yield
                            S.op('dve', lambda e: e.tensor_tensor(out=sl(pA, L - 7), in0=sv(pB, L - 3, 4, L - 3), in1=sv(pB, L - 3, 0, L - 7), op=ALU.add), reads=['pB'], writes=['pA'])
                            yield
                            comb(0, 64, pA, L - 7, 8, 'pA')
                            yield
                            S.op('dve', lambda e: e.tensor_tensor(out=sl(pB, L - 15), in0=sv(pA, L - 7, 8, L - 7), in1=sv(pA, L - 7, 0, L - 15), op=ALU.add), reads=['pA'], writes=['pB'])
                            yield
                            comb(64, 128, pB, L - 15, 0, 'pB')
                        if first and not sample:
                            for (plo, phi, buf, off) in ((0, 64, pA, 14 if c == 0 else 8), (64, 128, pB, 12 if c == 0 else 0)):
                                yield
                                S.op('dve', lambda e, plo=plo, phi=phi, buf=buf, off=off: e.tensor_tensor(out=Rb[plo:phi, 0:15], in0=buf[plo:phi, off:off + 15],
                                                                                                          in1=invcnt[plo:phi, c * 16:c * 16 + 15], op=ALU.mult),
                                     reads=['pA', 'pB', 'cst'], writes=['hb0'])
                                yield
                                S.op('dve', lambda e, plo=plo, phi=phi: e.tensor_tensor(out=dsb[plo:phi, c, 0:15], in0=Rb[plo:phi, 0:15], in1=u[plo:phi, c, 15:30], op=ALU.subtract),
                                     reads=['hb0', 'u'], writes=['d'])
                        yield
                        p, k = PS()
                        yield
                        mmg(p[:, 0:T], k, [(wbd[:, c, :], dsb[:, c, :])], ['wbd', 'd'])
                        yield
                        S.op('act', lambda e, p=p, c=c: e.activation(out=mixT[:, c, 0:T], in_=p[:, 0:T], func=AF.Identity, scale=pscale(c)), reads=[k, 'prm'], writes=['mixT%d' % c])
                    if not sample:
                        u = B['u']
                        if X.get('last'):
                            for c in range(2):
                                yield
                                p, k = PS()
                                yield
                                S.op('pe', lambda e, p=p, c=c: e.transpose(out=p[0:15, c * 128:(c + 1) * 128], in_=u[:, c, T:T + 15], identity=ident), reads=['u', 'cst'], writes=[k])
                                yield
                                S.op('dve', lambda e, p=p, c=c: e.tensor_copy(out=X['ppo'][0:15, c * 128:(c + 1) * 128], in_=p[0:15, c * 128:(c + 1) * 128]), reads=[k], writes=['ppo'])
                            S.dma('sp', 'o_pp', o_pool_p[:, :], X['ppo'][0:15, :], reads=['ppo'])
                        else:
                            yield
                            S.op('dve', lambda e: e.tensor_copy(out=u[:, :, 0:15], in_=u[:, :, T:T + 15]), reads=['u'], writes=['u'])

                    yield
                def g_hgrn():
                    qAT, kAT, eAe = B['qAT'], B['kAT'], B['eAe']
                    smk = rmask if sample else X['cm512'][:, :]
                    smkey = 'cst' if sample else 'cm512'

                    def stA(h):
                        par = h % 2
                        K1 = B['K1b'][par]
                        thk = 'TH%d' % h
                        S.op('act', lambda e: e.activation(out=K1[:], in_=THall[:, h, :], func=AF.Identity, scale=lbc[:, 12 + h:13 + h], bias=lbc[:, 4 + h:5 + h]),
                             reads=[thk, 'lbc'], writes=['K1%d' % par])
                        S.op('act', lambda e: e.activation(out=THall[:, h, :], in_=THall[:, h, :], func=AF.Ln, scale=lbc[:, 4 + h:5 + h], bias=lbc[:, h:h + 1]),
                             reads=[thk, 'lbc'], writes=[thk])

                    def stB(h):
                        par = h % 2
                        A = B['Ab'][par]
                        S.op('dve', lambda e: e.tensor_tensor_scan(out=A[:], data0=smk, data1=THall[:, h, :], initial=0.0, op0=ALU.mult, op1=ALU.add),
                             reads=['TH%d' % h, smkey], writes=['A%d' % par])

                    def stC(h):
                        par = h % 2
                        A, E2 = B['Ab'][par], B['E2b'][par]
                        S.op('act', lambda e: e.activation(out=E2[:], in_=A[:], func=AF.Exp, scale=-1.0), reads=['A%d' % par], writes=['E2%d' % par])
                        S.op('act', lambda e: e.activation(out=A[:], in_=A[:], func=AF.Exp), reads=['A%d' % par], writes=['A%d' % par])

                    def stD(h):
                        par = h % 2
                        A, E2, K1 = B['Ab'][par], B['E2b'][par], B['K1b'][par]
                        ka, ke, kk1 = 'A%d' % par, 'E2%d' % par, 'K1%d' % par
                        S.op('dve', lambda e: e.tensor_tensor(out=qAT[:, h, :], in0=QSall[:, h, :], in1=A[:], op=ALU.mult), reads=['QS%d' % h, ka], writes=['qAT'])
                        S.op('dve', lambda e: e.tensor_tensor(out=kAT[:, h, :], in0=K1[:], in1=E2[:], op=ALU.mult), reads=[kk1, ke], writes=['kAT'])
                        if sample:
                            S.op('dve', lambda e: e.tensor_copy(out=eAe[:, h, 0:16], in_=A[:].rearrange("p (b t) -> p b t", t=8)[:, :, 7]), reads=[ka], writes=['eAe'])
                        else:
                            S.op('dve', lambda e: e.tensor_copy(out=eAe[:, h, 0:nsub], in_=A[:].rearrange("p (c t) -> p c t", t=128)[:, :, 127]), reads=[ka], writes=['eAe'])

                    stA(0)
                    yield
                    stB(0)
                    yield
                    kstep()
                    stA(1)
                    yield
                    stC(0)
                    yield
                    stB(1)
                    yield
                    kstep()
                    stD(0)
                    yield
                    stA(2)
                    yield
                    stC(1)
                    yield
                    stB(2)
                    yield
                    kstep()
                    stD(1)
                    yield
                    stA(3)
                    yield
                    stC(2)
                    yield
                    stB(3)
                    yield
                    kstep()
                    stD(2)
                    yield
                    stC(3)
                    yield
                    stD(3)
                    yield
                    kstep(8)
                    chk('hgrn_ew')
                    while not done['inproj']:
                        yield
                    for h in range(4):
                        S.op('act', lambda e, h=h: e.activation(out=GS[:, h, :], in_=GS[:, h, :], func=AF.Silu), reads=['GS%d' % h], writes=['GS%d' % h])
                    for cc in range(nsub):
                        yield
                        p, k = PS()
                        pb = p[:].bitcast(BF16)
                        for h in range(4):
                            yield
                            S.op('pe', lambda e, h=h, cc=cc, pb=pb: e.transpose(out=pb[:, h * 128:(h + 1) * 128], in_=kAT[:, h, cc * 128:(cc + 1) * 128], identity=idb[:]),
                                 reads=['kAT', 'idb'], writes=[k], inc=(h == 3))
                        yield
                        S.op('act', lambda e, cc=cc, pb=pb: e.activation(out=B['kAk'][:, cc, :], in_=pb[:, 0:512], func=AF.Copy), reads=[k], writes=['kAk%d' % cc])
                    chk('katr')
                    PT, osb, kAk, vtk = B['PT'], B['osb'], B['kAk'], B['vtk']
                    for cc in range(nsub):
                        cs = slice(cc * 128, (cc + 1) * 128)
                        yield
                        pS, kS = PS()
                        for h in range(4):
                            yield
                            mmg(pS[:, h * 128:(h + 1) * 128], kS, [(kAT[:, h, cs], qAT[:, h, cs])], ['kAT', 'qAT'])
                        msk = smask if sample else cmask
                        yield
                        S.op('dve', lambda e, pS=pS, msk=msk: e.tensor_tensor(out=PT[:], in0=pS[:, :].rearrange("p (h t) -> p h t", h=4),
                                                                              in1=msk.unsqueeze(1).to_broadcast([128, 4, 128]), op=ALU.mult),
                             reads=[kS, 'cst'], writes=['PT'])
                        yield
                        pO, kO = PS()
                        if not sample:
                            Sf, Sb = X['Sf'], X['Sb']
                            for h in range(4):
                                yield
                                mmg(pO[:, h * 128:(h + 1) * 128], kO, [(vtk[:, cc, h * 128:(h + 1) * 128], PT[:, h, :]), (Sb[:, h, :], qAT[:, h, cs])],
                                    [], per=[['vtk', 'PT'], ['Sb', 'qAT']])
                            yield
                            S.op('act', lambda e, pO=pO, cs=cs: e.activation(out=osb[:, :, cs], in_=pO[:, :].rearrange("p (h t) -> p h t", h=4), func=AF.Copy), reads=[kO], writes=['osb'])
                            yield
                            pZ, kZ = PS()
                            for h in range(4):
                                yield
                                mmg(pZ[:, h * 128:(h + 1) * 128], kZ, [(kAk[:, cc, h * 128:(h + 1) * 128], vtk[:, cc, h * 128:(h + 1) * 128])], ['kAk%d' % cc, 'vtk'])
                            yield
                            S.op('dve', lambda e, pZ=pZ: e.tensor_tensor(out=Sf[:], in0=Sf[:], in1=pZ[:, :].rearrange("p (h v) -> p h v", h=4), op=ALU.add), reads=[kZ, 'Sf'], writes=['Sf'])
                            yield
                            S.op('dve', lambda e, cc=cc: e.tensor_tensor(out=Sf[:], in0=Sf[:], in1=eAe[:, :, cc:cc + 1].to_broadcast([128, 4, 128]), op=ALU.mult), reads=['Sf', 'eAe'], writes=['Sf'])
                            yield
                            S.op('act', lambda e: e.activation(out=Sb[:], in_=Sf[:], func=AF.Copy), reads=['Sf'], writes=['Sb'])
                        else:
                            S0f, S0b = X['S0f'], X['S0b']
                            for h in range(4):
                                pairs = [(vtk[:, 0, h * 128:(h + 1) * 128], PT[:, h, :])]
                                n = 17
                                yield
                                S.op('pe', lambda e, h=h: e.matmul(pO[:, h * 128:(h + 1) * 128], lhsT=vtk[:, 0, h * 128:(h + 1) * 128], rhs=PT[:, h, :], start=True, stop=False),
                                     reads=['vtk', 'PT'], writes=[kO], inc=False)
                                for bq in range(16):
                                    yield
                                    S.op('pe', lambda e, h=h, bq=bq: e.matmul(pO[:, h * 128 + bq * 8:h * 128 + bq * 8 + 8], lhsT=S0b[:, bq, h, :], rhs=qAT[:, h, bq * 8:bq * 8 + 8],
                                                                             start=False, stop=(bq == 15)),
                                         reads=['S0b', 'qAT'], writes=[kO], inc=(bq == 15))
                            yield
                            S.op('act', lambda e, pO=pO: e.activation(out=osb[:, :, 0:128], in_=pO[:, :].rearrange("p (h t) -> p h t", h=4), func=AF.Copy), reads=[kO], writes=['osb'])
                            Vb2 = X['Vblk2']

                            def stV(i):
                                h, bg = i // 4, i % 4
                                vb = Vb2[i % 2]
                                S.op('dve', lambda e: e.tensor_tensor(out=vb[:], in0=vtk[:, 0, h * 128:(h + 1) * 128].unsqueeze(1).to_broadcast([128, 4, 128]),
                                                                      in1=seqm[:, bg * 4:bg * 4 + 4].unsqueeze(2).to_broadcast([128, 4, 128]), op=ALU.mult),
                                     reads=['vtk', 'cst'], writes=['Vblk%d' % (i % 2)])

                            def stU(i):
                                h, bg = i // 4, i % 4
                                vb = Vb2[i % 2]
                                pZ, kZ = PS()
                                mmg(pZ[:, :], kZ, [(kAk[:, 0, h * 128:(h + 1) * 128], vb[:].rearrange("p b v -> p (b v)"))], ['kAk0', 'Vblk%d' % (i % 2)])
                                S.op('dve', lambda e: e.tensor_tensor(out=S0f[:, bg * 4:bg * 4 + 4, h, :], in0=S0f[:, bg * 4:bg * 4 + 4, h, :],
                                                                      in1=pZ[:, :].rearrange("p (b v) -> p b v", b=4), op=ALU.add),
                                     reads=[kZ, 'S0f'], writes=['S0f'])
                                S.op('dve', lambda e: e.tensor_tensor(out=S0f[:, bg * 4:bg * 4 + 4, h, :], in0=S0f[:, bg * 4:bg * 4 + 4, h, :],
                                                                      in1=eAe[:, h, bg * 4:bg * 4 + 4].unsqueeze(2).to_broadcast([128, 4, 128]), op=ALU.mult),
                                     reads=['S0f', 'eAe'], writes=['S0f'])
                            yield
                            stV(0)
                            for i in range(16):
                                if i + 1 < 16:
                                    yield
                                    stV(i + 1)
                                yield
                                stU(i)
                            S.dma('sp', 'o_hs', o_hgrn_s.rearrange("b h d v -> d b h v"), S0f[:], reads=['S0f'])
                    if (not sample) and X.get('last'):
                        S.dma('sp', 'o_hp', o_hgrn_p.rearrange("h d v -> d h v"), X['Sf'][:], reads=['Sf'])
                    chk('hgrn')
                    osq, R, t1 = B['osq'], B['R'], B['t1']
                    for h in range(4):
                        yield
                        S.op('act', lambda e, h=h: e.activation(out=osq[:], in_=osb[:, h, :], func=AF.Square), reads=['osb'], writes=['osq'])
                        yield
                        p, k = PS()
                        yield
                        mmg(p[:, 0:T], k, [(onesb[:], osq[:])], ['onesb', 'osq'])
                        yield
                        S.op('act', lambda e, p=p: e.activation(out=R[:], in_=p[:, 0:T], func=AF.Ln, scale=1.0 / 128, bias=epsc), reads=[k, 'small'], writes=['R'])
                        yield
                        S.op('act', lambda e: e.activation(out=R[:], in_=R[:], func=AF.Exp, scale=-0.5), reads=['R'], writes=['R'])
                        yield
                        S.op('dve', lambda e, h=h: e.tensor_tensor(out=t1[:], in0=osb[:, h, :], in1=R[:], op=ALU.mult), reads=['osb', 'R'], writes=['t1'])
                        yield
                        S.op('dve', lambda e, h=h: e.scalar_tensor_tensor(out=mixT[:, 2 + h, 0:T], in0=t1[:], scalar=onorm(h), in1=GS[:, h, :], op0=ALU.mult, op1=ALU.mult),
                             reads=['t1', 'GS%d' % h, 'prm'], writes=['mixT%d' % (2 + h)])

                    yield
                def g_attn():
                    qxT = B['qxT']
                    Ra = G['hbs'][1][:].bitcast(F32); Rb = G['hbs'][0][:].bitcast(F32)
                    while not done['inproj']:
                        yield
                    if sample:
                        for _ in range(24):
                            yield
                        kstep(8)
                    if not sample:
                        PTa = X['PTa']
                        for pr in range(2):
                            for hh in range(2):
                                h = pr * 2 + hh
                                rows = slice(hh * 64, hh * 64 + 64)
                                for mc in range(2):
                                    yield
                                    p, k = PS()
                                    yield
                                    mmg(p[:, 0:T], k, [(KT[rows, pr, mc * 128:(mc + 1) * 128], qxT[rows, pr, :])], ['KT', 'qxT'])
                                    yield
                                    S.op('act', lambda e, p=p, hh=hh, mc=mc: e.activation(out=PTa[:, hh, mc, :], in_=p[:, 0:T], func=AF.Exp, scale=0.125), reads=[k], writes=['PTa'])
                            yield
                            pO, kO = PS()
                            yield
                            pD, kD = PS()
                            for hh in range(2):
                                h = pr * 2 + hh
                                rows = slice(hh * 64, hh * 64 + 64)
                                for mc in range(2):
                                    yield
                                    S.op('pe', lambda e, hh=hh, mc=mc, h=h, rows=rows, pO=pO: e.matmul(pO[rows, 0:T], lhsT=Vb[:, mc, h * 64:(h + 1) * 64], rhs=PTa[:, hh, mc, :],
                                                                                                        start=(mc == 0), stop=(mc == 1)),
                                         reads=['Vb', 'PTa'], writes=[kO], inc=(mc == 1))
                                for mc in range(2):
                                    yield
                                    S.op('pe', lambda e, hh=hh, mc=mc, rows=rows, pD=pD: e.matmul(pD[rows, 0:T], lhsT=onesb[:, 0:64], rhs=PTa[:, hh, mc, :],
                                                                                                  start=(mc == 0), stop=(mc == 1)),
                                         reads=['onesb', 'PTa'], writes=[kD], inc=(mc == 1))
                            yield
                            S.op('act', lambda e, pD=pD: e.activation(out=Ra[:, 0:T], in_=pD[:, 0:T], func=AF.Ln), reads=[kD], writes=['hb1'])
                            S.op('act', lambda e: e.activation(out=Ra[:, 0:T], in_=Ra[:, 0:T], func=AF.Exp, scale=-1.0), reads=['hb1'], writes=['hb1'])
                            yield
                            S.op('dve', lambda e, pO=pO, pr=pr: e.tensor_tensor(out=mixT[:, 6 + pr, 0:T], in0=pO[:, 0:T], in1=Ra[:, 0:T], op=ALU.mult), reads=[kO, 'hb1'], writes=['mixT%d' % (6 + pr)])
                    else:
                        KTs, Vs, PTs = X['KTs'], X['Vs'], X['PTs']
                        for g8 in range(2):
                            yield
                            pp = [PS(), PS()]
                            for bi in range(8):
                                bq = g8 * 8 + bi
                                for mc in range(2):
                                    for h in range(4):
                                        par = h % 2
                                        rows = slice(par * 64, par * 64 + 64)
                                        col = bi * 32 + (mc * 2 + h // 2) * 8
                                        last = (bi == 7 and mc == 1 and h >= 2)
                                        p, k = pp[par]
                                        yield
                                        S.op('pe', lambda e, bq=bq, mc=mc, h=h, rows=rows, col=col, p=p: e.matmul(p[:, col:col + 8], lhsT=KTs[rows, bq, h // 2, mc * 128:(mc + 1) * 128],
                                                                                                                 rhs=qxT[rows, h // 2, bq * 8:bq * 8 + 8], start=True, stop=True),
                                             reads=['KTs', 'qxT'], writes=[k], inc=last)
                            for par in range(2):
                                p, k = pp[par]
                                yield
                                S.op('act', lambda e, p=p, g8=g8, par=par: e.activation(out=PTs[:, par, g8 * 256:(g8 + 1) * 256], in_=p[:, 0:256], func=AF.Exp, scale=0.125), reads=[k], writes=['PTs'])
                        yield
                        pO, kO = PS(2)
                        yield
                        pD, kD = PS(2)
                        for bq in range(16):
                            for h in range(4):
                                rows = slice((h % 2) * 64, (h % 2) * 64 + 64)
                                oc = (h // 2) * 128 + bq * 8
                                for mc in range(2):
                                    col = bq * 32 + (mc * 2 + h // 2) * 8
                                    yield
                                    S.op('pe', lambda e, bq=bq, h=h, mc=mc, rows=rows, oc=oc, col=col: e.matmul(pO[rows, oc:oc + 8], lhsT=Vs[:, bq, mc, h * 64:(h + 1) * 64], rhs=PTs[:, h % 2, col:col + 8],
                                                                                                               start=(mc == 0), stop=(mc == 1)),
                                         reads=['Vs', 'PTs'], writes=[kO], inc=(bq == 15 and h == 3 and mc == 1))
                                for mc in range(2):
                                    col = bq * 32 + (mc * 2 + h // 2) * 8
                                    yield
                                    S.op('pe', lambda e, bq=bq, h=h, mc=mc, rows=rows, oc=oc, col=col: e.matmul(pD[rows, oc:oc + 8], lhsT=onesb[:, 0:64], rhs=PTs[:, h % 2, col:col + 8],
                                                                                                               start=(mc == 0), stop=(mc == 1)),
                                         reads=['onesb', 'PTs'], writes=[kD], inc=(bq == 15 and h == 3 and mc == 1))
                        yield
                        S.op('act', lambda e: e.activation(out=Ra[:, 0:128], in_=pD[:, 0:128], func=AF.Ln), reads=[kD], writes=['hb1'])
                        S.op('act', lambda e: e.activation(out=Ra[:, 0:128], in_=Ra[:, 0:128], func=AF.Exp, scale=-1.0), reads=['hb1'], writes=['hb1'])
                        yield
                        S.op('dve', lambda e: e.tensor_tensor(out=mixT[:, 6, 0:128], in0=pO[:, 0:128], in1=Ra[:, 0:128], op=ALU.mult), reads=[kO, 'hb1'], writes=['mixT6'])
                        yield
                        S.op('act', lambda e: e.activation(out=Rb[:, 0:128], in_=pD[:, 128:256], func=AF.Ln), reads=[kD], writes=['hb0'])
                        S.op('act', lambda e: e.activation(out=Rb[:, 0:128], in_=Rb[:, 0:128], func=AF.Exp, scale=-1.0), reads=['hb0'], writes=['hb0'])
                        yield
                        S.op('dve', lambda e: e.tensor_tensor(out=mixT[:, 7, 0:128], in0=pO[:, 128:256], in1=Rb[:, 0:128], op=ALU.mult), reads=[kO, 'hb0'], writes=['mixT7'])

                    yield
                gens = [(g_inproj(3, 5), 2), (g_hgrn(), 3), (g_pool(), 1), (g_attn(), 1)]
                while gens:
                    for ge in list(gens):
                        for _ in range(ge[1]):
                            try:
                                next(ge[0])
                            except StopIteration:
                                gens.remove(ge)
                                break
                chk('pool')
                chk('hgrn_o')
                chk('attn')
                wo = [wneed(), wneed(prefetch=False)]
                for sub in range(nsub):
                    for c in range(2):
                        wsl, wk = wo[c]
                        wv = w8(wsl)
                        p, k = PS()
                        mmg(p[:, :], k, [(mixT[:, kk, sub * 128:(sub + 1) * 128], wv[:, kk, :]) for kk in (0, 1, 6, 7, 2, 3, 4, 5)], [wk], per=[['mixT%d' % kk] for kk in (0, 1, 6, 7, 2, 3, 4, 5)])
                        S.op('dve', lambda e, p=p, sub=sub, c=c: e.tensor_tensor(out=xres[:, sub, c * 512:(c + 1) * 512], in0=p[:, :], in1=xres[:, sub, c * 512:(c + 1) * 512], op=ALU.add),
                             reads=[k, 'xres%d' % sub], writes=['xres%d' % sub])
                    if sub >= 1:
                        rms_multi([(xres[:, sub - 1, :], ['xres%d' % (sub - 1)], (sub - 1) * 128)], 110, hT, ['hT'], base=sub - 1)
                rms_multi([(xres[:, nsub - 1, :], ['xres%d' % (nsub - 1)], (nsub - 1) * 128)], 110, hT, ['hT'], base=nsub - 1)
                chk('outproj')
                chk('norm2')
                pend2 = []
                nxt = X.get('nxt')
                if nxt is not None:
                    GSf = B['GS'][:].rearrange("p h t -> p (h t)"); osf = B['osb'][:].rearrange("p h t -> p (h t)")
                    xn = [GSf[:, 0:1024], GSf[:, 1024:2048], osf[:, 0:1024], osf[:, 1024:2048]]
                    xnk = [['GS0', 'GS1'], ['GS2', 'GS3'], ['osb'], ['osb']]
                    for s_ in range(4):
                        S.dma('sp', 'xn%d' % s_, xn[s_], x_tok[nxt + s_ * 128:nxt + (s_ + 1) * 128, :], writes=xnk[s_])
                for r in range(11):
                    wsl, wk = wneed()
                    wv = w8(wsl)
                    for jj in range(2):
                        j = 2 * r + jj
                        pa, ka = PS(2)
                        mmg(pa[:, 0:T], ka, [(wv[:, kk, jj * 128:(jj + 1) * 128], hT[:, kk, 0:T]) for kk in range(8)], ['hT', wk])
                        pb_, kb = PS()
                        mmg(pb_[:, 0:T], kb, [(wv[:, kk, 256 + jj * 128:256 + (jj + 1) * 128], hT[:, kk, 0:T]) for kk in range(8)], ['hT', wk])
                        cbuf = B['cbuf'][j % 2]; gbuf = B['gbuf'][j % 2]
                        ck_, gk_ = 'cbuf%d' % (j % 2), 'gbuf%d' % (j % 2)
                        if not sample:
                            asb = X['asb'][j % 2]; ak_ = 'asb%d' % (j % 2); carry = X['carry']
                            S.op('dve', lambda e, asb=asb, j=j: e.tensor_copy(out=asb[:, 0:2], in_=carry[:, j, :]), reads=['carry'], writes=[ak_])
                            S.op('act', lambda e, asb=asb, pa=pa: e.activation(out=asb[:, 2:2 + T], in_=pa[:, 0:T], func=AF.Copy), reads=[ka], writes=[ak_])
                            S.op('act', lambda e, pa=pa, j=j, cbuf=cbuf: e.activation(out=cbuf, in_=pa[:, 0:T], func=AF.Identity, scale=cw(2, j), bias=cb(j)), reads=[ka, 'prm'], writes=[ck_])
                            S.op('dve', lambda e, asb=asb, j=j: e.tensor_copy(out=carry[:, j, :], in_=asb[:, T:T + 2]), reads=[ak_], writes=['carry'])
                            S.op('dve', lambda e, asb=asb, j=j, cbuf=cbuf: e.scalar_tensor_tensor(out=cbuf, in0=asb[:, 1:1 + T], scalar=cw(1, j), in1=cbuf, op0=ALU.mult, op1=ALU.add),
                                 reads=[ak_, ck_, 'prm'], writes=[ck_])
                            S.op('dve', lambda e, asb=asb, j=j, cbuf=cbuf: e.scalar_tensor_tensor(out=cbuf, in0=asb[:, 0:T], scalar=cw(0, j), in1=cbuf, op0=ALU.mult, op1=ALU.add),
                                 reads=[ak_, ck_, 'prm'], writes=[ck_])
                        else:
                            a3 = X['a3'][j % 2]; ak_ = 'a3%d' % (j % 2); ahist, anew = X['ahist'], X['anew']
                            c3 = cbuf.rearrange("p (b t) -> p b t", t=8)
                            S.op('dve', lambda e, a3=a3, j=j: e.tensor_copy(out=a3[:, :, 0:2], in_=ahist[:, j, :, :]), reads=['ahist'], writes=[ak_])
                            S.op('act', lambda e, a3=a3, pa=pa: e.activation(out=a3[:, :, 2:10], in_=pa[:, 0:128].rearrange("p (b t) -> p b t", t=8), func=AF.Copy), reads=[ka], writes=[ak_])
                            S.op('act', lambda e, pa=pa, j=j, cbuf=cbuf: e.activation(out=cbuf, in_=pa[:, 0:T], func=AF.Identity, scale=cw(2, j), bias=cb(j)), reads=[ka, 'prm'], writes=[ck_])
                            S.op('dve', lambda e, a3=a3, j=j: e.tensor_copy(out=anew[:, j, :, :], in_=a3[:, :, 8:10]), reads=[ak_], writes=['anew'])
                            S.op('dve', lambda e, a3=a3, j=j, c3=c3: e.scalar_tensor_tensor(out=c3, in0=a3[:, :, 1:9], scalar=cw(1, j), in1=c3, op0=ALU.mult, op1=ALU.add),
                                 reads=[ak_, ck_, 'prm'], writes=[ck_])
                            S.op('dve', lambda e, a3=a3, j=j, c3=c3: e.scalar_tensor_tensor(out=c3, in0=a3[:, :, 0:8], scalar=cw(0, j), in1=c3, op0=ALU.mult, op1=ALU.add),
                                 reads=[ak_, ck_, 'prm'], writes=[ck_])
                        def stage2(cbuf=cbuf, gbuf=gbuf, pb_=pb_, j=j, ck_=ck_, gk_=gk_, kb=kb):
                            S.op('act', lambda e: e.activation(out=gbuf, in_=cbuf, func=AF.Gelu_apprx_tanh), reads=[ck_], writes=[gk_])
                            S.op('dve', lambda e: e.tensor_tensor(out=mT[:, j, 0:T], in0=pb_[:, 0:T], in1=gbuf, op=ALU.mult), reads=[kb, gk_], writes=['mT%d' % j] + (['kvt'] if (first and j < 4) else []))
                        if pend2:
                            pend2.pop()()
                        pend2.append(stage2)
                if pend2:
                    pend2.pop()()
                chk('up')
                if (not sample) and X.get('last'):
                    carry, rowb = X['carry'], X['rowb']
                    for g4 in range(6):
                        p, k = PS()
                        n4 = 4 if g4 < 5 else 2
                        for q in range(n4):
                            j = g4 * 4 + q
                            S.op('pe', lambda e, p=p, q=q, j=j: e.transpose(out=p[0:2, q * 128:(q + 1) * 128], in_=carry[:, j, :], identity=ident), reads=['carry', 'cst'], writes=[k], inc=(q == n4 - 1))
                        S.op('dve', lambda e, p=p, g4=g4, n4=n4: e.tensor_copy(out=rowb[0:2, g4 % 2, 0:n4 * 128], in_=p[0:2, 0:n4 * 128]), reads=[k], writes=['rowb%d' % (g4 % 2)])
                        S.dma('sp', 'o_cp%d' % (g4 % 2), o_conv_p[:, g4 * 512:g4 * 512 + n4 * 128], rowb[0:2, g4 % 2, 0:n4 * 128], reads=['rowb%d' % (g4 % 2)])
                if sample:
                    anew, rowb = X['anew'], X['rowb']
                    for g4 in range(6):
                        p, k = PS()
                        n4 = 4 if g4 < 5 else 2
                        for q in range(n4):
                            j = g4 * 4 + q
                            S.op('pe', lambda e, p=p, q=q, j=j: e.transpose(out=p[0:32, q * 128:(q + 1) * 128], in_=anew[:, j, :, :].rearrange("p b r -> p (b r)"), identity=ident),
                                 reads=['anew', 'cst'], writes=[k], inc=(q == n4 - 1))
                        S.op('dve', lambda e, p=p, g4=g4, n4=n4: e.tensor_copy(out=rowb[0:32, g4 % 2, 0:n4 * 128], in_=p[0:32, 0:n4 * 128]), reads=[k], writes=['rowb%d' % (g4 % 2)])
                        S.dma('sp', 'o_cs%d' % (g4 % 2), o_conv_s[:, g4 * 512:g4 * 512 + n4 * 128], rowb[0:32, g4 % 2, 0:n4 * 128], reads=['rowb%d' % (g4 % 2)])
                chk('convout')
                if nxt is not None:
                    rms_multi([(xn[s_], xnk[s_], s_ * 128) for s_ in range(4)], 102, hT, ['hT'], phase='stats')
                for q in range(4):
                    wsl, wk = wneed()
                    wv = w22(wsl)
                    for sub in range(nsub):
                        p, k = PS()
                        mmg(p[:, 0:256], k, [(mT[:, kk, sub * 128:(sub + 1) * 128], wv[:, kk, :]) for kk in range(22)], [wk], per=[['mT%d' % kk] for kk in range(22)])
                        S.op('dve', lambda e, p=p, sub=sub, q=q: e.tensor_tensor(out=xres[:, sub, q * 256:(q + 1) * 256], in0=p[:, 0:256], in1=xres[:, sub, q * 256:(q + 1) * 256], op=ALU.add),
                             reads=[k, 'xres%d' % sub], writes=['xres%d' % sub])
                    if q == 1 and nxt is not None:
                        rms_multi([(xn[s_], xnk[s_], s_ * 128) for s_ in range(4)], 102, hT, ['hT'], phase='apply')
                        X['prenormed'] = True
                hbs = G['hbs']
                for sub in range(nsub):
                    S.op('act', lambda e, sub=sub: e.activation(out=hbs[sub % 2][:], in_=xres[:, sub, :], func=AF.Square, accum_out=stat[:, 16 + sub:17 + sub]),
                         reads=['xres%d' % sub], writes=['hb%d' % (sub % 2), 'stat2'])
                S.op('act', lambda e: e.activation(out=stat[:, 20:20 + nsub], in_=stat[:, 16:16 + nsub], func=AF.Ln, scale=1.0 / 1024, bias=epsc), reads=['stat2', 'small'], writes=['stat2'])
                S.op('act', lambda e: e.activation(out=stat[:, 24:24 + nsub], in_=stat[:, 20:20 + nsub], func=AF.Exp, scale=-0.5), reads=['stat2'], writes=['stat2'])
                for sub in range(nsub):
                    xk = 'xres%d' % sub
                    S.op('dve', lambda e, sub=sub: e.scalar_tensor_tensor(out=xres[:, sub, :], in0=xres[:, sub, :], scalar=stat[:, 24 + sub:25 + sub], in1=gf[:], op0=ALU.mult, op1=ALU.mult),
                         reads=[xk, 'stat2', 'gf'], writes=[xk])
                    S.dma('sp', 'yout%d' % sub, y_tok[t0 + sub * 128:t0 + (sub + 1) * 128, :], xres[:, sub, :], reads=[xk])

            chk('prologue')
            Bp = alloc_phase(pst, 512, False)
            xres, hT, mixT, mT = G['xres'], G['hT'], G['mixT'], G['mT']
            memx = Bp['osb'][:].rearrange("p h t -> p (h t)").rearrange("p (s f) -> p s f", s=2); memT = mixT
            kvt = mT[:].rearrange("p k t -> p (k t)")[:, 0:2048].bitcast(F32).rearrange("p (s f) -> p s f", s=2)
            KT = sbt(pst, "KT", [128, 2, 256], BF16); Vb = sbt(pst, "Vb", [128, 2, 256], BF16)

            def memkv():
                rms_multi([(memx[:, sub, :], ['osb'], sub * 128) for sub in range(2)], 118, memT, ['memT'])
                wkv, wk = wneed()
                chk('kv_w')
                for sub in range(2):
                    p, k = PS(2)
                    mmg(p[:, :], k, [(memT[:, kk, sub * 128:(sub + 1) * 128], w8(wkv)[:, kk, :]) for kk in range(8)], ['memT', wk])
                    chk('kv_m')
                    S.op('act', lambda e, p=p, sub=sub: e.activation(out=kvt[:, sub, :], in_=p[:, :], func=AF.Copy), reads=[k], writes=['kvt'])
                    chk('kv_n')
                    S.op('dve', lambda e, p=p, sub=sub: e.tensor_copy(out=Vb[:, sub, :], in_=p[:, 256:512]), reads=[k], writes=['Vb'])
                    chk('kv_a%d' % sub)
                for j in range(2):
                    p, k = PS()
                    mmg(p[:, 0:256], k, [(w8(wkv)[:, kk, j * 128:(j + 1) * 128], memT[:, kk, 0:256]) for kk in range(8)], ['memT', wk])
                    S.op('act', lambda e, p=p, j=j: e.activation(out=KT[:, j, :], in_=p[:, 0:256], func=AF.Copy), reads=[k], writes=['KT'])
                    chk('kv_b%d' % j)
                S.dma('sp', 'o_mk', o_mk.rearrange("(s p) f -> p s f", p=128), kvt[:, :, 0:256], reads=['kvt'])
                chk('kv_c')
                S.dma('sp', 'o_mv', o_mv.rearrange("(s p) f -> p s f", p=128), kvt[:, :, 256:512], reads=['kvt'])


            chk('memkv')
            Xp = {}
            Xp['memkv'] = memkv
            Xp['memdma'] = lambda: S.dma('sp', 'c7', memx[:, 0:2, :], mem.rearrange("(s p) f -> p s f", p=128), writes=['osb'])
            Xp['Sf'] = sbt(pst, "Sf", [128, 4, 128]); Xp['Sb'] = sbt(pst, "Sb", [128, 4, 128], BF16)
            Xp['cm512'] = sbt(pst, "cm512", [128, 512])
            Xp['PTa'] = sbt(pst, "PTa", [128, 2, 2, 512], BF16)
            thf = Bp['THall'][:].rearrange("p h t -> p (h t)")
            Xp['asb'] = [thf[:, 0:514], thf[:, 1024:1538]]
            Xp['carry'] = sbt(pst, "carry", [128, 22, 2]); Xp['rowb'] = sbt(pst, "rowb_p", [2, 2, 512]); Xp['ppo'] = sbt(pst, "ppo", [16, 256])
            S.op('dve', lambda e: e.memset(Xp['Sf'][:], 0.0), writes=['Sf'])
            S.op('dve', lambda e: e.memset(Xp['Sb'][:], 0.0), writes=['Sb'])
            S.op('dve', lambda e: e.memset(Xp['cm512'][:], 1.0), writes=['cm512'])
            S.op('dve', lambda e: e.memset(Xp['cm512'][:].rearrange("p (c t) -> p c t", t=128)[:, :, 0:1], 0.0), writes=['cm512'])
            S.op('dve', lambda e: e.memset(Xp['carry'][:], 0.0), writes=['carry'])
            for ti in range(4):
                Xp['last'] = (ti == 3)
                Xp['nxt'] = (ti + 1) * 512 if ti < 3 else None
                do_tile(Bp, ti * 512, ti == 0, False, Xp)
                chk('tile%d' % ti)
            S.barrier()
            pst.close()

            chk('prompt')
            Bs = alloc_phase(sst, 128, True)
            Xs = {}
            Xs['xp'] = sbt(sst, "xp", [128, 2, 16, 23]); Xs['sp_tok'] = sbt(sst, "sp_tok", [120, 2, 256]); Xs['xpc'] = sbt(sst, "xpc", [128, 2, 16, 15])
            Xs['spo'] = Xs['sp_tok']
            Xs['S0f'] = sbt(sst, "S0f", [128, 16, 4, 128]); Xs['S0b'] = sbt(sst, "S0b", [128, 16, 4, 128], BF16)
            Xs['Vblk2'] = [sbt(sst, "Vblk%d" % i, [128, 4, 128], BF16) for i in range(2)]
            Xs['KTs'] = sbt(sst, "KTs", [128, 16, 2, 256], BF16); Xs['Vs'] = sbt(sst, "Vs", [128, 16, 2, 256], BF16)
            Xs['PTs'] = sbt(sst, "PTs", [128, 2, 512], BF16); kst2 = [sbt(sst, "kst%d" % i, [128, 2, 2, 256], BF16) for i in range(2)]
            thfs = Bs['THall'][:].rearrange("p h t -> p (h t)")
            Xs['a3'] = [thfs[:, 0:160].rearrange("p (b t) -> p b t", t=10), thfs[:, 256:416].rearrange("p (b t) -> p b t", t=10)]
            Xs['ahist'] = sbt(sst, "ahist", [128, 22, 16, 2]); Xs['anew'] = sbt(sst, "anew", [128, 22, 16, 2]); Xs['rowb'] = sbt(sst, "rowb_s", [32, 2, 512])
            cst_tok = sbt(sst, "cst_tok", [32, 2816])
            def pre1():
                S.dma('sp', 's3', Xs['sp_tok'][:], spool.rearrange("(h q) c -> q h c", q=120), writes=['sp_tok'])
                S.dma('sp', 's4', cst_tok[:], sconv[:, :], writes=['cst_tok'])
                kload(0)
                for hh in range(2):
                    for c in range(2):
                        p, k = PS()
                        S.op('pe', lambda e, p=p, hh=hh, c=c: e.transpose(out=p[:, 0:120], in_=Xs['sp_tok'][0:120, hh, c * 128:(c + 1) * 128], identity=cst[0:120, 0:120]),
                             reads=['sp_tok', 'cst'], writes=[k])
                        S.op('dve', lambda e, p=p, hh=hh, c=c: e.tensor_copy(out=Xs['xp'][:, c, hh * 8:(hh + 1) * 8, 0:15], in_=p[:, 0:120].rearrange("p (b r) -> p b r", r=15)),
                             reads=[k], writes=['xp'])
                for j in range(22):
                    p, k = PS()
                    S.op('pe', lambda e, p=p, j=j: e.transpose(out=p[:, 0:32], in_=cst_tok[0:32, j * 128:(j + 1) * 128], identity=cst[0:32, 0:32]), reads=['cst_tok', 'cst'], writes=[k])
                    S.op('dve', lambda e, p=p, j=j: e.tensor_copy(out=Xs['ahist'][:, j, :, :], in_=p[:, 0:32].rearrange("p (b r) -> p b r", r=2)), reads=[k], writes=['ahist'])

            def kload(g4):
                S.dma('pool', 's5%d' % (g4 % 2), kst2[g4 % 2][:], ck[g4 * 2:(g4 + 1) * 2].rearrange("b (mc p) f -> p b mc f", p=128), writes=['kst%d' % (g4 % 2)])

            def pre2():
                kload(1)
                S.dma('pool', 's1', Xs['S0b'][:], shgrn.rearrange("b h d v -> d b h v"), writes=['S0b'])
                for g4 in range(8):
                    kst = kst2[g4 % 2]
                    for bi in range(2):
                        bq = g4 * 2 + bi
                        p, k = PS()
                        pb = p[:].bitcast(BF16)
                        for hc in range(2):
                            for mc in range(2):
                                S.op('pe', lambda e, pb=pb, kst=kst, bi=bi, hc=hc, mc=mc: e.transpose(out=pb[:, (hc * 2 + mc) * 128:(hc * 2 + mc + 1) * 128], in_=kst[:, bi, mc, hc * 128:(hc + 1) * 128], identity=idb[:]),
                                     reads=['kst%d' % (g4 % 2), 'idb'], writes=[k], inc=(hc == 1 and mc == 1))
                        S.op('act', lambda e, pb=pb, bq=bq: e.activation(out=Xs['KTs'][:, bq, :, :], in_=pb[:, 0:512].rearrange("p (hc m) -> p hc m", hc=2), func=AF.Copy), reads=[k], writes=['KTs'])
                    if g4 + 2 < 8:
                        kload(g4 + 2)
                    if g4 == 7:
                        S.dma('pool', 's2', Xs['Vs'][:], cv.rearrange("b (mc p) f -> p b mc f", p=128), writes=['Vs'])
                        S.dma('sp', 's0', Xs['S0f'][:], shgrn.rearrange("b h d v -> d b h v"), writes=['S0f'])
                    yield
            Xs['pre1'] = pre1; Xs['pre2'] = pre2
            chk('sprologue')
            do_tile(Bs, 2048, False, True, Xs)
            S.op('dve', lambda e: e.tensor_copy(out=Xs['xpc'][:], in_=Xs['xp'][:, :, :, 8:23]), reads=['xp'], writes=['xpc'])
            for hh in range(2):
                for c in range(2):
                    p, k = PS()
                    S.op('pe', lambda e, p=p, hh=hh, c=c: e.transpose(out=p[0:120, 0:128], in_=Xs['xpc'][:, c, hh * 8:(hh + 1) * 8, :].rearrange("p b r -> p (b r)"), identity=ident),
                         reads=['xpc', 'cst'], writes=[k])
                    S.op('dve', lambda e, p=p, hh=hh, c=c: e.tensor_copy(out=Xs['spo'][0:120, hh, c * 128:(c + 1) * 128], in_=p[0:120, 0:128]), reads=[k], writes=['spo'])
            S.dma('sp', 'o_ps', o_pool_s.rearrange("(h b) r c -> (b r) h c", h=2), Xs['spo'][:], reads=['spo'])
        except StopBuild as ex:
            print('STOPPED at', ex)
        S.final()
        sst.close()
        import os
        if os.environ.get('KDEBUG'):
            print('CNT', S.cnt, {k: v[1] for k, v in S.dsem.items()})
    return nc


_NC = None


def kernel(**inp):
    global _NC
    f = lambda a: np.ascontiguousarray(np.asarray(a, dtype=np.float32))
    if _NC is None:
        _NC = build()
    cst = make_consts()
    prm = np.concatenate([f(inp['conv_w'][0]).reshape(66, 128), f(inp['conv_b'][0]).reshape(22, 128),
                          f(inp['hgrn_lb_logits']).reshape(8, 128), f(inp['pool_scale'][0]).reshape(2, 128),
                          f(inp['hgrn_onorm_g'][0]).reshape(4, 128), f(inp['ln1_g'][0]).reshape(8, 128),
                          f(inp['ln2_g'][0]).reshape(8, 128), f(inp['mem_norm_g'][0]).reshape(8, 128)], axis=0)
    shared = dict(lnf=f(inp['lnf_g']),
                  w_in=f(inp['w_in'][0]), w_kv=f(inp['w_mem_kv'][0]), w_out=f(inp['w_out'][0]), w_up=f(inp['w_up'][0]),
                  w_dn=f(inp['w_down'][0]), pool_w=f(inp['pool_w'][0]), prm_in=f(prm), cst=cst)
    in_maps = []
    for c in range(8):
        sl = slice(16 * c, 16 * c + 16)
        m = dict(shared)
        m['x_tok'] = f(np.concatenate([inp['x_prompt'][c], np.asarray(inp['x_sample'][sl]).reshape(128, 1024)], axis=0))
        m['mem'] = f(inp['mem_prompt'][c])
        m['spool'] = f(np.asarray(inp['state_pool'][0, sl]).reshape(240, 256))
        m['shgrn'] = f(inp['state_hgrn'][0, sl])
        m['sconv'] = f(np.asarray(inp['state_conv'][0, sl]).reshape(32, 2816))
        m['ck'] = f(np.asarray(inp['cache_mem_k'][0, sl]).reshape(16, 256, 256))
        m['cv'] = f(np.asarray(inp['cache_mem_v'][0, sl]).reshape(16, 256, 256))
        in_maps.append(m)
    res = run_bass_kernel_spmd(_NC, in_maps, core_ids=list(range(8)))
    R = res.results
    g = lambda k: np.stack([np.asarray(R[c][k], dtype=np.float32) for c in range(8)])
    y = g('y_tok')
    y_prompt = np.ascontiguousarray(y[:, :2048, :])
    y_sample = np.ascontiguousarray(y[:, 2048:, :].reshape(128, 8, 1024))
    return (y_prompt, y_sample,
            g('o_pool_p')[None], g('o_hgrn_p')[None], g('o_conv_p')[None],
            g('o_mk').reshape(1, 8, 256, 4, 64), g('o_mv').reshape(1, 8, 256, 4, 64),
            g('o_pool_s').reshape(1, 128, 15, 256), g('o_hgrn_s').reshape(1, 128, 4, 128, 128),
            g('o_conv_s').reshape(1, 128, 2, 2816))
```

```python
import numpy as np
from contextlib import ExitStack
import concourse.bass as bass
import concourse.mybir as mybir
from concourse.bass_utils import run_bass_kernel_spmd

F32, BF16 = mybir.dt.float32, mybir.dt.bfloat16
AF = mybir.ActivationFunctionType
ALU = mybir.AluOpType
EPS = 1e-6
NSLOT = 4
SAME_ENGINE_SYNC = True


import os


class StopBuild(Exception):
    pass


_hits = {}


def chk(name):
    if os.environ.get('KSTOP') == name:
        _hits[name] = _hits.get(name, 0) + 1
        if _hits[name] == int(os.environ.get('KHIT', '1')):
            raise StopBuild(name)


class Sched:
    def __init__(s, nc, es):
        s.nc, s.es = nc, es
        s.E = {'pe': nc.tensor, 'act': nc.scalar, 'dve': nc.vector, 'pool': nc.gpsimd, 'sp': nc.sync}
        s.sem = {k: es.enter_context(nc.semaphore('sem_' + k)) for k in s.E}
        s.cnt = {k: 0 for k in s.E}
        s.seen = {k: {} for k in s.E}
        s.lastw, s.reads, s.dsem = {}, {}, {}
        s.psn = 0
        s.ps_open = {}

    def _semh(s, key):
        return s.sem[key] if key in s.sem else s.dsem[key][0]

    def _wait(s, eng, key, val):
        if key == eng and (eng in ('pe', 'sp') or not SAME_ENGINE_SYNC):
            return
        if s.seen[eng].get(key, 0) >= val:
            return
        s.seen[eng][key] = val
        s.E[eng].wait_ge(s._semh(key), val)

    def deps(s, eng, reads, writes):
        need = {}
        for b in reads:
            if b in s.lastw:
                k, v = s.lastw[b]
                need[k] = max(need.get(k, 0), v)
        for b in writes:
            if b in s.lastw:
                k, v = s.lastw[b]
                need[k] = max(need.get(k, 0), v)
            for (k, v) in s.reads.get(b, ()):
                need[k] = max(need.get(k, 0), v)
        for k, v in need.items():
            s._wait(eng, k, v)

    def _record(s, tok, reads, writes):
        for b in reads:
            s.reads.setdefault(b, []).append(tok)
        for b in writes:
            s.lastw[b] = tok
            s.reads[b] = []

    def op(s, eng, fn, reads=(), writes=(), inc=True):
        psr = [b for b in reads if isinstance(b, tuple) and b[0] == 'ps']
        if psr:
            reads = [b for b in reads if b not in psr]
            writes = list(writes) + psr
            for b in psr:
                s.ps_open[b[1]] -= 1
                if s.ps_open[b[1]] <= 0:
                    del s.ps_open[b[1]]
        s.deps(eng, reads, writes)
        ins = fn(s.E[eng])
        if inc:
            s.cnt[eng] += 1
            ins.then_inc(s.sem[eng], 1)
            tok = (eng, s.cnt[eng])
        else:
            tok = (eng, s.cnt[eng] + 1)
        s._record(tok, reads, writes)
        return ins

    def dma(s, eng, chan, out, in_, reads=(), writes=(), **kw):
        s.deps(eng, reads, writes)
        if chan not in s.dsem:
            s.dsem[chan] = [s.es.enter_context(s.nc.semaphore('d_' + chan)), 0]
        ins = s.E[eng].dma_start(out=out, in_=in_, **kw)
        s.dsem[chan][1] += 16
        ins.then_inc(s.dsem[chan][0], 16)
        s._record((chan, s.dsem[chan][1]), reads, writes)

    def barrier(s):
        for eng in s.E:
            for k in s.sem:
                if s.cnt[k] > 0:
                    s._wait(eng, k, s.cnt[k])
            for k in s.dsem:
                s._wait(eng, k, s.dsem[k][1])

    def final(s):
        for k in s.sem:
            if s.cnt[k] > 0:
                s._wait('sp', k, s.cnt[k])
        for k in s.dsem:
            s._wait('sp', k, s.dsem[k][1])


def make_consts():
    c = np.zeros((128, 576), np.float32)
    c[:, 0:128] = np.eye(128, dtype=np.float32)
    s = np.arange(128)[:, None]
    t = np.arange(128)[None, :]
    c[:, 128:256] = (t >= s)
    c[:, 256:384] = (t >= s) & ((t // 8) == (s // 8))
    c[:, 384:400] = (np.arange(128)[:, None] // 8) == np.arange(16)[None, :]
    c[:, 400:528] = (np.arange(128)[None, :] % 8 != 0)
    for ch in range(2):
        for p in range(128):
            w = [2, 4, 8, 16][2 * ch + p // 64]
            c[p, 528 + ch * 16: 528 + ch * 16 + 16] = 1.0 / np.minimum(w, np.arange(16) + 1.0)
            c[p, 560 + ch] = 1.0 / w
    return c


def build():
    nc = bass.Bass("TRN2", target_bir_lowering=False)
    D = lambda n, sh, k="ExternalInput": nc.dram_tensor(n, sh, F32, kind=k).ap()
    x_tok = D("x_tok", [2176, 1024]); mem = D("mem", [256, 1024])
    spool = D("spool", [240, 256]); shgrn = D("shgrn", [16, 4, 128, 128]); sconv = D("sconv", [32, 2816])
    ck = D("ck", [16, 256, 256]); cv = D("cv", [16, 256, 256])
    lnf = D("lnf", [1024])
    w_in = D("w_in", [1024, 2560]); w_kv = D("w_kv", [1024, 512]); w_out = D("w_out", [1024, 1024])
    w_up = D("w_up", [1024, 5632]); w_dn = D("w_dn", [2816, 1024])
    pool_w = D("pool_w", [4, 64, 64]); prm_in = D("prm_in", [126, 128]); cst_in = D("cst", [128, 576])
    O = lambda n, sh: D(n, sh, "ExternalOutput")
    y_tok = O("y_tok", [2176, 1024]); o_pool_p = O("o_pool_p", [15, 256]); o_hgrn_p = O("o_hgrn_p", [4, 128, 128])
    o_conv_p = O("o_conv_p", [2, 2816]); o_mk = O("o_mk", [256, 256]); o_mv = O("o_mv", [256, 256])
    o_pool_s = O("o_pool_s", [16, 15, 256]); o_hgrn_s = O("o_hgrn_s", [16, 4, 128, 128]); o_conv_s = O("o_conv_s", [32, 2816])

    with ExitStack() as es:
        S = Sched(nc, es)
        uid = [0]
        def sbt(st, n, sh, dt=F32):
            uid[0] += 1
            return st.enter_context(nc.sbuf_tensor("sb%d_%s" % (uid[0], n), sh, dt))
        psb = [es.enter_context(nc.psum_tensor("ps%d" % i, [128, 512], F32)) for i in range(8)]

        def PS(n=1):
            for _ in range(8):
                i = S.psn % 8
                S.psn += 1
                if i not in S.ps_open:
                    break
            else:
                raise RuntimeError('no free PSUM bank')
            S.ps_open[i] = n
            return psb[i], ('ps', i)

        cst = sbt(es, "cst", [128, 576]); prm = sbt(es, "prm", [128, 128]); prm_st = sbt(es, "prm_st", [126, 128])
        idb = sbt(es, "idb", [128, 128], BF16); onesb = sbt(es, "onesb", [128, 128], BF16)
        gf = sbt(es, "gf", [128, 1024])
        wbd = sbt(es, "wbd", [128, 2, 128], BF16)
        lbc = sbt(es, "lbc", [128, 16])
        small = sbt(es, "small", [128, 8])
        ring = [sbt(es, "ring%d" % i, [128, 5632], BF16) for i in range(NSLOT)]
        G = {}
        stat = sbt(es, "stat", [128, 32])
        ident = cst[:, 0:128]; cmask = cst[:, 128:256]; smask = cst[:, 256:384]; seqm = cst[:, 384:400]
        rmask = cst[:, 400:528]; invw = cst[:, 560:562]
        invcnt = cst[:, 528:560]
        epsc = small[:, 0:1]; mhalf = small[:, 1:2]; onec = small[:, 2:3]
        cw = lambda r, j: prm[:, r * 22 + j: r * 22 + j + 1]
        cb = lambda j: prm[:, 66 + j: 67 + j]
        pscale = lambda c: prm[:, 96 + c: 97 + c]
        onorm = lambda h: prm[:, 98 + h: 99 + h]

        wseq = []

        def ld_in(b):
            def f(slot, key, chan):
                S.dma('pool', chan, slot[:, 0:4096].rearrange("p (k n) -> p k n", k=8),
                      w_in[:, b * 512:(b + 1) * 512].rearrange("(k p) n -> p k n", p=128), writes=[key])
            return f

        def ld_kv():
            def f(slot, key, chan):
                S.dma('pool', chan, slot[:, 0:4096].rearrange("p (k n) -> p k n", k=8),
                      w_kv.rearrange("(k p) n -> p k n", p=128), writes=[key])
            return f

        def ld_out(c):
            def f(slot, key, chan):
                S.dma('pool', chan, slot[:, 0:4096].rearrange("p (k n) -> p k n", k=8),
                      w_out[:, c * 512:(c + 1) * 512].rearrange("(k p) n -> p k n", p=128), writes=[key])
            return f

        def ld_up(r):
            def f(slot, key, chan):
                v = slot[:, 0:4096].rearrange("p (k n) -> p k n", k=8)
                S.dma('pool', chan, v[:, :, 0:256],
                      w_up[:, r * 256:(r + 1) * 256].rearrange("(k p) n -> p k n", p=128), writes=[key])
                S.dma('pool', chan, v[:, :, 256:512],
                      w_up[:, 2816 + r * 256:2816 + (r + 1) * 256].rearrange("(k p) n -> p k n", p=128), writes=[key])
            return f

        def ld_dn(q):
            def f(slot, key, chan):
                S.dma('pool', chan, slot[:, 0:5632].rearrange("p (k n) -> p k n", k=22),
                      w_dn[:, q * 256:(q + 1) * 256].rearrange("(k p) n -> p k n", p=128), writes=[key])
            return f

        wscr = nc.dram_tensor("wscr", [22, 128, 5632], BF16, kind="Internal").ap()
        blocks = [('in', b) for b in range(5)] + [('out', c) for c in range(2)] + [('up', r) for r in range(11)] + [('dn', q) for q in range(4)]
        mk = {'in': ld_in, 'out': ld_out, 'up': ld_up, 'dn': ld_dn}
        for ti in range(5):
            for bi, (kind, idx) in enumerate(blocks):
                n = 5632 if kind == 'dn' else 4096
                if ti == 0:
                    wseq.append((mk[kind](idx), bi, n))
                    if bi == 2:
                        wseq.append((ld_kv(), None, 0))
                else:
                    def f(slot, key, chan, bi=bi, n=n):
                        S.dma('pool', chan, slot[:, 0:n], wscr[bi, :, 0:n], reads=[('scr', bi)], writes=[key])
                    wseq.append((f, None, n))
        wstate = {'issued': 0, 'next': 0}

        def wneed(prefetch=True):
            i = wstate['next']
            wstate['next'] += 1
            upto = min(i + NSLOT - 1, len(wseq) - 1) if prefetch else i
            while wstate['issued'] <= upto:
                j = wstate['issued']
                wseq[j][0](ring[j % NSLOT], ('w', j % NSLOT), 'w%d' % (j % NSLOT))
                wstate['issued'] += 1
            sl = ring[i % NSLOT]
            bi, n = wseq[i][1], wseq[i][2]
            if bi is not None:
                S.dma('sp', 'wb%d' % (i % NSLOT), wscr[bi, :, 0:n], sl[:, 0:n], reads=[('w', i % NSLOT)], writes=[('scr', bi)])
            return sl, ('w', i % NSLOT)

        def w8(sl):
            return sl[:, 0:4096].rearrange("p (k n) -> p k n", k=8)

        def w22(sl):
            return sl[:, 0:5632].rearrange("p (k n) -> p k n", k=22)

        def mmg(out_ap, pskey, pairs, reads, per=None):
            n = len(pairs)
            for i, (l, r) in enumerate(pairs):
                S.op('pe', lambda e, l=l, r=r, i=i: e.matmul(out_ap, lhsT=l, rhs=r, start=(i == 0), stop=(i == n - 1)),
                     reads=list(reads) + (list(per[i]) if per else []), writes=[pskey], inc=(i == n - 1))

        pst = ExitStack(); sst = ExitStack()
        es.enter_context(pst); es.enter_context(sst)
        try:
            S.dma('sp', 'c0', cst[:], cst_in[:, :], writes=['cst'])
            S.dma('sp', 'c1', prm_st[:], prm_in[:, :], writes=['prm_st'])
            S.dma('sp', 'c4', gf[:], lnf.partition_broadcast(128), writes=['gf'])
            S.op('dve', lambda e: e.memset(small[:, 0:1], EPS), writes=['small'])
            S.op('dve', lambda e: e.memset(small[:, 1:2], -0.5), writes=['small'])
            S.op('dve', lambda e: e.memset(small[:, 2:3], 1.0), writes=['small'])
            S.op('dve', lambda e: e.memset(onesb[:], 1.0), writes=['onesb'])
            S.op('dve', lambda e: e.tensor_copy(out=idb[:], in_=ident), reads=['cst'], writes=['idb'])
            S.op('pool', lambda e: e.memset(wbd[:], 0.0), writes=['wbd'])
            for gi in range(4):
                c, o = gi // 2, (gi % 2) * 64
                S.dma('pool', 'c5', wbd[o:o + 64, c, o:o + 64], pool_w[gi, :, :], writes=['wbd'])
            p_, pk = PS()
            S.op('pe', lambda e: e.transpose(out=p_[:, 0:126], in_=prm_st[:, :], identity=cst[0:126, 0:126]),
                 reads=['prm_st', 'cst'], writes=[pk])
            S.op('dve', lambda e: e.tensor_copy(out=prm[:, 0:126], in_=p_[:, 0:126]), reads=[pk], writes=['prm'])
            S.op('dve', lambda e: e.tensor_sub(out=lbc[:, 8:12], in0=prm[:, 88:92], in1=prm[:, 92:96]), reads=['prm'], writes=['lbc'])
            S.op('act', lambda e: e.activation(out=lbc[:, 8:12], in_=lbc[:, 8:12], func=AF.Tanh, scale=0.5), reads=['lbc'], writes=['lbc'])
            S.op('dve', lambda e: e.tensor_scalar(out=lbc[:, 0:4], in0=lbc[:, 8:12], scalar1=0.25, scalar2=0.75, op0=ALU.mult, op1=ALU.add), reads=['lbc'], writes=['lbc'])
            S.op('dve', lambda e: e.tensor_scalar(out=lbc[:, 4:8], in0=lbc[:, 8:12], scalar1=-0.25, scalar2=0.25, op0=ALU.mult, op1=ALU.add), reads=['lbc'], writes=['lbc'])
            S.op('dve', lambda e: e.tensor_scalar(out=lbc[:, 12:16], in0=lbc[:, 8:12], scalar1=0.25, scalar2=-0.25, op0=ALU.mult, op1=ALU.add), reads=['lbc'], writes=['lbc'])

            def rms_to_T(src_ap, gcol, dstT, col0, rd, wr_extra=()):
                hb = G['hb']
                S.op('act', lambda e: e.activation(out=hb[:], in_=src_ap, func=AF.Square, accum_out=stat[:, 0:1]),
                     reads=rd, writes=['hb', 'stat'])
                S.op('dve', lambda e: e.tensor_scalar(out=stat[:, 1:2], in0=stat[:, 0:1], scalar1=1.0 / 1024, scalar2=EPS, op0=ALU.mult, op1=ALU.add),
                     reads=['stat'], writes=['stat'])
                S.op('pool', lambda e: e.tensor_tensor(out=stat[:, 2:3], in0=stat[:, 1:2], in1=mhalf, op=ALU.pow),
                     reads=['stat', 'small'], writes=['stat'])
                chk('rms_a')
                S.op('act', lambda e: e.activation(out=hb[:], in_=src_ap, func=AF.Copy, scale=stat[:, 2:3]),
                     reads=list(rd) + ['stat'], writes=['hb'])
                chk('rms_b')
                p, k = PS()
                pb = p[:].bitcast(BF16)
                for kk in range(8):
                    S.op('pe', lambda e, kk=kk: e.transpose(out=pb[:, kk * 128:(kk + 1) * 128], in_=hb[:, kk * 128:(kk + 1) * 128], identity=idb[:]),
                         reads=['hb', 'idb'], writes=[k], inc=(kk == 7))
                chk('rms_c')
                S.op('dve', lambda e: e.tensor_tensor(out=dstT[:, :, col0:col0 + 128], in0=pb.rearrange("p (k n) -> p k n", k=8),
                                                      in1=prm[:, gcol:gcol + 8].unsqueeze(2).to_broadcast([128, 8, 128]), op=ALU.mult),
                     reads=[k, 'prm'], writes=list(wr_extra))
                chk('rms_d')

            def rms_multi(items, gcol, dstT, wr, base=0, phase='all'):
                n = len(items)
                hbs = G['hbs']
                if phase in ('all', 'stats'):
                    for i_, (src, rd, col0) in enumerate(items):
                        S.op('act', lambda e, i_=i_, src=src: e.activation(out=hbs[(base + i_) % 2][:], in_=src, func=AF.Square, accum_out=stat[:, base + i_:base + i_ + 1]),
                             reads=rd, writes=['hb%d' % ((base + i_) % 2), 'stat'])
                    S.op('act', lambda e: e.activation(out=stat[:, 4 + base:4 + base + n], in_=stat[:, base:base + n], func=AF.Ln, scale=1.0 / 1024, bias=epsc), reads=['stat', 'small'], writes=['stat'])
                    S.op('act', lambda e: e.activation(out=stat[:, 8 + base:8 + base + n], in_=stat[:, 4 + base:4 + base + n], func=AF.Exp, scale=-0.5), reads=['stat'], writes=['stat'])
                if phase == 'stats':
                    return
                for i_, (src, rd, col0) in enumerate(items):
                    hb = hbs[(base + i_) % 2]
                    S.op('act', lambda e, i_=i_, src=src, hb=hb: e.activation(out=hb[:], in_=src, func=AF.Copy, scale=stat[:, 8 + base + i_:9 + base + i_]),
                         reads=list(rd) + ['stat'], writes=['hb%d' % ((base + i_) % 2)])
                    p, k = PS()
                    pb = p[:].bitcast(BF16)
                    for kk in range(8):
                        S.op('pe', lambda e, kk=kk, hb=hb, pb=pb: e.transpose(out=pb[:, kk * 128:(kk + 1) * 128], in_=hb[:, kk * 128:(kk + 1) * 128], identity=idb[:]),
                             reads=['hb%d' % ((base + i_) % 2), 'idb'], writes=[k], inc=(kk == 7))
                    S.op('dve', lambda e, col0=col0, pb=pb: e.tensor_tensor(out=dstT[:, :, col0:col0 + 128], in0=pb.rearrange("p (k n) -> p k n", k=8),
                                                                          in1=prm[:, gcol:gcol + 8].unsqueeze(2).to_broadcast([128, 8, 128]), op=ALU.mult),
                         reads=[k, 'prm'], writes=list(wr))

            def alloc_phase(st, T, sample):
                B = {}
                B['T'] = T
                ns = T // 128
                G['xres'] = sbt(st, "xres", [128, ns, 1024]); G['hT'] = sbt(st, "hT", [128, 8, T], BF16)
                G['mixT'] = sbt(st, "mixT", [128, 8, T], BF16); G['mT'] = sbt(st, "mT", [128, 22, T], BF16)
                G['hbs'] = [sbt(st, "hb%d" % i, [128, 1024], BF16) for i in range(2)]
                G['hb'] = G['hbs'][0]
                B['u'] = sbt(st, "u_sb", [128, 2, 15 + T]) if not sample else None
                B['pA'] = sbt(st, "pA", [128, max(16 + T, 368)]); B['pB'] = sbt(st, "pB", [128, max(16 + T, 368)])
                B['d'] = sbt(st, "d_sb", [128, 2, T], BF16)
                B['Ab'] = [sbt(st, "A_sb%d" % i, [128, T]) for i in range(2)]
                B['E2b'] = [sbt(st, "E2_%d" % i, [128, T]) for i in range(2)]
                B['K1b'] = [sbt(st, "K1_%d" % i, [128, T]) for i in range(2)]
                B['QSall'] = sbt(st, "QSall", [128, 4, T]); B['THall'] = sbt(st, "THall", [128, 4, T])
                B['GS'] = sbt(st, "GS", [128, 4, T])
                B['qAT'] = sbt(st, "qAT", [128, 4, T], BF16); B['kAT'] = sbt(st, "kAT", [128, 4, T], BF16)
                B['kAk'] = sbt(st, "kAk", [128, T // 128, 512], BF16); B['vtk'] = sbt(st, "vtk", [128, T // 128, 512], BF16)
                B['eAe'] = sbt(st, "eAe", [128, 4, 16])
                B['PT'] = sbt(st, "PT", [128, 4, 128], BF16)
                B['osb'] = sbt(st, "osb", [128, 4, T]); B['osq'] = sbt(st, "osq", [128, T], BF16)
                B['R'] = sbt(st, "R_sb", [128, T]); B['t1'] = sbt(st, "t1", [128, T])
                B['qxT'] = sbt(st, "qxT", [128, 2, T], BF16)
                B['cbuf'] = [B['QSall'][:, i, :] for i in range(2)]
                B['gbuf'] = [B['QSall'][:, 2 + i, :] for i in range(2)]
                return B

            def do_tile(B, t0, first, sample, X):
                T = B['T']; nsub = T // 128
                xres, hT, mixT, mT = G['xres'], G['hT'], G['mixT'], G['mT']
                for sub in range(nsub):
                    S.dma('sp', 'xin%d' % sub, xres[:, sub, :], x_tok[t0 + sub * 128:t0 + (sub + 1) * 128, :], writes=['xres%d' % sub])
                if first and not sample:
                    X['memdma']()
                if sample:
                    X['pre1']()
                if not X.get('prenormed'):
                    rms_multi([(xres[:, sub, :], ['xres%d' % sub], sub * 128) for sub in range(nsub)], 102, hT, ['hT'])
                X['prenormed'] = False
                chk('norm1')
                QSall, THall, GS = B['QSall'], B['THall'], B['GS']
                done = {'pool': False, 'attn': False, 'inproj': False}

                def g_inproj(b0, b1):
                    for b in range(b0, b1):
                        wsl, wk = wneed()
                        wv = w8(wsl)
                        for jj in range(4):
                            j = b * 4 + jj
                            if 10 <= j < 14:
                                continue
                            yield
                            p, k = PS()
                            mmg(p[:, 0:T], k, [(wv[:, kk, jj * 128:(jj + 1) * 128], hT[:, kk, 0:T]) for kk in range(8)], ['hT', wk])
                            src = p[:, 0:T]
                            if j < 2:
                                if sample:
                                    S.op('act', lambda e, j=j, src=src: e.activation(out=X['xp'][:, j, :, 15:23], in_=src.rearrange("p (b t) -> p b t", t=8), func=AF.Copy),
                                         reads=[k], writes=['xp'])
                                else:
                                    S.op('act', lambda e, j=j, src=src: e.activation(out=B['u'][:, j, 15:15 + T], in_=src, func=AF.Copy), reads=[k], writes=['u'])
                            elif j < 6:
                                S.op('act', lambda e, j=j, src=src: e.activation(out=QSall[:, j - 2, :], in_=src, func=AF.Silu), reads=[k], writes=['QS%d' % (j - 2)])
                            elif j < 10:
                                S.op('act', lambda e, j=j, src=src: e.activation(out=THall[:, j - 6, :], in_=src, func=AF.Tanh, scale=0.5), reads=[k], writes=['TH%d' % (j - 6)])
                            elif j < 18:
                                h = j - 14
                                S.op('act', lambda e, h=h, src=src: e.activation(out=GS[:, h, :], in_=src, func=AF.Copy), reads=[k], writes=['GS%d' % h])
                            else:
                                S.op('act', lambda e, j=j, src=src: e.activation(out=B['qxT'][:, j - 18, :], in_=src, func=AF.Copy), reads=[k], writes=['qxT'])
                        if b in (2, 3):
                            c0 = 256 if b == 2 else 0
                            o0 = 0 if b == 2 else 256
                            for sub in range(nsub):
                                yield
                                p, k = PS()
                                mmg(p[:, 0:256], k, [(hT[:, kk, sub * 128:(sub + 1) * 128], wv[:, kk, c0:c0 + 256]) for kk in range(8)], ['hT', wk])
                                S.op('dve', lambda e, p=p, sub=sub, o0=o0: e.tensor_copy(out=B['vtk'][:, sub, o0:o0 + 256], in_=p[:, 0:256]), reads=[k], writes=['vtk'])

                    if b1 == 5:
                        done['inproj'] = True
                    yield

                for _ in g_inproj(0, 3):
                    pass
                if first and not sample:
                    X['memkv']()
                chk('inproj')
                kgen = X['pre2']() if sample else iter(())

                def kstep(n=1):
                    for _ in range(n):
                        next(kgen, None)
                kstep(2)
                def g_pool():
                    pA, pB, dsb = B['pA'], B['pB'], B['d']
                    Rb = G['hbs'][0][:].bitcast(F32)
                    for c in range(2):
                        kstep(1)
                        if sample:
                            Xv = X['xp'][:, c, :, :]
                            L = 23
                            sl = lambda buf, n: buf[:, 0:16 * n].rearrange("p (b n) -> p b n", b=16)
                            xs = lambda a, b_: Xv[:, :, a:b_]
                            xkey = 'xp'
                        else:
                            u = B['u']
                            if first and c == 0:
                                yield
                                S.op('dve', lambda e: e.memset(u[:, :, 0:15], 0.0), writes=['u'])
                            L = 15 + T
                            sl = lambda buf, n: buf[:, 0:n]
                            xs = lambda a, b_, c=c: u[:, c, a:b_]
                            xkey = 'u'
                        sv = lambda buf, n, a, b_: (sl(buf, n)[:, :, a:b_] if sample else sl(buf, n)[:, a:b_])
                        yield
                        S.op('dve', lambda e: e.tensor_tensor(out=sl(pA, L - 1), in0=xs(1, L), in1=xs(0, L - 1), op=ALU.add), reads=[xkey], writes=['pA'])
                        yield
                        S.op('dve', lambda e: e.tensor_tensor(out=sl(pB, L - 3), in0=sv(pA, L - 1, 2, L - 1), in1=sv(pA, L - 1, 0, L - 3), op=ALU.add), reads=['pA'], writes=['pB'])
                        uview = xs(15, L)
                        dv = dsb[:, c, :].rearrange("p (b t) -> p b t", t=8) if sample else dsb[:, c, :]

                        def comb(plo, phi, buf, n, off, wsel, dv=dv, uview=uview):
                            S.op('dve', lambda e: e.scalar_tensor_tensor(out=dv[plo:phi], in0=sv(buf, n, off, off + (8 if sample else T))[plo:phi], scalar=invw[plo:phi, c:c + 1],
                                                                          in1=uview[plo:phi], op0=ALU.mult, op1=ALU.subtract),
                                 reads=[wsel, xkey, 'cst'], writes=['d'])
                        if c == 0:
                            yield
                            comb(0, 64, pA, L - 1, 14, 'pA')
                            yield
                            comb(64, 128, pB, L - 3, 12, 'pB')
                        else:
                            yield
                            S.op('dve', lambda e: e.tensor_tensor(out=sl(pA, L - 7), in0=sv(pB, L - 3, 4, L - 3), in1=sv(pB, L - 3, 0, L - 7), op=ALU.add), reads=['pB'], writes=['pA'])
                            yield
                            comb(0, 64, pA, L - 7, 8, 'pA')
                            yield
                            S.op('dve', lambda e: e.tensor_tensor(out=sl(pB, L - 15), in0=sv(pA, L - 7, 8, L - 7), in1=sv(pA, L - 7, 0, L - 15), op=ALU.add), reads=['pA'], writes=['pB'])
                            yield
                            comb(64, 128, pB, L - 15, 0, 'pB')
                        if first and not sample:
                            for (plo, phi, buf, off) in ((0, 64, pA, 14 if c == 0 else 8), (64, 128, pB, 12 if c == 0 else 0)):
                                yield
                                S.op('dve', lambda e, plo=plo, phi=phi, buf=buf, off=off: e.tensor_tensor(out=Rb[plo:phi, 0:15], in0=buf[plo:phi, off:off + 15],
                                                                                                          in1=invcnt[plo:phi, c * 16:c * 16 + 15], op=ALU.mult),
                                     reads=['pA', 'pB', 'cst'], writes=['hb0'])
                                yield
                                S.op('dve', lambda e, plo=plo, phi=phi: e.tensor_tensor(out=dsb[plo:phi, c, 0:15], in0=Rb[plo:phi, 0:15], in1=u[plo:phi, c, 15:30], op=ALU.subtract),
                                     reads=['hb0', 'u'], writes=['d'])
                        yield
                        p, k = PS()
                        yield
                        mmg(p[:, 0:T], k, [(wbd[:, c, :], dsb[:, c, :])], ['wbd', 'd'])
                        yield
                        S.op('act', lambda e, p=p, c=c: e.activation(out=mixT[:, c, 0:T], in_=p[:, 0:T], func=AF.Identity, scale=pscale(c)), reads=[k, 'prm'], writes=['mixT%d' % c])
                    if not sample:
                        u = B['u']
                        if X.get('last'):
                            for c in range(2):
                                yield
                                p, k = PS()
                                yield
                                S.op('pe', lambda e, p=p, c=c: e.transpose(out=p[0:15, c * 128:(c + 1) * 128], in_=u[:, c, T:T + 15], identity=ident), reads=['u', 'cst'], writes=[k])
                                yield
                                S.op('dve', lambda e, p=p, c=c: e.tensor_copy(out=X['ppo'][0:15, c * 128:(c + 1) * 128], in_=p[0:15, c * 128:(c + 1) * 128]), reads=[k], writes=['ppo'])
                            S.dma('sp', 'o_pp', o_pool_p[:, :], X['ppo'][0:15, :], reads=['ppo'])
                        else:
                            yield
                            S.op('dve', lambda e: e.tensor_copy(out=u[:, :, 0:15], in_=u[:, :, T:T + 15]), reads=['u'], writes=['u'])

                    yield
                def g_hgrn():
                    qAT, kAT, eAe = B['qAT'], B['kAT'], B['eAe']
                    smk = rmask if sample else X['cm512'][:, :]
                    smkey = 'cst' if sample else 'cm512'

                    def stA(h):
                        par = h % 2
                        K1 = B['K1b'][par]
                        thk = 'TH%d' % h
                        S.op('act', lambda e: e.activation(out=K1[:], in_=THall[:, h, :], func=AF.Identity, scale=lbc[:, 12 + h:13 + h], bias=lbc[:, 4 + h:5 + h]),
                             reads=[thk, 'lbc'], writes=['K1%d' % par])
                        S.op('act', lambda e: e.activation(out=THall[:, h, :], in_=THall[:, h, :], func=AF.Ln, scale=lbc[:, 4 + h:5 + h], bias=lbc[:, h:h + 1]),
                             reads=[thk, 'lbc'], writes=[thk])

                    def stB(h):
                        par = h % 2
                        A = B['Ab'][par]
                        S.op('dve', lambda e: e.tensor_tensor_scan(out=A[:], data0=smk, data1=THall[:, h, :], initial=0.0, op0=ALU.mult, op1=ALU.add),
                             reads=['TH%d' % h, smkey], writes=['A%d' % par])

                    def stC(h):
                        par = h % 2
                        A, E2 = B['Ab'][par], B['E2b'][par]
                        S.op('act', lambda e: e.activation(out=E2[:], in_=A[:], func=AF.Exp, scale=-1.0), reads=['A%d' % par], writes=['E2%d' % par])
                        S.op('act', lambda e: e.activation(out=A[:], in_=A[:], func=AF.Exp), reads=['A%d' % par], writes=['A%d' % par])

                    def stD(h):
                        par = h % 2
                        A, E2, K1 = B['Ab'][par], B['E2b'][par], B['K1b'][par]
                        ka, ke, kk1 = 'A%d' % par, 'E2%d' % par, 'K1%d' % par
                        S.op('dve', lambda e: e.tensor_tensor(out=qAT[:, h, :], in0=QSall[:, h, :], in1=A[:], op=ALU.mult), reads=['QS%d' % h, ka], writes=['qAT'])
                        S.op('dve', lambda e: e.tensor_tensor(out=kAT[:, h, :], in0=K1[:], in1=E2[:], op=ALU.mult), reads=[kk1, ke], writes=['kAT'])
                        if sample:
                            S.op('dve', lambda e: e.tensor_copy(out=eAe[:, h, 0:16], in_=A[:].rearrange("p (b t) -> p b t", t=8)[:, :, 7]), reads=[ka], writes=['eAe'])
                        else:
                            S.op('dve', lambda e: e.tensor_copy(out=eAe[:, h, 0:nsub], in_=A[:].rearrange("p (c t) -> p c t", t=128)[:, :, 127]), reads=[ka], writes=['eAe'])

                    stA(0)
                    yield
                    stB(0)
                    yield
                    kstep()
                    stA(1)
                    yield
                    stC(0)
                    yield
                    stB(1)
                    yield
                    kstep()
                    stD(0)
                    yield
                    stA(2)
                    yield
                    stC(1)
                    yield
                    stB(2)
                    yield
                    kstep()
                    stD(1)
                    yield
                    stA(3)
                    yield
                    stC(2)
                    yield
                    stB(3)
                    yield
                    kstep()
                    stD(2)
                    yield
                    stC(3)
                    yield
                    stD(3)
                    yield
                    kstep(8)
                    chk('hgrn_ew')
                    while not done['inproj']:
                        yield
                    for h in range(4):
                        S.op('act', lambda e, h=h: e.activation(out=GS[:, h, :], in_=GS[:, h, :], func=AF.Silu), reads=['GS%d' % h], writes=['GS%d' % h])
                    for cc in range(nsub):
                        yield
                        p, k = PS()
                        pb = p[:].bitcast(BF16)
                        for h in range(4):
                            yield
                            S.op('pe', lambda e, h=h, cc=cc, pb=pb: e.transpose(out=pb[:, h * 128:(h + 1) * 128], in_=kAT[:, h, cc * 128:(cc + 1) * 128], identity=idb[:]),
                                 reads=['kAT', 'idb'], writes=[k], inc=(h == 3))
                        yield
                        S.op('act', lambda e, cc=cc, pb=pb: e.activation(out=B['kAk'][:, cc, :], in_=pb[:, 0:512], func=AF.Copy), reads=[k], writes=['kAk%d' % cc])
                    chk('katr')
                    PT, osb, kAk, vtk = B['PT'], B['osb'], B['kAk'], B['vtk']
                    for cc in range(nsub):
                        cs = slice(cc * 128, (cc + 1) * 128)
                        yield
                        pS, kS = PS()
                        for h in range(4):
                            yield
                            mmg(pS[:, h * 128:(h + 1) * 128], kS, [(kAT[:, h, cs], qAT[:, h, cs])], ['kAT', 'qAT'])
                        msk = smask if sample else cmask
                        yield
                        S.op('dve', lambda e, pS=pS, msk=msk: e.tensor_tensor(out=PT[:], in0=pS[:, :].rearrange("p (h t) -> p h t", h=4),
                                                                              in1=msk.unsqueeze(1).to_broadcast([128, 4, 128]), op=ALU.mult),
                             reads=[kS, 'cst'], writes=['PT'])
                        yield
                        pO, kO = PS()
                        if not sample:
                            Sf, Sb = X['Sf'], X['Sb']
                            for h in range(4):
                                yield
                                mmg(pO[:, h * 128:(h + 1) * 128], kO, [(vtk[:, cc, h * 128:(h + 1) * 128], PT[:, h, :]), (Sb[:, h, :], qAT[:, h, cs])],
                                    [], per=[['vtk', 'PT'], ['Sb', 'qAT']])
                            yield
                            S.op('act', lambda e, pO=pO, cs=cs: e.activation(out=osb[:, :, cs], in_=pO[:, :].rearrange("p (h t) -> p h t", h=4), func=AF.Copy), reads=[kO], writes=['osb'])
                            yield
                            pZ, kZ = PS()
                            for h in range(4):
                                yield
                                mmg(pZ[:, h * 128:(h + 1) * 128], kZ, [(kAk[:, cc, h * 128:(h + 1) * 128], vtk[:, cc, h * 128:(h + 1) * 128])], ['kAk%d' % cc, 'vtk'])
                            yield
                            S.op('dve', lambda e, pZ=pZ: e.tensor_tensor(out=Sf[:], in0=Sf[:], in1=pZ[:, :].rearrange("p (h v) -> p h v", h=4), op=ALU.add), reads=[kZ, 'Sf'], writes=['Sf'])
                            yield
                            S.op('dve', lambda e, cc=cc: e.tensor_tensor(out=Sf[:], in0=Sf[:], in1=eAe[:, :, cc:cc + 1].to_broadcast([128, 4, 128]), op=ALU.mult), reads=['Sf', 'eAe'], writes=['Sf'])
                            yield
                            S.op('act', lambda e: e.activation(out=Sb[:], in_=Sf[:], func=AF.Copy), reads=['Sf'], writes=['Sb'])
                        else:
                            S0f, S0b = X['S0f'], X['S0b']
                            for h in range(4):
                                pairs = [(vtk[:, 0, h * 128:(h + 1) * 128], PT[:, h, :])]
                                n = 17
                                yield
                                S.op('pe', lambda e, h=h: e.matmul(pO[:, h * 128:(h + 1) * 128], lhsT=vtk[:, 0, h * 128:(h + 1) * 128], rhs=PT[:, h, :], start=True, stop=False),
                                     reads=['vtk', 'PT'], writes=[kO], inc=False)
                                for bq in range(16):
                                    yield
                                    S.op('pe', lambda e, h=h, bq=bq: e.matmul(pO[:, h * 128 + bq * 8:h * 128 + bq * 8 + 8], lhsT=S0b[:, bq, h, :], rhs=qAT[:, h, bq * 8:bq * 8 + 8],
                                                                             start=False, stop=(bq == 15)),
                                         reads=['S0b', 'qAT'], writes=[kO], inc=(bq == 15))
                            yield
                            S.op('act', lambda e, pO=pO: e.activation(out=osb[:, :, 0:128], in_=pO[:, :].rearrange("p (h t) -> p h t", h=4), func=AF.Copy), reads=[kO], writes=['osb'])
                            Vb2 = X['Vblk2']

                            def stV(i):
                                h, bg = i // 4, i % 4
                                vb = Vb2[i % 2]
                                S.op('dve', lambda e: e.tensor_tensor(out=vb[:], in0=vtk[:, 0, h * 128:(h + 1) * 128].unsqueeze(1).to_broadcast([128, 4, 128]),
                                                                      in1=seqm[:, bg * 4:bg * 4 + 4].unsqueeze(2).to_broadcast([128, 4, 128]), op=ALU.mult),
                                     reads=['vtk', 'cst'], writes=['Vblk%d' % (i % 2)])

                            def stU(i):
                                h, bg = i // 4, i % 4
                                vb = Vb2[i % 2]
                                pZ, kZ = PS()
                                mmg(pZ[:, :], kZ, [(kAk[:, 0, h * 128:(h + 1) * 128], vb[:].rearrange("p b v -> p (b v)"))], ['kAk0', 'Vblk%d' % (i % 2)])
                                S.op('dve', lambda e: e.tensor_tensor(out=S0f[:, bg * 4:bg * 4 + 4, h, :], in0=S0f[:, bg * 4:bg * 4 + 4, h, :],
                                                                      in1=pZ[:, :].rearrange("p (b v) -> p b v", b=4), op=ALU.add),
                                     reads=[kZ, 'S0f'], writes=['S0f'])
                                S.op('dve', lambda e: e.tensor_tensor(out=S0f[:, bg * 4:bg * 4 + 4, h, :], in0=S0f[:, bg * 4:bg * 4 + 4, h, :],
                                                                      in1=eAe[:, h, bg * 4:bg * 4 + 4].unsqueeze(2).to_broadcast([128, 4, 128]), op=ALU.mult),
                                     reads=['S0f', 'eAe'], writes=['S0f'])
                            yield
                            stV(0)
                            for i in range(16):
                                if i + 1 < 16:
                                    yield
                                    stV(i + 1)
                                yield
                                stU(i)
                            S.dma('sp', 'o_hs', o_hgrn_s.rearrange("b h d v -> d b h v"), S0f[:], reads=['S0f'])
                    if (not sample) and X.get('last'):
                        S.dma('sp', 'o_hp', o_hgrn_p.rearrange("h d v -> d h v"), X['Sf'][:], reads=['Sf'])
                    chk('hgrn')
                    osq, R, t1 = B['osq'], B['R'], B['t1']
                    for h in range(4):
                        yield
                        S.op('act', lambda e, h=h: e.activation(out=osq[:], in_=osb[:, h, :], func=AF.Square), reads=['osb'], writes=['osq'])
                        yield
                        p, k = PS()
                        yield
                        mmg(p[:, 0:T], k, [(onesb[:], osq[:])], ['onesb', 'osq'])
                        yield
                        S.op('act', lambda e, p=p: e.activation(out=R[:], in_=p[:, 0:T], func=AF.Ln, scale=1.0 / 128, bias=epsc), reads=[k, 'small'], writes=['R'])
                        yield
                        S.op('act', lambda e: e.activation(out=R[:], in_=R[:], func=AF.Exp, scale=-0.5), reads=['R'], writes=['R'])
                        yield
                        S.op('dve', lambda e, h=h: e.tensor_tensor(out=t1[:], in0=osb[:, h, :], in1=R[:], op=ALU.mult), reads=['osb', 'R'], writes=['t1'])
                        yield
                        S.op('dve', lambda e, h=h: e.scalar_tensor_tensor(out=mixT[:, 2 + h, 0:T], in0=t1[:], scalar=onorm(h), in1=GS[:, h, :], op0=ALU.mult, op1=ALU.mult),
                             reads=['t1', 'GS%d' % h, 'prm'], writes=['mixT%d' % (2 + h)])

                    yield
                def g_attn():
                    qxT = B['qxT']
                    Ra = G['hbs'][1][:].bitcast(F32); Rb = G['hbs'][0][:].bitcast(F32)
                    while not done['inproj']:
                        yield
                    if sample:
                        for _ in range(24):
                            yield
                        kstep(8)
                    if not sample:
                        PTa = X['PTa']
                        for pr in range(2):
                            for hh in range(2):
                                h = pr * 2 + hh
                                rows = slice(hh * 64, hh * 64 + 64)
                                for mc in range(2):
                                    yield
                                    p, k = PS()
                                    yield
                                    mmg(p[:, 0:T], k, [(KT[rows, pr, mc * 128:(mc + 1) * 128], qxT[rows, pr, :])], ['KT', 'qxT'])
                                    yield
                                    S.op('act', lambda e, p=p, hh=hh, mc=mc: e.activation(out=PTa[:, hh, mc, :], in_=p[:, 0:T], func=AF.Exp, scale=0.125), reads=[k], writes=['PTa%d%d' % (hh, mc)])
                            yield
                            pO, kO = PS()
                            yield
                            pD, kD = PS()
                            for hh in range(2):
                                h = pr * 2 + hh
                                rows = slice(hh * 64, hh * 64 + 64)
                                for mc in range(2):
                                    yield
                                    S.op('pe', lambda e, hh=hh, mc=mc, h=h, rows=rows, pO=pO: e.matmul(pO[rows, 0:T], lhsT=Vb[:, mc, h * 64:(h + 1) * 64], rhs=PTa[:, hh, mc, :],
                                                                                                        start=(mc == 0), stop=(mc == 1)),
                                         reads=['Vb', 'PTa%d%d' % (hh, mc)], writes=[kO], inc=(mc == 1))
                                for mc in range(2):
                                    yield
                                    S.op('pe', lambda e, hh=hh, mc=mc, rows=rows, pD=pD: e.matmul(pD[rows, 0:T], lhsT=onesb[:, 0:64], rhs=PTa[:, hh, mc, :],
                                                                                                  start=(mc == 0), stop=(mc == 1)),
                                         reads=['onesb', 'PTa%d%d' % (hh, mc)], writes=[kD], inc=(mc == 1))
                            yield
                            S.op('act', lambda e, pD=pD: e.activation(out=Ra[:, 0:T], in_=pD[:, 0:T], func=AF.Ln), reads=[kD], writes=['hb1'])
                            S.op('act', lambda e: e.activation(out=Ra[:, 0:T], in_=Ra[:, 0:T], func=AF.Exp, scale=-1.0), reads=['hb1'], writes=['hb1'])
                            yield
                            S.op('dve', lambda e, pO=pO, pr=pr: e.tensor_tensor(out=mixT[:, 6 + pr, 0:T], in0=pO[:, 0:T], in1=Ra[:, 0:T], op=ALU.mult), reads=[kO, 'hb1'], writes=['mixT%d' % (6 + pr)])
                    else:
                        KTs, Vs, PTs = X['KTs'], X['Vs'], X['PTs']
                        for g8 in range(2):
                            yield
                            pp = [PS(), PS()]
                            for bi in range(8):
                                bq = g8 * 8 + bi
                                for mc in range(2):
                                    for h in range(4):
                                        par = h % 2
                                        rows = slice(par * 64, par * 64 + 64)
                                        col = bi * 32 + (mc * 2 + h // 2) * 8
                                        last = (bi == 7 and mc == 1 and h >= 2)
                                        p, k = pp[par]
                                        yield
                                        S.op('pe', lambda e, bq=bq, mc=mc, h=h, rows=rows, col=col, p=p: e.matmul(p[:, col:col + 8], lhsT=KTs[rows, bq, h // 2, mc * 128:(mc + 1) * 128],
                                                                                                                 rhs=qxT[rows, h // 2, bq * 8:bq * 8 + 8], start=True, stop=True),
                                             reads=['KTs', 'qxT'], writes=[k], inc=last)
                            for par in range(2):
                                p, k = pp[par]
                                yield
                                S.op('act', lambda e, p=p, g8=g8, par=par: e.activation(out=PTs[:, par, g8 * 256:(g8 + 1) * 256], in_=p[:, 0:256], func=AF.Exp, scale=0.125), reads=[k], writes=['PTs'])
                        yield
                        pO, kO = PS(2)
                        yield
                        pD, kD = PS(2)
                        for bq in range(16):
                            for h in range(4):
                                rows = slice((h % 2) * 64, (h % 2) * 64 + 64)
                                oc = (h // 2) * 128 + bq * 8
                                for mc in range(2):
                                    col = bq * 32 + (mc * 2 + h // 2) * 8
                                    yield
                                    S.op('pe', lambda e, bq=bq, h=h, mc=mc, rows=rows, oc=oc, col=col: e.matmul(pO[rows, oc:oc + 8], lhsT=Vs[:, bq, mc, h * 64:(h + 1) * 64], rhs=PTs[:, h % 2, col:col + 8],
                                                                                                               start=(mc == 0), stop=(mc == 1)),
                                         reads=['Vs', 'PTs'], writes=[kO], inc=(bq == 15 and h == 3 and mc == 1))
                                for mc in range(2):
                                    col = bq * 32 + (mc * 2 + h // 2) * 8
                                    yield
                                    S.op('pe', lambda e, bq=bq, h=h, mc=mc, rows=rows, oc=oc, col=col: e.matmul(pD[rows, oc:oc + 8], lhsT=onesb[:, 0:64], rhs=PTs[:, h % 2, col:col + 8],
                                                                                                               start=(mc == 0), stop=(mc == 1)),
                                         reads=['onesb', 'PTs'], writes=[kD], inc=(bq == 15 and h == 3 and mc == 1))
                        yield
                        S.op('act', lambda e: e.activation(out=Ra[:, 0:128], in_=pD[:, 0:128], func=AF.Ln), reads=[kD], writes=['hb1'])
                        S.op('act', lambda e: e.activation(out=Ra[:, 0:128], in_=Ra[:, 0:128], func=AF.Exp, scale=-1.0), reads=['hb1'], writes=['hb1'])
                        yield
                        S.op('dve', lambda e: e.tensor_tensor(out=mixT[:, 6, 0:128], in0=pO[:, 0:128], in1=Ra[:, 0:128], op=ALU.mult), reads=[kO, 'hb1'], writes=['mixT6'])
                        yield
                        S.op('act', lambda e: e.activation(out=Rb[:, 0:128], in_=pD[:, 128:256], func=AF.Ln), reads=[kD], writes=['hb0'])
                        S.op('act', lambda e: e.activation(out=Rb[:, 0:128], in_=Rb[:, 0:128], func=AF.Exp, scale=-1.0), reads=['hb0'], writes=['hb0'])
                        yield
                        S.op('dve', lambda e: e.tensor_tensor(out=mixT[:, 7, 0:128], in0=pO[:, 128:256], in1=Rb[:, 0:128], op=ALU.mult), reads=[kO, 'hb0'], writes=['mixT7'])

                    yield
                gens = [(g_inproj(3, 5), 2), (g_hgrn(), 3), (g_pool(), 1), (g_attn(), 1)]
                while gens:
                    for ge in list(gens):
                        for _ in range(ge[1]):
                            try:
                                next(ge[0])
                            except StopIteration:
                                gens.remove(ge)
                                break
                chk('pool')
                chk('hgrn_o')
                chk('attn')
                wo = [wneed(), wneed(prefetch=False)]
                for sub in range(nsub):
                    for c in range(2):
                        wsl, wk = wo[c]
                        wv = w8(wsl)
                        p, k = PS()
                        mmg(p[:, :], k, [(mixT[:, kk, sub * 128:(sub + 1) * 128], wv[:, kk, :]) for kk in (0, 1, 6, 7, 2, 3, 4, 5)], [wk], per=[['mixT%d' % kk] for kk in (0, 1, 6, 7, 2, 3, 4, 5)])
                        S.op('dve', lambda e, p=p, sub=sub, c=c: e.tensor_tensor(out=xres[:, sub, c * 512:(c + 1) * 512], in0=p[:, :], in1=xres[:, sub, c * 512:(c + 1) * 512], op=ALU.add),
                             reads=[k, 'xres%d' % sub], writes=['xres%d' % sub])
                    if sub >= 1:
                        rms_multi([(xres[:, sub - 1, :], ['xres%d' % (sub - 1)], (sub - 1) * 128)], 110, hT, ['hT'], base=sub - 1)
                rms_multi([(xres[:, nsub - 1, :], ['xres%d' % (nsub - 1)], (nsub - 1) * 128)], 110, hT, ['hT'], base=nsub - 1)
                chk('outproj')
                chk('norm2')
                pend2 = []
                nxt = X.get('nxt')
                if nxt is not None:
                    GSf = B['GS'][:].rearrange("p h t -> p (h t)"); osf = B['osb'][:].rearrange("p h t -> p (h t)")
                    xn = [GSf[:, 0:1024], GSf[:, 1024:2048], osf[:, 0:1024], osf[:, 1024:2048]]
                    xnk = [['GS0', 'GS1'], ['GS2', 'GS3'], ['osb'], ['osb']]
                    for s_ in range(4):
                        S.dma('sp', 'xn%d' % s_, xn[s_], x_tok[nxt + s_ * 128:nxt + (s_ + 1) * 128, :], writes=xnk[s_])
                for r in range(11):
                    wsl, wk = wneed()
                    wv = w8(wsl)
                    for jj in range(2):
                        j = 2 * r + jj
                        pa, ka = PS(2)
                        mmg(pa[:, 0:T], ka, [(wv[:, kk, jj * 128:(jj + 1) * 128], hT[:, kk, 0:T]) for kk in range(8)], ['hT', wk])
                        pb_, kb = PS()
                        mmg(pb_[:, 0:T], kb, [(wv[:, kk, 256 + jj * 128:256 + (jj + 1) * 128], hT[:, kk, 0:T]) for kk in range(8)], ['hT', wk])
                        cbuf = B['cbuf'][j % 2]; gbuf = B['gbuf'][j % 2]
                        ck_, gk_ = 'cbuf%d' % (j % 2), 'gbuf%d' % (j % 2)
                        if not sample:
                            asb = X['asb'][j % 2]; ak_ = 'asb%d' % (j % 2); carry = X['carry']
                            S.op('dve', lambda e, asb=asb, j=j: e.tensor_copy(out=asb[:, 0:2], in_=carry[:, j, :]), reads=['carry'], writes=[ak_])
                            S.op('act', lambda e, asb=asb, pa=pa: e.activation(out=asb[:, 2:2 + T], in_=pa[:, 0:T], func=AF.Copy), reads=[ka], writes=[ak_])
                            S.op('act', lambda e, pa=pa, j=j, cbuf=cbuf: e.activation(out=cbuf, in_=pa[:, 0:T], func=AF.Identity, scale=cw(2, j), bias=cb(j)), reads=[ka, 'prm'], writes=[ck_])
                            S.op('dve', lambda e, asb=asb, j=j: e.tensor_copy(out=carry[:, j, :], in_=asb[:, T:T + 2]), reads=[ak_], writes=['carry'])
                            S.op('dve', lambda e, asb=asb, j=j, cbuf=cbuf: e.scalar_tensor_tensor(out=cbuf, in0=asb[:, 1:1 + T], scalar=cw(1, j), in1=cbuf, op0=ALU.mult, op1=ALU.add),
                                 reads=[ak_, ck_, 'prm'], writes=[ck_])
                            S.op('dve', lambda e, asb=asb, j=j, cbuf=cbuf: e.scalar_tensor_tensor(out=cbuf, in0=asb[:, 0:T], scalar=cw(0, j), in1=cbuf, op0=ALU.mult, op1=ALU.add),
                                 reads=[ak_, ck_, 'prm'], writes=[ck_])
                        else:
                            a3 = X['a3'][j % 2]; ak_ = 'a3%d' % (j % 2); ahist, anew = X['ahist'], X['anew']
                            c3 = cbuf.rearrange("p (b t) -> p b t", t=8)
                            S.op('dve', lambda e, a3=a3, j=j: e.tensor_copy(out=a3[:, :, 0:2], in_=ahist[:, j, :, :]), reads=['ahist'], writes=[ak_])
                            S.op('act', lambda e, a3=a3, pa=pa: e.activation(out=a3[:, :, 2:10], in_=pa[:, 0:128].rearrange("p (b t) -> p b t", t=8), func=AF.Copy), reads=[ka], writes=[ak_])
                            S.op('act', lambda e, pa=pa, j=j, cbuf=cbuf: e.activation(out=cbuf, in_=pa[:, 0:T], func=AF.Identity, scale=cw(2, j), bias=cb(j)), reads=[ka, 'prm'], writes=[ck_])
                            S.op('dve', lambda e, a3=a3, j=j: e.tensor_copy(out=anew[:, j, :, :], in_=a3[:, :, 8:10]), reads=[ak_], writes=['anew'])
                            S.op('dve', lambda e, a3=a3, j=j, c3=c3: e.scalar_tensor_tensor(out=c3, in0=a3[:, :, 1:9], scalar=cw(1, j), in1=c3, op0=ALU.mult, op1=ALU.add),
                                 reads=[ak_, ck_, 'prm'], writes=[ck_])
                            S.op('dve', lambda e, a3=a3, j=j, c3=c3: e.scalar_tensor_tensor(out=c3, in0=a3[:, :, 0:8], scalar=cw(0, j), in1=c3, op0=ALU.mult, op1=ALU.add),
                                 reads=[ak_, ck_, 'prm'], writes=[ck_])
                        def stage2(cbuf=cbuf, gbuf=gbuf, pb_=pb_, j=j, ck_=ck_, gk_=gk_, kb=kb):
                            S.op('act', lambda e: e.activation(out=gbuf, in_=cbuf, func=AF.Gelu_apprx_tanh), reads=[ck_], writes=[gk_])
                            S.op('dve', lambda e: e.tensor_tensor(out=mT[:, j, 0:T], in0=pb_[:, 0:T], in1=gbuf, op=ALU.mult), reads=[kb, gk_], writes=['mT%d' % j] + (['kvt'] if (first and j < 4) else []))
                        if pend2:
                            pend2.pop()()
                        pend2.append(stage2)
                if pend2:
                    pend2.pop()()
                chk('up')
                if (not sample) and X.get('last'):
                    carry, rowb = X['carry'], X['rowb']
                    for g4 in range(6):
                        p, k = PS()
                        n4 = 4 if g4 < 5 else 2
                        for q in range(n4):
                            j = g4 * 4 + q
                            S.op('pe', lambda e, p=p, q=q, j=j: e.transpose(out=p[0:2, q * 128:(q + 1) * 128], in_=carry[:, j, :], identity=ident), reads=['carry', 'cst'], writes=[k], inc=(q == n4 - 1))
                        S.op('dve', lambda e, p=p, g4=g4, n4=n4: e.tensor_copy(out=rowb[0:2, g4 % 2, 0:n4 * 128], in_=p[0:2, 0:n4 * 128]), reads=[k], writes=['rowb%d' % (g4 % 2)])
                        S.dma('sp', 'o_cp%d' % (g4 % 2), o_conv_p[:, g4 * 512:g4 * 512 + n4 * 128], rowb[0:2, g4 % 2, 0:n4 * 128], reads=['rowb%d' % (g4 % 2)])
                if sample:
                    anew, rowb = X['anew'], X['rowb']
                    for g4 in range(6):
                        p, k = PS()
                        n4 = 4 if g4 < 5 else 2
                        for q in range(n4):
                            j = g4 * 4 + q
                            S.op('pe', lambda e, p=p, q=q, j=j: e.transpose(out=p[0:32, q * 128:(q + 1) * 128], in_=anew[:, j, :, :].rearrange("p b r -> p (b r)"), identity=ident),
                                 reads=['anew', 'cst'], writes=[k], inc=(q == n4 - 1))
                        S.op('dve', lambda e, p=p, g4=g4, n4=n4: e.tensor_copy(out=rowb[0:32, g4 % 2, 0:n4 * 128], in_=p[0:32, 0:n4 * 128]), reads=[k], writes=['rowb%d' % (g4 % 2)])
                        S.dma('sp', 'o_cs%d' % (g4 % 2), o_conv_s[:, g4 * 512:g4 * 512 + n4 * 128], rowb[0:32, g4 % 2, 0:n4 * 128], reads=['rowb%d' % (g4 % 2)])
                chk('convout')
                if nxt is not None:
                    rms_multi([(xn[s_], xnk[s_], s_ * 128) for s_ in range(4)], 102, hT, ['hT'], phase='stats')
                for q in range(4):
                    wsl, wk = wneed()
                    wv = w22(wsl)
                    for sub in range(nsub):
                        p, k = PS()
                        mmg(p[:, 0:256], k, [(mT[:, kk, sub * 128:(sub + 1) * 128], wv[:, kk, :]) for kk in range(22)], [wk], per=[['mT%d' % kk] for kk in range(22)])
                        S.op('dve', lambda e, p=p, sub=sub, q=q: e.tensor_tensor(out=xres[:, sub, q * 256:(q + 1) * 256], in0=p[:, 0:256], in1=xres[:, sub, q * 256:(q + 1) * 256], op=ALU.add),
                             reads=[k, 'xres%d' % sub], writes=['xres%d' % sub])
                    if q == 1 and nxt is not None:
                        rms_multi([(xn[s_], xnk[s_], s_ * 128) for s_ in range(4)], 102, hT, ['hT'], phase='apply')
                        X['prenormed'] = True
                hbs = G['hbs']
                for sub in range(nsub):
                    S.op('act', lambda e, sub=sub: e.activation(out=hbs[sub % 2][:], in_=xres[:, sub, :], func=AF.Square, accum_out=stat[:, 16 + sub:17 + sub]),
                         reads=['xres%d' % sub], writes=['hb%d' % (sub % 2), 'stat2'])
                S.op('act', lambda e: e.activation(out=stat[:, 20:20 + nsub], in_=stat[:, 16:16 + nsub], func=AF.Ln, scale=1.0 / 1024, bias=epsc), reads=['stat2', 'small'], writes=['stat2'])
                S.op('act', lambda e: e.activation(out=stat[:, 24:24 + nsub], in_=stat[:, 20:20 + nsub], func=AF.Exp, scale=-0.5), reads=['stat2'], writes=['stat2'])
                for sub in range(nsub):
                    xk = 'xres%d' % sub
                    S.op('dve', lambda e, sub=sub: e.scalar_tensor_tensor(out=xres[:, sub, :], in0=xres[:, sub, :], scalar=stat[:, 24 + sub:25 + sub], in1=gf[:], op0=ALU.mult, op1=ALU.mult),
                         reads=[xk, 'stat2', 'gf'], writes=[xk])
                    S.dma('sp', 'yout%d' % sub, y_tok[t0 + sub * 128:t0 + (sub + 1) * 128, :], xres[:, sub, :], reads=[xk])

            chk('prologue')
            Bp = alloc_phase(pst, 512, False)
            xres, hT, mixT, mT = G['xres'], G['hT'], G['mixT'], G['mT']
            memx = Bp['osb'][:].rearrange("p h t -> p (h t)").rearrange("p (s f) -> p s f", s=2); memT = mixT
            kvt = mT[:].rearrange("p k t -> p (k t)")[:, 0:2048].bitcast(F32).rearrange("p (s f) -> p s f", s=2)
            KT = sbt(pst, "KT", [128, 2, 256], BF16); Vb = sbt(pst, "Vb", [128, 2, 256], BF16)

            def memkv():
                rms_multi([(memx[:, sub, :], ['osb'], sub * 128) for sub in range(2)], 118, memT, ['memT'])
                wkv, wk = wneed()
                chk('kv_w')
                for sub in range(2):
                    p, k = PS(2)
                    mmg(p[:, :], k, [(memT[:, kk, sub * 128:(sub + 1) * 128], w8(wkv)[:, kk, :]) for kk in range(8)], ['memT', wk])
                    chk('kv_m')
                    S.op('act', lambda e, p=p, sub=sub: e.activation(out=kvt[:, sub, :], in_=p[:, :], func=AF.Copy), reads=[k], writes=['kvt'])
                    chk('kv_n')
                    S.op('dve', lambda e, p=p, sub=sub: e.tensor_copy(out=Vb[:, sub, :], in_=p[:, 256:512]), reads=[k], writes=['Vb'])
                    chk('kv_a%d' % sub)
                for j in range(2):
                    p, k = PS()
                    mmg(p[:, 0:256], k, [(w8(wkv)[:, kk, j * 128:(j + 1) * 128], memT[:, kk, 0:256]) for kk in range(8)], ['memT', wk])
                    S.op('act', lambda e, p=p, j=j: e.activation(out=KT[:, j, :], in_=p[:, 0:256], func=AF.Copy), reads=[k], writes=['KT'])
                    chk('kv_b%d' % j)
                S.dma('sp', 'o_mk', o_mk.rearrange("(s p) f -> p s f", p=128), kvt[:, :, 0:256], reads=['kvt'])
                chk('kv_c')
                S.dma('sp', 'o_mv', o_mv.rearrange("(s p) f -> p s f", p=128), kvt[:, :, 256:512], reads=['kvt'])


            chk('memkv')
            Xp = {}
            Xp['memkv'] = memkv
            Xp['memdma'] = lambda: S.dma('sp', 'c7', memx[:, 0:2, :], mem.rearrange("(s p) f -> p s f", p=128), writes=['osb'])
            Xp['Sf'] = sbt(pst, "Sf", [128, 4, 128]); Xp['Sb'] = sbt(pst, "Sb", [128, 4, 128], BF16)
            Xp['cm512'] = sbt(pst, "cm512", [128, 512])
            Xp['PTa'] = sbt(pst, "PTa", [128, 2, 2, 512], BF16)
            thf = Bp['THall'][:].rearrange("p h t -> p (h t)")
            Xp['asb'] = [thf[:, 0:514], thf[:, 1024:1538]]
            Xp['carry'] = sbt(pst, "carry", [128, 22, 2]); Xp['rowb'] = sbt(pst, "rowb_p", [2, 2, 512]); Xp['ppo'] = sbt(pst, "ppo", [16, 256])
            S.op('dve', lambda e: e.memset(Xp['Sf'][:], 0.0), writes=['Sf'])
            S.op('dve', lambda e: e.memset(Xp['Sb'][:], 0.0), writes=['Sb'])
            S.op('dve', lambda e: e.memset(Xp['cm512'][:], 1.0), writes=['cm512'])
            S.op('dve', lambda e: e.memset(Xp['cm512'][:].rearrange("p (c t) -> p c t", t=128)[:, :, 0:1], 0.0), writes=['cm512'])
            S.op('dve', lambda e: e.memset(Xp['carry'][:], 0.0), writes=['carry'])
            for ti in range(4):
                Xp['last'] = (ti == 3)
                Xp['nxt'] = (ti + 1) * 512 if ti < 3 else None
                do_tile(Bp, ti * 512, ti == 0, False, Xp)
                chk('tile%d' % ti)
            S.barrier()
            pst.close()

            chk('prompt')
            Bs = alloc_phase(sst, 128, True)
            Xs = {}
            Xs['xp'] = sbt(sst, "xp", [128, 2, 16, 23]); Xs['sp_tok'] = sbt(sst, "sp_tok", [120, 2, 256]); Xs['xpc'] = sbt(sst, "xpc", [128, 2, 16, 15])
            Xs['spo'] = Xs['sp_tok']
            Xs['S0f'] = sbt(sst, "S0f", [128, 16, 4, 128]); Xs['S0b'] = sbt(sst, "S0b", [128, 16, 4, 128], BF16)
            Xs['Vblk2'] = [sbt(sst, "Vblk%d" % i, [128, 4, 128], BF16) for i in range(2)]
            Xs['KTs'] = sbt(sst, "KTs", [128, 16, 2, 256], BF16); Xs['Vs'] = sbt(sst, "Vs", [128, 16, 2, 256], BF16)
            Xs['PTs'] = sbt(sst, "PTs", [128, 2, 512], BF16); kst2 = [sbt(sst, "kst%d" % i, [128, 2, 2, 256], BF16) for i in range(2)]
            thfs = Bs['THall'][:].rearrange("p h t -> p (h t)")
            Xs['a3'] = [thfs[:, 0:160].rearrange("p (b t) -> p b t", t=10), thfs[:, 256:416].rearrange("p (b t) -> p b t", t=10)]
            Xs['ahist'] = sbt(sst, "ahist", [128, 22, 16, 2]); Xs['anew'] = sbt(sst, "anew", [128, 22, 16, 2]); Xs['rowb'] = sbt(sst, "rowb_s", [32, 2, 512])
            cst_tok = sbt(sst, "cst_tok", [32, 2816])
            def pre1():
                S.dma('sp', 's3', Xs['sp_tok'][:], spool.rearrange("(h q) c -> q h c", q=120), writes=['sp_tok'])
                S.dma('sp', 's4', cst_tok[:], sconv[:, :], writes=['cst_tok'])
                kload(0)
                for hh in range(2):
                    for c in range(2):
                        p, k = PS()
                        S.op('pe', lambda e, p=p, hh=hh, c=c: e.transpose(out=p[:, 0:120], in_=Xs['sp_tok'][0:120, hh, c * 128:(c + 1) * 128], identity=cst[0:120, 0:120]),
                             reads=['sp_tok', 'cst'], writes=[k])
                        S.op('dve', lambda e, p=p, hh=hh, c=c: e.tensor_copy(out=Xs['xp'][:, c, hh * 8:(hh + 1) * 8, 0:15], in_=p[:, 0:120].rearrange("p (b r) -> p b r", r=15)),
                             reads=[k], writes=['xp'])
                for j in range(22):
                    p, k = PS()
                    S.op('pe', lambda e, p=p, j=j: e.transpose(out=p[:, 0:32], in_=cst_tok[0:32, j * 128:(j + 1) * 128], identity=cst[0:32, 0:32]), reads=['cst_tok', 'cst'], writes=[k])
                    S.op('dve', lambda e, p=p, j=j: e.tensor_copy(out=Xs['ahist'][:, j, :, :], in_=p[:, 0:32].rearrange("p (b r) -> p b r", r=2)), reads=[k], writes=['ahist'])

            def kload(g4):
                S.dma('pool', 's5%d' % (g4 % 2), kst2[g4 % 2][:], ck[g4 * 2:(g4 + 1) * 2].rearrange("b (mc p) f -> p b mc f", p=128), writes=['kst%d' % (g4 % 2)])

            def pre2():
                kload(1)
                S.dma('pool', 's1', Xs['S0b'][:], shgrn.rearrange("b h d v -> d b h v"), writes=['S0b'])
                for g4 in range(8):
                    kst = kst2[g4 % 2]
                    for bi in range(2):
                        bq = g4 * 2 + bi
                        p, k = PS()
                        pb = p[:].bitcast(BF16)
                        for hc in range(2):
                            for mc in range(2):
                                S.op('pe', lambda e, pb=pb, kst=kst, bi=bi, hc=hc, mc=mc: e.transpose(out=pb[:, (hc * 2 + mc) * 128:(hc * 2 + mc + 1) * 128], in_=kst[:, bi, mc, hc * 128:(hc + 1) * 128], identity=idb[:]),
                                     reads=['kst%d' % (g4 % 2), 'idb'], writes=[k], inc=(hc == 1 and mc == 1))
                        S.op('act', lambda e, pb=pb, bq=bq: e.activation(out=Xs['KTs'][:, bq, :, :], in_=pb[:, 0:512].rearrange("p (hc m) -> p hc m", hc=2), func=AF.Copy), reads=[k], writes=['KTs'])
                    if g4 + 2 < 8:
                        kload(g4 + 2)
                    if g4 == 7:
                        S.dma('pool', 's2', Xs['Vs'][:], cv.rearrange("b (mc p) f -> p b mc f", p=128), writes=['Vs'])
                        S.dma('sp', 's0', Xs['S0f'][:], shgrn.rearrange("b h d v -> d b h v"), writes=['S0f'])
                    yield
            Xs['pre1'] = pre1; Xs['pre2'] = pre2
            chk('sprologue')
            do_tile(Bs, 2048, False, True, Xs)
            S.op('dve', lambda e: e.tensor_copy(out=Xs['xpc'][:], in_=Xs['xp'][:, :, :, 8:23]), reads=['xp'], writes=['xpc'])
            for hh in range(2):
                for c in range(2):
                    p, k = PS()
                    S.op('pe', lambda e, p=p, hh=hh, c=c: e.transpose(out=p[0:120, 0:128], in_=Xs['xpc'][:, c, hh * 8:(hh + 1) * 8, :].rearrange("p b r -> p (b r)"), identity=ident),
                         reads=['xpc', 'cst'], writes=[k])
                    S.op('dve', lambda e, p=p, hh=hh, c=c: e.tensor_copy(out=Xs['spo'][0:120, hh, c * 128:(c + 1) * 128], in_=p[0:120, 0:128]), reads=[k], writes=['spo'])
            S.dma('sp', 'o_ps', o_pool_s.rearrange("(h b) r c -> (b r) h c", h=2), Xs['spo'][:], reads=['spo'])
        except StopBuild as ex:
            print('STOPPED at', ex)
        S.final()
        sst.close()
        import os
        if os.environ.get('KDEBUG'):
            print('CNT', S.cnt, {k: v[1] for k, v in S.dsem.items()})
    return nc


_NC = None


def kernel(**inp):
    global _NC
    f = lambda a: np.ascontiguousarray(np.asarray(a, dtype=np.float32))
    if _NC is None:
        _NC = build()
    cst = make_consts()
    prm = np.concatenate([f(inp['conv_w'][0]).reshape(66, 128), f(inp['conv_b'][0]).reshape(22, 128),
                          f(inp['hgrn_lb_logits']).reshape(8, 128), f(inp['pool_scale'][0]).reshape(2, 128),
                          f(inp['hgrn_onorm_g'][0]).reshape(4, 128), f(inp['ln1_g'][0]).reshape(8, 128),
                          f(inp['ln2_g'][0]).reshape(8, 128), f(inp['mem_norm_g'][0]).reshape(8, 128)], axis=0)
    shared = dict(lnf=f(inp['lnf_g']),
                  w_in=f(inp['w_in'][0]), w_kv=f(inp['w_mem_kv'][0]), w_out=f(inp['w_out'][0]), w_up=f(inp['w_up'][0]),
                  w_dn=f(inp['w_down'][0]), pool_w=f(inp['pool_w'][0]), prm_in=f(prm), cst=cst)
    in_maps = []
    for c in range(8):
        sl = slice(16 * c, 16 * c + 16)
        m = dict(shared)
        m['x_tok'] = f(np.concatenate([inp['x_prompt'][c], np.asarray(inp['x_sample'][sl]).reshape(128, 1024)], axis=0))
        m['mem'] = f(inp['mem_prompt'][c])
        m['spool'] = f(np.asarray(inp['state_pool'][0, sl]).reshape(240, 256))
        m['shgrn'] = f(inp['state_hgrn'][0, sl])
        m['sconv'] = f(np.asarray(inp['state_conv'][0, sl]).reshape(32, 2816))
        m['ck'] = f(np.asarray(inp['cache_mem_k'][0, sl]).reshape(16, 256, 256))
        m['cv'] = f(np.asarray(inp['cache_mem_v'][0, sl]).reshape(16, 256, 256))
        in_maps.append(m)
    res = run_bass_kernel_spmd(_NC, in_maps, core_ids=list(range(8)))
    R = res.results
    g = lambda k: np.stack([np.asarray(R[c][k], dtype=np.float32) for c in range(8)])
    y = g('y_tok')
    y_prompt = np.ascontiguousarray(y[:, :2048, :])
    y_sample = np.ascontiguousarray(y[:, 2048:, :].reshape(128, 8, 1024))
    return (y_prompt, y_sample,
            g('o_pool_p')[None], g('o_hgrn_p')[None], g('o_conv_p')[None],
            g('o_mk').reshape(1, 8, 256, 4, 64), g('o_mv').reshape(1, 8, 256, 4, 64),
            g('o_pool_s').reshape(1, 128, 15, 256), g('o_hgrn_s').reshape(1, 128, 4, 128, 128),
            g('o_conv_s').reshape(1, 128, 2, 2816))
```
